# Optimizing a Trainium2 kernel written in Bass

```python
import math
import jax, jax.numpy as jnp
from jax import lax
import numpy as np

D_MODEL = 2048
BATCH = 2
SEQ = 8192
DEPTH = 4

EPS = 1e-6
CONV_K = 4
D_FF = ((8 * D_MODEL // 3 + 127) // 128) * 128

GDN_HEADS = 4
GDN_DK = 128
GDN_DV = 128
GDN_CHUNK = 64
GDN_CONV_CH = GDN_HEADS * (2 * GDN_DK + GDN_DV)

NSA_HEADS = 4
NSA_KV_GROUPS = 2
NSA_DK = 128
NSA_DV = 128
CMP_BLOCK = 32
CMP_STRIDE = 16
SEL_BLOCK = 64
N_SELECT = 16
WINDOW = 512
Q_BLOCK = 128
SEL_OVERLAP_WEIGHTS = (1.0, 2.0, 2.0, 2.0, 1.0)
FORCE_SCORE = 1e4
NEG = -1e30

SSM_HEADS = 16
SSM_HEAD_DIM = 64
SSM_GROUPS = 2
SSM_STATE = 128
SSM_CHUNK = 128
SSM_INNER = SSM_HEADS * SSM_HEAD_DIM
SSM_CONV_CH = SSM_INNER + 2 * SSM_GROUPS * SSM_STATE

REL_BUCKETS = 32
REL_MAX_EXACT = 16
REL_MAX_DIST = 1024

IN_SIZES = (GDN_HEADS * GDN_DK, GDN_HEADS * GDN_DK, GDN_HEADS * GDN_DV, GDN_HEADS * GDN_DV, GDN_HEADS, GDN_HEADS,
            NSA_HEADS * NSA_DK, NSA_KV_GROUPS * NSA_DK, NSA_KV_GROUPS * NSA_DV, NSA_KV_GROUPS * NSA_DK,
            NSA_KV_GROUPS * NSA_DV, NSA_KV_GROUPS * NSA_DK, NSA_KV_GROUPS * NSA_DV, 3 * NSA_HEADS,
            SSM_INNER, SSM_CONV_CH, SSM_HEADS,
            3 * D_MODEL)
N_IN = sum(IN_SIZES)

kernel_name = "hybrid_gdn_nsa_ssd_macaron"


def rmsnorm(x, g):
    xf = x.astype(jnp.float32)
    y = xf * lax.rsqrt(jnp.mean(xf * xf, axis=-1, keepdims=True) + EPS)
    return (y * g.astype(jnp.float32)).astype(x.dtype)


def l2norm(t):
    return t * lax.rsqrt(jnp.sum(t * t, axis=-1, keepdims=True) + EPS)


def swiglu(h, w_up, w_down):
    a, b = jnp.split(h @ w_up, 2, axis=-1)
    return (jax.nn.silu(a) * b) @ w_down


def causal_dwconv(x, w):
    return lax.conv_general_dilated(x, w[:, None, :], window_strides=(1,), padding=[(CONV_K - 1, 0)],
                                    dimension_numbers=('NWC', 'WIO', 'NWC'), feature_group_count=x.shape[-1])


def rel_bucket(dist):
    n = jnp.maximum(dist, 0)
    nf = jnp.maximum(n, 1).astype(jnp.float32)
    large = REL_MAX_EXACT + (jnp.log(nf / REL_MAX_EXACT) / math.log(REL_MAX_DIST / REL_MAX_EXACT)
                             * (REL_BUCKETS - REL_MAX_EXACT)).astype(jnp.int32)
    large = jnp.minimum(large, REL_BUCKETS - 1)
    return jnp.where(n < REL_MAX_EXACT, n, large)


def gated_deltanet(q, k, v, z, a, b, conv_w, a_log, dt_bias, norm_g):
    f32 = jnp.float32
    Bsz, T, _ = q.shape
    H, C = GDN_HEADS, GDN_CHUNK
    n_c = T // C
    qk_w = H * GDN_DK
    qkv = jax.nn.silu(causal_dwconv(jnp.concatenate([q, k, v], axis=-1), conv_w)).astype(f32)

    def heads(t):
        return t.reshape(Bsz, n_c, C, H, -1).transpose(0, 3, 1, 2, 4)

    qh = l2norm(heads(qkv[..., :qk_w])) * GDN_DK ** -0.5
    kh = l2norm(heads(qkv[..., qk_w:2 * qk_w]))
    vh = heads(qkv[..., 2 * qk_w:])
    beta = jax.nn.sigmoid(b.astype(f32)).reshape(Bsz, n_c, C, H).transpose(0, 3, 1, 2)
    g = -jnp.exp(a_log.astype(f32)) * jax.nn.softplus(a.astype(f32) + dt_bias.astype(f32))
    gc = jnp.cumsum(g.reshape(Bsz, n_c, C, H).transpose(0, 3, 1, 2), axis=-1)
    causal = jnp.tril(jnp.ones((C, C), bool))
    strict = jnp.tril(jnp.ones((C, C), bool), -1)
    diff = gc[..., :, None] - gc[..., None, :]
    decay = jnp.where(causal, jnp.exp(jnp.where(causal, diff, 0.0)), 0.0)
    kb = kh * beta[..., None]
    m = jnp.where(strict, jnp.einsum('bhncd,bhnsd->bhncs', kb, kh) * decay, 0.0) + jnp.eye(C, dtype=f32)
    rhs = jnp.concatenate([vh * beta[..., None], kb * jnp.exp(gc)[..., None]], axis=-1)
    sol = lax.linalg.triangular_solve(m, rhs, left_side=True, lower=True, unit_diagonal=True)
    u, w = sol[..., :GDN_DV], sol[..., GDN_DV:]
    attn = jnp.einsum('bhncd,bhnsd->bhncs', qh, kh) * decay
    q_dec = qh * jnp.exp(gc)[..., None]
    k_dec = kh * jnp.exp(gc[..., -1:] - gc)[..., None]
    g_last = jnp.exp(gc[..., -1])

    def step(state, inp):
        u_c, w_c, a_c, qd_c, kd_c, gl_c = inp
        v_new = u_c - jnp.einsum('bhck,bhkv->bhcv', w_c, state)
        o_c = jnp.einsum('bhck,bhkv->bhcv', qd_c, state) + jnp.einsum('bhcs,bhsv->bhcv', a_c, v_new)
        state = state * gl_c[..., None, None] + jnp.einsum('bhck,bhcv->bhkv', kd_c, v_new)
        return state, o_c

    xs = (jnp.moveaxis(u, 2, 0), jnp.moveaxis(w, 2, 0), jnp.moveaxis(attn, 2, 0),
          jnp.moveaxis(q_dec, 2, 0), jnp.moveaxis(k_dec, 2, 0), jnp.moveaxis(g_last, 2, 0))
    _, o = lax.scan(step, jnp.zeros((Bsz, H, GDN_DK, GDN_DV), f32), xs)
    o = o.transpose(1, 0, 3, 2, 4).reshape(Bsz, T, H, GDN_DV)
    o = rmsnorm(o, norm_g) * jax.nn.silu(z.astype(f32).reshape(Bsz, T, H, GDN_DV))
    return o.reshape(Bsz, T, H * GDN_DV).astype(z.dtype)


def nsa_attention(q_in, kc, vc, ks, vs, kw, vw, gate_logits, q_norm, k_norm, pe_k, pe_v, w_ck, w_cv, rel_table):
    f32 = jnp.float32
    Bsz, T, _ = q_in.shape
    G, Hg = NSA_KV_GROUPS, NSA_HEADS // NSA_KV_GROUPS
    n_cmp = T // CMP_STRIDE - 1
    n_sel = T // SEL_BLOCK
    n_top = min(N_SELECT, n_sel)
    n_qb = T // Q_BLOCK
    ratio = SEL_BLOCK // CMP_STRIDE

    q = rmsnorm(q_in.reshape(Bsz, T, NSA_HEADS, NSA_DK), q_norm).astype(f32) * NSA_DK ** -0.5
    q_blocks = q.reshape(Bsz, n_qb, Q_BLOCK, G, Hg, NSA_DK).transpose(1, 0, 3, 4, 2, 5)

    def groups(t):
        return t.reshape(Bsz, T, G, -1).astype(f32)

    def compress(t, pe, w):
        c = groups(t).reshape(Bsz, T // CMP_STRIDE, CMP_STRIDE, G, -1)
        blocks = jnp.concatenate([c[:, :-1], c[:, 1:]], axis=2)
        return jnp.einsum('bnlgd,lde->bgne', blocks + pe.astype(f32)[:, None, :], w.astype(f32))

    k_cmp = rmsnorm(compress(kc, pe_k, w_ck), k_norm)
    v_cmp = compress(vc, pe_v, w_cv)
    k_sel = rmsnorm(groups(ks), k_norm).reshape(Bsz, n_sel, SEL_BLOCK, G, NSA_DK).transpose(0, 3, 1, 2, 4)
    v_sel = groups(vs).reshape(Bsz, n_sel, SEL_BLOCK, G, NSA_DV).transpose(0, 3, 1, 2, 4)
    pad = ((0, 0), (0, 0), (WINDOW, 0), (0, 0))
    k_win = jnp.pad(rmsnorm(groups(kw), k_norm).transpose(0, 2, 1, 3), pad)
    v_win = jnp.pad(groups(vw).transpose(0, 2, 1, 3), pad)

    table = rel_table.astype(f32).T.reshape(G, Hg, REL_BUCKETS)
    cmp_end = jnp.arange(n_cmp) * CMP_STRIDE + CMP_BLOCK - 1
    blk = jnp.arange(n_sel)
    bi = jnp.arange(Bsz)[:, None, None, None]
    gi = jnp.arange(G)[None, :, None, None]
    gi6 = jnp.arange(G)[None, :, None, None, None, None]
    hi6 = jnp.arange(Hg)[None, None, :, None, None, None]

    def block(args):
        qb, qi = args
        t = qi * Q_BLOCK + jnp.arange(Q_BLOCK)
        dist = t[:, None] - cmp_end[None, :]
        ok = dist >= 0
        s = jnp.einsum('bghqd,bgnd->bghqn', qb, k_cmp) + table[:, :, rel_bucket(dist)]
        p_cmp = jnp.where(ok, jax.nn.softmax(jnp.where(ok, s, NEG), axis=-1), 0.0)
        o_cmp = jnp.einsum('bghqn,bgnd->bghqd', p_cmp, v_cmp)
        imp_c = jnp.pad(p_cmp.sum(axis=2), ((0, 0), (0, 0), (0, 0), (1, ratio * n_sel - n_cmp)))
        imp = sum(wt * imp_c[..., j:j + ratio * n_sel:ratio] for j, wt in enumerate(SEL_OVERLAP_WEIGHTS))
        cur = t // SEL_BLOCK
        ok_s = blk[None, :] <= cur[:, None]
        forced = ok_s & ((blk[None, :] == 0) | (blk[None, :] >= cur[:, None] - 1))
        score = jnp.where(forced, FORCE_SCORE, jnp.where(ok_s, imp, -FORCE_SCORE))
        _, idx = lax.top_k(score, n_top)
        kg = k_sel[bi, gi, idx]
        vg = v_sel[bi, gi, idx]
        dist = t[:, None, None] - (idx[..., None] * SEL_BLOCK + jnp.arange(SEL_BLOCK))
        ok = (dist >= 0)[:, :, None]
        s = jnp.einsum('bghqd,bgqkld->bghqkl', qb, kg) + table[gi6, hi6, rel_bucket(dist)[:, :, None]]
        s = jnp.where(ok, s, NEG).reshape(Bsz, G, Hg, Q_BLOCK, n_top * SEL_BLOCK)
        p = jax.nn.softmax(s, axis=-1).reshape(Bsz, G, Hg, Q_BLOCK, n_top, SEL_BLOCK)
        o_sel = jnp.einsum('bghqkl,bgqkld->bghqd', p, vg)
        kwb = lax.dynamic_slice_in_dim(k_win, qi * Q_BLOCK, Q_BLOCK + WINDOW, axis=2)
        vwb = lax.dynamic_slice_in_dim(v_win, qi * Q_BLOCK, Q_BLOCK + WINDOW, axis=2)
        pos = qi * Q_BLOCK - WINDOW + jnp.arange(Q_BLOCK + WINDOW)
        dist = t[:, None] - pos[None, :]
        ok = (dist >= 0) & (dist < WINDOW) & (pos >= 0)[None, :]
        s = jnp.einsum('bghqd,bgkd->bghqk', qb, kwb) + table[:, :, rel_bucket(dist)]
        p = jax.nn.softmax(jnp.where(ok, s, NEG), axis=-1)
        o_win = jnp.einsum('bghqk,bgkd->bghqd', p, vwb)
        return jnp.stack([o_cmp, o_sel, o_win], axis=-2)

    out = lax.map(block, (q_blocks, jnp.arange(n_qb)))
    out = out.transpose(1, 0, 4, 2, 3, 5, 6).reshape(Bsz, T, NSA_HEADS, 3, NSA_DV)
    gates = jax.nn.sigmoid(gate_logits.astype(f32)).reshape(Bsz, T, NSA_HEADS, 3)
    o = jnp.einsum('bthr,bthrd->bthd', gates, out)
    return o.reshape(Bsz, T, NSA_HEADS * NSA_DV).astype(q_in.dtype)


def mamba2_ssd(z, xbc, dt, conv_w, conv_b, dt_bias, a_log, d_skip, norm_g):
    f32 = jnp.float32
    Bsz, T, _ = z.shape
    H, P, G, N, L = SSM_HEADS, SSM_HEAD_DIM, SSM_GROUPS, SSM_STATE, SSM_CHUNK
    n_c = T // L
    xbc = jax.nn.silu(causal_dwconv(xbc, conv_w) + conv_b).astype(f32)
    xs = xbc[..., :SSM_INNER].reshape(Bsz, n_c, L, H, P)
    bm = jnp.repeat(xbc[..., SSM_INNER:SSM_INNER + G * N].reshape(Bsz, n_c, L, G, N), H // G, axis=3)
    cm = jnp.repeat(xbc[..., SSM_INNER + G * N:].reshape(Bsz, n_c, L, G, N), H // G, axis=3)
    dt = jax.nn.softplus(dt.astype(f32) + dt_bias.astype(f32)).reshape(Bsz, n_c, L, H)
    ac = jnp.cumsum(dt * (-jnp.exp(a_log.astype(f32))), axis=2)
    xdt = xs * dt[..., None]
    ach = jnp.swapaxes(ac, 2, 3)
    causal = jnp.tril(jnp.ones((L, L), bool))
    seg = ach[..., :, None] - ach[..., None, :]
    decay = jnp.where(causal, jnp.exp(jnp.where(causal, seg, 0.0)), 0.0)
    scores = jnp.einsum('bclhn,bcshn->bchls', cm, bm) * decay
    y = jnp.einsum('bchls,bcshp->bclhp', scores, xdt)
    states = jnp.einsum('bclhn,bclhp->bchpn', bm * jnp.exp(ac[:, :, -1:, :] - ac)[..., None], xdt)
    chunk_decay = jnp.exp(ac[:, :, -1, :])

    def step(s, inp):
        st, cd = inp
        return s * cd[..., None, None] + st, s

    _, s_in = lax.scan(step, jnp.zeros((Bsz, H, P, N), f32),
                       (jnp.moveaxis(states, 1, 0), jnp.moveaxis(chunk_decay, 1, 0)))
    s_in = jnp.moveaxis(s_in, 0, 1)
    y = y + jnp.einsum('bclhn,bchpn->bclhp', cm * jnp.exp(ac)[..., None], s_in) + xs * d_skip.astype(f32)[:, None]
    y = y.reshape(Bsz, T, SSM_INNER) * jax.nn.silu(z.astype(f32))
    y = y.reshape(Bsz, T, G, SSM_INNER // G)
    y = y * lax.rsqrt(jnp.mean(y * y, axis=-1, keepdims=True) + EPS)
    return (y.reshape(Bsz, T, SSM_INNER) * norm_g.astype(f32)).astype(z.dtype)


def setup_inputs(seed: int = 0) -> dict:
    key = jax.random.key(seed)
    keys = iter(jax.random.split(key, 40))
    f32 = jnp.float32
    Ld = DEPTH

    def nrm(shape, scale):
        return scale * jax.random.normal(next(keys), shape, f32)

    def gain(shape):
        return 1.0 + 0.02 * jax.random.normal(next(keys), shape, f32)

    def a_log(n):
        return jnp.log(jax.random.uniform(next(keys), (Ld, n), f32, 1.0, 16.0))

    def dt_bias(n):
        dtv = jnp.exp(jax.random.uniform(next(keys), (Ld, n), f32, math.log(1e-3), math.log(1e-1)))
        return dtv + jnp.log(-jnp.expm1(-dtv))

    return {
        "x": nrm((BATCH, SEQ, D_MODEL), 1.0),
        "rel_table": nrm((REL_BUCKETS, NSA_HEADS), 0.5),
        "g_ffn1": gain((Ld, D_MODEL)),
        "w_up1": nrm((Ld, D_MODEL, 2 * D_FF), D_MODEL ** -0.5),
        "w_down1": nrm((Ld, D_FF, D_MODEL), D_FF ** -0.5),
        "g_mix": gain((Ld, D_MODEL)),
        "w_in": nrm((Ld, D_MODEL, N_IN), D_MODEL ** -0.5),
        "gdn_conv": nrm((Ld, CONV_K, GDN_CONV_CH), CONV_K ** -0.5),
        "gdn_a_log": a_log(GDN_HEADS),
        "gdn_dt_bias": dt_bias(GDN_HEADS),
        "gdn_norm": gain((Ld, GDN_DV)),
        "nsa_q_norm": gain((Ld, NSA_DK)),
        "nsa_k_norm": gain((Ld, NSA_DK)),
        "nsa_pe_k": nrm((Ld, CMP_BLOCK, NSA_DK), 0.1),
        "nsa_pe_v": nrm((Ld, CMP_BLOCK, NSA_DV), 0.1),
        "nsa_w_ck": nrm((Ld, CMP_BLOCK, NSA_DK, NSA_DK), (CMP_BLOCK * NSA_DK) ** -0.5),
        "nsa_w_cv": nrm((Ld, CMP_BLOCK, NSA_DV, NSA_DV), (CMP_BLOCK * NSA_DV) ** -0.5),
        "ssm_conv_w": nrm((Ld, CONV_K, SSM_CONV_CH), CONV_K ** -0.5),
        "ssm_conv_b": nrm((Ld, SSM_CONV_CH), 0.02),
        "ssm_dt_bias": dt_bias(SSM_HEADS),
        "ssm_a_log": a_log(SSM_HEADS),
        "ssm_d": 1.0 + 0.1 * jax.random.normal(next(keys), (Ld, SSM_HEADS), f32),
        "ssm_norm": gain((Ld, SSM_INNER)),
        "p_a": nrm((Ld, GDN_HEADS * GDN_DV, D_MODEL), (GDN_HEADS * GDN_DV) ** -0.5),
        "p_b": nrm((Ld, NSA_HEADS * NSA_DV, D_MODEL), (NSA_HEADS * NSA_DV) ** -0.5),
        "p_c": nrm((Ld, SSM_INNER, D_MODEL), SSM_INNER ** -0.5),
        "w_o": nrm((Ld, D_MODEL, D_MODEL), D_MODEL ** -0.5),
        "g_ffn2": gain((Ld, D_MODEL)),
        "w_up2": nrm((Ld, D_MODEL, 2 * D_FF), D_MODEL ** -0.5),
        "w_down2": nrm((Ld, D_FF, D_MODEL), D_FF ** -0.5),
    }


def reference(x, rel_table, g_ffn1, w_up1, w_down1, g_mix, w_in, gdn_conv, gdn_a_log, gdn_dt_bias, gdn_norm,
              nsa_q_norm, nsa_k_norm, nsa_pe_k, nsa_pe_v, nsa_w_ck, nsa_w_cv, ssm_conv_w, ssm_conv_b,
              ssm_dt_bias, ssm_a_log, ssm_d, ssm_norm, p_a, p_b, p_c, w_o, g_ffn2, w_up2, w_down2):
    Bsz, T, _ = x.shape
    split_at = np.cumsum(IN_SIZES)[:-1].tolist()
    for l in range(DEPTH):
        x = x + 0.5 * swiglu(rmsnorm(x, g_ffn1[l]), w_up1[l], w_down1[l])
        h = rmsnorm(x, g_mix[l])
        (a_q, a_k, a_v, a_z, a_a, a_b, b_q, b_kc, b_vc, b_ks, b_vs, b_kw, b_vw, b_g,
         c_z, c_xbc, c_dt, m_gate) = jnp.split(h @ w_in[l], split_at, axis=-1)
        y_a = gated_deltanet(a_q, a_k, a_v, a_z, a_a, a_b, gdn_conv[l], gdn_a_log[l], gdn_dt_bias[l], gdn_norm[l])
        y_b = nsa_attention(b_q, b_kc, b_vc, b_ks, b_vs, b_kw, b_vw, b_g, nsa_q_norm[l], nsa_k_norm[l],
                            nsa_pe_k[l], nsa_pe_v[l], nsa_w_ck[l], nsa_w_cv[l], rel_table)
        y_c = mamba2_ssd(c_z, c_xbc, c_dt, ssm_conv_w[l], ssm_conv_b[l], ssm_dt_bias[l], ssm_a_log[l],
                         ssm_d[l], ssm_norm[l])
        gates = jax.nn.sigmoid(m_gate).reshape(Bsz, T, 3, D_MODEL)
        merged = (gates[:, :, 0] * (y_a @ p_a[l]) + gates[:, :, 1] * (y_b @ p_b[l])
                  + gates[:, :, 2] * (y_c @ p_c[l]))
        x = x + merged @ w_o[l]
        x = x + 0.5 * swiglu(rmsnorm(x, g_ffn2[l]), w_up2[l], w_down2[l])
    return x
```

```python
import bisect
import contextlib
import math
import numpy as np
import concourse.bass as bass
import concourse.mybir as mybir
from concourse.bass_utils import run_bass_kernel_spmd

F32 = mybir.dt.float32
BF16 = mybir.dt.bfloat16
AF = mybir.ActivationFunctionType
ALU = mybir.AluOpType
AX = mybir.AxisListType


class Buf:
    __slots__ = ("name", "lw", "rd", "excl")

    def __init__(self, name):
        self.name = name
        self.excl = False
        self.lw = None
        self.rd = {}


class Prog:
    ENG = ("pe", "act", "dve", "pool", "sp")

    def __init__(self, nc):
        self.nc = nc
        self.stack = contextlib.ExitStack()
        self.q = {e: [] for e in self.ENG}
        self.cnt = {e: 0 for e in self.ENG}
        self.seen = {e: {} for e in self.ENG}
        self.dcount = {}
        self.waited = {e: set() for e in self.ENG}
        self.nbuf = 0

    def buf(self, name=None):
        self.nbuf += 1
        return Buf(name or f"b{self.nbuf}")

    def bufs(self, n, name=None):
        return [self.buf(f"{name}{i}") for i in range(n)]

    def sb(self, name, shape, dtype):
        return self.stack.enter_context(self.nc.sbuf_tensor(name, list(shape), dtype))

    def ps(self, name, shape, dtype=F32):
        return self.stack.enter_context(self.nc.psum_tensor(name, list(shape), dtype))

    def op(self, eng, fn, reads=(), writes=(), dma=None, pe_acc=False):
        deps = {}

        def add(ev):
            if ev is None:
                return
            k, v = ev
            if deps.get(k, 0) < v:
                deps[k] = v

        for b in reads:
            add(b.lw)
            if b.excl:
                for k, v in b.rd.items():
                    if k != eng:
                        add((k, v))
        for b in writes:
            if not (pe_acc and b.lw is not None and b.lw[0] == "pe"):
                add(b.lw)
            for k, v in b.rd.items():
                add((k, v))
        waits = []
        seen = self.seen[eng]
        for k, v in deps.items():
            if seen.get(k, 0) >= v:
                continue
            seen[k] = v
            waits.append((k, v))
            if k in self.waited:
                self.waited[k].add(v)
        if dma is None:
            self.cnt[eng] += 1
            ev = (eng, self.cnt[eng])
        else:
            key = dma if dma.startswith("c:") else "d:" + dma
            self.dcount[key] = self.dcount.get(key, 0) + 1
            ev = (key, self.dcount[key])
        for b in reads:
            if b.rd.get(ev[0], 0) < ev[1]:
                b.rd[ev[0]] = ev[1]
        for b in writes:
            b.lw = ev
            b.rd = {}
        self.q[eng].append((waits, fn, ev))
        return ev

    def barrier(self):
        evs = [(e, self.cnt[e]) for e in self.ENG if self.cnt[e] > 0]
        evs += [(k, c) for k, c in self.dcount.items()]
        for eng in self.ENG:
            waits = []
            seen = self.seen[eng]
            for k, v in evs:
                if seen.get(k, 0) >= v:
                    continue
                seen[k] = v
                waits.append((k, v))
                if k in self.waited:
                    self.waited[k].add(v)
            if waits:
                self.q[eng].append((waits, None, None))

    def emit(self):
        nc = self.nc
        self.barrier()
        sems = {e: self.stack.enter_context(nc.semaphore("s_" + e)) for e in self.ENG}
        for i, k in enumerate(sorted(self.dcount)):
            sems[k] = self.stack.enter_context(nc.semaphore(f"sd{i}"))
        miles = {e: sorted(self.waited[e]) for e in self.ENG}

        def val(k, v):
            if k in miles:
                return bisect.bisect_right(miles[k], v)
            return v if k.startswith("c:") else 16 * v

        wsets = {e: self.waited[e] for e in self.ENG}
        with nc.Block() as block:
            decs = {"pe": block.tensor, "act": block.scalar, "dve": block.vector,
                    "pool": block.gpsimd, "sp": block.sync}
            for e in self.ENG:
                items = self.q[e]

                def body(engobj, items=items, e=e):
                    for waits, fn, ev in items:
                        for k, v in waits:
                            engobj.wait_ge(sems[k], val(k, v))
                        if fn is None:
                            continue
                        ins = fn(engobj)
                        if ev[0] in miles:
                            if ev[1] in wsets[ev[0]]:
                                ins.then_inc(sems[ev[0]], 1)
                        else:
                            ins.then_inc(sems[ev[0]], 1 if ev[0].startswith("c:") else 16)
                decs[e](body)
        self.stack.close()


def MM(P, out, lhsT, rhs, start, stop, reads, writes):
    P.op("pe", lambda e: e.matmul(out, lhsT=lhsT, rhs=rhs, start=start, stop=stop),
         reads, writes, pe_acc=not start)


def TR(P, out, in_, ident, reads, writes):
    P.op("pe", lambda e: e.transpose(out, in_, ident), reads, writes)


def ACT(P, out, in_, func, reads, writes, bias=None, scale=None, accum=None):
    kw = {}
    if bias is not None:
        kw["bias"] = bias
    if scale is not None:
        kw["scale"] = scale
    if accum is not None:
        kw["accum_out"] = accum
    P.op("act", lambda e: e.activation(out=out, in_=in_, func=func, **kw), reads, writes)


def TT(P, eng, out, in0, in1, op, reads, writes):
    P.op(eng, lambda e: e.tensor_tensor(out=out, in0=in0, in1=in1, op=op), reads, writes)


def TS(P, eng, out, in0, s1, s2, op0, op1, reads, writes, accum=None):
    if op1 is None:
        P.op(eng, lambda e: e.tensor_scalar(out=out, in0=in0, scalar1=s1, scalar2=None, op0=op0), reads, writes)
    elif accum is None:
        P.op(eng, lambda e: e.tensor_scalar(out=out, in0=in0, scalar1=s1, scalar2=s2, op0=op0, op1=op1), reads, writes)
    else:
        P.op(eng, lambda e: e.tensor_scalar(out=out, in0=in0, scalar1=s1, scalar2=s2, op0=op0, op1=op1,
                                            accum_out=accum), reads, writes)


def STT(P, out, in0, scalar, in1, op0, op1, reads, writes):
    P.op("dve", lambda e: e.scalar_tensor_tensor(out=out, in0=in0, scalar=scalar, in1=in1, op0=op0, op1=op1),
         reads, writes)


def CP(P, eng, out, in_, reads, writes):
    if eng == "act":
        P.op("act", lambda e: e.copy(out=out, in_=in_), reads, writes)
    else:
        P.op(eng, lambda e: e.tensor_copy(out=out, in_=in_), reads, writes)


def MS(P, eng, ap, val, writes):
    P.op(eng, lambda e: e.memset(ap, val), (), writes)


def DMA(P, eng, out, in_, reads, writes, key):
    P.op(eng, lambda e: e.dma_start(out=out, in_=in_), reads, writes, dma=key)


D_MODEL = 2048
DC = 16
D_FF = 5504
FC = 43
EPS = 1e-6
WSLOT = 5504
NWS = 6


class RCtx:
    def __init__(self, P, TT_):
        self.P = P
        self.TT = TT_
        n = TT_
        self.xT = P.sb("xT", [128, DC * n], F32)
        self.Bx = P.bufs(DC, "x")
        self.hT = P.sb("hT", [128, DC * n], BF16)
        self.Bh = P.bufs(DC, "h")
        self.big = P.sb("big", [128, FC * n], BF16)
        self.Bbig = P.bufs(FC, "big")
        self.ws = [P.sb(f"ws{i}", [128, WSLOT], BF16) for i in range(NWS)]
        self.Bws = P.bufs(NWS, "ws")
        self.wsi = 0
        self.sq = [P.sb(f"sq{i}", [128, n], F32) for i in range(2)]
        self.Bsq = P.bufs(2, "sq")
        self.rstd = P.sb("rstd", [128, n], F32)
        self.Brstd = P.buf("rstd")
        self.tmp = [P.sb(f"tmp{i}", [128, n], F32) for i in range(6)]
        self.Btmp = P.bufs(6, "tmp")
        self.tmpi = 0
        self.yc32 = P.sb("yc32", [128, 4 * n], F32)
        self.Byc = P.bufs(4, "yc")
        self.ones = P.sb("ones", [128, 128], F32)
        self.Bones = P.buf("ones")
        self.ps = [P.ps(f"ps{i}", [128, 512]) for i in range(8)]
        self.Bps = P.bufs(8, "ps")
        for b_ in self.Bps:
            b_.excl = True
        MS(P, "dve", self.ones[:], 1.0, [self.Bones])
        self.sqi = 0

    def x(self, k):
        return self.xT[:, k * self.TT:(k + 1) * self.TT]

    def h(self, k):
        return self.hT[:, k * self.TT:(k + 1) * self.TT]

    def bg(self, k):
        return self.big[:, k * self.TT:(k + 1) * self.TT]

    def slot(self):
        s = self.wsi % NWS
        self.wsi += 1
        return s

    def tmpslot(self):
        s = self.tmpi % 6
        self.tmpi += 1
        return s


def r_rstd(C, srcs, Bsrcs, nelem, out_rstd, Bout, psb):
    P = C.P
    n = len(srcs)
    for k in range(n):
        s = C.sqi % 2
        C.sqi += 1
        ACT(P, C.sq[s][:], srcs[k], AF.Square, [Bsrcs[k]], [C.Bsq[s]])
        MM(P, C.ps[psb][:, :C.TT], C.ones[:], C.sq[s][:], k == 0, k == n - 1, [C.Bones, C.Bsq[s]], [C.Bps[psb]])
    ACT(P, out_rstd, C.ps[psb][:, :C.TT], AF.Sqrt, [C.Bps[psb]], [Bout], bias=EPS, scale=1.0 / nelem)
    P.op("dve", lambda e: e.reciprocal(out=out_rstd, in_=out_rstd), [Bout], [Bout])


def r_norm(C, g_sb, Bg):
    P = C.P
    r_rstd(C, [C.x(k) for k in range(DC)], C.Bx, D_MODEL, C.rstd[:], C.Brstd, 6)
    for k in range(DC):
        STT(P, C.h(k), C.x(k), g_sb[:, k:k + 1], C.rstd[:], ALU.mult, ALU.mult,
            [C.Bx[k], Bg, C.Brstd], [C.Bh[k]])


def wload(C, s, dram_ap, nk, ncols):
    P = C.P
    out = C.ws[s][:, 0:nk * ncols].rearrange("p (k c) -> p k c", k=nk)
    DMA(P, "pool", out, dram_ap.rearrange("(k p) c -> p k c", p=128), [], [C.Bws[s]], f"ws{s}")


def r_ffn(C, g_sb, Bg, wu, wd):
    P = C.P
    n = C.TT
    r_norm(C, g_sb, Bg)
    GW = 256
    groups = [(c0, min(GW, D_FF - c0)) for c0 in range(0, D_FF, GW)]

    def load_up(gi):
        c0, nc_ = groups[gi]
        sa, sb_ = C.slot(), C.slot()
        wload(C, sa, wu[:, c0:c0 + nc_], DC, nc_)
        wload(C, sb_, wu[:, D_FF + c0:D_FF + c0 + nc_], DC, nc_)
        return sa, sb_

    pend = [load_up(0)]
    cnt = 0
    for gi, (c0, nc_) in enumerate(groups):
        if gi + 1 < len(groups):
            pend.append(load_up(gi + 1))
        sa, sb_ = pend.pop(0)
        for sub in range(nc_ // 128):
            c = (c0 + sub * 128) // 128
            pa, pb = cnt % 2, 2 + cnt % 2
            cnt += 1
            for k in range(DC):
                MM(P, C.ps[pa][:, :n], C.ws[sa][:, k * nc_ + sub * 128:k * nc_ + sub * 128 + 128], C.h(k),
                   k == 0, k == DC - 1, [C.Bws[sa], C.Bh[k]], [C.Bps[pa]])
            for k in range(DC):
                MM(P, C.ps[pb][:, :n], C.ws[sb_][:, k * nc_ + sub * 128:k * nc_ + sub * 128 + 128], C.h(k),
                   k == 0, k == DC - 1, [C.Bws[sb_], C.Bh[k]], [C.Bps[pb]])
            t = C.tmpslot()
            ACT(P, C.tmp[t][:], C.ps[pa][:, :n], AF.Silu, [C.Bps[pa]], [C.Btmp[t]])
            TT(P, "dve", C.bg(c), C.tmp[t][:], C.ps[pb][:, :n], ALU.mult, [C.Btmp[t], C.Bps[pb]], [C.Bbig[c]])

    def load_dn(j):
        s = C.slot()
        wload(C, s, wd[:, j * 128:(j + 1) * 128], FC, 128)
        return s

    pend = [load_dn(0)]
    for j in range(DC):
        if j + 1 < DC:
            pend.append(load_dn(j + 1))
        s = pend.pop(0)
        pb = 4 + j % 2
        for c in range(FC):
            MM(P, C.ps[pb][:, :n], C.ws[s][:, c * 128:(c + 1) * 128], C.bg(c), c == 0, c == FC - 1,
               [C.Bws[s], C.Bbig[c]], [C.Bps[pb]])
        STT(P, C.x(j), C.ps[pb][:, :n], 0.5, C.x(j), ALU.mult, ALU.add, [C.Bps[pb], C.Bx[j]], [C.Bx[j]])


def r_merge(C, t0, yin, gm_sb, Bgm, sn_sb, Bsn, wg, pa_w, pb_w, pc_w, wo):
    P = C.P
    n = C.TT
    r_norm(C, gm_sb, Bgm)
    for k in range(8):
        DMA(P, "pool", C.bg(k), yin[k * 128:(k + 1) * 128, t0:t0 + n], [], [C.Bbig[k]], f"y{k}")
    for g in range(2):
        for q in range(4):
            r = 1024 + (g * 4 + q) * 128
            DMA(P, "sp", C.yc32[:, q * n:(q + 1) * n], yin[r:r + 128, t0:t0 + n], [], [C.Byc[q]], f"yc{q}")
        t = C.tmpslot()
        r_rstd(C, [C.yc32[:, q * n:(q + 1) * n] for q in range(4)], C.Byc, 512, C.tmp[t][:], C.Btmp[t], 7)
        for q in range(4):
            k = g * 4 + q
            STT(P, C.bg(8 + k), C.yc32[:, q * n:(q + 1) * n], sn_sb[:, k:k + 1], C.tmp[t][:], ALU.mult, ALU.mult,
                [C.Byc[q], Bsn, C.Btmp[t]], [C.Bbig[8 + k]])
    GW = 256

    def load_m(j2):
        s = [C.slot() for _ in range(4)]
        for i in range(3):
            wload(C, s[i], wg[:, i * D_MODEL + j2 * GW: i * D_MODEL + (j2 + 1) * GW], DC, GW)
        o = C.ws[s[3]]
        DMA(P, "pool", o[:, 0:4 * GW].rearrange("p (k c) -> p k c", k=4),
            pa_w[:, j2 * GW:(j2 + 1) * GW].rearrange("(k p) c -> p k c", p=128), [], [C.Bws[s[3]]], f"ws{s[3]}")
        DMA(P, "pool", o[:, 4 * GW:8 * GW].rearrange("p (k c) -> p k c", k=4),
            pb_w[:, j2 * GW:(j2 + 1) * GW].rearrange("(k p) c -> p k c", p=128), [], [C.Bws[s[3]]], f"ws{s[3]}")
        DMA(P, "pool", o[:, 8 * GW:16 * GW].rearrange("p (k c) -> p k c", k=8),
            pc_w[:, j2 * GW:(j2 + 1) * GW].rearrange("(k p) c -> p k c", p=128), [], [C.Bws[s[3]]], f"ws{s[3]}")
        return s

    for j2 in range(D_MODEL // GW):
        s = load_m(j2)
        for sub in range(GW // 128):
            j = j2 * 2 + sub
            tg = []
            for i in range(3):
                for k in range(DC):
                    MM(P, C.ps[i][:, :n], C.ws[s[i]][:, k * GW + sub * 128:k * GW + sub * 128 + 128], C.h(k),
                       k == 0, k == DC - 1, [C.Bws[s[i]], C.Bh[k]], [C.Bps[i]])
                t = C.tmpslot()
                tg.append(t)
                ACT(P, C.tmp[t][:], C.ps[i][:, :n], AF.Sigmoid, [C.Bps[i]], [C.Btmp[t]])
            o = C.ws[s[3]]
            for i, (base, nk, yoff) in enumerate(((0, 4, 0), (4 * GW, 4, 4), (8 * GW, 8, 8))):
                for k in range(nk):
                    MM(P, C.ps[3 + i][:, :n], o[:, base + k * GW + sub * 128: base + k * GW + sub * 128 + 128],
                       C.bg(yoff + k), k == 0, k == nk - 1, [C.Bws[s[3]], C.Bbig[yoff + k]], [C.Bps[3 + i]])
            for i in range(3):
                TT(P, "dve", C.tmp[tg[i]][:], C.tmp[tg[i]][:], C.ps[3 + i][:, :n], ALU.mult,
                   [C.Btmp[tg[i]], C.Bps[3 + i]], [C.Btmp[tg[i]]])
            TT(P, "pool", C.tmp[tg[0]][:], C.tmp[tg[0]][:], C.tmp[tg[1]][:], ALU.add,
               [C.Btmp[tg[0]], C.Btmp[tg[1]]], [C.Btmp[tg[0]]])
            TT(P, "dve", C.bg(16 + j), C.tmp[tg[0]][:], C.tmp[tg[2]][:], ALU.add,
               [C.Btmp[tg[0]], C.Btmp[tg[2]]], [C.Bbig[16 + j]])
    for j2 in range(D_MODEL // GW):
        s = C.slot()
        wload(C, s, wo[:, j2 * GW:(j2 + 1) * GW], DC, GW)
        for sub in range(2):
            j = j2 * 2 + sub
            pb = 6 + j % 2
            for k in range(DC):
                MM(P, C.ps[pb][:, :n], C.ws[s][:, k * GW + sub * 128:k * GW + sub * 128 + 128], C.bg(16 + k),
                   k == 0, k == DC - 1, [C.Bws[s], C.Bbig[16 + k]], [C.Bps[pb]])
            TT(P, "dve", C.x(j), C.x(j), C.ps[pb][:, :n], ALU.add, [C.Bx[j], C.Bps[pb]], [C.Bx[j]])


def build_R(mode, TC=2048, TT_=512):
    nc = bass.Bass("TRN2", target_bir_lowering=False)
    P = Prog(nc)

    def din(name, shape):
        return nc.dram_tensor(name, list(shape), F32, kind="ExternalInput").ap()

    xin = din("xin", [D_MODEL, TC])
    xo = nc.dram_tensor("xo", [D_MODEL, TC], F32, kind="ExternalOutput").ap()
    C = RCtx(P, TT_)
    gains = {}

    def gain(name, ncol=DC):
        a = din(name, [128, ncol])
        t = P.sb(name + "_sb", [128, ncol], F32)
        b = P.buf(name)
        DMA(P, "sp", t[:], a[:, :], [], [b], name)
        gains[name] = (t, b)

    if mode in ("RA", "R"):
        yin = din("yin", [D_MODEL, TC])
        gain("gm"); gain("sn", 8); gain("g2")
        wg = din("wg", [D_MODEL, 3 * D_MODEL])
        pa_w = din("pa", [512, D_MODEL]); pb_w = din("pb", [512, D_MODEL]); pc_w = din("pc", [1024, D_MODEL])
        wo = din("wo", [D_MODEL, D_MODEL])
        wu2 = din("wu2", [D_MODEL, 2 * D_FF]); wd2 = din("wd2", [D_FF, D_MODEL])
    if mode in ("A", "RA"):
        gain("g1")
        wu1 = din("wu1", [D_MODEL, 2 * D_FF]); wd1 = din("wd1", [D_FF, D_MODEL])
    n = TT_
    for ti in range(TC // n):
        t0 = ti * n
        DMA(P, "sp", C.xT[:, :].rearrange("p (k t) -> p k t", k=DC),
            xin[:, t0:t0 + n].rearrange("(k p) t -> p k t", p=128), [], C.Bx, "xin")
        if mode in ("RA", "R"):
            r_merge(C, t0, yin, gains["gm"][0], gains["gm"][1], gains["sn"][0], gains["sn"][1],
                    wg, pa_w, pb_w, pc_w, wo)
            r_ffn(C, gains["g2"][0], gains["g2"][1], wu2, wd2)
        if mode in ("A", "RA"):
            r_ffn(C, gains["g1"][0], gains["g1"][1], wu1, wd1)
        DMA(P, "sp", xo[:, t0:t0 + n].rearrange("(k p) t -> p k t", p=128),
            C.xT[:, :].rearrange("p (k t) -> p k t", k=DC), C.Bx, [], "xout")
    P.emit()
    return nc


NCH_M = 18 * 128 + 9
PJ_SMALL = 18 * 128
NEGBIG = -30000.0
DEBUG_CUT = 0


class Arena:
    def __init__(self, P, name, ncols):
        self.t = P.sb(name, [128, ncols], F32)
        self.n = ncols
        self.off = 0

    def reset(self):
        self.off = 0

    def f32(self, n):
        assert self.off + n <= self.n, (self.off, n, self.n)
        ap = self.t[:, self.off:self.off + n]
        self.off += n
        return ap

    def bf16(self, n):
        m = (n + 1) // 2
        assert self.off + m <= self.n, (self.off, m, self.n)
        ap = self.t[:, self.off:self.off + m].bitcast(BF16)[:, 0:n]
        self.off += m
        return ap


class MCtx:
    def __init__(self, P, nc, T):
        self.P = P
        self.nc = nc
        self.T = T
        self.A = Arena(P, "arena", 43000)
        self.ps = [P.ps(f"ps{i}", [128, 512]) for i in range(8)]
        self.Bps = P.bufs(8, "ps")
        for b_ in self.Bps:
            b_.excl = True
        self.ones = P.sb("ones", [128, 128], F32)
        self.Bones = P.buf("ones")
        self.ident = P.sb("ident_sb", [128, 128], F32)
        self.Bid = P.buf("ident")
        MS(P, "dve", self.ones[:], 1.0, [self.Bones])
        self.pj = nc.dram_tensor("pj", [19 * 128, T], F32).ap()
        self.Bpj = P.buf("pj")

    def din(self, name, shape):
        return self.nc.dram_tensor(name, list(shape), F32, kind="ExternalInput").ap()

    def const(self, name, shape, eng="sp"):
        a = self.din(name, shape)
        t = self.P.sb(name + "_sb", shape, F32)
        b = self.P.buf(name)
        DMA(self.P, eng, t[:], a, [], [b], name)
        return t, b


def m_inproj(M):
    P, T, A = M.P, M.T, M.A
    n = 512
    x1f = M.din("x1f", [2, D_MODEL, T])
    selb, Bsel = M.const("selb", [128, 2])
    gm, Bgm = M.const("gmix", [128, DC])
    wm = M.din("wm", [D_MODEL, NCH_M])
    A.reset()
    wsb = A.bf16(DC * NCH_M)
    Bw = P.buf("wm")
    for k0 in range(0, DC, 4):
        DMA(P, "pool", wsb[:, k0 * NCH_M:(k0 + 4) * NCH_M].rearrange("p (k c) -> p k c", k=4),
            wm[k0 * 128:(k0 + 4) * 128, :].rearrange("(k p) c -> p k c", p=128), [], [Bw], "wm")
    xa = A.f32(DC * n)
    xb = A.f32(DC * n)
    Bxa, Bxb = P.buf("xa"), P.buf("xb")
    hT = A.bf16(DC * n)
    Bh = P.bufs(DC, "h")
    sq = [A.f32(n) for _ in range(2)]
    Bsq = P.bufs(2, "sq")
    rstd = A.f32(n)
    Brstd = P.buf("rstd")
    st = [A.f32(n) for _ in range(4)]
    Bst = P.bufs(4, "st")
    cnt = 0
    for ti in range(T // n):
        t0 = ti * n
        DMA(P, "sp", xa.rearrange("p (k t) -> p k t", k=DC),
            x1f[0, :, t0:t0 + n].rearrange("(k p) t -> p k t", p=128), [], [Bxa], "xa")
        DMA(P, "sp", xb.rearrange("p (k t) -> p k t", k=DC),
            x1f[1, :, t0:t0 + n].rearrange("(k p) t -> p k t", p=128), [], [Bxb], "xb")
        TS(P, "pool", xa, xa, selb[:, 0:1], None, ALU.mult, None, [Bxa, Bsel], [Bxa])
        STT(P, xa, xb, selb[:, 1:2], xa, ALU.mult, ALU.add, [Bxa, Bxb, Bsel], [Bxa])
        for k in range(DC):
            s = k % 2
            ACT(P, sq[s], xa[:, k * n:(k + 1) * n], AF.Square, [Bxa], [Bsq[s]])
            MM(P, M.ps[6][:, :n], M.ones[:], sq[s], k == 0, k == DC - 1, [M.Bones, Bsq[s]], [M.Bps[6]])
        ACT(P, rstd, M.ps[6][:, :n], AF.Sqrt, [M.Bps[6]], [Brstd], bias=EPS, scale=1.0 / D_MODEL)
        P.op("dve", lambda e: e.reciprocal(out=rstd, in_=rstd), [Brstd], [Brstd])
        for k in range(DC):
            STT(P, hT[:, k * n:(k + 1) * n], xa[:, k * n:(k + 1) * n], gm[:, k:k + 1], rstd, ALU.mult, ALU.mult,
                [Bxa, Bgm, Brstd], [Bh[k]])
        for c in range(19):
            cols = 128 if c < 18 else 9
            pb = cnt % 4
            s = cnt % 4
            cnt += 1
            for k in range(DC):
                MM(P, M.ps[pb][0:cols, :n], wsb[:, k * NCH_M + c * 128:k * NCH_M + c * 128 + cols],
                   hT[:, k * n:(k + 1) * n], k == 0, k == DC - 1, [Bw, Bh[k]], [M.Bps[pb]])
            CP(P, "act" if cnt % 2 else "dve", st[s][0:cols, :], M.ps[pb][0:cols, :n], [M.Bps[pb]], [Bst[s]])
            DMA(P, "sp", M.pj[c * 128:c * 128 + cols, t0:t0 + n], st[s][0:cols, :], [Bst[s]], [], f"pj{s}")
    P.barrier()


def conv_silu(P, out, raw, w4, bias, n, reads, writes, eng="dve"):
    if bias is None:
        TS(P, eng, out, raw[:, 3:3 + n], w4[:, 3:4], None, ALU.mult, None, reads, writes)
    else:
        TS(P, eng, out, raw[:, 3:3 + n], w4[:, 3:4], bias, ALU.mult, ALU.add, reads, writes)
    for k in range(3):
        STT(P, out, raw[:, k:k + n], w4[:, k:k + 1], out, ALU.mult, ALU.add, reads + writes, writes)
    ACT(P, out, out, AF.Silu, writes, writes)


def load_halo(P, dst, pj_rows, t0, n, reads, writes, key, eng="sp"):
    if t0 == 0:
        MS(P, "pool", dst[:, 0:3], 0.0, writes)
        DMA(P, eng, dst[:, 3:3 + n], pj_rows[:, 0:n], reads, writes, key)
    else:
        DMA(P, eng, dst[:, 0:3 + n], pj_rows[:, t0 - 3:t0 + n], reads, writes, key)


def m_ssd(M):
    P, T, A = M.P, M.T, M.A
    n = 512
    sconv, Bsc = M.const("sconv", [128, 16])
    sbias, Bsb = M.const("sbias", [128, 4])
    sdtb, Bdtb = M.const("sdtb", [1, 4])
    salog, Balog = M.const("salog", [1, 4])
    sD, BsD = M.const("sD", [128, 2])
    negtri, Bnt = M.const("negtriS", [128, 128])
    reset, Brs = M.const("reset128", [1, 2048])
    pj = M.pj
    A.reset()
    sT = A.f32(256)
    BsT = P.bufs(4, "sT")
    MS(P, "dve", sT, 0.0, BsT)
    negA = P.sb("negA", [1, 4], F32)
    BnA = P.buf("negA")
    ACT(P, negA[:], salog[:], AF.Exp, [Balog], [BnA])
    TS(P, "dve", negA[:], negA[:], -1.0, None, ALU.mult, None, [BnA], [BnA])
    raw = [[A.f32(n + 3) for _ in range(4)] for _ in range(2)]
    Braw = [P.bufs(4, f"raw{i}") for i in range(2)]
    zr = [[A.f32(n) for _ in range(2)] for _ in range(2)]
    Bzr = [P.bufs(2, f"zr{i}") for i in range(2)]
    dtr = [A.f32(4 * n) for _ in range(2)]
    Bdtr = P.bufs(2, "dtr")
    cv = [A.f32(n) for _ in range(4)]
    Bcv = P.bufs(4, "cv")
    dA = A.f32(4 * n)
    BdA = P.buf("dA")
    ac = A.f32(4 * n)
    Bac = P.buf("ac")
    yst = [A.f32(n) for _ in range(2)]
    Byst = P.bufs(2, "yst")
    tok = A.f32(384)
    Btok = P.buf("tok")
    NB = 2
    cl3 = [A.f32(3) for _ in range(NB)]
    Bcl = P.bufs(NB, "cl3")
    e1 = [A.f32(128) for _ in range(NB)]
    Be1 = P.bufs(NB, "e1")
    sg = [A.f32(128) for _ in range(NB)]
    Bsg = P.bufs(NB, "sg")
    sc = [A.f32(128) for _ in range(NB)]
    Bscb = P.bufs(NB, "sc")
    xdt = [A.f32(64) for _ in range(NB)]
    Bxdt = P.bufs(NB, "xdt")
    xdd = [A.f32(64) for _ in range(NB)]
    Bxdd = P.bufs(NB, "xdd")
    decl = [A.f32(1) for _ in range(NB)]
    Bdecl = P.bufs(NB, "decl")
    cdec = [A.f32(128) for _ in range(NB)]
    Bcdec = P.bufs(NB, "cdec")
    ps, Bps = M.ps, M.Bps
    chrow = (14, 15, 16, 17)
    nst = T // n

    def load(si):
        s = si % 2
        t0 = si * n
        for q in range(4):
            r = chrow[q] * 128
            load_halo(P, raw[s][q], pj[r:r + 128, :], t0, n, [M.Bpj], [Braw[s][q]], f"sraw{s}{q}")
        for q in range(2):
            r = (12 + q) * 128
            DMA(P, "sp", zr[s][q], pj[r:r + 128, t0:t0 + n], [M.Bpj], [Bzr[s][q]], f"sz{s}{q}")
        DMA(P, "sp", dtr[s][0:1, :].rearrange("o (r t) -> o r t", r=4),
            pj[PJ_SMALL + 5:PJ_SMALL + 9, t0:t0 + n].rearrange("(o r) t -> o r t", o=1), [M.Bpj], [Bdtr[s]], f"sdt{s}")

    load(0)
    it = 0
    for si in range(nst):
        s = si % 2
        t0 = si * n
        if si + 1 < nst:
            load(si + 1)
        for q in range(4):
            conv_silu(P, cv[q], raw[s][q], sconv[:, q * 4:q * 4 + 4], sbias[:, q:q + 1], n,
                      [Braw[s][q], Bsc, Bsb], [Bcv[q]])
        for q in range(2):
            ACT(P, zr[s][q], zr[s][q], AF.Silu, [Bzr[s][q]], [Bzr[s][q]])
        d = dtr[s]
        if DEBUG_CUT == 1:
            continue
        for p in range(4):
            ACT(P, d[0:1, p * n:(p + 1) * n], d[0:1, p * n:(p + 1) * n], AF.Exp, [Bdtr[s], Bdtb], [Bdtr[s]],
                bias=sdtb[0:1, p:p + 1])
        ACT(P, d[0:1, :], d[0:1, :], AF.Ln, [Bdtr[s]], [Bdtr[s]], bias=1.0)
        for p in range(4):
            TS(P, "dve", dA[0:1, p * n:(p + 1) * n], d[0:1, p * n:(p + 1) * n], negA[0:1, p:p + 1], None,
               ALU.mult, None, [Bdtr[s], BnA], [BdA])
        P.op("dve", lambda e: e.tensor_tensor_scan(out=ac[0:1, :], data0=reset[0:1, :], data1=dA[0:1, :],
                                                   initial=0.0, op0=ALU.mult, op1=ALU.add),
             [BdA, Brs], [Bac])
        if DEBUG_CUT == 2:
            continue
        for c in range(4):
            l0 = c * 128
            TR(P, ps[7][:, 0:128], cv[2][:, l0:l0 + 128], M.ident[:], [Bcv[2], M.Bid], [Bps[7]])
            TR(P, ps[7][:, 128:256], cv[0][:, l0:l0 + 128], M.ident[:], [Bcv[0], M.Bid], [Bps[7]])
            TR(P, ps[7][:, 256:384], cv[1][:, l0:l0 + 128], M.ident[:], [Bcv[1], M.Bid], [Bps[7]])
            CP(P, "act", tok, ps[7][:, 0:384], [Bps[7]], [Btok])
            MM(P, ps[6][:, 0:128], cv[2][:, l0:l0 + 128], cv[3][:, l0:l0 + 128], True, True,
               [Bcv[2], Bcv[3]], [Bps[6]])
            if DEBUG_CUT == 3:
                continue
            for p in range(4):
                o = p * n + l0
                b = it % NB
                it += 1
                pbk = 4 + p % 2
                MM(P, ps[pbk][:, 0:128], M.ones[0:1, 0:128], ac[0:1, o:o + 128], True, True,
                   [M.Bones, Bac], [Bps[pbk]])
                MM(P, ps[pbk][:, 128:129], ac[0:1, o:o + 128], M.ones[0:1, 0:1], True, True,
                   [M.Bones, Bac], [Bps[pbk]])
                MM(P, ps[pbk][:, 129:130], d[0:1, o:o + 128], M.ones[0:1, 0:1], True, True,
                   [M.Bones, Bdtr[s]], [Bps[pbk]])
                CP(P, "dve", cl3[b], ps[pbk][:, 127:130], [Bps[pbk]], [Bcl[b]])
                if DEBUG_CUT == 4:
                    continue
                ACT(P, e1[b], ps[pbk][:, 0:128], AF.Exp, [Bps[pbk]], [Be1[b]])
                STT(P, sg[b], ps[pbk][:, 0:128], cl3[b][:, 1:2], negtri[:], ALU.subtract, ALU.add,
                    [Bps[pbk], Bcl[b], Bnt], [Bsg[b]])
                ACT(P, sg[b], sg[b], AF.Exp, [Bsg[b]], [Bsg[b]])
                TT(P, "dve", sc[b], ps[6][:, 0:128], sg[b], ALU.mult, [Bps[6], Bsg[b]], [Bscb[b]])
                if DEBUG_CUT == 5:
                    continue
                TS(P, "pool", xdt[b], tok[:, 128 + p * 64:128 + (p + 1) * 64], cl3[b][:, 2:3], None, ALU.mult, None,
                   [Btok, Bcl[b]], [Bxdt[b]])
                ACT(P, decl[b], cl3[b][:, 1:2], AF.Exp, [Bcl[b]], [Bdecl[b]], bias=cl3[b][:, 0:1], scale=-1.0)
                TS(P, "pool", xdd[b], xdt[b], decl[b][:, 0:1], None, ALU.mult, None, [Bxdt[b], Bdecl[b]], [Bxdd[b]])
                TT(P, "pool", cdec[b], cv[3][:, l0:l0 + 128], e1[b], ALU.mult, [Bcv[3], Be1[b]], [Bcdec[b]])
                if DEBUG_CUT == 6:
                    continue
                pr = p // 2
                yo_ = ps[pr][(p % 2) * 64:(p % 2) * 64 + 64, 0:128]
                MM(P, yo_, xdt[b], sc[b], True, False, [Bxdt[b], Bscb[b]], [Bps[pr]])
                MM(P, yo_, sT[:, p * 64:(p + 1) * 64], cdec[b], False, True, [BsT[p], Bcdec[b]], [Bps[pr]])
                if DEBUG_CUT == 7:
                    continue
                pd = 2 + p % 2
                MM(P, ps[pd][:, 0:64], tok[:, 0:128], xdd[b], True, True, [Btok, Bxdd[b]], [Bps[pd]])
                STT(P, sT[:, p * 64:(p + 1) * 64], sT[:, p * 64:(p + 1) * 64], e1[b][:, 127:128], ps[pd][:, 0:64],
                    ALU.mult, ALU.add, [BsT[p], Be1[b], Bps[pd]], [BsT[p]])
            for pr in range(2):
                STT(P, yst[pr][:, l0:l0 + 128], cv[pr][:, l0:l0 + 128], sD[:, pr:pr + 1], ps[pr][:, 0:128],
                    ALU.mult, ALU.add, [Bcv[pr], BsD, Bps[pr]], [Byst[pr]])
        for pr in range(2):
            TT(P, "pool", yst[pr], yst[pr], zr[s][pr], ALU.mult, [Byst[pr], Bzr[s][pr]], [Byst[pr]])
            DMA(P, "sp", M.yo[256 + pr * 128:256 + (pr + 1) * 128, t0:t0 + n], yst[pr], [Byst[pr]], [], f"yc{pr}")
    P.barrier()


def build_M(T=8192, stages=("ssd", "gdn", "nsa")):
    nc = bass.Bass("TRN2", target_bir_lowering=False)
    P = Prog(nc)
    M = MCtx(P, nc, T)
    idd = M.din("ident", [128, 128])
    DMA(P, "sp", M.ident[:], idd, [], [M.Bid], "ident")
    M.yo = nc.dram_tensor("yo", [512, T], F32, kind="ExternalOutput").ap()
    m_inproj(M)
    if "ssd" in stages:
        m_ssd(M)
    if "gdn" in stages:
        m_gdn(M)
    if "nsa" in stages:
        m_nsa(M)
    P.emit()
    return nc


def gl(g, ncol=DC):
    return np.ascontiguousarray(np.asarray(g, np.float32).reshape(ncol, 128).T)


def m_cols(h):
    g = h // 2
    cols = []
    for base in (0 + 128 * h, 512 + 128 * h, 1024 + 128 * h, 1536 + 128 * h,
                 2056 + 128 * h, 2056 + 128 * (h ^ 1), 2568 + 128 * g, 2824 + 128 * g,
                 3080 + 128 * g, 3336 + 128 * g, 3592 + 128 * g, 3848 + 128 * g,
                 4116 + 256 * h, 4116 + 256 * h + 128, 5140 + 256 * h, 5140 + 256 * h + 128,
                 5140 + 1024 + 128 * g, 5140 + 1280 + 128 * g):
        cols += list(range(base, base + 128))
    cols += [2048 + h, 2052 + h, 4104 + 3 * h, 4104 + 3 * h + 1, 4104 + 3 * h + 2]
    cols += [6676 + 4 * h + i for i in range(4)]
    return np.array(cols)


def m_consts(T):
    c = {}
    c["ident"] = np.eye(128, dtype=np.float32)
    p = np.arange(128)[:, None]
    f = np.arange(128)[None, :]
    c["negtriS"] = np.where(f < p, NEGBIG, 0.0).astype(np.float32)
    r = np.ones((1, 2048), np.float32)
    r[0, ::128] = 0.0
    c["reset128"] = r
    r = np.ones((1, 512), np.float32)
    r[0, ::64] = 0.0
    c["reset64"] = r
    p = np.arange(64)[:, None]
    f = np.arange(64)[None, :]
    c["gmaskU"] = np.tile(np.where(f < p, NEGBIG, 0.0).astype(np.float32), (1, 8))
    c["gmaskL"] = np.tile(np.where(f >= p, -NEGBIG, 0.0).astype(np.float32), (1, 8))
    c["gstrict"] = np.tile((f > p).astype(np.float32), (1, 8))
    c.update(nsa_consts())
    return c


def m_layer_inputs(prm, l, h, T):
    g = h // 2
    d = {}
    d["gmix"] = gl(prm["g_mix"][l])
    d["wm"] = np.ascontiguousarray(prm["w_in"][l][:, m_cols(h)])
    cw = prm["ssm_conv_w"][l]
    cb = prm["ssm_conv_b"][l]
    chans = [256 * h + np.arange(128), 256 * h + 128 + np.arange(128), 1024 + 128 * g + np.arange(128),
             1280 + 128 * g + np.arange(128)]
    d["sconv"] = np.ascontiguousarray(np.concatenate([cw[:, ch].T for ch in chans], axis=1))
    d["sbias"] = np.ascontiguousarray(np.stack([cb[ch] for ch in chans], axis=1))
    d["sdtb"] = np.ascontiguousarray(prm["ssm_dt_bias"][l][4 * h:4 * h + 4][None, :])
    d["salog"] = np.ascontiguousarray(prm["ssm_a_log"][l][4 * h:4 * h + 4][None, :])
    dd = prm["ssm_d"][l][4 * h:4 * h + 4]
    d["sD"] = np.ascontiguousarray(np.stack([np.repeat(dd[0:2], 64), np.repeat(dd[2:4], 64)], axis=1))
    gcw = prm["gdn_conv"][l]
    d["gconv"] = np.ascontiguousarray(np.concatenate([gcw[:, q * 512 + 128 * h + np.arange(128)].T for q in range(3)], axis=1))
    d["gsc"] = np.array([[prm["gdn_a_log"][l][h], prm["gdn_dt_bias"][l][h]]], np.float32)
    d["gnorm"] = np.ascontiguousarray(prm["gdn_norm"][l][:, None])
    d["nqn"] = np.ascontiguousarray(prm["nsa_q_norm"][l][:, None])
    d["nkn"] = np.ascontiguousarray(prm["nsa_k_norm"][l][:, None])
    d["pekT"] = np.ascontiguousarray(prm["nsa_pe_k"][l].T)
    d["pevT"] = np.ascontiguousarray(prm["nsa_pe_v"][l].T)
    d["wck"] = np.ascontiguousarray(prm["nsa_w_ck"][l].transpose(1, 0, 2).reshape(128, 4096))
    d["wcv"] = np.ascontiguousarray(prm["nsa_w_cv"][l].transpose(1, 0, 2).reshape(128, 4096))
    rt = prm["rel_table"]
    d["ntab"] = np.ascontiguousarray(rt[:, [h, h ^ 1]])
    d["nt31"] = np.ascontiguousarray(np.tile(rt[31, [h, h ^ 1]][None, :], (128, 1)))
    return d


def m_gdn(M):
    P, T, A = M.P, M.T, M.A
    n = 512
    NCK = 8
    gconv, Bgc = M.const("gconv", [128, 12])
    gsc, Bgs = M.const("gsc", [1, 2])
    gnorm, Bgn = M.const("gnorm", [128, 1])
    mU, BmU = M.const("gmaskU", [64, 512])
    mL, BmL = M.const("gmaskL", [64, 512])
    mS, BmS = M.const("gstrict", [64, 512])
    reset, Brs = M.const("reset64", [1, 512])
    pj = M.pj
    ps, Bps = M.ps, M.Bps
    A.reset()
    S = A.f32(128)
    BS = P.buf("S")
    MS(P, "dve", S, 0.0, [BS])
    negA = P.sb("gnegA", [1, 1], F32)
    BnA = P.buf("gnegA")
    ACT(P, negA[:], gsc[0:1, 0:1], AF.Exp, [Bgs], [BnA])
    TS(P, "dve", negA[:], negA[:], -1.0, None, ALU.mult, None, [BnA], [BnA])
    raw = [[A.f32(n + 3) for _ in range(3)] for _ in range(2)]
    Braw = [P.bufs(3, f"graw{i}") for i in range(2)]
    zr = [A.f32(n) for _ in range(2)]
    Bzr = P.bufs(2, "gz")
    abr = [A.f32(2 * n) for _ in range(2)]
    Bab = P.bufs(2, "gab")
    cv = [A.f32(n) for _ in range(3)]
    Bcv = P.bufs(3, "gcv")
    sq = A.f32(n)
    Bsq = P.buf("gsq")
    rs = A.f32(n)
    Brs_ = P.buf("grs")
    gcr = A.f32(n)
    Bgcr = P.buf("gcr")
    ktok = A.f32(NCK * 128)
    vtok = A.f32(NCK * 128)
    Bktok, Bvtok = P.buf("ktok"), P.buf("vtok")
    cols = A.f32(NCK * 4)
    Bcols = P.buf("gcols")
    ex = A.f32(NCK * 4)
    Bex = P.buf("gex")
    E1 = A.f32(n)
    BE1 = P.buf("gE1")
    dm = A.f32(n)
    Bdm = P.buf("gdm")
    dmT = A.f32(n)
    BdmT = P.buf("gdmT")
    X = [A.f32(n) for _ in range(2)]
    Xt = [A.f32(n) for _ in range(2)]
    BX = P.bufs(2, "gX")
    BXt = P.bufs(2, "gXt")
    Rm = A.f32(n)
    BR = P.buf("gR")
    attn = A.f32(n)
    Battn = P.buf("gattn")
    kb = A.f32(NCK * 128)
    vb = A.f32(NCK * 128)
    kdec = A.f32(NCK * 128)
    Bkb, Bvb, Bkdec = P.buf("kb"), P.buf("vb"), P.buf("kdec")
    qd = A.f32(n)
    Bqd = P.buf("qd")
    u = A.f32(NCK * 128)
    Bu = P.buf("gu")
    wT = A.f32(n)
    BwT = P.buf("gwT")
    vnew = [A.f32(128) for _ in range(2)]
    Bvn = P.bufs(2, "gvn")
    osb = A.f32(n)
    Bosb = P.buf("gosb")
    nst = T // n

    def load(si):
        s = si % 2
        t0 = si * n
        for q in range(3):
            load_halo(P, raw[s][q], pj[q * 128:(q + 1) * 128, :], t0, n, [M.Bpj], [Braw[s][q]], f"graw{s}{q}")
        DMA(P, "sp", zr[s], pj[3 * 128:4 * 128, t0:t0 + n], [M.Bpj], [Bzr[s]], f"gz{s}")
        DMA(P, "sp", abr[s][0:1, :].rearrange("o (r t) -> o r t", r=2),
            pj[PJ_SMALL:PJ_SMALL + 2, t0:t0 + n].rearrange("(o r) t -> o r t", o=1), [M.Bpj], [Bab[s]], f"gab{s}")

    load(0)
    for si in range(nst):
        s = si % 2
        t0 = si * n
        if si + 1 < nst:
            load(si + 1)
        for q in range(3):
            conv_silu(P, cv[q], raw[s][q], gconv[:, q * 4:q * 4 + 4], None, n, [Braw[s][q], Bgc], [Bcv[q]])
        ACT(P, zr[s], zr[s], AF.Silu, [Bzr[s]], [Bzr[s]])
        for q in range(2):
            ACT(P, sq, cv[q], AF.Square, [Bcv[q]], [Bsq])
            MM(P, ps[7][:, :n], M.ones[:], sq, True, True, [M.Bones, Bsq], [Bps[7]])
            ACT(P, rs, ps[7][:, :n], AF.Sqrt, [Bps[7]], [Brs_], bias=EPS)
            P.op("dve", lambda e: e.reciprocal(out=rs, in_=rs), [Brs_], [Brs_])
            if q == 0:
                STT(P, cv[q], cv[q], 128.0 ** -0.5, rs, ALU.mult, ALU.mult, [Bcv[q], Brs_], [Bcv[q]])
            else:
                TT(P, "dve", cv[q], cv[q], rs, ALU.mult, [Bcv[q], Brs_], [Bcv[q]])
        if DEBUG_CUT == 12:
            for q in range(3):
                DMA(P, "sp", M.yo[128 + q * 128:256 + q * 128, t0:t0 + n], cv[q], [Bcv[q]], [], f"dbg{q}")
        ar = abr[s][0:1, 0:n]
        br = abr[s][0:1, n:2 * n]
        ACT(P, ar, ar, AF.Exp, [Bab[s], Bgs], [Bab[s]], bias=gsc[0:1, 1:2])
        ACT(P, ar, ar, AF.Ln, [Bab[s]], [Bab[s]], bias=1.0)
        TS(P, "dve", ar, ar, negA[0:1, 0:1], None, ALU.mult, None, [Bab[s], BnA], [Bab[s]])
        ACT(P, br, br, AF.Sigmoid, [Bab[s]], [Bab[s]])
        P.op("dve", lambda e, ar=ar: e.tensor_tensor_scan(out=gcr[0:1, :], data0=reset[0:1, :], data1=ar,
                                                          initial=0.0, op0=ALU.mult, op1=ALU.add),
             [Bab[s], Brs], [Bgcr])
        MM(P, ps[7][:, :n], M.ones[0:1, 0:128], gcr[0:1, :], True, True, [M.Bones, Bgcr], [Bps[7]])
        ACT(P, E1, ps[7][:, :n], AF.Exp, [Bps[7]], [BE1])
        for i in range(NCK):
            c0 = i * 64
            MM(P, ps[6][0:64, 2 * i:2 * i + 1], gcr[0:1, c0:c0 + 64], M.ones[0:1, 0:1], True, True,
               [M.Bones, Bgcr], [Bps[6]])
            MM(P, ps[6][0:64, 2 * i + 1:2 * i + 2], br[:, c0:c0 + 64], M.ones[0:1, 0:1], True, True,
               [M.Bones, Bab[s]], [Bps[6]])
        cv3 = cols[0:64, :].rearrange("p (i c) -> p i c", c=4)
        CP(P, "dve", cv3[:, :, 0:2], ps[6][0:64, 0:2 * NCK].rearrange("p (i c) -> p i c", c=2), [Bps[6]], [Bcols])
        CP(P, "dve", cv3[:, :, 2:3], ps[7][0:64, :n].rearrange("p (i c) -> p i c", c=64)[:, :, 63:64],
           [Bps[7]], [Bcols])
        ex3 = ex[0:64, :].rearrange("p (i c) -> p i c", c=4)
        ACT(P, ex3[:, :, 0:1], cv3[:, :, 0:1], AF.Exp, [Bcols], [Bex])
        TT(P, "dve", ex3[:, :, 0:1], ex3[:, :, 0:1], cv3[:, :, 1:2], ALU.mult, [Bex, Bcols], [Bex])
        TT(P, "dve", ex3[:, :, 1:2], cv3[:, :, 2:3], cv3[:, :, 0:1], ALU.subtract, [Bcols], [Bex])
        ACT(P, ex3[:, :, 1:2], ex3[:, :, 1:2], AF.Exp, [Bex], [Bex])
        TS(P, "dve", ex3[:, :, 2:3], cv3[:, :, 1:2], -1.0, None, ALU.mult, None, [Bcols], [Bex])
        for i in range(NCK):
            c0 = i * 64
            bk = 0 + i // 4
            TR(P, ps[bk][0:64, (i % 4) * 128:(i % 4 + 1) * 128], cv[1][:, c0:c0 + 64], M.ident[:],
               [Bcv[1], M.Bid], [Bps[bk]])
        for i in range(NCK):
            c0 = i * 64
            bk = 2 + i // 4
            TR(P, ps[bk][0:64, (i % 4) * 128:(i % 4 + 1) * 128], cv[2][:, c0:c0 + 64], M.ident[:],
               [Bcv[2], M.Bid], [Bps[bk]])
        for hh in range(2):
            CP(P, "act", ktok[0:64, hh * 512:(hh + 1) * 512], ps[hh][0:64, :], [Bps[hh]], [Bktok])
            CP(P, "dve", vtok[0:64, hh * 512:(hh + 1) * 512], ps[2 + hh][0:64, :], [Bps[2 + hh]], [Bvtok])
        for i in range(NCK):
            sl = slice(i * 128, (i + 1) * 128)
            TS(P, "pool", kb[0:64, sl], ktok[0:64, sl], ex[0:64, 4 * i:4 * i + 1], None, ALU.mult, None,
               [Bktok, Bex], [Bkb])
            TS(P, "pool", vb[0:64, sl], vtok[0:64, sl], cols[0:64, 4 * i + 1:4 * i + 2], None, ALU.mult, None,
               [Bvtok, Bcols], [Bvb])
            TS(P, "pool", kdec[0:64, sl], ktok[0:64, sl], ex[0:64, 4 * i + 1:4 * i + 2], None, ALU.mult, None,
               [Bktok, Bex], [Bkdec])
        for i in range(NCK):
            sl = slice(i * 64, (i + 1) * 64)
            STT(P, dm[0:64, sl], ps[7][0:64, sl], cols[0:64, 4 * i:4 * i + 1], mU[:, sl], ALU.subtract, ALU.add,
                [Bps[7], Bcols, BmU], [Bdm])
            STT(P, dmT[0:64, sl], ps[7][0:64, sl], cols[0:64, 4 * i:4 * i + 1], mL[:, sl], ALU.subtract, ALU.add,
                [Bps[7], Bcols, BmL], [BdmT])
        ACT(P, dm[0:64, :], dm[0:64, :], AF.Exp, [Bdm], [Bdm])
        ACT(P, dmT[0:64, :], dmT[0:64, :], AF.Exp, [BdmT], [BdmT], scale=-1.0)
        for i in range(NCK):
            sl = slice(i * 64, (i + 1) * 64)
            MM(P, ps[4][0:64, sl], cv[1][:, sl], cv[1][:, sl], True, True, [Bcv[1]], [Bps[4]])
        for i in range(NCK):
            sl = slice(i * 64, (i + 1) * 64)
            MM(P, ps[5][0:64, sl], cv[1][:, sl], cv[0][:, sl], True, True, [Bcv[1], Bcv[0]], [Bps[5]])
        MM(P, ps[6][0:64, :n], M.ones[0:1, 0:64], br, True, True, [M.Bones, Bab[s]], [Bps[6]])
        STT(P, X[0][0:64, :], ps[4][0:64, :n], -1.0, dm[0:64, :], ALU.mult, ALU.mult, [Bps[4], Bdm], [BX[0]])
        TT(P, "dve", X[0][0:64, :], X[0][0:64, :], ps[6][0:64, :n], ALU.mult, [BX[0], Bps[6]], [BX[0]])
        TT(P, "pool", X[0][0:64, :], X[0][0:64, :], mS[:, :], ALU.mult, [BX[0], BmS], [BX[0]])
        TT(P, "dve", Xt[0][0:64, :], ps[4][0:64, :n], dmT[0:64, :], ALU.mult, [Bps[4], BdmT], [BXt[0]])
        for i in range(NCK):
            sl = slice(i * 64, (i + 1) * 64)
            TS(P, "pool", Xt[0][0:64, sl], Xt[0][0:64, sl], ex[0:64, 4 * i + 2:4 * i + 3], None, ALU.mult, None,
               [BXt[0], Bex], [BXt[0]])
        TT(P, "dve", attn[0:64, :], ps[5][0:64, :n], dm[0:64, :], ALU.mult, [Bps[5], Bdm], [Battn])
        for i in range(NCK):
            sl = slice(i * 64, (i + 1) * 64)
            TT(P, "pool", Rm[0:64, sl], X[0][0:64, sl], M.ident[0:64, 0:64], ALU.add, [BX[0], M.Bid], [BR])
        cur = 0
        for j in range(5):
            nx = 1 - cur
            for i in range(NCK):
                sl = slice(i * 64, (i + 1) * 64)
                MM(P, ps[0][0:64, sl], Xt[cur][0:64, sl], X[cur][0:64, sl], True, True, [BXt[cur], BX[cur]], [Bps[0]])
            for i in range(NCK):
                sl = slice(i * 64, (i + 1) * 64)
                MM(P, ps[1][0:64, sl], X[cur][0:64, sl], Xt[cur][0:64, sl], True, True, [BXt[cur], BX[cur]], [Bps[1]])
            CP(P, "act", X[nx][0:64, :], ps[0][0:64, :n], [Bps[0]], [BX[nx]])
            CP(P, "dve", Xt[nx][0:64, :], ps[1][0:64, :n], [Bps[1]], [BXt[nx]])
            for i in range(NCK):
                sl = slice(i * 64, (i + 1) * 64)
                MM(P, ps[2][0:64, sl], Xt[nx][0:64, sl], Rm[0:64, sl], True, True, [BXt[nx], BR], [Bps[2]])
            TT(P, "dve", Rm[0:64, :], Rm[0:64, :], ps[2][0:64, :n], ALU.add, [BR, Bps[2]], [BR])
            cur = nx
        for i in range(NCK):
            sl = slice(i * 64, (i + 1) * 64)
            bk = i // 4
            MM(P, ps[bk][0:64, (i % 4) * 128:(i % 4 + 1) * 128], Rm[0:64, sl], vb[0:64, i * 128:(i + 1) * 128],
               True, True, [BR, Bvb], [Bps[bk]])
        for hh in range(2):
            CP(P, "act" if hh else "dve", u[0:64, hh * 512:(hh + 1) * 512], ps[hh][0:64, :], [Bps[hh]], [Bu])
        for i in range(NCK):
            sl = slice(i * 64, (i + 1) * 64)
            MM(P, ps[2][:, sl], kb[0:64, i * 128:(i + 1) * 128], Rm[0:64, sl], True, True, [Bkb, BR], [Bps[2]])
        CP(P, "act", wT, ps[2][:, :n], [Bps[2]], [BwT])
        TT(P, "pool", qd, cv[0], E1, ALU.mult, [Bcv[0], BE1], [Bqd])
        for i in range(NCK):
            sl = slice(i * 64, (i + 1) * 64)
            s128 = slice(i * 128, (i + 1) * 128)
            vb_ = i % 2
            MM(P, ps[3][0:64, 0:128], wT[:, sl], S, True, True, [BwT, BS], [Bps[3]])
            TT(P, "dve", vnew[vb_][0:64, :], u[0:64, s128], ps[3][0:64, 0:128], ALU.subtract, [Bu, Bps[3]], [Bvn[vb_]])
            MM(P, ps[5][:, sl], S, qd[:, sl], True, False, [BS, Bqd], [Bps[5]])
            MM(P, ps[5][:, sl], vnew[vb_][0:64, :], attn[0:64, sl], False, True, [Bvn[vb_], Battn], [Bps[5]])
            MM(P, ps[4][:, 0:128], kdec[0:64, s128], vnew[vb_][0:64, :], True, True, [Bkdec, Bvn[vb_]], [Bps[4]])
            STT(P, S, S, E1[:, i * 64 + 63:i * 64 + 64], ps[4][:, 0:128], ALU.mult, ALU.add, [BS, BE1, Bps[4]], [BS])
        CP(P, "act", osb, ps[5][:, :n], [Bps[5]], [Bosb])
        ACT(P, sq, osb, AF.Square, [Bosb], [Bsq])
        MM(P, ps[7][:, :n], M.ones[:], sq, True, True, [M.Bones, Bsq], [Bps[7]])
        ACT(P, rs, ps[7][:, :n], AF.Sqrt, [Bps[7]], [Brs_], bias=EPS, scale=1.0 / 128)
        P.op("dve", lambda e: e.reciprocal(out=rs, in_=rs), [Brs_], [Brs_])
        STT(P, osb, osb, gnorm[:, 0:1], rs, ALU.mult, ALU.mult, [Bosb, Bgn, Brs_], [Bosb])
        TT(P, "dve", osb, osb, zr[s], ALU.mult, [Bosb, Bzr[s]], [Bosb])
        DMA(P, "sp", M.yo[0:128, t0:t0 + n], osb, [Bosb], [], "ya")
        if DEBUG_CUT == 11:
            P.barrier()
    P.barrier()


def MMA(P, out, lhsT, rhs, reads, writes):
    P.op("pe", lambda e: e.matmul(out, lhsT=lhsT, rhs=rhs, start=False, stop=False, skip_group_check=True),
         reads, writes, pe_acc=True)


PD_S = 2048
PD_C = 5120
W_S = 1792
W_W = 1408
W_C = 3072


def m_nsa(M):
    P, T, A, nc = M.P, M.T, M.A, M.nc
    n = 512
    NKT = T // 128
    NQ = T // n
    pj = M.pj
    ps, Bps = M.ps, M.Bps
    qn, Bqn = M.const("nqn", [128, 1])
    kn, Bkn = M.const("nkn", [128, 1])
    pekT, Bpek = M.const("pekT", [128, 32])
    pevT, Bpev = M.const("pevT", [128, 32])
    tabs, Btabs = M.const("ntab", [32, 2])
    t31, Bt31 = M.const("nt31", [128, 2])
    wck_d = M.din("wck", [128, 4096])
    wcv_d = M.din("wcv", [128, 4096])
    ohS = M.din("ohS", [32, PD_S]); ngS = M.din("ngS", [1, PD_S])
    ohW = M.din("ohW", [32, PD_S]); ngW = M.din("ngW", [1, PD_S])
    ohC = M.din("ohC", [32, PD_C]); ngC = M.din("ngC", [1, PD_C])
    E_d = M.din("Esel", [128, 8192])
    wimp_d = M.din("wimp", [128, 512])
    dS = nc.dram_tensor("dS", [129 * PD_S], F32)
    dW = nc.dram_tensor("dW", [129 * PD_S], F32)
    dCM = nc.dram_tensor("dCM", [129 * PD_C], F32)
    dCO = nc.dram_tensor("dCO", [129 * PD_C], F32)
    A.reset()
    BTs = A.f32(W_S); BTw = A.f32(W_W); BTcM = A.f32(W_C); BTcO = A.f32(W_C)
    BBTs, BBTw, BBTcM, BBTcO = P.buf("BTs"), P.buf("BTw"), P.buf("BTcM"), P.buf("BTcO")
    mark0 = A.off
    fb = A.f32(PD_C)
    Bfb = P.buf("fb")
    frow = A.f32(PD_C)
    Bfrow = P.buf("frow")
    oh = A.f32(PD_C)
    Boh = P.buf("oh")
    ngt = A.f32(PD_C)
    Bng = P.buf("ng")

    def strip(oh_d, ng_d, Pd, tcol, dram, dst, Bdst, W, step, base, key):
        DMA(P, "sp", oh[0:32, 0:Pd], oh_d, [], [Boh], "oh")
        DMA(P, "sp", ngt[0:1, 0:Pd], ng_d, [], [Bng], "ng")
        for b0 in range(0, Pd, 512):
            MM(P, ps[0][0:1, :], tabs[:, tcol:tcol + 1], oh[0:32, b0:b0 + 512], True, True, [Btabs, Boh], [Bps[0]])
            TT(P, "dve", frow[0:1, b0:b0 + 512], ps[0][0:1, :], ngt[0:1, b0:b0 + 512], ALU.add,
               [Bps[0], Bng], [Bfrow])
        for b0 in range(0, Pd, 512):
            pb = 1 + (b0 // 512) % 2
            MM(P, ps[pb][:, :], M.ones[0:1, 0:128], frow[0:1, b0:b0 + 512], True, True, [M.Bones, Bfrow], [Bps[pb]])
            CP(P, "act" if pb == 1 else "dve", fb[:, b0:b0 + 512], ps[pb][:, :], [Bps[pb]], [Bfb])
        Bd = P.buf("d" + key)
        dap = dram.ap()
        DMA(P, "sp", dap[0:128 * Pd].rearrange("(p c) -> p c", c=Pd), fb[:, 0:Pd], [Bfb], [Bd], "dw" + key)
        DMA(P, "sp", dap[128 * Pd:129 * Pd].rearrange("(p c) -> p c", c=Pd), fb[0:1, 0:Pd], [Bfb], [Bd], "dw" + key)
        src = bass.AP(tensor=dap.tensor, offset=base, ap=[[Pd - step, 128], [1, W]])
        DMA(P, "sp", dst, src, [Bd], [Bdst], "bt" + key)

    strip(ohS, ngS, PD_S, 0, dS, BTs, BBTs, W_S, 1, 127, "S")
    strip(ohW, ngW, PD_S, 0, dW, BTw, BBTw, W_W, 1, 127, "W")
    strip(ohC, ngC, PD_C, 0, dCM, BTcM, BBTcM, W_C, 16, 2032, "CM")
    strip(ohC, ngC, PD_C, 1, dCO, BTcO, BBTcO, W_C, 16, 2032, "CO")
    P.barrier()
    A.off = mark0
    ksT = A.bf16(T); kwT = A.bf16(T)
    Bks, Bkw = P.buf("ksT"), P.buf("kwT")
    Vs = A.bf16(NKT * 129); Vw = A.bf16(NKT * 129)
    BVs, BVw = P.buf("Vs"), P.buf("Vw")
    Eb = A.bf16(T)
    BE = P.buf("E")
    kcmpT = A.f32(512)
    Bkcmp = P.buf("kcmp")
    WV = A.f32(4 * 256)
    BWV = P.buf("WV")
    Wimp = A.f32(512)
    BWimp = P.buf("Wimp")
    mark = A.off
    DMA(P, "pool", Eb, E_d[:, 0:T], [], [BE], "E")
    DMA(P, "sp", Wimp, wimp_d, [], [BWimp], "wimp")
    DMA(P, "sp", WV.rearrange("p (c x) -> p c x", c=4)[:, :, 0:128], wimp_d.rearrange("p (c x) -> p c x", c=4),
        [], [BWV], "wv")
    MS(P, "pool", Vs.rearrange("p (k x) -> p k x", x=129)[:, :, 128:129], 1.0, [BVs])
    MS(P, "pool", Vw.rearrange("p (k x) -> p k x", x=129)[:, :, 128:129], 1.0, [BVw])
    kraw = [A.f32(n) for _ in range(2)]
    Bkraw = P.bufs(2, "kraw")
    sq = A.f32(n); Bsq = P.buf("nsq")
    rs = A.f32(n); Brs_ = P.buf("nrs")
    it = 0
    for (chunk, dstT, Bdst) in ((8, ksT, Bks), (10, kwT, Bkw)):
        for ti in range(NQ):
            s = it % 2
            it += 1
            t0 = ti * n
            DMA(P, "sp", kraw[s], pj[chunk * 128:(chunk + 1) * 128, t0:t0 + n], [M.Bpj], [Bkraw[s]], f"kraw{s}")
            ACT(P, sq, kraw[s], AF.Square, [Bkraw[s]], [Bsq])
            MM(P, ps[0][:, :n], M.ones[:], sq, True, True, [M.Bones, Bsq], [Bps[0]])
            ACT(P, rs, ps[0][:, :n], AF.Sqrt, [Bps[0]], [Brs_], bias=EPS, scale=1.0 / 128)
            P.op("dve", lambda e: e.reciprocal(out=rs, in_=rs), [Brs_], [Brs_])
            STT(P, dstT[:, t0:t0 + n], kraw[s], kn[:, 0:1], rs, ALU.mult, ALU.mult, [Bkraw[s], Bkn, Brs_], [Bdst])
    for (chunk, dstV, Bdst) in ((9, Vs, BVs), (11, Vw, BVw)):
        dv3 = dstV.rearrange("p (k x) -> p k x", x=129)
        for ti in range(NQ):
            s = it % 2
            it += 1
            t0 = ti * n
            DMA(P, "sp", kraw[s], pj[chunk * 128:(chunk + 1) * 128, t0:t0 + n], [M.Bpj], [Bkraw[s]], f"kraw{s}")
            pb = 1 + ti % 2
            for c in range(4):
                TR(P, ps[pb][:, c * 128:(c + 1) * 128], kraw[s][:, c * 128:(c + 1) * 128], M.ident[:],
                   [Bkraw[s], M.Bid], [Bps[pb]])
            CP(P, "act" if ti % 2 else "dve", dv3[:, ti * 4:ti * 4 + 4, 0:128],
               ps[pb][:, :].rearrange("p (c x) -> p c x", c=4), [Bps[pb]], [Bdst])
    wc = A.f32(4096)
    Bwc = P.buf("wc")
    kct = [A.f32(1040) for _ in range(2)]
    Bkct = P.bufs(2, "kct")
    cbias = A.f32(1)
    Bcb = P.buf("cbias")
    vcT = A.f32(512)
    BvcT = P.buf("vcT")
    MS(P, "dve", kcmpT, 0.0, [Bkcmp])
    MS(P, "dve", vcT, 0.0, [BvcT])
    for (chunk, w_d, peT, Bpe, dst, Bdst) in ((6, wck_d, pekT, Bpek, kcmpT, Bkcmp), (7, wcv_d, pevT, Bpev, vcT, BvcT)):
        DMA(P, "sp", wc, w_d, [], [Bwc], "wc")
        for l in range(32):
            MM(P, ps[0][:, 0:1], wc[:, l * 128:(l + 1) * 128], peT[:, l:l + 1], l == 0, l == 31, [Bwc, Bpe], [Bps[0]])
        CP(P, "dve", cbias, ps[0][:, 0:1], [Bps[0]], [Bcb])
        ntile = T // 1024
        for ti in range(ntile):
            s = it % 2
            it += 1
            t0 = ti * 1024
            ntok = min(1040, T - t0)
            nb = 64 if ntok == 1040 else 63
            DMA(P, "sp", kct[s][:, 0:ntok], pj[chunk * 128:(chunk + 1) * 128, t0:t0 + ntok], [M.Bpj], [Bkct[s]],
                f"kct{s}")
            pb = 1 + ti % 2
            for l in range(32):
                rhs = kct[s][:, l:l + 16 * (nb - 1) + 1:16]
                MM(P, ps[pb][:, 0:nb], wc[:, l * 128:(l + 1) * 128], rhs, l == 0, l == 31, [Bwc, Bkct[s]], [Bps[pb]])
            TS(P, "dve", dst[:, ti * 64:ti * 64 + nb], ps[pb][:, 0:nb], cbias[:, 0:1], None, ALU.add, None,
               [Bps[pb], Bcb], [Bdst])
    ACT(P, sq, kcmpT, AF.Square, [Bkcmp], [Bsq])
    MM(P, ps[0][:, :n], M.ones[:], sq, True, True, [M.Bones, Bsq], [Bps[0]])
    ACT(P, rs, ps[0][:, :n], AF.Sqrt, [Bps[0]], [Brs_], bias=EPS, scale=1.0 / 128)
    P.op("dve", lambda e: e.reciprocal(out=rs, in_=rs), [Brs_], [Brs_])
    STT(P, kcmpT, kcmpT, kn[:, 0:1], rs, ALU.mult, ALU.mult, [Bkcmp, Bkn, Brs_], [Bkcmp])
    for c in range(4):
        TR(P, ps[1][:, c * 128:(c + 1) * 128], vcT[:, c * 128:(c + 1) * 128], M.ident[:], [BvcT, M.Bid], [Bps[1]])
    CP(P, "dve", WV.rearrange("p (c x) -> p c x", c=4)[:, :, 128:256], ps[1][:, :].rearrange("p (c x) -> p c x", c=4),
       [Bps[1]], [BWV])
    P.barrier()
    A.off = mark
    qraw = [A.f32(n) for _ in range(2)]; Bqraw = P.bufs(2, "qraw")
    qM = A.f32(n); qO = A.f32(n); BqM, BqO = P.buf("qM"), P.buf("qO")
    qMb = A.bf16(n); BqMb = P.buf("qMb")
    glog = A.f32(n); Bglog = P.buf("glog")
    gq = A.f32(12); Bgq = P.buf("gq")
    tmpf = [A.f32(n) for _ in range(3)]; Btmpf = P.bufs(3, "ntmp")
    Pf = [A.f32(n) for _ in range(2)]; BPf = P.bufs(2, "Pf")
    Pb = [A.bf16(n) for _ in range(3)]; BPb = P.bufs(3, "Pb")
    negmT = A.bf16(n); BnegmT = P.buf("negmT")
    imp = A.f32(128); Bimp = P.buf("imp")
    imp2 = A.f32(128); Bimp2 = P.buf("imp2")
    scr = A.f32(128); Bscr = P.buf("scr")
    m8 = A.f32(16); Bm8 = P.buf("m8")
    sm = A.f32(16); Bsm = P.buf("sm")
    negm = A.f32(128); Bnegm = P.buf("negm")
    ocomb = [A.f32(128) for _ in range(4)]; Boc = P.bufs(4, "ocomb")
    ost = A.f32(n); Bost = P.buf("ost")
    ti_ = 0
    pi = 0
    for Q in range(NQ):
        t0 = Q * n
        for hd, (chunk, dst, Bdst) in enumerate(((4, qM, BqM), (5, qO, BqO))):
            s = hd
            DMA(P, "sp", qraw[s], pj[chunk * 128:(chunk + 1) * 128, t0:t0 + n], [M.Bpj], [Bqraw[s]], f"qraw{s}")
            ACT(P, sq, qraw[s], AF.Square, [Bqraw[s]], [Bsq])
            MM(P, ps[7][:, :n], M.ones[:], sq, True, True, [M.Bones, Bsq], [Bps[7]])
            ACT(P, rs, ps[7][:, :n], AF.Sqrt, [Bps[7]], [Brs_], bias=128 * EPS, scale=1.0)
            P.op("dve", lambda e: e.reciprocal(out=rs, in_=rs), [Brs_], [Brs_])
            STT(P, dst, qraw[s], qn[:, 0:1], rs, ALU.mult, ALU.mult, [Bqraw[s], Bqn, Brs_], [Bdst])
        CP(P, "pool", qMb, qM, [BqM], [BqMb])
        DMA(P, "sp", glog[0:3, :], pj[PJ_SMALL + 2:PJ_SMALL + 5, t0:t0 + n], [M.Bpj], [Bglog], "glog")
        for sb in range(4):
            TR(P, ps[7][:, sb * 3:sb * 3 + 3], glog[0:3, sb * 128:(sb + 1) * 128], M.ident[0:3, 0:3],
               [Bglog, M.Bid], [Bps[7]])
        ACT(P, gq, ps[7][:, 0:12], AF.Sigmoid, [Bps[7]], [Bgq])
        for bk in (2, 3, 4):
            MS(P, "dve", ps[bk][:, :], 0.0, [Bps[bk]])
        for hd, (qh, Bqh, BT, BBT) in enumerate(((qM, BqM, BTcM, BBTcM), (qO, BqO, BTcO, BBTcO))):
            for ct in range(Q // 4 + 1):
                Mq = Q - 4 * ct
                pb = pi % 2
                pi += 1
                MM(P, ps[pb][:, :n], kcmpT[:, ct * 128:(ct + 1) * 128], qh, True, True, [Bkcmp, Bqh], [Bps[pb]])
                f = ti_ % 2
                ti_ += 1
                if Mq <= 5:
                    t = ti_ % 3
                    TT(P, "dve", tmpf[t], ps[pb][:, :n], BT[:, 512 * Mq:512 * Mq + 512], ALU.add,
                       [Bps[pb], BBT], [Btmpf[t]])
                    ACT(P, Pf[f], tmpf[t], AF.Exp, [Btmpf[t]], [BPf[f]])
                else:
                    ACT(P, Pf[f], ps[pb][:, :n], AF.Exp, [Bps[pb], Bt31], [BPf[f]], bias=t31[:, hd:hd + 1])
                for sb in range(4):
                    lhsT = Pf[f][:, sb * 128:(sb + 1) * 128]
                    if hd == 0:
                        bk = 2 + sb // 2
                        MMA(P, ps[bk][:, (sb % 2) * 256:(sb % 2) * 256 + 256], lhsT, WV[:, ct * 256:(ct + 1) * 256],
                            [BPf[f], BWV], [Bps[bk]])
                    else:
                        MMA(P, ps[4][:, sb * 128:(sb + 1) * 128], lhsT, Wimp[:, ct * 128:(ct + 1) * 128],
                            [BPf[f], BWimp], [Bps[4]])
        for sb in range(4):
            qt = 4 * Q + sb
            bk = 2 + sb // 2
            aM = ps[bk][:, (sb % 2) * 256:(sb % 2) * 256 + 128]
            vM = ps[bk][:, (sb % 2) * 256 + 128:(sb % 2) * 256 + 256]
            aO = ps[4][:, sb * 128:(sb + 1) * 128]
            TS(P, "dve", scr, aM, 0.5, 0.0, ALU.mult, ALU.add, [Bps[bk]], [Bscr, Bsm], accum=sm[:, 0:1])
            TS(P, "dve", scr, aO, 0.5, 0.0, ALU.mult, ALU.add, [Bps[4]], [Bscr, Bsm], accum=sm[:, 1:2])
            TS(P, "dve", sm[:, 0:2], sm[:, 0:2], 1e-30, None, ALU.max, None, [Bsm], [Bsm])
            P.op("dve", lambda e: e.reciprocal(out=sm[:, 2:4], in_=sm[:, 0:2]), [Bsm], [Bsm])
            TS(P, "dve", imp, aM, sm[:, 2:3], None, ALU.mult, None, [Bps[bk], Bsm], [Bimp])
            STT(P, imp, aO, sm[:, 3:4], imp, ALU.mult, ALU.add, [Bps[4], Bsm, Bimp], [Bimp])
            TT(P, "dve", sm[:, 4:5], sm[:, 2:3], gq[:, sb * 3:sb * 3 + 1], ALU.mult, [Bsm, Bgq], [Bsm])
            TS(P, "dve", ocomb[sb], vM, sm[:, 4:5], None, ALU.mult, None, [Bps[bk], Bsm], [Boc[sb]])
            MS(P, "pool", imp[:, 0:1], 1e4, [Bimp])
            MS(P, "pool", imp[:, 2 * qt:2 * qt + 1], 1e4, [Bimp])
            if qt > 0:
                MS(P, "pool", imp[0:64, 2 * qt - 1:2 * qt], 1e4, [Bimp])
            MS(P, "pool", imp[64:128, 2 * qt + 1:2 * qt + 2], 1e4, [Bimp])
            P.op("dve", lambda e: e.max(out=m8[:, 0:8], in_=imp), [Bimp], [Bm8])
            P.op("dve", lambda e: e.match_replace(out=imp2, in_to_replace=m8[:, 0:8], in_values=imp, imm_value=-1e30),
                 [Bimp, Bm8], [Bimp2])
            P.op("dve", lambda e: e.max(out=m8[:, 8:16], in_=imp2), [Bimp2], [Bm8])
            TS(P, "dve", negm, imp, m8[:, 15:16], -32768.0, ALU.is_lt, ALU.mult, [Bimp, Bm8], [Bnegm])
            TR(P, ps[7][:, 128:256], negm, M.ident[:], [Bnegm, M.Bid], [Bps[7]])
            CP(P, "act", negmT[:, sb * 128:(sb + 1) * 128], ps[7][:, 128:256], [Bps[7]], [BnegmT])
        for br_, (kT, BkT, Vv, BVv, BT, BBT, lo, accb) in enumerate(
                ((ksT, Bks, Vs, BVs, BTs, BBTs, 0, (5, 6)), (kwT, Bkw, Vw, BVw, BTw, BBTw, max(0, 4 * Q - 4), (2, 3)))):
            for bk in accb:
                MS(P, "dve", ps[bk][:, :], 0.0, [Bps[bk]])
            for kt in range(lo, 4 * Q + 4):
                m = 4 * Q - kt
                pb = pi % 2
                pi += 1
                if br_ == 0:
                    MM(P, ps[pb][:, :n], kT[:, kt * 128:(kt + 1) * 128], qMb, True, False, [BkT, BqMb], [Bps[pb]])
                    MM(P, ps[pb][:, :n], Eb[:, kt * 128:(kt + 1) * 128], negmT, False, True, [BE, BnegmT], [Bps[pb]])
                else:
                    MM(P, ps[pb][:, :n], kT[:, kt * 128:(kt + 1) * 128], qMb, True, True, [BkT, BqMb], [Bps[pb]])
                f = ti_ % 3
                ti_ += 1
                if m <= 7:
                    t = ti_ % 3
                    TT(P, "dve", tmpf[t], ps[pb][:, :n], BT[:, 128 * (m + 3):128 * (m + 3) + 512], ALU.add,
                       [Bps[pb], BBT], [Btmpf[t]])
                    ACT(P, Pb[f], tmpf[t], AF.Exp, [Btmpf[t]], [BPb[f]])
                else:
                    ACT(P, Pb[f], ps[pb][:, :n], AF.Exp, [Bps[pb], Bt31], [BPb[f]], bias=t31[:, 0:1])
                for sb in range(4):
                    if 4 * Q + sb < kt:
                        continue
                    bk = accb[sb // 2]
                    MMA(P, ps[bk][:, (sb % 2) * 129:(sb % 2) * 129 + 129], Pb[f][:, sb * 128:(sb + 1) * 128],
                        Vv[:, kt * 129:(kt + 1) * 129], [BPb[f], BVv], [Bps[bk]])
            for sb in range(4):
                bk = accb[sb // 2]
                acc = ps[bk][:, (sb % 2) * 129:(sb % 2) * 129 + 129]
                P.op("dve", lambda e, acc=acc: e.reciprocal(out=sm[:, 5:6], in_=acc[:, 128:129]), [Bps[bk]], [Bsm])
                TT(P, "dve", sm[:, 6:7], sm[:, 5:6], gq[:, sb * 3 + 1 + br_:sb * 3 + 2 + br_], ALU.mult, [Bsm, Bgq], [Bsm])
                STT(P, ocomb[sb], acc[:, 0:128], sm[:, 6:7], ocomb[sb], ALU.mult, ALU.add, [Bps[bk], Bsm, Boc[sb]], [Boc[sb]])
        for sb in range(4):
            TR(P, ps[7][:, 256:384], ocomb[sb], M.ident[:], [Boc[sb], M.Bid], [Bps[7]])
            CP(P, "act", ost[:, sb * 128:(sb + 1) * 128], ps[7][:, 256:384], [Bps[7]], [Bost])
        DMA(P, "sp", M.yo[128:256, t0:t0 + n], ost, [Bost], [], "yb")
    P.barrier()


def _bucket(dist):
    n = np.maximum(dist, 0)
    nf = np.maximum(n, 1).astype(np.float32)
    large = 16 + (np.log(nf / np.float32(16)) / np.float32(math.log(1024 / 16)) * np.float32(16)).astype(np.int32)
    large = np.minimum(large, 31)
    return np.where(n < 16, n, large)


_NSA_CONSTS = {}


def nsa_consts():
    if _NSA_CONSTS:
        return _NSA_CONSTS
    c = _NSA_CONSTS
    j = np.arange(PD_S)
    dist = j - 511
    bk = _bucket(dist)
    for nm, valid in (("S", dist >= 0), ("W", (dist >= 0) & (dist < 512))):
        oh = np.zeros((32, PD_S), np.float32)
        oh[bk[valid], j[valid]] = 1.0
        c["oh" + nm] = oh
        c["ng" + nm] = np.where(valid, 0.0, NEGBIG).astype(np.float32)[None, :]
    j = np.arange(PD_C)
    dist = j - 2063
    bk = _bucket(dist)
    valid = dist >= 0
    oh = np.zeros((32, PD_C), np.float32)
    oh[bk[valid], j[valid]] = 1.0
    c["ohC"] = oh
    c["ngC"] = np.where(valid, 0.0, NEGBIG).astype(np.float32)[None, :]
    E = np.zeros((128, 64, 128), np.float32)
    for kt in range(64):
        for k in range(128):
            E[2 * kt + k // 64, kt, k] = 1.0
    c["Esel"] = E.reshape(128, 8192)
    W = np.zeros((4, 128, 128), np.float32)
    wts = (1.0, 2.0, 2.0, 2.0, 1.0)
    for ct in range(4):
        for i in range(128):
            gi = ct * 128 + i
            for jj in range(128):
                w = gi - 4 * jj + 1
                if 0 <= w <= 4 and gi <= 510:
                    W[ct, i, jj] = wts[w]
    c["wimp"] = np.ascontiguousarray(W.transpose(1, 0, 2).reshape(128, 512))
    return c


_PROGS = {}


def _prog(kind):
    if kind not in _PROGS:
        _PROGS[kind] = build_M(8192) if kind == "M" else build_R(kind, 2048)
    return _PROGS[kind]


def _ffn_inputs(prm, l, which, suffix):
    return {"g" + suffix: gl(prm["g_ffn" + which][l]),
            "wu" + suffix: np.ascontiguousarray(prm["w_up" + which][l]),
            "wd" + suffix: np.ascontiguousarray(prm["w_down" + which][l])}


def kernel(**inputs):
    prm = {k: np.asarray(v, dtype=np.float32) for k, v in inputs.items()}
    x = prm.pop("x")
    B, T, D = x.shape
    NCORE = 8
    TC = T // 4
    cores = list(range(NCORE))

    def run(kind, maps):
        res = run_bass_kernel_spmd(_prog(kind), maps, core_ids=cores)
        return res.results

    xs = [np.ascontiguousarray(x[c // 4, (c % 4) * TC:(c % 4 + 1) * TC, :].T) for c in cores]
    shared = _ffn_inputs(prm, 0, "1", "1")
    outs = run("A", [dict(xin=xs[c], **shared) for c in cores])
    x1s = [o["xo"] for o in outs]
    consts = m_consts(T)
    selbs = [np.ascontiguousarray(np.tile(np.eye(2, dtype=np.float32)[c // 4][None, :], (128, 1))) for c in cores]
    n_layers = prm["g_mix"].shape[0]
    for l in range(n_layers):
        x1f = np.empty((2, D, T), np.float32)
        for c in cores:
            x1f[c // 4][:, (c % 4) * TC:(c % 4 + 1) * TC] = x1s[c]
        lay = [m_layer_inputs(prm, l, h, T) for h in range(4)]
        outs = run("M", [dict(x1f=x1f, selb=selbs[c], **consts, **lay[c % 4]) for c in cores])
        yfull = np.empty((2, D, T), np.float32)
        for c in cores:
            b, h = c // 4, c % 4
            yo = outs[c]["yo"]
            yfull[b][h * 128:(h + 1) * 128] = yo[0:128]
            yfull[b][512 + h * 128:512 + (h + 1) * 128] = yo[128:256]
            yfull[b][1024 + h * 256:1024 + (h + 1) * 256] = yo[256:512]
        shared = {"gm": gl(prm["g_mix"][l]), "sn": gl(prm["ssm_norm"][l], 8),
                  "wg": np.ascontiguousarray(prm["w_in"][l][:, 6692:]),
                  "pa": np.ascontiguousarray(prm["p_a"][l]), "pb": np.ascontiguousarray(prm["p_b"][l]),
                  "pc": np.ascontiguousarray(prm["p_c"][l]), "wo": np.ascontiguousarray(prm["w_o"][l])}
        shared.update(_ffn_inputs(prm, l, "2", "2"))
        last = l == n_layers - 1
        if not last:
            shared.update(_ffn_inputs(prm, l + 1, "1", "1"))
        maps = [dict(xin=x1s[c],
                     yin=np.ascontiguousarray(yfull[c // 4][:, (c % 4) * TC:(c % 4 + 1) * TC]), **shared)
                for c in cores]
        outs = run("R" if last else "RA", maps)
        x1s = [o["xo"] for o in outs]
    out = np.empty((B, T, D), np.float32)
    for c in cores:
        out[c // 4, (c % 4) * TC:(c % 4 + 1) * TC, :] = x1s[c].T
    return out
```

```python
import bisect
import contextlib
import math
import numpy as np
import concourse.bass as bass
import concourse.mybir as mybir
from concourse.bass_utils import run_bass_kernel_spmd

F32 = mybir.dt.float32
BF16 = mybir.dt.bfloat16
AF = mybir.ActivationFunctionType
ALU = mybir.AluOpType
AX = mybir.AxisListType


class Buf:
    __slots__ = ("name", "lw", "rd", "excl")

    def __init__(self, name):
        self.name = name
        self.excl = False
        self.lw = None
        self.rd = {}


class Prog:
    ENG = ("pe", "act", "dve", "pool", "sp")

    def __init__(self, nc):
        self.nc = nc
        self.stack = contextlib.ExitStack()
        self.q = {e: [] for e in self.ENG}
        self.cnt = {e: 0 for e in self.ENG}
        self.seen = {e: {} for e in self.ENG}
        self.dcount = {}
        self.waited = {e: set() for e in self.ENG}
        self.nbuf = 0

    def buf(self, name=None):
        self.nbuf += 1
        return Buf(name or f"b{self.nbuf}")

    def bufs(self, n, name=None):
        return [self.buf(f"{name}{i}") for i in range(n)]

    def sb(self, name, shape, dtype):
        return self.stack.enter_context(self.nc.sbuf_tensor(name, list(shape), dtype))

    def ps(self, name, shape, dtype=F32):
        return self.stack.enter_context(self.nc.psum_tensor(name, list(shape), dtype))

    def op(self, eng, fn, reads=(), writes=(), dma=None, pe_acc=False):
        deps = {}

        def add(ev):
            if ev is None:
                return
            k, v = ev
            if deps.get(k, 0) < v:
                deps[k] = v

        for b in reads:
            add(b.lw)
            if b.excl:
                for k, v in b.rd.items():
                    if k != eng:
                        add((k, v))
        for b in writes:
            if not (pe_acc and b.lw is not None and b.lw[0] == "pe"):
                add(b.lw)
            for k, v in b.rd.items():
                add((k, v))
        waits = []
        seen = self.seen[eng]
        for k, v in deps.items():
            if seen.get(k, 0) >= v:
                continue
            seen[k] = v
            waits.append((k, v))
            if k in self.waited:
                self.waited[k].add(v)
        if dma is None:
            self.cnt[eng] += 1
            ev = (eng, self.cnt[eng])
        else:
            key = dma if dma.startswith("c:") else "d:" + dma
            self.dcount[key] = self.dcount.get(key, 0) + 1
            ev = (key, self.dcount[key])
        for b in reads:
            if b.rd.get(ev[0], 0) < ev[1]:
                b.rd[ev[0]] = ev[1]
        for b in writes:
            b.lw = ev
            b.rd = {}
        self.q[eng].append((waits, fn, ev))
        return ev

    def barrier(self):
        evs = [(e, self.cnt[e]) for e in self.ENG if self.cnt[e] > 0]
        evs += [(k, c) for k, c in self.dcount.items()]
        for eng in self.ENG:
            waits = []
            seen = self.seen[eng]
            for k, v in evs:
                if seen.get(k, 0) >= v:
                    continue
                seen[k] = v
                waits.append((k, v))
                if k in self.waited:
                    self.waited[k].add(v)
            if waits:
                self.q[eng].append((waits, None, None))

    def emit(self):
        nc = self.nc
        self.barrier()
        sems = {e: self.stack.enter_context(nc.semaphore("s_" + e)) for e in self.ENG}
        for i, k in enumerate(sorted(self.dcount)):
            sems[k] = self.stack.enter_context(nc.semaphore(f"sd{i}"))
        miles = {e: sorted(self.waited[e]) for e in self.ENG}

        def val(k, v):
            if k in miles:
                return bisect.bisect_right(miles[k], v)
            return v if k.startswith("c:") else 16 * v

        wsets = {e: self.waited[e] for e in self.ENG}
        with nc.Block() as block:
            decs = {"pe": block.tensor, "act": block.scalar, "dve": block.vector,
                    "pool": block.gpsimd, "sp": block.sync}
            for e in self.ENG:
                items = self.q[e]

                def body(engobj, items=items, e=e):
                    for waits, fn, ev in items:
                        for k, v in waits:
                            engobj.wait_ge(sems[k], val(k, v))
                        if fn is None:
                            continue
                        ins = fn(engobj)
                        if ev[0] in miles:
                            if ev[1] in wsets[ev[0]]:
                                ins.then_inc(sems[ev[0]], 1)
                        else:
                            ins.then_inc(sems[ev[0]], 1 if ev[0].startswith("c:") else 16)
                decs[e](body)
        self.stack.close()


def MM(P, out, lhsT, rhs, start, stop, reads, writes):
    P.op("pe", lambda e: e.matmul(out, lhsT=lhsT, rhs=rhs, start=start, stop=stop),
         reads, writes, pe_acc=not start)


def TR(P, out, in_, ident, reads, writes):
    P.op("pe", lambda e: e.transpose(out, in_, ident), reads, writes)


def ACT(P, out, in_, func, reads, writes, bias=None, scale=None, accum=None):
    kw = {}
    if bias is not None:
        kw["bias"] = bias
    if scale is not None:
        kw["scale"] = scale
    if accum is not None:
        kw["accum_out"] = accum
    P.op("act", lambda e: e.activation(out=out, in_=in_, func=func, **kw), reads, writes)


def TT(P, eng, out, in0, in1, op, reads, writes):
    P.op(eng, lambda e: e.tensor_tensor(out=out, in0=in0, in1=in1, op=op), reads, writes)


def TS(P, eng, out, in0, s1, s2, op0, op1, reads, writes, accum=None):
    if op1 is None:
        P.op(eng, lambda e: e.tensor_scalar(out=out, in0=in0, scalar1=s1, scalar2=None, op0=op0), reads, writes)
    elif accum is None:
        P.op(eng, lambda e: e.tensor_scalar(out=out, in0=in0, scalar1=s1, scalar2=s2, op0=op0, op1=op1), reads, writes)
    else:
        P.op(eng, lambda e: e.tensor_scalar(out=out, in0=in0, scalar1=s1, scalar2=s2, op0=op0, op1=op1,
                                            accum_out=accum), reads, writes)


def STT(P, out, in0, scalar, in1, op0, op1, reads, writes):
    P.op("dve", lambda e: e.scalar_tensor_tensor(out=out, in0=in0, scalar=scalar, in1=in1, op0=op0, op1=op1),
         reads, writes)


def CP(P, eng, out, in_, reads, writes):
    if eng == "act":
        P.op("act", lambda e: e.copy(out=out, in_=in_), reads, writes)
    else:
        P.op(eng, lambda e: e.tensor_copy(out=out, in_=in_), reads, writes)


def MS(P, eng, ap, val, writes):
    P.op(eng, lambda e: e.memset(ap, val), (), writes)


def DMA(P, eng, out, in_, reads, writes, key):
    P.op(eng, lambda e: e.dma_start(out=out, in_=in_), reads, writes, dma=key)


D_MODEL = 2048
DC = 16
D_FF = 5504
FC = 43
EPS = 1e-6
WSLOT = 5504
NWS = 6


class RCtx:
    def __init__(self, P, TT_):
        self.P = P
        self.TT = TT_
        n = TT_
        self.xT = P.sb("xT", [128, DC * n], F32)
        self.Bx = P.bufs(DC, "x")
        self.hT = P.sb("hT", [128, DC * n], BF16)
        self.Bh = P.bufs(DC, "h")
        self.big = P.sb("big", [128, FC * n], BF16)
        self.Bbig = P.bufs(FC, "big")
        self.ws = [P.sb(f"ws{i}", [128, WSLOT], BF16) for i in range(NWS)]
        self.Bws = P.bufs(NWS, "ws")
        self.wsi = 0
        self.sq = [P.sb(f"sq{i}", [128, n], F32) for i in range(2)]
        self.Bsq = P.bufs(2, "sq")
        self.rstd = P.sb("rstd", [128, n], F32)
        self.Brstd = P.buf("rstd")
        self.tmp = [P.sb(f"tmp{i}", [128, n], F32) for i in range(6)]
        self.Btmp = P.bufs(6, "tmp")
        self.tmpi = 0
        self.yc32 = P.sb("yc32", [128, 4 * n], F32)
        self.Byc = P.bufs(4, "yc")
        self.ones = P.sb("ones", [128, 128], F32)
        self.Bones = P.buf("ones")
        self.ps = [P.ps(f"ps{i}", [128, 512]) for i in range(8)]
        self.Bps = P.bufs(8, "ps")
        for b_ in self.Bps:
            b_.excl = True
        MS(P, "dve", self.ones[:], 1.0, [self.Bones])
        self.sqi = 0

    def x(self, k):
        return self.xT[:, k * self.TT:(k + 1) * self.TT]

    def h(self, k):
        return self.hT[:, k * self.TT:(k + 1) * self.TT]

    def bg(self, k):
        return self.big[:, k * self.TT:(k + 1) * self.TT]

    def slot(self):
        s = self.wsi % NWS
        self.wsi += 1
        return s

    def tmpslot(self):
        s = self.tmpi % 6
        self.tmpi += 1
        return s


def r_rstd(C, srcs, Bsrcs, nelem, out_rstd, Bout, psb):
    P = C.P
    n = len(srcs)
    for k in range(n):
        s = C.sqi % 2
        C.sqi += 1
        ACT(P, C.sq[s][:], srcs[k], AF.Square, [Bsrcs[k]], [C.Bsq[s]])
        MM(P, C.ps[psb][:, :C.TT], C.ones[:], C.sq[s][:], k == 0, k == n - 1, [C.Bones, C.Bsq[s]], [C.Bps[psb]])
    ACT(P, out_rstd, C.ps[psb][:, :C.TT], AF.Sqrt, [C.Bps[psb]], [Bout], bias=EPS, scale=1.0 / nelem)
    P.op("dve", lambda e: e.reciprocal(out=out_rstd, in_=out_rstd), [Bout], [Bout])


def r_norm(C, g_sb, Bg):
    P = C.P
    r_rstd(C, [C.x(k) for k in range(DC)], C.Bx, D_MODEL, C.rstd[:], C.Brstd, 6)
    for k in range(DC):
        STT(P, C.h(k), C.x(k), g_sb[:, k:k + 1], C.rstd[:], ALU.mult, ALU.mult,
            [C.Bx[k], Bg, C.Brstd], [C.Bh[k]])


def wload(C, s, dram_ap, nk, ncols):
    P = C.P
    out = C.ws[s][:, 0:nk * ncols].rearrange("p (k c) -> p k c", k=nk)
    DMA(P, "pool", out, dram_ap.rearrange("(k p) c -> p k c", p=128), [], [C.Bws[s]], f"ws{s}")


def r_ffn(C, g_sb, Bg, wu, wd):
    P = C.P
    n = C.TT
    r_norm(C, g_sb, Bg)
    GW = 256
    groups = [(c0, min(GW, D_FF - c0)) for c0 in range(0, D_FF, GW)]

    def load_up(gi):
        c0, nc_ = groups[gi]
        sa, sb_ = C.slot(), C.slot()
        wload(C, sa, wu[:, c0:c0 + nc_], DC, nc_)
        wload(C, sb_, wu[:, D_FF + c0:D_FF + c0 + nc_], DC, nc_)
        return sa, sb_

    pend = [load_up(0)]
    cnt = 0
    for gi, (c0, nc_) in enumerate(groups):
        if gi + 1 < len(groups):
            pend.append(load_up(gi + 1))
        sa, sb_ = pend.pop(0)
        for sub in range(nc_ // 128):
            c = (c0 + sub * 128) // 128
            pa, pb = cnt % 2, 2 + cnt % 2
            cnt += 1
            for k in range(DC):
                MM(P, C.ps[pa][:, :n], C.ws[sa][:, k * nc_ + sub * 128:k * nc_ + sub * 128 + 128], C.h(k),
                   k == 0, k == DC - 1, [C.Bws[sa], C.Bh[k]], [C.Bps[pa]])
            for k in range(DC):
                MM(P, C.ps[pb][:, :n], C.ws[sb_][:, k * nc_ + sub * 128:k * nc_ + sub * 128 + 128], C.h(k),
                   k == 0, k == DC - 1, [C.Bws[sb_], C.Bh[k]], [C.Bps[pb]])
            t = C.tmpslot()
            ACT(P, C.tmp[t][:], C.ps[pa][:, :n], AF.Silu, [C.Bps[pa]], [C.Btmp[t]])
            TT(P, "dve", C.bg(c), C.tmp[t][:], C.ps[pb][:, :n], ALU.mult, [C.Btmp[t], C.Bps[pb]], [C.Bbig[c]])

    def load_dn(j):
        s = C.slot()
        wload(C, s, wd[:, j * 128:(j + 1) * 128], FC, 128)
        return s

    pend = [load_dn(0)]
    for j in range(DC):
        if j + 1 < DC:
            pend.append(load_dn(j + 1))
        s = pend.pop(0)
        pb = 4 + j % 2
        for c in range(FC):
            MM(P, C.ps[pb][:, :n], C.ws[s][:, c * 128:(c + 1) * 128], C.bg(c), c == 0, c == FC - 1,
               [C.Bws[s], C.Bbig[c]], [C.Bps[pb]])
        STT(P, C.x(j), C.ps[pb][:, :n], 0.5, C.x(j), ALU.mult, ALU.add, [C.Bps[pb], C.Bx[j]], [C.Bx[j]])


def r_merge(C, t0, yin, gm_sb, Bgm, sn_sb, Bsn, wg, pa_w, pb_w, pc_w, wo):
    P = C.P
    n = C.TT
    r_norm(C, gm_sb, Bgm)
    for k in range(8):
        DMA(P, "pool", C.bg(k), yin[k * 128:(k + 1) * 128, t0:t0 + n], [], [C.Bbig[k]], f"y{k}")
    for g in range(2):
        for q in range(4):
            r = 1024 + (g * 4 + q) * 128
            DMA(P, "sp", C.yc32[:, q * n:(q + 1) * n], yin[r:r + 128, t0:t0 + n], [], [C.Byc[q]], f"yc{q}")
        t = C.tmpslot()
        r_rstd(C, [C.yc32[:, q * n:(q + 1) * n] for q in range(4)], C.Byc, 512, C.tmp[t][:], C.Btmp[t], 7)
        for q in range(4):
            k = g * 4 + q
            STT(P, C.bg(8 + k), C.yc32[:, q * n:(q + 1) * n], sn_sb[:, k:k + 1], C.tmp[t][:], ALU.mult, ALU.mult,
                [C.Byc[q], Bsn, C.Btmp[t]], [C.Bbig[8 + k]])
    GW = 256

    def load_m(j2):
        s = [C.slot() for _ in range(4)]
        for i in range(3):
            wload(C, s[i], wg[:, i * D_MODEL + j2 * GW: i * D_MODEL + (j2 + 1) * GW], DC, GW)
        o = C.ws[s[3]]
        DMA(P, "pool", o[:, 0:4 * GW].rearrange("p (k c) -> p k c", k=4),
            pa_w[:, j2 * GW:(j2 + 1) * GW].rearrange("(k p) c -> p k c", p=128), [], [C.Bws[s[3]]], f"ws{s[3]}")
        DMA(P, "pool", o[:, 4 * GW:8 * GW].rearrange("p (k c) -> p k c", k=4),
            pb_w[:, j2 * GW:(j2 + 1) * GW].rearrange("(k p) c -> p k c", p=128), [], [C.Bws[s[3]]], f"ws{s[3]}")
        DMA(P, "pool", o[:, 8 * GW:16 * GW].rearrange("p (k c) -> p k c", k=8),
            pc_w[:, j2 * GW:(j2 + 1) * GW].rearrange("(k p) c -> p k c", p=128), [], [C.Bws[s[3]]], f"ws{s[3]}")
        return s

    for j2 in range(D_MODEL // GW):
        s = load_m(j2)
        for sub in range(GW // 128):
            j = j2 * 2 + sub
            tg = []
            for i in range(3):
                for k in range(DC):
                    MM(P, C.ps[i][:, :n], C.ws[s[i]][:, k * GW + sub * 128:k * GW + sub * 128 + 128], C.h(k),
                       k == 0, k == DC - 1, [C.Bws[s[i]], C.Bh[k]], [C.Bps[i]])
                t = C.tmpslot()
                tg.append(t)
                ACT(P, C.tmp[t][:], C.ps[i][:, :n], AF.Sigmoid, [C.Bps[i]], [C.Btmp[t]])
            o = C.ws[s[3]]
            for i, (base, nk, yoff) in enumerate(((0, 4, 0), (4 * GW, 4, 4), (8 * GW, 8, 8))):
                for k in range(nk):
                    MM(P, C.ps[3 + i][:, :n], o[:, base + k * GW + sub * 128: base + k * GW + sub * 128 + 128],
                       C.bg(yoff + k), k == 0, k == nk - 1, [C.Bws[s[3]], C.Bbig[yoff + k]], [C.Bps[3 + i]])
            for i in range(3):
                TT(P, "dve", C.tmp[tg[i]][:], C.tmp[tg[i]][:], C.ps[3 + i][:, :n], ALU.mult,
                   [C.Btmp[tg[i]], C.Bps[3 + i]], [C.Btmp[tg[i]]])
            TT(P, "pool", C.tmp[tg[0]][:], C.tmp[tg[0]][:], C.tmp[tg[1]][:], ALU.add,
               [C.Btmp[tg[0]], C.Btmp[tg[1]]], [C.Btmp[tg[0]]])
            TT(P, "dve", C.bg(16 + j), C.tmp[tg[0]][:], C.tmp[tg[2]][:], ALU.add,
               [C.Btmp[tg[0]], C.Btmp[tg[2]]], [C.Bbig[16 + j]])
    for j2 in range(D_MODEL // GW):
        s = C.slot()
        wload(C, s, wo[:, j2 * GW:(j2 + 1) * GW], DC, GW)
        for sub in range(2):
            j = j2 * 2 + sub
            pb = 6 + j % 2
            for k in range(DC):
                MM(P, C.ps[pb][:, :n], C.ws[s][:, k * GW + sub * 128:k * GW + sub * 128 + 128], C.bg(16 + k),
                   k == 0, k == DC - 1, [C.Bws[s], C.Bbig[16 + k]], [C.Bps[pb]])
            TT(P, "dve", C.x(j), C.x(j), C.ps[pb][:, :n], ALU.add, [C.Bx[j], C.Bps[pb]], [C.Bx[j]])


def build_R(mode, TC=2048, TT_=512):
    nc = bass.Bass("TRN2", target_bir_lowering=False)
    P = Prog(nc)

    def din(name, shape):
        return nc.dram_tensor(name, list(shape), F32, kind="ExternalInput").ap()

    xin = din("xin", [D_MODEL, TC])
    xo = nc.dram_tensor("xo", [D_MODEL, TC], F32, kind="ExternalOutput").ap()
    C = RCtx(P, TT_)
    gains = {}

    def gain(name, ncol=DC):
        a = din(name, [128, ncol])
        t = P.sb(name + "_sb", [128, ncol], F32)
        b = P.buf(name)
        DMA(P, "sp", t[:], a[:, :], [], [b], name)
        gains[name] = (t, b)

    if mode in ("RA", "R"):
        yin = din("yin", [D_MODEL, TC])
        gain("gm"); gain("sn", 8); gain("g2")
        wg = din("wg", [D_MODEL, 3 * D_MODEL])
        pa_w = din("pa", [512, D_MODEL]); pb_w = din("pb", [512, D_MODEL]); pc_w = din("pc", [1024, D_MODEL])
        wo = din("wo", [D_MODEL, D_MODEL])
        wu2 = din("wu2", [D_MODEL, 2 * D_FF]); wd2 = din("wd2", [D_FF, D_MODEL])
    if mode in ("A", "RA"):
        gain("g1")
        wu1 = din("wu1", [D_MODEL, 2 * D_FF]); wd1 = din("wd1", [D_FF, D_MODEL])
    n = TT_
    for ti in range(TC // n):
        t0 = ti * n
        DMA(P, "sp", C.xT[:, :].rearrange("p (k t) -> p k t", k=DC),
            xin[:, t0:t0 + n].rearrange("(k p) t -> p k t", p=128), [], C.Bx, "xin")
        if mode in ("RA", "R"):
            r_merge(C, t0, yin, gains["gm"][0], gains["gm"][1], gains["sn"][0], gains["sn"][1],
                    wg, pa_w, pb_w, pc_w, wo)
            r_ffn(C, gains["g2"][0], gains["g2"][1], wu2, wd2)
        if mode in ("A", "RA"):
            r_ffn(C, gains["g1"][0], gains["g1"][1], wu1, wd1)
        DMA(P, "sp", xo[:, t0:t0 + n].rearrange("(k p) t -> p k t", p=128),
            C.xT[:, :].rearrange("p (k t) -> p k t", k=DC), C.Bx, [], "xout")
    P.emit()
    return nc


NCH_M = 18 * 128 + 9
PJ_SMALL = 18 * 128
NEGBIG = -30000.0
DEBUG_CUT = 0


class Arena:
    def __init__(self, P, name, ncols):
        self.t = P.sb(name, [128, ncols], F32)
        self.n = ncols
        self.off = 0

    def reset(self):
        self.off = 0

    def f32(self, n):
        assert self.off + n <= self.n, (self.off, n, self.n)
        ap = self.t[:, self.off:self.off + n]
        self.off += n
        return ap

    def bf16(self, n):
        m = (n + 1) // 2
        assert self.off + m <= self.n, (self.off, m, self.n)
        ap = self.t[:, self.off:self.off + m].bitcast(BF16)[:, 0:n]
        self.off += m
        return ap


class MCtx:
    def __init__(self, P, nc, T):
        self.P = P
        self.nc = nc
        self.T = T
        self.A = Arena(P, "arena", 47000)
        self.ps = [P.ps(f"ps{i}", [128, 512]) for i in range(8)]
        self.Bps = P.bufs(8, "ps")
        for b_ in self.Bps:
            b_.excl = True
        self.ones = P.sb("ones", [128, 128], F32)
        self.Bones = P.buf("ones")
        self.ident = P.sb("ident_sb", [128, 128], F32)
        self.Bid = P.buf("ident")
        MS(P, "dve", self.ones[:], 1.0, [self.Bones])
        self.pj = nc.dram_tensor("pj", [19 * 128, T], F32).ap()
        self.Bpj = P.buf("pj")

    def din(self, name, shape):
        return self.nc.dram_tensor(name, list(shape), F32, kind="ExternalInput").ap()

    def const(self, name, shape, eng="sp"):
        a = self.din(name, shape)
        t = self.P.sb(name + "_sb", shape, F32)
        b = self.P.buf(name)
        DMA(self.P, eng, t[:], a, [], [b], name)
        return t, b


def m_inproj(M):
    P, T, A = M.P, M.T, M.A
    n = 512
    x1f = M.din("x1f", [2, D_MODEL, T])
    selb, Bsel = M.const("selb", [128, 2])
    gm, Bgm = M.const("gmix", [128, DC])
    wm = M.din("wm", [D_MODEL, NCH_M])
    A.reset()
    wsb = A.bf16(DC * NCH_M)
    Bw = P.buf("wm")
    for k0 in range(0, DC, 4):
        DMA(P, "pool", wsb[:, k0 * NCH_M:(k0 + 4) * NCH_M].rearrange("p (k c) -> p k c", k=4),
            wm[k0 * 128:(k0 + 4) * 128, :].rearrange("(k p) c -> p k c", p=128), [], [Bw], "wm")
    xa = A.f32(DC * n)
    xb = A.f32(DC * n)
    Bxa4, Bxb4 = P.bufs(4, "xa"), P.bufs(4, "xb")
    hTs = [A.bf16(DC * n) for _ in range(2)]
    Bhs = [P.bufs(DC, f"h{i}") for i in range(2)]
    sq = [A.f32(n) for _ in range(2)]
    Bsq = P.bufs(2, "sq")
    rstd = A.f32(n)
    Brstd = P.buf("rstd")
    st = [A.f32(n) for _ in range(4)]
    Bst = P.bufs(4, "st")
    cnt = 0

    def load_x(ti):
        t0 = ti * n
        for q in range(4):
            DMA(P, "sp", xa[:, q * 4 * n:(q + 1) * 4 * n].rearrange("p (k t) -> p k t", k=4),
                x1f[0, q * 512:(q + 1) * 512, t0:t0 + n].rearrange("(k p) t -> p k t", p=128), [], [Bxa4[q]], f"xa{q}")
            DMA(P, "sp", xb[:, q * 4 * n:(q + 1) * 4 * n].rearrange("p (k t) -> p k t", k=4),
                x1f[1, q * 512:(q + 1) * 512, t0:t0 + n].rearrange("(k p) t -> p k t", p=128), [], [Bxb4[q]], f"xb{q}")

    load_x(0)
    for ti in range(T // n):
        t0 = ti * n
        hT = hTs[ti % 2]
        Bh = Bhs[ti % 2]
        for q in range(4):
            qs = slice(q * 4 * n, (q + 1) * 4 * n)
            TS(P, "pool", xa[:, qs], xa[:, qs], selb[:, 0:1], None, ALU.mult, None, [Bxa4[q], Bsel], [Bxa4[q]])
            STT(P, xa[:, qs], xb[:, qs], selb[:, 1:2], xa[:, qs], ALU.mult, ALU.add, [Bxa4[q], Bxb4[q], Bsel], [Bxa4[q]])
        for k in range(DC):
            s = k % 2
            Bxa = Bxa4[k // 4]
            ACT(P, sq[s], xa[:, k * n:(k + 1) * n], AF.Square, [Bxa], [Bsq[s]])
            MM(P, M.ps[6][:, :n], M.ones[:], sq[s], k == 0, k == DC - 1, [M.Bones, Bsq[s]], [M.Bps[6]])
        ACT(P, rstd, M.ps[6][:, :n], AF.Sqrt, [M.Bps[6]], [Brstd], bias=EPS, scale=1.0 / D_MODEL)
        P.op("dve", lambda e: e.reciprocal(out=rstd, in_=rstd), [Brstd], [Brstd])
        for k in range(DC):
            STT(P, hT[:, k * n:(k + 1) * n], xa[:, k * n:(k + 1) * n], gm[:, k:k + 1], rstd, ALU.mult, ALU.mult,
                [Bxa4[k // 4], Bgm, Brstd], [Bh[k]])
        if ti + 1 < T // n:
            load_x(ti + 1)
        for c in range(19):
            cols = 128 if c < 18 else 9
            pb = cnt % 4
            s = cnt % 4
            cnt += 1
            for k in range(DC):
                MM(P, M.ps[pb][0:cols, :n], wsb[:, k * NCH_M + c * 128:k * NCH_M + c * 128 + cols],
                   hT[:, k * n:(k + 1) * n], k == 0, k == DC - 1, [Bw, Bh[k]], [M.Bps[pb]])
            CP(P, "act" if cnt % 2 else "dve", st[s][0:cols, :], M.ps[pb][0:cols, :n], [M.Bps[pb]], [Bst[s]])
            DMA(P, "sp", M.pj[c * 128:c * 128 + cols, t0:t0 + n], st[s][0:cols, :], [Bst[s]], [], f"pj{s}")
    P.barrier()


def conv_silu(P, out, raw, w4, bias, n, reads, writes, eng="dve"):
    if bias is None:
        TS(P, eng, out, raw[:, 3:3 + n], w4[:, 3:4], None, ALU.mult, None, reads, writes)
    else:
        TS(P, eng, out, raw[:, 3:3 + n], w4[:, 3:4], bias, ALU.mult, ALU.add, reads, writes)
    for k in range(3):
        STT(P, out, raw[:, k:k + n], w4[:, k:k + 1], out, ALU.mult, ALU.add, reads + writes, writes)
    ACT(P, out, out, AF.Silu, writes, writes)


def load_halo(P, dst, pj_rows, t0, n, reads, writes, key, eng="sp"):
    if t0 == 0:
        MS(P, "pool", dst[:, 0:3], 0.0, writes)
        DMA(P, eng, dst[:, 3:3 + n], pj_rows[:, 0:n], reads, writes, key)
    else:
        DMA(P, eng, dst[:, 0:3 + n], pj_rows[:, t0 - 3:t0 + n], reads, writes, key)


def m_ssd(M):
    P, T, A = M.P, M.T, M.A
    n = 512
    sconv, Bsc = M.const("sconv", [128, 16])
    sbias, Bsb = M.const("sbias", [128, 4])
    sdtb, Bdtb = M.const("sdtb", [1, 4])
    salog, Balog = M.const("salog", [1, 4])
    sD, BsD = M.const("sD", [128, 2])
    negtri, Bnt = M.const("negtriS", [128, 128])
    reset, Brs = M.const("reset128", [1, 2048])
    pj = M.pj
    A.reset()
    sT = A.f32(256)
    BsT = P.bufs(4, "sT")
    MS(P, "dve", sT, 0.0, BsT)
    negA = P.sb("negA", [1, 4], F32)
    BnA = P.buf("negA")
    ACT(P, negA[:], salog[:], AF.Exp, [Balog], [BnA])
    TS(P, "dve", negA[:], negA[:], -1.0, None, ALU.mult, None, [BnA], [BnA])
    raw = [[A.f32(n + 3) for _ in range(4)] for _ in range(2)]
    Braw = [P.bufs(4, f"raw{i}") for i in range(2)]
    zr = [[A.f32(n) for _ in range(2)] for _ in range(2)]
    Bzr = [P.bufs(2, f"zr{i}") for i in range(2)]
    dtr = [A.f32(4 * n) for _ in range(2)]
    Bdtr = P.bufs(2, "dtr")
    cv = [A.f32(n) for _ in range(4)]
    Bcv = P.bufs(4, "cv")
    dA = A.f32(4 * n)
    BdA = P.buf("dA")
    ac = A.f32(4 * n)
    Bac = P.buf("ac")
    yst = [A.f32(n) for _ in range(2)]
    Byst = P.bufs(2, "yst")
    tok = A.f32(384)
    Btok = P.buf("tok")
    NB = 4
    cl3 = [A.f32(3) for _ in range(NB)]
    Bcl = P.bufs(NB, "cl3")
    e1 = [A.f32(128) for _ in range(NB)]
    Be1 = P.bufs(NB, "e1")
    sg = [A.f32(128) for _ in range(NB)]
    Bsg = P.bufs(NB, "sg")
    sc = [A.f32(128) for _ in range(NB)]
    Bscb = P.bufs(NB, "sc")
    xdt = [A.f32(64) for _ in range(NB)]
    Bxdt = P.bufs(NB, "xdt")
    xdd = [A.f32(64) for _ in range(NB)]
    Bxdd = P.bufs(NB, "xdd")
    decl = [A.f32(1) for _ in range(NB)]
    Bdecl = P.bufs(NB, "decl")
    cdec = [A.f32(128) for _ in range(NB)]
    Bcdec = P.bufs(NB, "cdec")
    ps, Bps = M.ps, M.Bps
    chrow = (14, 15, 16, 17)
    nst = T // n

    def load(si):
        s = si % 2
        t0 = si * n
        for q in range(4):
            r = chrow[q] * 128
            load_halo(P, raw[s][q], pj[r:r + 128, :], t0, n, [M.Bpj], [Braw[s][q]], f"sraw{s}{q}")
        for q in range(2):
            r = (12 + q) * 128
            DMA(P, "sp", zr[s][q], pj[r:r + 128, t0:t0 + n], [M.Bpj], [Bzr[s][q]], f"sz{s}{q}")
        DMA(P, "sp", dtr[s][0:1, :].rearrange("o (r t) -> o r t", r=4),
            pj[PJ_SMALL + 5:PJ_SMALL + 9, t0:t0 + n].rearrange("(o r) t -> o r t", o=1), [M.Bpj], [Bdtr[s]], f"sdt{s}")

    load(0)
    it = 0
    for si in range(nst):
        s = si % 2
        t0 = si * n
        if si + 1 < nst:
            load(si + 1)
        for q in range(4):
            conv_silu(P, cv[q], raw[s][q], sconv[:, q * 4:q * 4 + 4], sbias[:, q:q + 1], n,
                      [Braw[s][q], Bsc, Bsb], [Bcv[q]])
        for q in range(2):
            ACT(P, zr[s][q], zr[s][q], AF.Silu, [Bzr[s][q]], [Bzr[s][q]])
        d = dtr[s]
        if DEBUG_CUT == 1:
            continue
        for p in range(4):
            ACT(P, d[0:1, p * n:(p + 1) * n], d[0:1, p * n:(p + 1) * n], AF.Exp, [Bdtr[s], Bdtb], [Bdtr[s]],
                bias=sdtb[0:1, p:p + 1])
        ACT(P, d[0:1, :], d[0:1, :], AF.Ln, [Bdtr[s]], [Bdtr[s]], bias=1.0)
        for p in range(4):
            TS(P, "dve", dA[0:1, p * n:(p + 1) * n], d[0:1, p * n:(p + 1) * n], negA[0:1, p:p + 1], None,
               ALU.mult, None, [Bdtr[s], BnA], [BdA])
        P.op("dve", lambda e: e.tensor_tensor_scan(out=ac[0:1, :], data0=reset[0:1, :], data1=dA[0:1, :],
                                                   initial=0.0, op0=ALU.mult, op1=ALU.add),
             [BdA, Brs], [Bac])
        if DEBUG_CUT == 2:
            continue
        for c in range(4):
            l0 = c * 128
            TR(P, ps[7][:, 0:128], cv[2][:, l0:l0 + 128], M.ident[:], [Bcv[2], M.Bid], [Bps[7]])
            TR(P, ps[7][:, 128:256], cv[0][:, l0:l0 + 128], M.ident[:], [Bcv[0], M.Bid], [Bps[7]])
            TR(P, ps[7][:, 256:384], cv[1][:, l0:l0 + 128], M.ident[:], [Bcv[1], M.Bid], [Bps[7]])
            CP(P, "act", tok, ps[7][:, 0:384], [Bps[7]], [Btok])
            MM(P, ps[6][:, 0:128], cv[2][:, l0:l0 + 128], cv[3][:, l0:l0 + 128], True, True,
               [Bcv[2], Bcv[3]], [Bps[6]])
            if DEBUG_CUT == 3:
                continue
            def head_ops(p):
                o = p * n + l0
                b = p
                pbk = 4 + p % 2
                c0 = (p // 2) * 256
                MM(P, ps[pbk][:, c0:c0 + 128], M.ones[0:1, 0:128], ac[0:1, o:o + 128], True, True,
                   [M.Bones, Bac], [Bps[pbk]])
                MM(P, ps[pbk][:, c0 + 128:c0 + 129], ac[0:1, o:o + 128], M.ones[0:1, 0:1], True, True,
                   [M.Bones, Bac], [Bps[pbk]])
                MM(P, ps[pbk][:, c0 + 129:c0 + 130], d[0:1, o:o + 128], M.ones[0:1, 0:1], True, True,
                   [M.Bones, Bdtr[s]], [Bps[pbk]])
                yield
                CP(P, "dve", cl3[b], ps[pbk][:, c0 + 127:c0 + 130], [Bps[pbk]], [Bcl[b]])
                ACT(P, e1[b], ps[pbk][:, c0:c0 + 128], AF.Exp, [Bps[pbk]], [Be1[b]])
                yield
                STT(P, sg[b], ps[pbk][:, c0:c0 + 128], cl3[b][:, 1:2], negtri[:], ALU.subtract, ALU.add,
                    [Bps[pbk], Bcl[b], Bnt], [Bsg[b]])
                TS(P, "pool", xdt[b], tok[:, 128 + p * 64:128 + (p + 1) * 64], cl3[b][:, 2:3], None, ALU.mult, None,
                   [Btok, Bcl[b]], [Bxdt[b]])
                ACT(P, decl[b], cl3[b][:, 1:2], AF.Exp, [Bcl[b]], [Bdecl[b]], bias=cl3[b][:, 0:1], scale=-1.0)
                yield
                ACT(P, sg[b], sg[b], AF.Exp, [Bsg[b]], [Bsg[b]])
                TS(P, "pool", xdd[b], xdt[b], decl[b][:, 0:1], None, ALU.mult, None, [Bxdt[b], Bdecl[b]], [Bxdd[b]])
                TT(P, "pool", cdec[b], cv[3][:, l0:l0 + 128], e1[b], ALU.mult, [Bcv[3], Be1[b]], [Bcdec[b]])
                yield
                TT(P, "dve", sc[b], ps[6][:, 0:128], sg[b], ALU.mult, [Bps[6], Bsg[b]], [Bscb[b]])
                yield
                pr = p // 2
                yo_ = ps[pr][(p % 2) * 64:(p % 2) * 64 + 64, 0:128]
                MM(P, yo_, xdt[b], sc[b], True, False, [Bxdt[b], Bscb[b]], [Bps[pr]])
                MM(P, yo_, sT[:, p * 64:(p + 1) * 64], cdec[b], False, True, [BsT[p], Bcdec[b]], [Bps[pr]])
                pd = 2 + p % 2
                MM(P, ps[pd][:, (p // 2) * 64:(p // 2) * 64 + 64], tok[:, 0:128], xdd[b], True, True, [Btok, Bxdd[b]], [Bps[pd]])
                yield
                STT(P, sT[:, p * 64:(p + 1) * 64], sT[:, p * 64:(p + 1) * 64], e1[b][:, 127:128], ps[pd][:, (p // 2) * 64:(p // 2) * 64 + 64],
                    ALU.mult, ALU.add, [BsT[p], Be1[b], Bps[pd]], [BsT[p]])

            gens = [head_ops(p) for p in range(4)]
            while gens:
                for g_ in list(gens):
                    try:
                        next(g_)
                    except StopIteration:
                        gens.remove(g_)
            for pr in range(2):
                STT(P, yst[pr][:, l0:l0 + 128], cv[pr][:, l0:l0 + 128], sD[:, pr:pr + 1], ps[pr][:, 0:128],
                    ALU.mult, ALU.add, [Bcv[pr], BsD, Bps[pr]], [Byst[pr]])
        for pr in range(2):
            TT(P, "pool", yst[pr], yst[pr], zr[s][pr], ALU.mult, [Byst[pr], Bzr[s][pr]], [Byst[pr]])
            DMA(P, "sp", M.yo[256 + pr * 128:256 + (pr + 1) * 128, t0:t0 + n], yst[pr], [Byst[pr]], [], f"yc{pr}")
    P.barrier()


def build_M(T=8192, stages=("ssd", "gdn", "nsa")):
    nc = bass.Bass("TRN2", target_bir_lowering=False)
    P = Prog(nc)
    M = MCtx(P, nc, T)
    idd = M.din("ident", [128, 128])
    DMA(P, "sp", M.ident[:], idd, [], [M.Bid], "ident")
    M.yo = nc.dram_tensor("yo", [512, T], F32, kind="ExternalOutput").ap()
    m_inproj(M)
    if "ssd" in stages:
        m_ssd(M)
    if "gdn" in stages:
        m_gdn(M)
    if "nsa" in stages:
        m_nsa(M)
    P.emit()
    return nc


def gl(g, ncol=DC):
    return np.ascontiguousarray(np.asarray(g, np.float32).reshape(ncol, 128).T)


def m_cols(h):
    g = h // 2
    cols = []
    for base in (0 + 128 * h, 512 + 128 * h, 1024 + 128 * h, 1536 + 128 * h,
                 2056 + 128 * h, 2056 + 128 * (h ^ 1), 2568 + 128 * g, 2824 + 128 * g,
                 3080 + 128 * g, 3336 + 128 * g, 3592 + 128 * g, 3848 + 128 * g,
                 4116 + 256 * h, 4116 + 256 * h + 128, 5140 + 256 * h, 5140 + 256 * h + 128,
                 5140 + 1024 + 128 * g, 5140 + 1280 + 128 * g):
        cols += list(range(base, base + 128))
    cols += [2048 + h, 2052 + h, 4104 + 3 * h, 4104 + 3 * h + 1, 4104 + 3 * h + 2]
    cols += [6676 + 4 * h + i for i in range(4)]
    return np.array(cols)


def m_consts(T):
    c = {}
    c["ident"] = np.eye(128, dtype=np.float32)
    p = np.arange(128)[:, None]
    f = np.arange(128)[None, :]
    c["negtriS"] = np.where(f < p, NEGBIG, 0.0).astype(np.float32)
    r = np.ones((1, 2048), np.float32)
    r[0, ::128] = 0.0
    c["reset128"] = r
    r = np.ones((1, 512), np.float32)
    r[0, ::64] = 0.0
    c["reset64"] = r
    p = np.arange(64)[:, None]
    f = np.arange(64)[None, :]
    c["gmaskU"] = np.tile(np.where(f < p, NEGBIG, 0.0).astype(np.float32), (1, 8))
    c["gmaskL"] = np.tile(np.where(f >= p, -NEGBIG, 0.0).astype(np.float32), (1, 8))
    c["gstrict"] = np.tile((f > p).astype(np.float32), (1, 8))
    c.update(nsa_consts())
    return c


def m_layer_inputs(prm, l, h, T):
    g = h // 2
    d = {}
    d["gmix"] = gl(prm["g_mix"][l])
    d["wm"] = np.ascontiguousarray(prm["w_in"][l][:, m_cols(h)])
    cw = prm["ssm_conv_w"][l]
    cb = prm["ssm_conv_b"][l]
    chans = [256 * h + np.arange(128), 256 * h + 128 + np.arange(128), 1024 + 128 * g + np.arange(128),
             1280 + 128 * g + np.arange(128)]
    d["sconv"] = np.ascontiguousarray(np.concatenate([cw[:, ch].T for ch in chans], axis=1))
    d["sbias"] = np.ascontiguousarray(np.stack([cb[ch] for ch in chans], axis=1))
    d["sdtb"] = np.ascontiguousarray(prm["ssm_dt_bias"][l][4 * h:4 * h + 4][None, :])
    d["salog"] = np.ascontiguousarray(prm["ssm_a_log"][l][4 * h:4 * h + 4][None, :])
    dd = prm["ssm_d"][l][4 * h:4 * h + 4]
    d["sD"] = np.ascontiguousarray(np.stack([np.repeat(dd[0:2], 64), np.repeat(dd[2:4], 64)], axis=1))
    gcw = prm["gdn_conv"][l]
    d["gconv"] = np.ascontiguousarray(np.concatenate([gcw[:, q * 512 + 128 * h + np.arange(128)].T for q in range(3)], axis=1))
    d["gsc"] = np.array([[prm["gdn_a_log"][l][h], prm["gdn_dt_bias"][l][h]]], np.float32)
    d["gnorm"] = np.ascontiguousarray(prm["gdn_norm"][l][:, None])
    d["nqn"] = np.ascontiguousarray(prm["nsa_q_norm"][l][:, None])
    d["nkn"] = np.ascontiguousarray(prm["nsa_k_norm"][l][:, None])
    d["pekT"] = np.ascontiguousarray(prm["nsa_pe_k"][l].T)
    d["pevT"] = np.ascontiguousarray(prm["nsa_pe_v"][l].T)
    d["wck"] = np.ascontiguousarray(prm["nsa_w_ck"][l].transpose(1, 0, 2).reshape(128, 4096))
    d["wcv"] = np.ascontiguousarray(prm["nsa_w_cv"][l].transpose(1, 0, 2).reshape(128, 4096))
    rt = prm["rel_table"]
    d["ntab"] = np.ascontiguousarray(rt[:, [h, h ^ 1]])
    d["nt31"] = np.ascontiguousarray(np.tile(rt[31, [h, h ^ 1]][None, :], (128, 1)))
    return d


def m_gdn(M):
    P, T, A = M.P, M.T, M.A
    n = 512
    NCK = 8
    gconv, Bgc = M.const("gconv", [128, 12])
    gsc, Bgs = M.const("gsc", [1, 2])
    gnorm, Bgn = M.const("gnorm", [128, 1])
    mU, BmU = M.const("gmaskU", [64, 512])
    mL, BmL = M.const("gmaskL", [64, 512])
    mS, BmS = M.const("gstrict", [64, 512])
    reset, Brs = M.const("reset64", [1, 512])
    pj = M.pj
    ps, Bps = M.ps, M.Bps
    A.reset()
    S = A.f32(128)
    BS = P.buf("S")
    MS(P, "dve", S, 0.0, [BS])
    negA = P.sb("gnegA", [1, 1], F32)
    BnA = P.buf("gnegA")
    ACT(P, negA[:], gsc[0:1, 0:1], AF.Exp, [Bgs], [BnA])
    TS(P, "dve", negA[:], negA[:], -1.0, None, ALU.mult, None, [BnA], [BnA])
    raw = [[A.f32(n + 3) for _ in range(3)] for _ in range(2)]
    Braw = [P.bufs(3, f"graw{i}") for i in range(2)]
    zr = [A.f32(n) for _ in range(2)]
    Bzr = P.bufs(2, "gz")
    abr = [A.f32(2 * n) for _ in range(2)]
    Bab = P.bufs(2, "gab")
    cv = [A.f32(n) for _ in range(3)]
    Bcv = P.bufs(3, "gcv")
    sq = A.f32(n)
    Bsq = P.buf("gsq")
    rs = A.f32(n)
    Brs_ = P.buf("grs")
    gcr = A.f32(n)
    Bgcr = P.buf("gcr")
    ktok = A.f32(NCK * 128)
    vtok = A.f32(NCK * 128)
    Bktok, Bvtok = P.buf("ktok"), P.buf("vtok")
    cols = A.f32(NCK * 4)
    Bcols = P.buf("gcols")
    ex = A.f32(NCK * 4)
    Bex = P.buf("gex")
    E1 = A.f32(n)
    BE1 = P.buf("gE1")
    dm = A.f32(n)
    Bdm = P.buf("gdm")
    dmT = A.f32(n)
    BdmT = P.buf("gdmT")
    X = [A.f32(n) for _ in range(2)]
    Xt = [A.f32(n) for _ in range(2)]
    BX = P.bufs(2, "gX")
    BXt = P.bufs(2, "gXt")
    Rm = A.f32(n)
    BR = P.buf("gR")
    attn = A.f32(n)
    Battn = P.buf("gattn")
    kb = A.f32(NCK * 128)
    vb = A.f32(NCK * 128)
    kdec = A.f32(NCK * 128)
    Bkb, Bvb, Bkdec = P.buf("kb"), P.buf("vb"), P.buf("kdec")
    qd = A.f32(n)
    Bqd = P.buf("qd")
    u = A.f32(NCK * 128)
    Bu = P.buf("gu")
    wT = A.f32(n)
    BwT = P.buf("gwT")
    vnew = [A.f32(128) for _ in range(2)]
    Bvn = P.bufs(2, "gvn")
    osb = A.f32(n)
    Bosb = P.buf("gosb")
    nst = T // n

    def load(si):
        s = si % 2
        t0 = si * n
        for q in range(3):
            load_halo(P, raw[s][q], pj[q * 128:(q + 1) * 128, :], t0, n, [M.Bpj], [Braw[s][q]], f"graw{s}{q}")
        DMA(P, "sp", zr[s], pj[3 * 128:4 * 128, t0:t0 + n], [M.Bpj], [Bzr[s]], f"gz{s}")
        DMA(P, "sp", abr[s][0:1, :].rearrange("o (r t) -> o r t", r=2),
            pj[PJ_SMALL:PJ_SMALL + 2, t0:t0 + n].rearrange("(o r) t -> o r t", o=1), [M.Bpj], [Bab[s]], f"gab{s}")

    load(0)
    for si in range(nst):
        s = si % 2
        t0 = si * n
        if si + 1 < nst:
            load(si + 1)
        for q in range(3):
            conv_silu(P, cv[q], raw[s][q], gconv[:, q * 4:q * 4 + 4], None, n, [Braw[s][q], Bgc], [Bcv[q]])
        ACT(P, zr[s], zr[s], AF.Silu, [Bzr[s]], [Bzr[s]])
        for q in range(2):
            ACT(P, sq, cv[q], AF.Square, [Bcv[q]], [Bsq])
            MM(P, ps[7][:, :n], M.ones[:], sq, True, True, [M.Bones, Bsq], [Bps[7]])
            ACT(P, rs, ps[7][:, :n], AF.Sqrt, [Bps[7]], [Brs_], bias=EPS)
            P.op("dve", lambda e: e.reciprocal(out=rs, in_=rs), [Brs_], [Brs_])
            if q == 0:
                STT(P, cv[q], cv[q], 128.0 ** -0.5, rs, ALU.mult, ALU.mult, [Bcv[q], Brs_], [Bcv[q]])
            else:
                TT(P, "dve", cv[q], cv[q], rs, ALU.mult, [Bcv[q], Brs_], [Bcv[q]])
        if DEBUG_CUT == 12:
            for q in range(3):
                DMA(P, "sp", M.yo[128 + q * 128:256 + q * 128, t0:t0 + n], cv[q], [Bcv[q]], [], f"dbg{q}")
        ar = abr[s][0:1, 0:n]
        br = abr[s][0:1, n:2 * n]
        ACT(P, ar, ar, AF.Exp, [Bab[s], Bgs], [Bab[s]], bias=gsc[0:1, 1:2])
        ACT(P, ar, ar, AF.Ln, [Bab[s]], [Bab[s]], bias=1.0)
        TS(P, "dve", ar, ar, negA[0:1, 0:1], None, ALU.mult, None, [Bab[s], BnA], [Bab[s]])
        ACT(P, br, br, AF.Sigmoid, [Bab[s]], [Bab[s]])
        P.op("dve", lambda e, ar=ar: e.tensor_tensor_scan(out=gcr[0:1, :], data0=reset[0:1, :], data1=ar,
                                                          initial=0.0, op0=ALU.mult, op1=ALU.add),
             [Bab[s], Brs], [Bgcr])
        MM(P, ps[7][:, :n], M.ones[0:1, 0:128], gcr[0:1, :], True, True, [M.Bones, Bgcr], [Bps[7]])
        ACT(P, E1, ps[7][:, :n], AF.Exp, [Bps[7]], [BE1])
        for i in range(NCK):
            c0 = i * 64
            MM(P, ps[6][0:64, 2 * i:2 * i + 1], gcr[0:1, c0:c0 + 64], M.ones[0:1, 0:1], True, True,
               [M.Bones, Bgcr], [Bps[6]])
            MM(P, ps[6][0:64, 2 * i + 1:2 * i + 2], br[:, c0:c0 + 64], M.ones[0:1, 0:1], True, True,
               [M.Bones, Bab[s]], [Bps[6]])
        cv3 = cols[0:64, :].rearrange("p (i c) -> p i c", c=4)
        CP(P, "dve", cv3[:, :, 0:2], ps[6][0:64, 0:2 * NCK].rearrange("p (i c) -> p i c", c=2), [Bps[6]], [Bcols])
        CP(P, "dve", cv3[:, :, 2:3], ps[7][0:64, :n].rearrange("p (i c) -> p i c", c=64)[:, :, 63:64],
           [Bps[7]], [Bcols])
        ex3 = ex[0:64, :].rearrange("p (i c) -> p i c", c=4)
        ACT(P, ex3[:, :, 0:1], cv3[:, :, 0:1], AF.Exp, [Bcols], [Bex])
        TT(P, "dve", ex3[:, :, 0:1], ex3[:, :, 0:1], cv3[:, :, 1:2], ALU.mult, [Bex, Bcols], [Bex])
        TT(P, "dve", ex3[:, :, 1:2], cv3[:, :, 2:3], cv3[:, :, 0:1], ALU.subtract, [Bcols], [Bex])
        ACT(P, ex3[:, :, 1:2], ex3[:, :, 1:2], AF.Exp, [Bex], [Bex])
        TS(P, "dve", ex3[:, :, 2:3], cv3[:, :, 1:2], -1.0, None, ALU.mult, None, [Bcols], [Bex])
        for i in range(NCK):
            c0 = i * 64
            bk = 0 + i // 4
            TR(P, ps[bk][0:64, (i % 4) * 128:(i % 4 + 1) * 128], cv[1][:, c0:c0 + 64], M.ident[:],
               [Bcv[1], M.Bid], [Bps[bk]])
        for i in range(NCK):
            c0 = i * 64
            bk = 2 + i // 4
            TR(P, ps[bk][0:64, (i % 4) * 128:(i % 4 + 1) * 128], cv[2][:, c0:c0 + 64], M.ident[:],
               [Bcv[2], M.Bid], [Bps[bk]])
        for hh in range(2):
            CP(P, "act", ktok[0:64, hh * 512:(hh + 1) * 512], ps[hh][0:64, :], [Bps[hh]], [Bktok])
            CP(P, "dve", vtok[0:64, hh * 512:(hh + 1) * 512], ps[2 + hh][0:64, :], [Bps[2 + hh]], [Bvtok])
        for i in range(NCK):
            sl = slice(i * 128, (i + 1) * 128)
            TS(P, "pool", kb[0:64, sl], ktok[0:64, sl], ex[0:64, 4 * i:4 * i + 1], None, ALU.mult, None,
               [Bktok, Bex], [Bkb])
            TS(P, "pool", vb[0:64, sl], vtok[0:64, sl], cols[0:64, 4 * i + 1:4 * i + 2], None, ALU.mult, None,
               [Bvtok, Bcols], [Bvb])
            TS(P, "pool", kdec[0:64, sl], ktok[0:64, sl], ex[0:64, 4 * i + 1:4 * i + 2], None, ALU.mult, None,
               [Bktok, Bex], [Bkdec])
        for i in range(NCK):
            sl = slice(i * 64, (i + 1) * 64)
            STT(P, dm[0:64, sl], ps[7][0:64, sl], cols[0:64, 4 * i:4 * i + 1], mU[:, sl], ALU.subtract, ALU.add,
                [Bps[7], Bcols, BmU], [Bdm])
            STT(P, dmT[0:64, sl], ps[7][0:64, sl], cols[0:64, 4 * i:4 * i + 1], mL[:, sl], ALU.subtract, ALU.add,
                [Bps[7], Bcols, BmL], [BdmT])
        ACT(P, dm[0:64, :], dm[0:64, :], AF.Exp, [Bdm], [Bdm])
        ACT(P, dmT[0:64, :], dmT[0:64, :], AF.Exp, [BdmT], [BdmT], scale=-1.0)
        for i in range(NCK):
            sl = slice(i * 64, (i + 1) * 64)
            MM(P, ps[4][0:64, sl], cv[1][:, sl], cv[1][:, sl], True, True, [Bcv[1]], [Bps[4]])
        for i in range(NCK):
            sl = slice(i * 64, (i + 1) * 64)
            MM(P, ps[5][0:64, sl], cv[1][:, sl], cv[0][:, sl], True, True, [Bcv[1], Bcv[0]], [Bps[5]])
        MM(P, ps[6][0:64, :n], M.ones[0:1, 0:64], br, True, True, [M.Bones, Bab[s]], [Bps[6]])
        STT(P, X[0][0:64, :], ps[4][0:64, :n], -1.0, dm[0:64, :], ALU.mult, ALU.mult, [Bps[4], Bdm], [BX[0]])
        TT(P, "dve", X[0][0:64, :], X[0][0:64, :], ps[6][0:64, :n], ALU.mult, [BX[0], Bps[6]], [BX[0]])
        TT(P, "pool", X[0][0:64, :], X[0][0:64, :], mS[:, :], ALU.mult, [BX[0], BmS], [BX[0]])
        TT(P, "dve", Xt[0][0:64, :], ps[4][0:64, :n], dmT[0:64, :], ALU.mult, [Bps[4], BdmT], [BXt[0]])
        for i in range(NCK):
            sl = slice(i * 64, (i + 1) * 64)
            TS(P, "pool", Xt[0][0:64, sl], Xt[0][0:64, sl], ex[0:64, 4 * i + 2:4 * i + 3], None, ALU.mult, None,
               [BXt[0], Bex], [BXt[0]])
        TT(P, "dve", attn[0:64, :], ps[5][0:64, :n], dm[0:64, :], ALU.mult, [Bps[5], Bdm], [Battn])
        for i in range(NCK):
            sl = slice(i * 64, (i + 1) * 64)
            TT(P, "pool", Rm[0:64, sl], X[0][0:64, sl], M.ident[0:64, 0:64], ALU.add, [BX[0], M.Bid], [BR])
        cur = 0
        for j in range(5):
            nx = 1 - cur
            for i in range(NCK):
                sl = slice(i * 64, (i + 1) * 64)
                MM(P, ps[0][0:64, sl], Xt[cur][0:64, sl], X[cur][0:64, sl], True, True, [BXt[cur], BX[cur]], [Bps[0]])
            for i in range(NCK):
                sl = slice(i * 64, (i + 1) * 64)
                MM(P, ps[1][0:64, sl], X[cur][0:64, sl], Xt[cur][0:64, sl], True, True, [BXt[cur], BX[cur]], [Bps[1]])
            CP(P, "act", X[nx][0:64, :], ps[0][0:64, :n], [Bps[0]], [BX[nx]])
            CP(P, "dve", Xt[nx][0:64, :], ps[1][0:64, :n], [Bps[1]], [BXt[nx]])
            for i in range(NCK):
                sl = slice(i * 64, (i + 1) * 64)
                MM(P, ps[2][0:64, sl], Xt[nx][0:64, sl], Rm[0:64, sl], True, True, [BXt[nx], BR], [Bps[2]])
            TT(P, "dve", Rm[0:64, :], Rm[0:64, :], ps[2][0:64, :n], ALU.add, [BR, Bps[2]], [BR])
            cur = nx
        for i in range(NCK):
            sl = slice(i * 64, (i + 1) * 64)
            bk = i // 4
            MM(P, ps[bk][0:64, (i % 4) * 128:(i % 4 + 1) * 128], Rm[0:64, sl], vb[0:64, i * 128:(i + 1) * 128],
               True, True, [BR, Bvb], [Bps[bk]])
        for hh in range(2):
            CP(P, "act" if hh else "dve", u[0:64, hh * 512:(hh + 1) * 512], ps[hh][0:64, :], [Bps[hh]], [Bu])
        for i in range(NCK):
            sl = slice(i * 64, (i + 1) * 64)
            MM(P, ps[2][:, sl], kb[0:64, i * 128:(i + 1) * 128], Rm[0:64, sl], True, True, [Bkb, BR], [Bps[2]])
        CP(P, "act", wT, ps[2][:, :n], [Bps[2]], [BwT])
        TT(P, "pool", qd, cv[0], E1, ALU.mult, [Bcv[0], BE1], [Bqd])
        for i in range(NCK if DEBUG_CUT != 31 else 0):
            sl = slice(i * 64, (i + 1) * 64)
            s128 = slice(i * 128, (i + 1) * 128)
            vb_ = i % 2
            MM(P, ps[3][0:64, 0:128], wT[:, sl], S, True, True, [BwT, BS], [Bps[3]])
            TT(P, "dve", vnew[vb_][0:64, :], u[0:64, s128], ps[3][0:64, 0:128], ALU.subtract, [Bu, Bps[3]], [Bvn[vb_]])
            MM(P, ps[5][:, sl], S, qd[:, sl], True, False, [BS, Bqd], [Bps[5]])
            MM(P, ps[5][:, sl], vnew[vb_][0:64, :], attn[0:64, sl], False, True, [Bvn[vb_], Battn], [Bps[5]])
            MM(P, ps[4][:, 0:128], kdec[0:64, s128], vnew[vb_][0:64, :], True, True, [Bkdec, Bvn[vb_]], [Bps[4]])
            STT(P, S, S, E1[:, i * 64 + 63:i * 64 + 64], ps[4][:, 0:128], ALU.mult, ALU.add, [BS, BE1, Bps[4]], [BS])
        CP(P, "act", osb, ps[5][:, :n], [Bps[5]], [Bosb])
        ACT(P, sq, osb, AF.Square, [Bosb], [Bsq])
        MM(P, ps[7][:, :n], M.ones[:], sq, True, True, [M.Bones, Bsq], [Bps[7]])
        ACT(P, rs, ps[7][:, :n], AF.Sqrt, [Bps[7]], [Brs_], bias=EPS, scale=1.0 / 128)
        P.op("dve", lambda e: e.reciprocal(out=rs, in_=rs), [Brs_], [Brs_])
        STT(P, osb, osb, gnorm[:, 0:1], rs, ALU.mult, ALU.mult, [Bosb, Bgn, Brs_], [Bosb])
        TT(P, "dve", osb, osb, zr[s], ALU.mult, [Bosb, Bzr[s]], [Bosb])
        DMA(P, "sp", M.yo[0:128, t0:t0 + n], osb, [Bosb], [], "ya")
        if DEBUG_CUT == 11:
            P.barrier()
    P.barrier()


def MMA(P, out, lhsT, rhs, reads, writes):
    P.op("pe", lambda e: e.matmul(out, lhsT=lhsT, rhs=rhs, start=False, stop=False, skip_group_check=True),
         reads, writes, pe_acc=True)


PD_S = 2048
PD_C = 5120
W_S = 1792
W_W = 1408
W_C = 3072


def m_nsa(M):
    P, T, A, nc = M.P, M.T, M.A, M.nc
    n = 512
    NKT = T // 128
    NQ = T // n
    pj = M.pj
    ps, Bps = M.ps, M.Bps
    qn, Bqn = M.const("nqn", [128, 1])
    kn, Bkn = M.const("nkn", [128, 1])
    pekT, Bpek = M.const("pekT", [128, 32])
    pevT, Bpev = M.const("pevT", [128, 32])
    tabs, Btabs = M.const("ntab", [32, 2])
    t31, Bt31 = M.const("nt31", [128, 2])
    wck_d = M.din("wck", [128, 4096])
    wcv_d = M.din("wcv", [128, 4096])
    ohS = M.din("ohS", [32, PD_S]); ngS = M.din("ngS", [1, PD_S])
    ohW = M.din("ohW", [32, PD_S]); ngW = M.din("ngW", [1, PD_S])
    ohC = M.din("ohC", [32, PD_C]); ngC = M.din("ngC", [1, PD_C])
    E_d = M.din("Esel", [128, 8192])
    wimp_d = M.din("wimp", [128, 512])
    dS = nc.dram_tensor("dS", [129 * PD_S], F32)
    dW = nc.dram_tensor("dW", [129 * PD_S], F32)
    dCM = nc.dram_tensor("dCM", [129 * PD_C], F32)
    dCO = nc.dram_tensor("dCO", [129 * PD_C], F32)
    A.reset()
    BTs = A.f32(W_S); BTw = A.f32(W_W); BTcM = A.f32(W_C); BTcO = A.f32(W_C)
    BBTs, BBTw, BBTcM, BBTcO = P.buf("BTs"), P.buf("BTw"), P.buf("BTcM"), P.buf("BTcO")
    mark0 = A.off
    fb = A.f32(PD_C)
    Bfb = P.buf("fb")
    frow = A.f32(PD_C)
    Bfrow = P.buf("frow")
    oh = A.f32(PD_C)
    Boh = P.buf("oh")
    ngt = A.f32(PD_C)
    Bng = P.buf("ng")

    def strip(oh_d, ng_d, Pd, tcol, dram, dst, Bdst, W, step, base, key):
        DMA(P, "sp", oh[0:32, 0:Pd], oh_d, [], [Boh], "oh")
        DMA(P, "sp", ngt[0:1, 0:Pd], ng_d, [], [Bng], "ng")
        for b0 in range(0, Pd, 512):
            MM(P, ps[0][0:1, :], tabs[:, tcol:tcol + 1], oh[0:32, b0:b0 + 512], True, True, [Btabs, Boh], [Bps[0]])
            TT(P, "dve", frow[0:1, b0:b0 + 512], ps[0][0:1, :], ngt[0:1, b0:b0 + 512], ALU.add,
               [Bps[0], Bng], [Bfrow])
        for b0 in range(0, Pd, 512):
            pb = 1 + (b0 // 512) % 2
            MM(P, ps[pb][:, :], M.ones[0:1, 0:128], frow[0:1, b0:b0 + 512], True, True, [M.Bones, Bfrow], [Bps[pb]])
            CP(P, "act" if pb == 1 else "dve", fb[:, b0:b0 + 512], ps[pb][:, :], [Bps[pb]], [Bfb])
        Bd = P.buf("d" + key)
        dap = dram.ap()
        DMA(P, "sp", dap[0:128 * Pd].rearrange("(p c) -> p c", c=Pd), fb[:, 0:Pd], [Bfb], [Bd], "dw" + key)
        DMA(P, "sp", dap[128 * Pd:129 * Pd].rearrange("(p c) -> p c", c=Pd), fb[0:1, 0:Pd], [Bfb], [Bd], "dw" + key)
        src = bass.AP(tensor=dap.tensor, offset=base, ap=[[Pd - step, 128], [1, W]])
        DMA(P, "sp", dst, src, [Bd], [Bdst], "bt" + key)

    strip(ohS, ngS, PD_S, 0, dS, BTs, BBTs, W_S, 1, 127, "S")
    strip(ohW, ngW, PD_S, 0, dW, BTw, BBTw, W_W, 1, 127, "W")
    strip(ohC, ngC, PD_C, 0, dCM, BTcM, BBTcM, W_C, 16, 2032, "CM")
    strip(ohC, ngC, PD_C, 1, dCO, BTcO, BBTcO, W_C, 16, 2032, "CO")
    P.barrier()
    A.off = mark0
    ksT = A.bf16(T); kwT = A.bf16(T)
    Bks, Bkw = P.buf("ksT"), P.buf("kwT")
    Vs = A.bf16(NKT * 129); Vw = A.bf16(NKT * 129)
    BVs, BVw = P.buf("Vs"), P.buf("Vw")
    Eb = A.bf16(T)
    BE = P.buf("E")
    kcmpT = A.f32(512)
    Bkcmp = P.buf("kcmp")
    WV = A.f32(4 * 256)
    BWV = P.buf("WV")
    Wimp = A.f32(512)
    BWimp = P.buf("Wimp")
    mark = A.off
    DMA(P, "pool", Eb, E_d[:, 0:T], [], [BE], "E")
    DMA(P, "sp", Wimp, wimp_d, [], [BWimp], "wimp")
    DMA(P, "sp", WV.rearrange("p (c x) -> p c x", c=4)[:, :, 0:128], wimp_d.rearrange("p (c x) -> p c x", c=4),
        [], [BWV], "wv")
    MS(P, "pool", Vs.rearrange("p (k x) -> p k x", x=129)[:, :, 128:129], 1.0, [BVs])
    MS(P, "pool", Vw.rearrange("p (k x) -> p k x", x=129)[:, :, 128:129], 1.0, [BVw])
    kraw = [A.f32(n) for _ in range(2)]
    Bkraw = P.bufs(2, "kraw")
    sq = A.f32(n); Bsq = P.buf("nsq")
    rs = A.f32(n); Brs_ = P.buf("nrs")
    it = 0
    for (chunk, dstT, Bdst) in ((8, ksT, Bks), (10, kwT, Bkw)):
        for ti in range(NQ):
            s = it % 2
            it += 1
            t0 = ti * n
            DMA(P, "sp", kraw[s], pj[chunk * 128:(chunk + 1) * 128, t0:t0 + n], [M.Bpj], [Bkraw[s]], f"kraw{s}")
            ACT(P, sq, kraw[s], AF.Square, [Bkraw[s]], [Bsq])
            MM(P, ps[0][:, :n], M.ones[:], sq, True, True, [M.Bones, Bsq], [Bps[0]])
            ACT(P, rs, ps[0][:, :n], AF.Sqrt, [Bps[0]], [Brs_], bias=EPS, scale=1.0 / 128)
            P.op("dve", lambda e: e.reciprocal(out=rs, in_=rs), [Brs_], [Brs_])
            STT(P, dstT[:, t0:t0 + n], kraw[s], kn[:, 0:1], rs, ALU.mult, ALU.mult, [Bkraw[s], Bkn, Brs_], [Bdst])
    for (chunk, dstV, Bdst) in ((9, Vs, BVs), (11, Vw, BVw)):
        dv3 = dstV.rearrange("p (k x) -> p k x", x=129)
        for ti in range(NQ):
            s = it % 2
            it += 1
            t0 = ti * n
            DMA(P, "sp", kraw[s], pj[chunk * 128:(chunk + 1) * 128, t0:t0 + n], [M.Bpj], [Bkraw[s]], f"kraw{s}")
            pb = 1 + ti % 2
            for c in range(4):
                TR(P, ps[pb][:, c * 128:(c + 1) * 128], kraw[s][:, c * 128:(c + 1) * 128], M.ident[:],
                   [Bkraw[s], M.Bid], [Bps[pb]])
            CP(P, "act" if ti % 2 else "dve", dv3[:, ti * 4:ti * 4 + 4, 0:128],
               ps[pb][:, :].rearrange("p (c x) -> p c x", c=4), [Bps[pb]], [Bdst])
    wc = A.f32(4096)
    Bwc = P.buf("wc")
    kct = [A.f32(1040) for _ in range(2)]
    Bkct = P.bufs(2, "kct")
    cbias = A.f32(1)
    Bcb = P.buf("cbias")
    vcT = A.f32(512)
    BvcT = P.buf("vcT")
    MS(P, "dve", kcmpT, 0.0, [Bkcmp])
    MS(P, "dve", vcT, 0.0, [BvcT])
    for (chunk, w_d, peT, Bpe, dst, Bdst) in ((6, wck_d, pekT, Bpek, kcmpT, Bkcmp), (7, wcv_d, pevT, Bpev, vcT, BvcT)):
        DMA(P, "sp", wc, w_d, [], [Bwc], "wc")
        for l in range(32):
            MM(P, ps[0][:, 0:1], wc[:, l * 128:(l + 1) * 128], peT[:, l:l + 1], l == 0, l == 31, [Bwc, Bpe], [Bps[0]])
        CP(P, "dve", cbias, ps[0][:, 0:1], [Bps[0]], [Bcb])
        ntile = T // 1024
        for ti in range(ntile):
            s = it % 2
            it += 1
            t0 = ti * 1024
            ntok = min(1040, T - t0)
            nb = 64 if ntok == 1040 else 63
            DMA(P, "sp", kct[s][:, 0:ntok], pj[chunk * 128:(chunk + 1) * 128, t0:t0 + ntok], [M.Bpj], [Bkct[s]],
                f"kct{s}")
            pb = 1 + ti % 2
            for l in range(32):
                rhs = kct[s][:, l:l + 16 * (nb - 1) + 1:16]
                MM(P, ps[pb][:, 0:nb], wc[:, l * 128:(l + 1) * 128], rhs, l == 0, l == 31, [Bwc, Bkct[s]], [Bps[pb]])
            TS(P, "dve", dst[:, ti * 64:ti * 64 + nb], ps[pb][:, 0:nb], cbias[:, 0:1], None, ALU.add, None,
               [Bps[pb], Bcb], [Bdst])
    ACT(P, sq, kcmpT, AF.Square, [Bkcmp], [Bsq])
    MM(P, ps[0][:, :n], M.ones[:], sq, True, True, [M.Bones, Bsq], [Bps[0]])
    ACT(P, rs, ps[0][:, :n], AF.Sqrt, [Bps[0]], [Brs_], bias=EPS, scale=1.0 / 128)
    P.op("dve", lambda e: e.reciprocal(out=rs, in_=rs), [Brs_], [Brs_])
    STT(P, kcmpT, kcmpT, kn[:, 0:1], rs, ALU.mult, ALU.mult, [Bkcmp, Bkn, Brs_], [Bkcmp])
    for c in range(4):
        TR(P, ps[1][:, c * 128:(c + 1) * 128], vcT[:, c * 128:(c + 1) * 128], M.ident[:], [BvcT, M.Bid], [Bps[1]])
    CP(P, "dve", WV.rearrange("p (c x) -> p c x", c=4)[:, :, 128:256], ps[1][:, :].rearrange("p (c x) -> p c x", c=4),
       [Bps[1]], [BWV])
    P.barrier()
    A.off = mark
    qraw = [A.f32(n) for _ in range(2)]; Bqraw = P.bufs(2, "qraw")
    qM = A.f32(n); qO = A.f32(n); BqM, BqO = P.buf("qM"), P.buf("qO")
    qMb = A.bf16(n); BqMb = P.buf("qMb")
    glog = A.f32(n); Bglog = P.buf("glog")
    gq = A.f32(12); Bgq = P.buf("gq")
    tmpf = [A.f32(n) for _ in range(3)]; Btmpf = P.bufs(3, "ntmp")
    Pf = [A.f32(n) for _ in range(2)]; BPf = P.bufs(2, "Pf")
    Pb = [A.bf16(n) for _ in range(3)]; BPb = P.bufs(3, "Pb")
    negmT = A.bf16(n); BnegmT = P.buf("negmT")
    imp = A.f32(128); Bimp = P.buf("imp")
    imp2 = A.f32(128); Bimp2 = P.buf("imp2")
    scr = A.f32(128); Bscr = P.buf("scr")
    m8 = A.f32(16); Bm8 = P.buf("m8")
    sm = A.f32(16); Bsm = P.buf("sm")
    negm = A.f32(128); Bnegm = P.buf("negm")
    ocomb = [A.f32(128) for _ in range(4)]; Boc = P.bufs(4, "ocomb")
    ost = A.f32(n); Bost = P.buf("ost")
    ti_ = 0
    pi = 0
    for Q in range(NQ):
        if DEBUG_CUT == 21:
            break
        t0 = Q * n
        for hd, (chunk, dst, Bdst) in enumerate(((4, qM, BqM), (5, qO, BqO))):
            s = hd
            DMA(P, "sp", qraw[s], pj[chunk * 128:(chunk + 1) * 128, t0:t0 + n], [M.Bpj], [Bqraw[s]], f"qraw{s}")
            ACT(P, sq, qraw[s], AF.Square, [Bqraw[s]], [Bsq])
            MM(P, ps[7][:, :n], M.ones[:], sq, True, True, [M.Bones, Bsq], [Bps[7]])
            ACT(P, rs, ps[7][:, :n], AF.Sqrt, [Bps[7]], [Brs_], bias=128 * EPS, scale=1.0)
            P.op("dve", lambda e: e.reciprocal(out=rs, in_=rs), [Brs_], [Brs_])
            STT(P, dst, qraw[s], qn[:, 0:1], rs, ALU.mult, ALU.mult, [Bqraw[s], Bqn, Brs_], [Bdst])
        CP(P, "pool", qMb, qM, [BqM], [BqMb])
        DMA(P, "sp", glog[0:3, :], pj[PJ_SMALL + 2:PJ_SMALL + 5, t0:t0 + n], [M.Bpj], [Bglog], "glog")
        for sb in range(4):
            TR(P, ps[7][:, sb * 3:sb * 3 + 3], glog[0:3, sb * 128:(sb + 1) * 128], M.ident[0:3, 0:3],
               [Bglog, M.Bid], [Bps[7]])
        ACT(P, gq, ps[7][:, 0:12], AF.Sigmoid, [Bps[7]], [Bgq])
        for bk in (2, 3, 4):
            MS(P, "dve", ps[bk][:, :], 0.0, [Bps[bk]])
        def cmp_s1(hd, ct, qh, Bqh, BT, BBT):
            nonlocal pi, ti_
            Mq = Q - 4 * ct
            pb = pi % 2
            pi += 1
            MM(P, ps[pb][:, :n], kcmpT[:, ct * 128:(ct + 1) * 128], qh, True, True, [Bkcmp, Bqh], [Bps[pb]])
            f = ti_ % 2
            ti_ += 1
            if Mq <= 5:
                t = ti_ % 3
                TT(P, "dve", tmpf[t], ps[pb][:, :n], BT[:, 512 * Mq:512 * Mq + 512], ALU.add,
                   [Bps[pb], BBT], [Btmpf[t]])
                ACT(P, Pf[f], tmpf[t], AF.Exp, [Btmpf[t]], [BPf[f]])
            else:
                ACT(P, Pf[f], ps[pb][:, :n], AF.Exp, [Bps[pb], Bt31], [BPf[f]], bias=t31[:, hd:hd + 1])
            return f

        def cmp_s2(hd, ct, f):
            for sb in range(4):
                lhsT = Pf[f][:, sb * 128:(sb + 1) * 128]
                if hd == 0:
                    bk = 2 + sb // 2
                    MMA(P, ps[bk][:, (sb % 2) * 256:(sb % 2) * 256 + 256], lhsT, WV[:, ct * 256:(ct + 1) * 256],
                        [BPf[f], BWV], [Bps[bk]])
                else:
                    MMA(P, ps[4][:, sb * 128:(sb + 1) * 128], lhsT, Wimp[:, ct * 128:(ct + 1) * 128],
                        [BPf[f], BWimp], [Bps[4]])

        prev = None
        for hd, (qh, Bqh, BT, BBT) in enumerate(((qM, BqM, BTcM, BBTcM), (qO, BqO, BTcO, BBTcO))):
            for ct in range(Q // 4 + 1):
                f = cmp_s1(hd, ct, qh, Bqh, BT, BBT)
                if prev is not None:
                    cmp_s2(*prev)
                prev = (hd, ct, f)
        cmp_s2(*prev)
        for sb in range(4):
            qt = 4 * Q + sb
            bk = 2 + sb // 2
            aM = ps[bk][:, (sb % 2) * 256:(sb % 2) * 256 + 128]
            vM = ps[bk][:, (sb % 2) * 256 + 128:(sb % 2) * 256 + 256]
            aO = ps[4][:, sb * 128:(sb + 1) * 128]
            TS(P, "dve", scr, aM, 0.5, 0.0, ALU.mult, ALU.add, [Bps[bk]], [Bscr, Bsm], accum=sm[:, 0:1])
            TS(P, "dve", scr, aO, 0.5, 0.0, ALU.mult, ALU.add, [Bps[4]], [Bscr, Bsm], accum=sm[:, 1:2])
            TS(P, "dve", sm[:, 0:2], sm[:, 0:2], 1e-30, None, ALU.max, None, [Bsm], [Bsm])
            P.op("dve", lambda e: e.reciprocal(out=sm[:, 2:4], in_=sm[:, 0:2]), [Bsm], [Bsm])
            TS(P, "dve", imp, aM, sm[:, 2:3], None, ALU.mult, None, [Bps[bk], Bsm], [Bimp])
            STT(P, imp, aO, sm[:, 3:4], imp, ALU.mult, ALU.add, [Bps[4], Bsm, Bimp], [Bimp])
            TT(P, "dve", sm[:, 4:5], sm[:, 2:3], gq[:, sb * 3:sb * 3 + 1], ALU.mult, [Bsm, Bgq], [Bsm])
            TS(P, "dve", ocomb[sb], vM, sm[:, 4:5], None, ALU.mult, None, [Bps[bk], Bsm], [Boc[sb]])
            MS(P, "pool", imp[:, 0:1], 1e4, [Bimp])
            MS(P, "pool", imp[:, 2 * qt:2 * qt + 1], 1e4, [Bimp])
            if qt > 0:
                MS(P, "pool", imp[0:64, 2 * qt - 1:2 * qt], 1e4, [Bimp])
            MS(P, "pool", imp[64:128, 2 * qt + 1:2 * qt + 2], 1e4, [Bimp])
            P.op("dve", lambda e: e.max(out=m8[:, 0:8], in_=imp), [Bimp], [Bm8])
            P.op("dve", lambda e: e.match_replace(out=imp2, in_to_replace=m8[:, 0:8], in_values=imp, imm_value=-1e30),
                 [Bimp, Bm8], [Bimp2])
            P.op("dve", lambda e: e.max(out=m8[:, 8:16], in_=imp2), [Bimp2], [Bm8])
            TS(P, "dve", negm, imp, m8[:, 15:16], -32768.0, ALU.is_lt, ALU.mult, [Bimp, Bm8], [Bnegm])
            TR(P, ps[7][:, 128:256], negm, M.ident[:], [Bnegm, M.Bid], [Bps[7]])
            CP(P, "act", negmT[:, sb * 128:(sb + 1) * 128], ps[7][:, 128:256], [Bps[7]], [BnegmT])
        if DEBUG_CUT == 22:
            continue
        for br_, (kT, BkT, Vv, BVv, BT, BBT, lo, accb) in enumerate(
                ((ksT, Bks, Vs, BVs, BTs, BBTs, 0, (5, 6)), (kwT, Bkw, Vw, BVw, BTw, BBTw, max(0, 4 * Q - 4), (2, 3)))):
            for bk in accb:
                MS(P, "dve", ps[bk][:, :], 0.0, [Bps[bk]])
            def sw_s1(kt):
                nonlocal pi, ti_
                m = 4 * Q - kt
                pb = pi % 2
                pi += 1
                if br_ == 0:
                    MM(P, ps[pb][:, :n], kT[:, kt * 128:(kt + 1) * 128], qMb, True, False, [BkT, BqMb], [Bps[pb]])
                    MM(P, ps[pb][:, :n], Eb[:, kt * 128:(kt + 1) * 128], negmT, False, True, [BE, BnegmT], [Bps[pb]])
                else:
                    MM(P, ps[pb][:, :n], kT[:, kt * 128:(kt + 1) * 128], qMb, True, True, [BkT, BqMb], [Bps[pb]])
                f = ti_ % 3
                ti_ += 1
                if m <= 7:
                    t = ti_ % 3
                    TT(P, "dve", tmpf[t], ps[pb][:, :n], BT[:, 128 * (m + 3):128 * (m + 3) + 512], ALU.add,
                       [Bps[pb], BBT], [Btmpf[t]])
                    ACT(P, Pb[f], tmpf[t], AF.Exp, [Btmpf[t]], [BPb[f]])
                else:
                    ACT(P, Pb[f], ps[pb][:, :n], AF.Exp, [Bps[pb], Bt31], [BPb[f]], bias=t31[:, 0:1])
                return f

            def sw_s2(kt, f):
                for sb in range(4):
                    if 4 * Q + sb < kt:
                        continue
                    bk = accb[sb // 2]
                    MMA(P, ps[bk][:, (sb % 2) * 129:(sb % 2) * 129 + 129], Pb[f][:, sb * 128:(sb + 1) * 128],
                        Vv[:, kt * 129:(kt + 1) * 129], [BPb[f], BVv], [Bps[bk]])

            prev = None
            for kt in range(lo, 4 * Q + 4):
                f = sw_s1(kt)
                if prev is not None:
                    sw_s2(*prev)
                prev = (kt, f)
            sw_s2(*prev)
            for sb in range(4):
                bk = accb[sb // 2]
                acc = ps[bk][:, (sb % 2) * 129:(sb % 2) * 129 + 129]
                P.op("dve", lambda e, acc=acc: e.reciprocal(out=sm[:, 5:6], in_=acc[:, 128:129]), [Bps[bk]], [Bsm])
                TT(P, "dve", sm[:, 6:7], sm[:, 5:6], gq[:, sb * 3 + 1 + br_:sb * 3 + 2 + br_], ALU.mult, [Bsm, Bgq], [Bsm])
                STT(P, ocomb[sb], acc[:, 0:128], sm[:, 6:7], ocomb[sb], ALU.mult, ALU.add, [Bps[bk], Bsm, Boc[sb]], [Boc[sb]])
        for sb in range(4):
            TR(P, ps[7][:, 256:384], ocomb[sb], M.ident[:], [Boc[sb], M.Bid], [Bps[7]])
            CP(P, "act", ost[:, sb * 128:(sb + 1) * 128], ps[7][:, 256:384], [Bps[7]], [Bost])
        DMA(P, "sp", M.yo[128:256, t0:t0 + n], ost, [Bost], [], "yb")
    P.barrier()


def _bucket(dist):
    n = np.maximum(dist, 0)
    nf = np.maximum(n, 1).astype(np.float32)
    large = 16 + (np.log(nf / np.float32(16)) / np.float32(math.log(1024 / 16)) * np.float32(16)).astype(np.int32)
    large = np.minimum(large, 31)
    return np.where(n < 16, n, large)


_NSA_CONSTS = {}


def nsa_consts():
    if _NSA_CONSTS:
        return _NSA_CONSTS
    c = _NSA_CONSTS
    j = np.arange(PD_S)
    dist = j - 511
    bk = _bucket(dist)
    for nm, valid in (("S", dist >= 0), ("W", (dist >= 0) & (dist < 512))):
        oh = np.zeros((32, PD_S), np.float32)
        oh[bk[valid], j[valid]] = 1.0
        c["oh" + nm] = oh
        c["ng" + nm] = np.where(valid, 0.0, NEGBIG).astype(np.float32)[None, :]
    j = np.arange(PD_C)
    dist = j - 2063
    bk = _bucket(dist)
    valid = dist >= 0
    oh = np.zeros((32, PD_C), np.float32)
    oh[bk[valid], j[valid]] = 1.0
    c["ohC"] = oh
    c["ngC"] = np.where(valid, 0.0, NEGBIG).astype(np.float32)[None, :]
    E = np.zeros((128, 64, 128), np.float32)
    for kt in range(64):
        for k in range(128):
            E[2 * kt + k // 64, kt, k] = 1.0
    c["Esel"] = E.reshape(128, 8192)
    W = np.zeros((4, 128, 128), np.float32)
    wts = (1.0, 2.0, 2.0, 2.0, 1.0)
    for ct in range(4):
        for i in range(128):
            gi = ct * 128 + i
            for jj in range(128):
                w = gi - 4 * jj + 1
                if 0 <= w <= 4 and gi <= 510:
                    W[ct, i, jj] = wts[w]
    c["wimp"] = np.ascontiguousarray(W.transpose(1, 0, 2).reshape(128, 512))
    return c


_PROGS = {}


def _prog(kind):
    if kind not in _PROGS:
        _PROGS[kind] = build_M(8192) if kind == "M" else build_R(kind, 2048)
    return _PROGS[kind]


def _ffn_inputs(prm, l, which, suffix):
    return {"g" + suffix: gl(prm["g_ffn" + which][l]),
            "wu" + suffix: np.ascontiguousarray(prm["w_up" + which][l]),
            "wd" + suffix: np.ascontiguousarray(prm["w_down" + which][l])}


def kernel(**inputs):
    prm = {k: np.asarray(v, dtype=np.float32) for k, v in inputs.items()}
    x = prm.pop("x")
    B, T, D = x.shape
    NCORE = 8
    TC = T // 4
    cores = list(range(NCORE))

    def run(kind, maps):
        res = run_bass_kernel_spmd(_prog(kind), maps, core_ids=cores)
        return res.results

    xs = [np.ascontiguousarray(x[c // 4, (c % 4) * TC:(c % 4 + 1) * TC, :].T) for c in cores]
    shared = _ffn_inputs(prm, 0, "1", "1")
    outs = run("A", [dict(xin=xs[c], **shared) for c in cores])
    x1s = [o["xo"] for o in outs]
    consts = m_consts(T)
    selbs = [np.ascontiguousarray(np.tile(np.eye(2, dtype=np.float32)[c // 4][None, :], (128, 1))) for c in cores]
    n_layers = prm["g_mix"].shape[0]
    for l in range(n_layers):
        x1f = np.empty((2, D, T), np.float32)
        for c in cores:
            x1f[c // 4][:, (c % 4) * TC:(c % 4 + 1) * TC] = x1s[c]
        lay = [m_layer_inputs(prm, l, h, T) for h in range(4)]
        outs = run("M", [dict(x1f=x1f, selb=selbs[c], **consts, **lay[c % 4]) for c in cores])
        yfull = np.empty((2, D, T), np.float32)
        for c in cores:
            b, h = c // 4, c % 4
            yo = outs[c]["yo"]
            yfull[b][h * 128:(h + 1) * 128] = yo[0:128]
            yfull[b][512 + h * 128:512 + (h + 1) * 128] = yo[128:256]
            yfull[b][1024 + h * 256:1024 + (h + 1) * 256] = yo[256:512]
        shared = {"gm": gl(prm["g_mix"][l]), "sn": gl(prm["ssm_norm"][l], 8),
                  "wg": np.ascontiguousarray(prm["w_in"][l][:, 6692:]),
                  "pa": np.ascontiguousarray(prm["p_a"][l]), "pb": np.ascontiguousarray(prm["p_b"][l]),
                  "pc": np.ascontiguousarray(prm["p_c"][l]), "wo": np.ascontiguousarray(prm["w_o"][l])}
        shared.update(_ffn_inputs(prm, l, "2", "2"))
        last = l == n_layers - 1
        if not last:
            shared.update(_ffn_inputs(prm, l + 1, "1", "1"))
        maps = [dict(xin=x1s[c],
                     yin=np.ascontiguousarray(yfull[c // 4][:, (c % 4) * TC:(c % 4 + 1) * TC]), **shared)
                for c in cores]
        outs = run("R" if last else "RA", maps)
        x1s = [o["xo"] for o in outs]
    out = np.empty((B, T, D), np.float32)
    for c in cores:
        out[c // 4, (c % 4) * TC:(c % 4 + 1) * TC, :] = x1s[c].T
    return out
```

```python
import bisect
import contextlib
import math
import numpy as np
import concourse.bass as bass
import concourse.mybir as mybir
from concourse.bass_utils import run_bass_kernel_spmd

F32 = mybir.dt.float32
BF16 = mybir.dt.bfloat16
AF = mybir.ActivationFunctionType
ALU = mybir.AluOpType
AX = mybir.AxisListType


class Buf:
    __slots__ = ("name", "lw", "rd", "excl")

    def __init__(self, name):
        self.name = name
        self.excl = False
        self.lw = None
        self.rd = {}


class Prog:
    ENG = ("pe", "act", "dve", "pool", "sp")

    def __init__(self, nc):
        self.nc = nc
        self.stack = contextlib.ExitStack()
        self.q = {e: [] for e in self.ENG}
        self.cnt = {e: 0 for e in self.ENG}
        self.seen = {e: {} for e in self.ENG}
        self.dcount = {}
        self.waited = {e: set() for e in self.ENG}
        self.nbuf = 0

    def buf(self, name=None):
        self.nbuf += 1
        return Buf(name or f"b{self.nbuf}")

    def bufs(self, n, name=None):
        return [self.buf(f"{name}{i}") for i in range(n)]

    def sb(self, name, shape, dtype):
        return self.stack.enter_context(self.nc.sbuf_tensor(name, list(shape), dtype))

    def ps(self, name, shape, dtype=F32):
        return self.stack.enter_context(self.nc.psum_tensor(name, list(shape), dtype))

    def op(self, eng, fn, reads=(), writes=(), dma=None, pe_acc=False):
        deps = {}

        def add(ev):
            if ev is None:
                return
            k, v = ev
            if deps.get(k, 0) < v:
                deps[k] = v

        for b in reads:
            add(b.lw)
            if b.excl:
                for k, v in b.rd.items():
                    if k != eng:
                        add((k, v))
        for b in writes:
            if not (pe_acc and b.lw is not None and b.lw[0] == "pe"):
                add(b.lw)
            for k, v in b.rd.items():
                add((k, v))
        waits = []
        seen = self.seen[eng]
        for k, v in deps.items():
            if seen.get(k, 0) >= v:
                continue
            seen[k] = v
            waits.append((k, v))
            if k in self.waited:
                self.waited[k].add(v)
        if dma is None:
            self.cnt[eng] += 1
            ev = (eng, self.cnt[eng])
        else:
            key = dma if dma.startswith("c:") else "d:" + dma
            self.dcount[key] = self.dcount.get(key, 0) + 1
            ev = (key, self.dcount[key])
        for b in reads:
            if b.rd.get(ev[0], 0) < ev[1]:
                b.rd[ev[0]] = ev[1]
        for b in writes:
            b.lw = ev
            b.rd = {}
        self.q[eng].append((waits, fn, ev))
        return ev

    def barrier(self):
        evs = [(e, self.cnt[e]) for e in self.ENG if self.cnt[e] > 0]
        evs += [(k, c) for k, c in self.dcount.items()]
        for eng in self.ENG:
            waits = []
            seen = self.seen[eng]
            for k, v in evs:
                if seen.get(k, 0) >= v:
                    continue
                seen[k] = v
                waits.append((k, v))
                if k in self.waited:
                    self.waited[k].add(v)
            if waits:
                self.q[eng].append((waits, None, None))

    def emit(self):
        nc = self.nc
        self.barrier()
        sems = {e: self.stack.enter_context(nc.semaphore("s_" + e)) for e in self.ENG}
        for i, k in enumerate(sorted(self.dcount)):
            sems[k] = self.stack.enter_context(nc.semaphore(f"sd{i}"))
        miles = {e: sorted(self.waited[e]) for e in self.ENG}

        def val(k, v):
            if k in miles:
                return bisect.bisect_right(miles[k], v)
            return v if k.startswith("c:") else 16 * v

        wsets = {e: self.waited[e] for e in self.ENG}
        with nc.Block() as block:
            decs = {"pe": block.tensor, "act": block.scalar, "dve": block.vector,
                    "pool": block.gpsimd, "sp": block.sync}
            for e in self.ENG:
                items = self.q[e]

                def body(engobj, items=items, e=e):
                    for waits, fn, ev in items:
                        for k, v in waits:
                            engobj.wait_ge(sems[k], val(k, v))
                        if fn is None:
                            continue
                        ins = fn(engobj)
                        if ev[0] in miles:
                            if ev[1] in wsets[ev[0]]:
                                ins.then_inc(sems[ev[0]], 1)
                        else:
                            ins.then_inc(sems[ev[0]], 1 if ev[0].startswith("c:") else 16)
                decs[e](body)
        self.stack.close()


def MM(P, out, lhsT, rhs, start, stop, reads, writes):
    P.op("pe", lambda e: e.matmul(out, lhsT=lhsT, rhs=rhs, start=start, stop=stop),
         reads, writes, pe_acc=not start)


def TR(P, out, in_, ident, reads, writes):
    P.op("pe", lambda e: e.transpose(out, in_, ident), reads, writes)


def ACT(P, out, in_, func, reads, writes, bias=None, scale=None, accum=None):
    kw = {}
    if bias is not None:
        kw["bias"] = bias
    if scale is not None:
        kw["scale"] = scale
    if accum is not None:
        kw["accum_out"] = accum
    P.op("act", lambda e: e.activation(out=out, in_=in_, func=func, **kw), reads, writes)


def TT(P, eng, out, in0, in1, op, reads, writes):
    P.op(eng, lambda e: e.tensor_tensor(out=out, in0=in0, in1=in1, op=op), reads, writes)


def TS(P, eng, out, in0, s1, s2, op0, op1, reads, writes, accum=None):
    if op1 is None:
        P.op(eng, lambda e: e.tensor_scalar(out=out, in0=in0, scalar1=s1, scalar2=None, op0=op0), reads, writes)
    elif accum is None:
        P.op(eng, lambda e: e.tensor_scalar(out=out, in0=in0, scalar1=s1, scalar2=s2, op0=op0, op1=op1), reads, writes)
    else:
        P.op(eng, lambda e: e.tensor_scalar(out=out, in0=in0, scalar1=s1, scalar2=s2, op0=op0, op1=op1,
                                            accum_out=accum), reads, writes)


def STT(P, out, in0, scalar, in1, op0, op1, reads, writes):
    P.op("dve", lambda e: e.scalar_tensor_tensor(out=out, in0=in0, scalar=scalar, in1=in1, op0=op0, op1=op1),
         reads, writes)


def CP(P, eng, out, in_, reads, writes):
    if eng == "act":
        P.op("act", lambda e: e.copy(out=out, in_=in_), reads, writes)
    else:
        P.op(eng, lambda e: e.tensor_copy(out=out, in_=in_), reads, writes)


def MS(P, eng, ap, val, writes):
    P.op(eng, lambda e: e.memset(ap, val), (), writes)


def DMA(P, eng, out, in_, reads, writes, key):
    P.op(eng, lambda e: e.dma_start(out=out, in_=in_), reads, writes, dma=key)


D_MODEL = 2048
DC = 16
D_FF = 5504
FC = 43
EPS = 1e-6
WSLOT = 5504
NWS = 6


class RCtx:
    def __init__(self, P, TT_):
        self.P = P
        self.TT = TT_
        n = TT_
        self.xT = P.sb("xT", [128, DC * n], F32)
        self.Bx = P.bufs(DC, "x")
        self.hT = P.sb("hT", [128, DC * n], BF16)
        self.Bh = P.bufs(DC, "h")
        self.big = P.sb("big", [128, FC * n], BF16)
        self.Bbig = P.bufs(FC, "big")
        self.ws = [P.sb(f"ws{i}", [128, WSLOT], BF16) for i in range(NWS)]
        self.Bws = P.bufs(NWS, "ws")
        self.wsi = 0
        self.sq = [P.sb(f"sq{i}", [128, n], F32) for i in range(2)]
        self.Bsq = P.bufs(2, "sq")
        self.rstd = P.sb("rstd", [128, n], F32)
        self.Brstd = P.buf("rstd")
        self.tmp = [P.sb(f"tmp{i}", [128, n], F32) for i in range(6)]
        self.Btmp = P.bufs(6, "tmp")
        self.tmpi = 0
        self.yc32 = P.sb("yc32", [128, 4 * n], F32)
        self.Byc = P.bufs(4, "yc")
        self.ones = P.sb("ones", [128, 128], F32)
        self.Bones = P.buf("ones")
        self.ps = [P.ps(f"ps{i}", [128, 512]) for i in range(8)]
        self.Bps = P.bufs(8, "ps")
        for b_ in self.Bps:
            b_.excl = True
        MS(P, "dve", self.ones[:], 1.0, [self.Bones])
        self.sqi = 0

    def x(self, k):
        return self.xT[:, k * self.TT:(k + 1) * self.TT]

    def h(self, k):
        return self.hT[:, k * self.TT:(k + 1) * self.TT]

    def bg(self, k):
        return self.big[:, k * self.TT:(k + 1) * self.TT]

    def slot(self):
        s = self.wsi % NWS
        self.wsi += 1
        return s

    def tmpslot(self):
        s = self.tmpi % 6
        self.tmpi += 1
        return s


def r_rstd(C, srcs, Bsrcs, nelem, out_rstd, Bout, psb):
    P = C.P
    n = len(srcs)
    for k in range(n):
        s = C.sqi % 2
        C.sqi += 1
        ACT(P, C.sq[s][:], srcs[k], AF.Square, [Bsrcs[k]], [C.Bsq[s]])
        MM(P, C.ps[psb][:, :C.TT], C.ones[:], C.sq[s][:], k == 0, k == n - 1, [C.Bones, C.Bsq[s]], [C.Bps[psb]])
    ACT(P, out_rstd, C.ps[psb][:, :C.TT], AF.Sqrt, [C.Bps[psb]], [Bout], bias=EPS, scale=1.0 / nelem)
    P.op("dve", lambda e: e.reciprocal(out=out_rstd, in_=out_rstd), [Bout], [Bout])


def r_norm(C, g_sb, Bg):
    P = C.P
    r_rstd(C, [C.x(k) for k in range(DC)], C.Bx, D_MODEL, C.rstd[:], C.Brstd, 6)
    for k in range(DC):
        STT(P, C.h(k), C.x(k), g_sb[:, k:k + 1], C.rstd[:], ALU.mult, ALU.mult,
            [C.Bx[k], Bg, C.Brstd], [C.Bh[k]])


def wload(C, s, dram_ap, nk, ncols):
    P = C.P
    out = C.ws[s][:, 0:nk * ncols].rearrange("p (k c) -> p k c", k=nk)
    DMA(P, "pool", out, dram_ap.rearrange("(k p) c -> p k c", p=128), [], [C.Bws[s]], f"ws{s}")


def r_ffn(C, g_sb, Bg, wu, wd):
    P = C.P
    n = C.TT
    r_norm(C, g_sb, Bg)
    GW = 256
    groups = [(c0, min(GW, D_FF - c0)) for c0 in range(0, D_FF, GW)]

    def load_up(gi):
        c0, nc_ = groups[gi]
        sa, sb_ = C.slot(), C.slot()
        wload(C, sa, wu[:, c0:c0 + nc_], DC, nc_)
        wload(C, sb_, wu[:, D_FF + c0:D_FF + c0 + nc_], DC, nc_)
        return sa, sb_

    pend = [load_up(0)]
    cnt = 0
    for gi, (c0, nc_) in enumerate(groups):
        if gi + 1 < len(groups):
            pend.append(load_up(gi + 1))
        sa, sb_ = pend.pop(0)
        for sub in range(nc_ // 128):
            c = (c0 + sub * 128) // 128
            pa, pb = cnt % 2, 2 + cnt % 2
            cnt += 1
            for k in range(DC):
                MM(P, C.ps[pa][:, :n], C.ws[sa][:, k * nc_ + sub * 128:k * nc_ + sub * 128 + 128], C.h(k),
                   k == 0, k == DC - 1, [C.Bws[sa], C.Bh[k]], [C.Bps[pa]])
            for k in range(DC):
                MM(P, C.ps[pb][:, :n], C.ws[sb_][:, k * nc_ + sub * 128:k * nc_ + sub * 128 + 128], C.h(k),
                   k == 0, k == DC - 1, [C.Bws[sb_], C.Bh[k]], [C.Bps[pb]])
            t = C.tmpslot()
            ACT(P, C.tmp[t][:], C.ps[pa][:, :n], AF.Silu, [C.Bps[pa]], [C.Btmp[t]])
            TT(P, "dve", C.bg(c), C.tmp[t][:], C.ps[pb][:, :n], ALU.mult, [C.Btmp[t], C.Bps[pb]], [C.Bbig[c]])

    def load_dn(j):
        s = C.slot()
        wload(C, s, wd[:, j * 128:(j + 1) * 128], FC, 128)
        return s

    pend = [load_dn(0)]
    for j in range(DC):
        if j + 1 < DC:
            pend.append(load_dn(j + 1))
        s = pend.pop(0)
        pb = 4 + j % 2
        for c in range(FC):
            MM(P, C.ps[pb][:, :n], C.ws[s][:, c * 128:(c + 1) * 128], C.bg(c), c == 0, c == FC - 1,
               [C.Bws[s], C.Bbig[c]], [C.Bps[pb]])
        STT(P, C.x(j), C.ps[pb][:, :n], 0.5, C.x(j), ALU.mult, ALU.add, [C.Bps[pb], C.Bx[j]], [C.Bx[j]])


def r_merge(C, t0, yin, gm_sb, Bgm, sn_sb, Bsn, wg, pa_w, pb_w, pc_w, wo):
    P = C.P
    n = C.TT
    r_norm(C, gm_sb, Bgm)
    for k in range(8):
        DMA(P, "pool", C.bg(k), yin[k * 128:(k + 1) * 128, t0:t0 + n], [], [C.Bbig[k]], f"y{k}")
    for g in range(2):
        for q in range(4):
            r = 1024 + (g * 4 + q) * 128
            DMA(P, "sp", C.yc32[:, q * n:(q + 1) * n], yin[r:r + 128, t0:t0 + n], [], [C.Byc[q]], f"yc{q}")
        t = C.tmpslot()
        r_rstd(C, [C.yc32[:, q * n:(q + 1) * n] for q in range(4)], C.Byc, 512, C.tmp[t][:], C.Btmp[t], 7)
        for q in range(4):
            k = g * 4 + q
            STT(P, C.bg(8 + k), C.yc32[:, q * n:(q + 1) * n], sn_sb[:, k:k + 1], C.tmp[t][:], ALU.mult, ALU.mult,
                [C.Byc[q], Bsn, C.Btmp[t]], [C.Bbig[8 + k]])
    GW = 256

    def load_m(j2):
        s = [C.slot() for _ in range(4)]
        for i in range(3):
            wload(C, s[i], wg[:, i * D_MODEL + j2 * GW: i * D_MODEL + (j2 + 1) * GW], DC, GW)
        o = C.ws[s[3]]
        DMA(P, "pool", o[:, 0:4 * GW].rearrange("p (k c) -> p k c", k=4),
            pa_w[:, j2 * GW:(j2 + 1) * GW].rearrange("(k p) c -> p k c", p=128), [], [C.Bws[s[3]]], f"ws{s[3]}")
        DMA(P, "pool", o[:, 4 * GW:8 * GW].rearrange("p (k c) -> p k c", k=4),
            pb_w[:, j2 * GW:(j2 + 1) * GW].rearrange("(k p) c -> p k c", p=128), [], [C.Bws[s[3]]], f"ws{s[3]}")
        DMA(P, "pool", o[:, 8 * GW:16 * GW].rearrange("p (k c) -> p k c", k=8),
            pc_w[:, j2 * GW:(j2 + 1) * GW].rearrange("(k p) c -> p k c", p=128), [], [C.Bws[s[3]]], f"ws{s[3]}")
        return s

    for j2 in range(D_MODEL // GW):
        s = load_m(j2)
        for sub in range(GW // 128):
            j = j2 * 2 + sub
            tg = []
            for i in range(3):
                for k in range(DC):
                    MM(P, C.ps[i][:, :n], C.ws[s[i]][:, k * GW + sub * 128:k * GW + sub * 128 + 128], C.h(k),
                       k == 0, k == DC - 1, [C.Bws[s[i]], C.Bh[k]], [C.Bps[i]])
                t = C.tmpslot()
                tg.append(t)
                ACT(P, C.tmp[t][:], C.ps[i][:, :n], AF.Sigmoid, [C.Bps[i]], [C.Btmp[t]])
            o = C.ws[s[3]]
            for i, (base, nk, yoff) in enumerate(((0, 4, 0), (4 * GW, 4, 4), (8 * GW, 8, 8))):
                for k in range(nk):
                    MM(P, C.ps[3 + i][:, :n], o[:, base + k * GW + sub * 128: base + k * GW + sub * 128 + 128],
                       C.bg(yoff + k), k == 0, k == nk - 1, [C.Bws[s[3]], C.Bbig[yoff + k]], [C.Bps[3 + i]])
            for i in range(3):
                TT(P, "dve", C.tmp[tg[i]][:], C.tmp[tg[i]][:], C.ps[3 + i][:, :n], ALU.mult,
                   [C.Btmp[tg[i]], C.Bps[3 + i]], [C.Btmp[tg[i]]])
            TT(P, "pool", C.tmp[tg[0]][:], C.tmp[tg[0]][:], C.tmp[tg[1]][:], ALU.add,
               [C.Btmp[tg[0]], C.Btmp[tg[1]]], [C.Btmp[tg[0]]])
            TT(P, "dve", C.bg(16 + j), C.tmp[tg[0]][:], C.tmp[tg[2]][:], ALU.add,
               [C.Btmp[tg[0]], C.Btmp[tg[2]]], [C.Bbig[16 + j]])
    for j2 in range(D_MODEL // GW):
        s = C.slot()
        wload(C, s, wo[:, j2 * GW:(j2 + 1) * GW], DC, GW)
        for sub in range(2):
            j = j2 * 2 + sub
            pb = 6 + j % 2
            for k in range(DC):
                MM(P, C.ps[pb][:, :n], C.ws[s][:, k * GW + sub * 128:k * GW + sub * 128 + 128], C.bg(16 + k),
                   k == 0, k == DC - 1, [C.Bws[s], C.Bbig[16 + k]], [C.Bps[pb]])
            TT(P, "dve", C.x(j), C.x(j), C.ps[pb][:, :n], ALU.add, [C.Bx[j], C.Bps[pb]], [C.Bx[j]])


def build_R(mode, TC=2048, TT_=512):
    nc = bass.Bass("TRN2", target_bir_lowering=False)
    P = Prog(nc)

    def din(name, shape):
        return nc.dram_tensor(name, list(shape), F32, kind="ExternalInput").ap()

    xin = din("xin", [D_MODEL, TC])
    xo = nc.dram_tensor("xo", [D_MODEL, TC], F32, kind="ExternalOutput").ap()
    C = RCtx(P, TT_)
    gains = {}

    def gain(name, ncol=DC):
        a = din(name, [128, ncol])
        t = P.sb(name + "_sb", [128, ncol], F32)
        b = P.buf(name)
        DMA(P, "sp", t[:], a[:, :], [], [b], name)
        gains[name] = (t, b)

    if mode in ("RA", "R"):
        yin = din("yin", [D_MODEL, TC])
        gain("gm"); gain("sn", 8); gain("g2")
        wg = din("wg", [D_MODEL, 3 * D_MODEL])
        pa_w = din("pa", [512, D_MODEL]); pb_w = din("pb", [512, D_MODEL]); pc_w = din("pc", [1024, D_MODEL])
        wo = din("wo", [D_MODEL, D_MODEL])
        wu2 = din("wu2", [D_MODEL, 2 * D_FF]); wd2 = din("wd2", [D_FF, D_MODEL])
    if mode in ("A", "RA"):
        gain("g1")
        wu1 = din("wu1", [D_MODEL, 2 * D_FF]); wd1 = din("wd1", [D_FF, D_MODEL])
    n = TT_
    for ti in range(TC // n):
        t0 = ti * n
        DMA(P, "sp", C.xT[:, :].rearrange("p (k t) -> p k t", k=DC),
            xin[:, t0:t0 + n].rearrange("(k p) t -> p k t", p=128), [], C.Bx, "xin")
        if mode in ("RA", "R"):
            r_merge(C, t0, yin, gains["gm"][0], gains["gm"][1], gains["sn"][0], gains["sn"][1],
                    wg, pa_w, pb_w, pc_w, wo)
            r_ffn(C, gains["g2"][0], gains["g2"][1], wu2, wd2)
        if mode in ("A", "RA"):
            r_ffn(C, gains["g1"][0], gains["g1"][1], wu1, wd1)
        DMA(P, "sp", xo[:, t0:t0 + n].rearrange("(k p) t -> p k t", p=128),
            C.xT[:, :].rearrange("p (k t) -> p k t", k=DC), C.Bx, [], "xout")
    P.emit()
    return nc


NCH_M = 18 * 128 + 9
PJ_SMALL = 18 * 128
NEGBIG = -30000.0
DEBUG_CUT = 0


class Arena:
    def __init__(self, P, name, ncols):
        self.t = P.sb(name, [128, ncols], F32)
        self.n = ncols
        self.off = 0

    def reset(self):
        self.off = 0

    def f32(self, n):
        assert self.off + n <= self.n, (self.off, n, self.n)
        ap = self.t[:, self.off:self.off + n]
        self.off += n
        return ap

    def bf16(self, n):
        m = (n + 1) // 2
        assert self.off + m <= self.n, (self.off, m, self.n)
        ap = self.t[:, self.off:self.off + m].bitcast(BF16)[:, 0:n]
        self.off += m
        return ap


class MCtx:
    def __init__(self, P, nc, T):
        self.P = P
        self.nc = nc
        self.T = T
        self.A = Arena(P, "arena", 47000)
        self.ps = [P.ps(f"ps{i}", [128, 512]) for i in range(8)]
        self.Bps = P.bufs(8, "ps")
        for b_ in self.Bps:
            b_.excl = True
        self.ones = P.sb("ones", [128, 128], F32)
        self.Bones = P.buf("ones")
        self.ident = P.sb("ident_sb", [128, 128], F32)
        self.Bid = P.buf("ident")
        MS(P, "dve", self.ones[:], 1.0, [self.Bones])
        self.pj = nc.dram_tensor("pj", [19 * 128, T], F32).ap()
        self.Bpj = P.buf("pj")

    def din(self, name, shape):
        return self.nc.dram_tensor(name, list(shape), F32, kind="ExternalInput").ap()

    def const(self, name, shape, eng="sp"):
        a = self.din(name, shape)
        t = self.P.sb(name + "_sb", shape, F32)
        b = self.P.buf(name)
        DMA(self.P, eng, t[:], a, [], [b], name)
        return t, b


def m_inproj(M):
    P, T, A = M.P, M.T, M.A
    n = 512
    x1f = M.din("x1f", [2, D_MODEL, T])
    selb, Bsel = M.const("selb", [128, 2])
    gm, Bgm = M.const("gmix", [128, DC])
    wm = M.din("wm", [D_MODEL, NCH_M])
    A.reset()
    wsb = A.bf16(DC * NCH_M)
    Bw = P.buf("wm")
    for k0 in range(0, DC, 4):
        DMA(P, "pool", wsb[:, k0 * NCH_M:(k0 + 4) * NCH_M].rearrange("p (k c) -> p k c", k=4),
            wm[k0 * 128:(k0 + 4) * 128, :].rearrange("(k p) c -> p k c", p=128), [], [Bw], "wm")
    xa = A.f32(DC * n)
    xb = A.f32(DC * n)
    Bxa4, Bxb4 = P.bufs(4, "xa"), P.bufs(4, "xb")
    hTs = [A.bf16(DC * n) for _ in range(2)]
    Bhs = [P.bufs(DC, f"h{i}") for i in range(2)]
    sq = [A.f32(n) for _ in range(2)]
    Bsq = P.bufs(2, "sq")
    rstd = A.f32(n)
    Brstd = P.buf("rstd")
    st = [A.f32(n) for _ in range(4)]
    Bst = P.bufs(4, "st")
    cnt = 0

    def load_x(ti):
        t0 = ti * n
        if DEBUG_CUT == 41 and ti > 0:
            return
        for q in range(4):
            DMA(P, "sp", xa[:, q * 4 * n:(q + 1) * 4 * n].rearrange("p (k t) -> p k t", k=4),
                x1f[0, q * 512:(q + 1) * 512, t0:t0 + n].rearrange("(k p) t -> p k t", p=128), [], [Bxa4[q]], f"xa{q}")
            DMA(P, "sp", xb[:, q * 4 * n:(q + 1) * 4 * n].rearrange("p (k t) -> p k t", k=4),
                x1f[1, q * 512:(q + 1) * 512, t0:t0 + n].rearrange("(k p) t -> p k t", p=128), [], [Bxb4[q]], f"xb{q}")

    NT_ = T // n

    def norm_x(ti):
        hT = hTs[ti % 2]
        Bh = Bhs[ti % 2]
        for q in range(4):
            qs = slice(q * 4 * n, (q + 1) * 4 * n)
            TS(P, "pool", xa[:, qs], xa[:, qs], selb[:, 0:1], None, ALU.mult, None, [Bxa4[q], Bsel], [Bxa4[q]])
            STT(P, xa[:, qs], xb[:, qs], selb[:, 1:2], xa[:, qs], ALU.mult, ALU.add, [Bxa4[q], Bxb4[q], Bsel], [Bxa4[q]])
        for k in range(DC):
            s = k % 2
            Bxa = Bxa4[k // 4]
            ACT(P, sq[s], xa[:, k * n:(k + 1) * n], AF.Square, [Bxa], [Bsq[s]])
            MM(P, M.ps[6][:, :n], M.ones[:], sq[s], k == 0, k == DC - 1, [M.Bones, Bsq[s]], [M.Bps[6]])
        ACT(P, rstd, M.ps[6][:, :n], AF.Sqrt, [M.Bps[6]], [Brstd], bias=EPS, scale=1.0 / D_MODEL)
        P.op("dve", lambda e: e.reciprocal(out=rstd, in_=rstd), [Brstd], [Brstd])
        for k in range(DC):
            STT(P, hT[:, k * n:(k + 1) * n], xa[:, k * n:(k + 1) * n], gm[:, k:k + 1], rstd, ALU.mult, ALU.mult,
                [Bxa4[k // 4], Bgm, Brstd], [Bh[k]])
        if ti + 1 < NT_:
            load_x(ti + 1)

    load_x(0)
    norm_x(0)
    for ti in range(NT_):
        t0 = ti * n
        hT = hTs[ti % 2]
        Bh = Bhs[ti % 2]
        for c in range(19):
            if c == 8 and ti + 1 < NT_:
                norm_x(ti + 1)
            cols = 128 if c < 18 else 9
            pb = cnt % 4
            s = cnt % 4
            cnt += 1
            for k in range(DC):
                MM(P, M.ps[pb][0:cols, :n], wsb[:, k * NCH_M + c * 128:k * NCH_M + c * 128 + cols],
                   hT[:, k * n:(k + 1) * n], k == 0, k == DC - 1, [Bw, Bh[k]], [M.Bps[pb]])
            CP(P, "act" if cnt % 2 else "dve", st[s][0:cols, :], M.ps[pb][0:cols, :n], [M.Bps[pb]], [Bst[s]])
            DMA(P, "sp", M.pj[c * 128:c * 128 + cols, t0:t0 + n], st[s][0:cols, :], [Bst[s]], [], f"pj{s}")
    P.barrier()


def conv_silu(P, out, raw, w4, bias, n, reads, writes, eng="dve"):
    if bias is None:
        TS(P, eng, out, raw[:, 3:3 + n], w4[:, 3:4], None, ALU.mult, None, reads, writes)
    else:
        TS(P, eng, out, raw[:, 3:3 + n], w4[:, 3:4], bias, ALU.mult, ALU.add, reads, writes)
    for k in range(3):
        STT(P, out, raw[:, k:k + n], w4[:, k:k + 1], out, ALU.mult, ALU.add, reads + writes, writes)
    ACT(P, out, out, AF.Silu, writes, writes)


def load_halo(P, dst, pj_rows, t0, n, reads, writes, key, eng="sp"):
    if t0 == 0:
        MS(P, "pool", dst[:, 0:3], 0.0, writes)
        DMA(P, eng, dst[:, 3:3 + n], pj_rows[:, 0:n], reads, writes, key)
    else:
        DMA(P, eng, dst[:, 0:3 + n], pj_rows[:, t0 - 3:t0 + n], reads, writes, key)


def m_ssd(M):
    P, T, A = M.P, M.T, M.A
    n = 512
    sconv, Bsc = M.const("sconv", [128, 16])
    sbias, Bsb = M.const("sbias", [128, 4])
    sdtb, Bdtb = M.const("sdtb", [1, 4])
    salog, Balog = M.const("salog", [1, 4])
    sD, BsD = M.const("sD", [128, 2])
    negtri, Bnt = M.const("negtriS", [128, 128])
    reset, Brs = M.const("reset128", [1, 2048])
    pj = M.pj
    A.reset()
    sT = A.f32(256)
    BsT = P.bufs(4, "sT")
    MS(P, "dve", sT, 0.0, BsT)
    negA = P.sb("negA", [1, 4], F32)
    BnA = P.buf("negA")
    ACT(P, negA[:], salog[:], AF.Exp, [Balog], [BnA])
    TS(P, "dve", negA[:], negA[:], -1.0, None, ALU.mult, None, [BnA], [BnA])
    raw = [[A.f32(n + 3) for _ in range(4)] for _ in range(2)]
    Braw = [P.bufs(4, f"raw{i}") for i in range(2)]
    zr = [[A.f32(n) for _ in range(2)] for _ in range(2)]
    Bzr = [P.bufs(2, f"zr{i}") for i in range(2)]
    dtr = [A.f32(4 * n) for _ in range(2)]
    Bdtr = P.bufs(2, "dtr")
    cv = [A.f32(n) for _ in range(4)]
    Bcv = P.bufs(4, "cv")
    dA = A.f32(4 * n)
    BdA = P.buf("dA")
    ac = A.f32(4 * n)
    Bac = P.buf("ac")
    yst = [A.f32(n) for _ in range(2)]
    Byst = P.bufs(2, "yst")
    tok = A.f32(384)
    Btok = P.buf("tok")
    NB = 4
    cl3 = [A.f32(3) for _ in range(NB)]
    Bcl = P.bufs(NB, "cl3")
    e1 = [A.f32(128) for _ in range(NB)]
    Be1 = P.bufs(NB, "e1")
    sg = [A.f32(128) for _ in range(NB)]
    Bsg = P.bufs(NB, "sg")
    sc = [A.f32(128) for _ in range(NB)]
    Bscb = P.bufs(NB, "sc")
    xdt = [A.f32(64) for _ in range(NB)]
    Bxdt = P.bufs(NB, "xdt")
    xdd = [A.f32(64) for _ in range(NB)]
    Bxdd = P.bufs(NB, "xdd")
    decl = [A.f32(1) for _ in range(NB)]
    Bdecl = P.bufs(NB, "decl")
    cdec = [A.f32(128) for _ in range(NB)]
    Bcdec = P.bufs(NB, "cdec")
    ps, Bps = M.ps, M.Bps
    chrow = (14, 15, 16, 17)
    nst = T // n

    def load(si):
        s = si % 2
        t0 = si * n
        for q in range(4):
            r = chrow[q] * 128
            load_halo(P, raw[s][q], pj[r:r + 128, :], t0, n, [M.Bpj], [Braw[s][q]], f"sraw{s}{q}")
        for q in range(2):
            r = (12 + q) * 128
            DMA(P, "sp", zr[s][q], pj[r:r + 128, t0:t0 + n], [M.Bpj], [Bzr[s][q]], f"sz{s}{q}")
        DMA(P, "sp", dtr[s][0:1, :].rearrange("o (r t) -> o r t", r=4),
            pj[PJ_SMALL + 5:PJ_SMALL + 9, t0:t0 + n].rearrange("(o r) t -> o r t", o=1), [M.Bpj], [Bdtr[s]], f"sdt{s}")

    load(0)
    it = 0
    for si in range(nst):
        s = si % 2
        t0 = si * n
        if si + 1 < nst:
            load(si + 1)
        for q in range(4):
            conv_silu(P, cv[q], raw[s][q], sconv[:, q * 4:q * 4 + 4], sbias[:, q:q + 1], n,
                      [Braw[s][q], Bsc, Bsb], [Bcv[q]])
        for q in range(2):
            ACT(P, zr[s][q], zr[s][q], AF.Silu, [Bzr[s][q]], [Bzr[s][q]])
        d = dtr[s]
        if DEBUG_CUT == 1:
            continue
        for p in range(4):
            ACT(P, d[0:1, p * n:(p + 1) * n], d[0:1, p * n:(p + 1) * n], AF.Exp, [Bdtr[s], Bdtb], [Bdtr[s]],
                bias=sdtb[0:1, p:p + 1])
        ACT(P, d[0:1, :], d[0:1, :], AF.Ln, [Bdtr[s]], [Bdtr[s]], bias=1.0)
        for p in range(4):
            TS(P, "dve", dA[0:1, p * n:(p + 1) * n], d[0:1, p * n:(p + 1) * n], negA[0:1, p:p + 1], None,
               ALU.mult, None, [Bdtr[s], BnA], [BdA])
        P.op("dve", lambda e: e.tensor_tensor_scan(out=ac[0:1, :], data0=reset[0:1, :], data1=dA[0:1, :],
                                                   initial=0.0, op0=ALU.mult, op1=ALU.add),
             [BdA, Brs], [Bac])
        if DEBUG_CUT == 2:
            continue
        for c in range(4):
            l0 = c * 128
            TR(P, ps[7][:, 0:128], cv[2][:, l0:l0 + 128], M.ident[:], [Bcv[2], M.Bid], [Bps[7]])
            TR(P, ps[7][:, 128:256], cv[0][:, l0:l0 + 128], M.ident[:], [Bcv[0], M.Bid], [Bps[7]])
            TR(P, ps[7][:, 256:384], cv[1][:, l0:l0 + 128], M.ident[:], [Bcv[1], M.Bid], [Bps[7]])
            CP(P, "act", tok, ps[7][:, 0:384], [Bps[7]], [Btok])
            MM(P, ps[6][:, 0:128], cv[2][:, l0:l0 + 128], cv[3][:, l0:l0 + 128], True, True,
               [Bcv[2], Bcv[3]], [Bps[6]])
            if DEBUG_CUT == 3:
                continue
            def head_ops(p):
                o = p * n + l0
                b = p
                pbk = 4 + p % 2
                c0 = (p // 2) * 256
                MM(P, ps[pbk][:, c0:c0 + 128], M.ones[0:1, 0:128], ac[0:1, o:o + 128], True, True,
                   [M.Bones, Bac], [Bps[pbk]])
                MM(P, ps[pbk][:, c0 + 128:c0 + 129], ac[0:1, o:o + 128], M.ones[0:1, 0:1], True, True,
                   [M.Bones, Bac], [Bps[pbk]])
                MM(P, ps[pbk][:, c0 + 129:c0 + 130], d[0:1, o:o + 128], M.ones[0:1, 0:1], True, True,
                   [M.Bones, Bdtr[s]], [Bps[pbk]])
                yield
                CP(P, "dve", cl3[b], ps[pbk][:, c0 + 127:c0 + 130], [Bps[pbk]], [Bcl[b]])
                ACT(P, e1[b], ps[pbk][:, c0:c0 + 128], AF.Exp, [Bps[pbk]], [Be1[b]])
                yield
                STT(P, sg[b], ps[pbk][:, c0:c0 + 128], cl3[b][:, 1:2], negtri[:], ALU.subtract, ALU.add,
                    [Bps[pbk], Bcl[b], Bnt], [Bsg[b]])
                TS(P, "pool", xdt[b], tok[:, 128 + p * 64:128 + (p + 1) * 64], cl3[b][:, 2:3], None, ALU.mult, None,
                   [Btok, Bcl[b]], [Bxdt[b]])
                ACT(P, decl[b], cl3[b][:, 1:2], AF.Exp, [Bcl[b]], [Bdecl[b]], bias=cl3[b][:, 0:1], scale=-1.0)
                yield
                ACT(P, sg[b], sg[b], AF.Exp, [Bsg[b]], [Bsg[b]])
                TS(P, "pool", xdd[b], xdt[b], decl[b][:, 0:1], None, ALU.mult, None, [Bxdt[b], Bdecl[b]], [Bxdd[b]])
                TT(P, "pool", cdec[b], cv[3][:, l0:l0 + 128], e1[b], ALU.mult, [Bcv[3], Be1[b]], [Bcdec[b]])
                yield
                TT(P, "dve", sc[b], ps[6][:, 0:128], sg[b], ALU.mult, [Bps[6], Bsg[b]], [Bscb[b]])
                yield
                pr = p // 2
                yo_ = ps[pr][(p % 2) * 64:(p % 2) * 64 + 64, 0:128]
                MM(P, yo_, xdt[b], sc[b], True, False, [Bxdt[b], Bscb[b]], [Bps[pr]])
                MM(P, yo_, sT[:, p * 64:(p + 1) * 64], cdec[b], False, True, [BsT[p], Bcdec[b]], [Bps[pr]])
                pd = 2 + p % 2
                MM(P, ps[pd][:, (p // 2) * 64:(p // 2) * 64 + 64], tok[:, 0:128], xdd[b], True, True, [Btok, Bxdd[b]], [Bps[pd]])
                yield
                STT(P, sT[:, p * 64:(p + 1) * 64], sT[:, p * 64:(p + 1) * 64], e1[b][:, 127:128], ps[pd][:, (p // 2) * 64:(p // 2) * 64 + 64],
                    ALU.mult, ALU.add, [BsT[p], Be1[b], Bps[pd]], [BsT[p]])

            gens = [head_ops(p) for p in range(4)]
            while gens:
                for g_ in list(gens):
                    try:
                        next(g_)
                    except StopIteration:
                        gens.remove(g_)
            for pr in range(2):
                STT(P, yst[pr][:, l0:l0 + 128], cv[pr][:, l0:l0 + 128], sD[:, pr:pr + 1], ps[pr][:, 0:128],
                    ALU.mult, ALU.add, [Bcv[pr], BsD, Bps[pr]], [Byst[pr]])
        for pr in range(2):
            TT(P, "pool", yst[pr], yst[pr], zr[s][pr], ALU.mult, [Byst[pr], Bzr[s][pr]], [Byst[pr]])
            DMA(P, "sp", M.yo[256 + pr * 128:256 + (pr + 1) * 128, t0:t0 + n], yst[pr], [Byst[pr]], [], f"yc{pr}")
    P.barrier()


def build_M(T=8192, stages=("ssd", "gdn", "nsa")):
    nc = bass.Bass("TRN2", target_bir_lowering=False)
    P = Prog(nc)
    M = MCtx(P, nc, T)
    idd = M.din("ident", [128, 128])
    DMA(P, "sp", M.ident[:], idd, [], [M.Bid], "ident")
    M.yo = nc.dram_tensor("yo", [512, T], F32, kind="ExternalOutput").ap()
    m_inproj(M)
    if "ssd" in stages:
        m_ssd(M)
    if "gdn" in stages:
        m_gdn(M)
    if "nsa" in stages:
        m_nsa(M)
    P.emit()
    return nc


def gl(g, ncol=DC):
    return np.ascontiguousarray(np.asarray(g, np.float32).reshape(ncol, 128).T)


def m_cols(h):
    g = h // 2
    cols = []
    for base in (0 + 128 * h, 512 + 128 * h, 1024 + 128 * h, 1536 + 128 * h,
                 2056 + 128 * h, 2056 + 128 * (h ^ 1), 2568 + 128 * g, 2824 + 128 * g,
                 3080 + 128 * g, 3336 + 128 * g, 3592 + 128 * g, 3848 + 128 * g,
                 4116 + 256 * h, 4116 + 256 * h + 128, 5140 + 256 * h, 5140 + 256 * h + 128,
                 5140 + 1024 + 128 * g, 5140 + 1280 + 128 * g):
        cols += list(range(base, base + 128))
    cols += [2048 + h, 2052 + h, 4104 + 3 * h, 4104 + 3 * h + 1, 4104 + 3 * h + 2]
    cols += [6676 + 4 * h + i for i in range(4)]
    return np.array(cols)


def m_consts(T):
    c = {}
    c["ident"] = np.eye(128, dtype=np.float32)
    p = np.arange(128)[:, None]
    f = np.arange(128)[None, :]
    c["negtriS"] = np.where(f < p, NEGBIG, 0.0).astype(np.float32)
    r = np.ones((1, 2048), np.float32)
    r[0, ::128] = 0.0
    c["reset128"] = r
    r = np.ones((1, 512), np.float32)
    r[0, ::64] = 0.0
    c["reset64"] = r
    p = np.arange(64)[:, None]
    f = np.arange(64)[None, :]
    c["gmaskU"] = np.tile(np.where(f < p, NEGBIG, 0.0).astype(np.float32), (1, 8))
    c["gmaskL"] = np.tile(np.where(f >= p, -NEGBIG, 0.0).astype(np.float32), (1, 8))
    c["gstrict"] = np.tile((f > p).astype(np.float32), (1, 8))
    c["gident8"] = np.tile(np.eye(64, dtype=np.float32), (1, 8))
    c.update(nsa_consts())
    return c


def m_layer_inputs(prm, l, h, T):
    g = h // 2
    d = {}
    d["gmix"] = gl(prm["g_mix"][l])
    d["wm"] = np.ascontiguousarray(prm["w_in"][l][:, m_cols(h)])
    cw = prm["ssm_conv_w"][l]
    cb = prm["ssm_conv_b"][l]
    chans = [256 * h + np.arange(128), 256 * h + 128 + np.arange(128), 1024 + 128 * g + np.arange(128),
             1280 + 128 * g + np.arange(128)]
    d["sconv"] = np.ascontiguousarray(np.concatenate([cw[:, ch].T for ch in chans], axis=1))
    d["sbias"] = np.ascontiguousarray(np.stack([cb[ch] for ch in chans], axis=1))
    d["sdtb"] = np.ascontiguousarray(prm["ssm_dt_bias"][l][4 * h:4 * h + 4][None, :])
    d["salog"] = np.ascontiguousarray(prm["ssm_a_log"][l][4 * h:4 * h + 4][None, :])
    dd = prm["ssm_d"][l][4 * h:4 * h + 4]
    d["sD"] = np.ascontiguousarray(np.stack([np.repeat(dd[0:2], 64), np.repeat(dd[2:4], 64)], axis=1))
    gcw = prm["gdn_conv"][l]
    d["gconv"] = np.ascontiguousarray(np.concatenate([gcw[:, q * 512 + 128 * h + np.arange(128)].T for q in range(3)], axis=1))
    d["gsc"] = np.array([[prm["gdn_a_log"][l][h], prm["gdn_dt_bias"][l][h]]], np.float32)
    d["gnorm"] = np.ascontiguousarray(prm["gdn_norm"][l][:, None])
    d["nqn"] = np.ascontiguousarray(prm["nsa_q_norm"][l][:, None])
    d["nkn"] = np.ascontiguousarray(prm["nsa_k_norm"][l][:, None])
    d["pekT"] = np.ascontiguousarray(prm["nsa_pe_k"][l].T)
    d["pevT"] = np.ascontiguousarray(prm["nsa_pe_v"][l].T)
    d["wck"] = np.ascontiguousarray(prm["nsa_w_ck"][l].transpose(1, 0, 2).reshape(128, 4096))
    d["wcv"] = np.ascontiguousarray(prm["nsa_w_cv"][l].transpose(1, 0, 2).reshape(128, 4096))
    rt = prm["rel_table"]
    d["ntab"] = np.ascontiguousarray(rt[:, [h, h ^ 1]])
    d["nt31"] = np.ascontiguousarray(np.tile(rt[31, [h, h ^ 1]][None, :], (128, 1)))
    return d


def m_gdn(M):
    P, T, A = M.P, M.T, M.A
    n = 512
    NCK = 8
    gconv, Bgc = M.const("gconv", [128, 12])
    gsc, Bgs = M.const("gsc", [1, 2])
    gnorm, Bgn = M.const("gnorm", [128, 1])
    mU, BmU = M.const("gmaskU", [64, 512])
    mL, BmL = M.const("gmaskL", [64, 512])
    mS, BmS = M.const("gstrict", [64, 512])
    reset, Brs = M.const("reset64", [1, 512])
    id8, Bid8 = M.const("gident8", [64, 512])
    pj = M.pj
    ps, Bps = M.ps, M.Bps
    A.reset()
    S = A.f32(128)
    BS = P.buf("S")
    MS(P, "dve", S, 0.0, [BS])
    negA = P.sb("gnegA", [1, 1], F32)
    BnA = P.buf("gnegA")
    ACT(P, negA[:], gsc[0:1, 0:1], AF.Exp, [Bgs], [BnA])
    TS(P, "dve", negA[:], negA[:], -1.0, None, ALU.mult, None, [BnA], [BnA])
    raw = [[A.f32(n + 3) for _ in range(3)] for _ in range(2)]
    Braw = [P.bufs(3, f"graw{i}") for i in range(2)]
    zr = [A.f32(n) for _ in range(2)]
    Bzr = P.bufs(2, "gz")
    abr = [A.f32(2 * n) for _ in range(2)]
    Bab = P.bufs(2, "gab")
    cv = [A.f32(n) for _ in range(3)]
    Bcv = P.bufs(3, "gcv")
    sq = A.f32(n)
    Bsq = P.buf("gsq")
    rs = A.f32(n)
    Brs_ = P.buf("grs")
    gcr = A.f32(n)
    Bgcr = P.buf("gcr")
    ktok = A.f32(NCK * 128)
    vtok = A.f32(NCK * 128)
    Bktok, Bvtok = P.buf("ktok"), P.buf("vtok")
    cols = A.f32(NCK * 4)
    Bcols = P.buf("gcols")
    ex = A.f32(NCK * 4)
    Bex = P.buf("gex")
    E1 = A.f32(n)
    BE1 = P.buf("gE1")
    dm = A.f32(n)
    Bdm = P.buf("gdm")
    dmT = A.f32(n)
    BdmT = P.buf("gdmT")
    X = [A.f32(n) for _ in range(2)]
    Xt = [A.f32(n) for _ in range(2)]
    BX = P.bufs(2, "gX")
    BXt = P.bufs(2, "gXt")
    Rm = A.f32(n)
    BR = P.buf("gR")
    attn = A.f32(n)
    Battn = P.buf("gattn")
    kb = A.f32(NCK * 128)
    vb = A.f32(NCK * 128)
    kdec = A.f32(NCK * 128)
    Bkb, Bvb, Bkdec = P.buf("kb"), P.buf("vb"), P.buf("kdec")
    qd = A.f32(n)
    Bqd = P.buf("qd")
    u = A.f32(NCK * 128)
    Bu = P.buf("gu")
    wT = A.f32(n)
    BwT = P.buf("gwT")
    vnew = [A.f32(128) for _ in range(2)]
    Bvn = P.bufs(2, "gvn")
    osb = A.f32(n)
    Bosb = P.buf("gosb")
    nst = T // n

    def load(si):
        s = si % 2
        t0 = si * n
        for q in range(3):
            load_halo(P, raw[s][q], pj[q * 128:(q + 1) * 128, :], t0, n, [M.Bpj], [Braw[s][q]], f"graw{s}{q}")
        DMA(P, "sp", zr[s], pj[3 * 128:4 * 128, t0:t0 + n], [M.Bpj], [Bzr[s]], f"gz{s}")
        DMA(P, "sp", abr[s][0:1, :].rearrange("o (r t) -> o r t", r=2),
            pj[PJ_SMALL:PJ_SMALL + 2, t0:t0 + n].rearrange("(o r) t -> o r t", o=1), [M.Bpj], [Bab[s]], f"gab{s}")

    load(0)
    for si in range(nst):
        s = si % 2
        t0 = si * n
        if si + 1 < nst:
            load(si + 1)
        for q in range(3):
            conv_silu(P, cv[q], raw[s][q], gconv[:, q * 4:q * 4 + 4], None, n, [Braw[s][q], Bgc], [Bcv[q]])
        ACT(P, zr[s], zr[s], AF.Silu, [Bzr[s]], [Bzr[s]])
        for q in range(2):
            ACT(P, sq, cv[q], AF.Square, [Bcv[q]], [Bsq])
            MM(P, ps[7][:, :n], M.ones[:], sq, True, True, [M.Bones, Bsq], [Bps[7]])
            ACT(P, rs, ps[7][:, :n], AF.Sqrt, [Bps[7]], [Brs_], bias=EPS)
            P.op("dve", lambda e: e.reciprocal(out=rs, in_=rs), [Brs_], [Brs_])
            if q == 0:
                STT(P, cv[q], cv[q], 128.0 ** -0.5, rs, ALU.mult, ALU.mult, [Bcv[q], Brs_], [Bcv[q]])
            else:
                TT(P, "dve", cv[q], cv[q], rs, ALU.mult, [Bcv[q], Brs_], [Bcv[q]])
        if DEBUG_CUT == 12:
            for q in range(3):
                DMA(P, "sp", M.yo[128 + q * 128:256 + q * 128, t0:t0 + n], cv[q], [Bcv[q]], [], f"dbg{q}")
        ar = abr[s][0:1, 0:n]
        br = abr[s][0:1, n:2 * n]
        ACT(P, ar, ar, AF.Exp, [Bab[s], Bgs], [Bab[s]], bias=gsc[0:1, 1:2])
        ACT(P, ar, ar, AF.Ln, [Bab[s]], [Bab[s]], bias=1.0)
        TS(P, "dve", ar, ar, negA[0:1, 0:1], None, ALU.mult, None, [Bab[s], BnA], [Bab[s]])
        ACT(P, br, br, AF.Sigmoid, [Bab[s]], [Bab[s]])
        P.op("dve", lambda e, ar=ar: e.tensor_tensor_scan(out=gcr[0:1, :], data0=reset[0:1, :], data1=ar,
                                                          initial=0.0, op0=ALU.mult, op1=ALU.add),
             [Bab[s], Brs], [Bgcr])
        MM(P, ps[7][:, :n], M.ones[0:1, 0:128], gcr[0:1, :], True, True, [M.Bones, Bgcr], [Bps[7]])
        ACT(P, E1, ps[7][:, :n], AF.Exp, [Bps[7]], [BE1])
        for i in range(NCK):
            c0 = i * 64
            MM(P, ps[6][0:64, 2 * i:2 * i + 1], gcr[0:1, c0:c0 + 64], M.ones[0:1, 0:1], True, True,
               [M.Bones, Bgcr], [Bps[6]])
            MM(P, ps[6][0:64, 2 * i + 1:2 * i + 2], br[:, c0:c0 + 64], M.ones[0:1, 0:1], True, True,
               [M.Bones, Bab[s]], [Bps[6]])
        cv3 = cols[0:64, :].rearrange("p (i c) -> p i c", c=4)
        CP(P, "dve", cv3[:, :, 0:2], ps[6][0:64, 0:2 * NCK].rearrange("p (i c) -> p i c", c=2), [Bps[6]], [Bcols])
        CP(P, "dve", cv3[:, :, 2:3], ps[7][0:64, :n].rearrange("p (i c) -> p i c", c=64)[:, :, 63:64],
           [Bps[7]], [Bcols])
        ex3 = ex[0:64, :].rearrange("p (i c) -> p i c", c=4)
        ACT(P, ex3[:, :, 0:1], cv3[:, :, 0:1], AF.Exp, [Bcols], [Bex])
        TT(P, "dve", ex3[:, :, 0:1], ex3[:, :, 0:1], cv3[:, :, 1:2], ALU.mult, [Bex, Bcols], [Bex])
        TT(P, "dve", ex3[:, :, 1:2], cv3[:, :, 2:3], cv3[:, :, 0:1], ALU.subtract, [Bcols], [Bex])
        ACT(P, ex3[:, :, 1:2], ex3[:, :, 1:2], AF.Exp, [Bex], [Bex])
        TS(P, "dve", ex3[:, :, 2:3], cv3[:, :, 1:2], -1.0, None, ALU.mult, None, [Bcols], [Bex])
        for i in range(NCK):
            c0 = i * 64
            bk = 0 + i // 4
            TR(P, ps[bk][0:64, (i % 4) * 128:(i % 4 + 1) * 128], cv[1][:, c0:c0 + 64], M.ident[:],
               [Bcv[1], M.Bid], [Bps[bk]])
        for i in range(NCK):
            c0 = i * 64
            bk = 2 + i // 4
            TR(P, ps[bk][0:64, (i % 4) * 128:(i % 4 + 1) * 128], cv[2][:, c0:c0 + 64], M.ident[:],
               [Bcv[2], M.Bid], [Bps[bk]])
        for hh in range(2):
            CP(P, "act", ktok[0:64, hh * 512:(hh + 1) * 512], ps[hh][0:64, :], [Bps[hh]], [Bktok])
            CP(P, "dve", vtok[0:64, hh * 512:(hh + 1) * 512], ps[2 + hh][0:64, :], [Bps[2 + hh]], [Bvtok])
        k3 = ktok[0:64, :].rearrange("p (i c) -> p i c", c=128)
        v3 = vtok[0:64, :].rearrange("p (i c) -> p i c", c=128)
        TT(P, "dve", kb[0:64, :].rearrange("p (i c) -> p i c", c=128), k3,
           ex3[:, :, 0:1].to_broadcast([64, NCK, 128]), ALU.mult, [Bktok, Bex], [Bkb])
        TT(P, "pool", vb[0:64, :].rearrange("p (i c) -> p i c", c=128), v3,
           cv3[:, :, 1:2].to_broadcast([64, NCK, 128]), ALU.mult, [Bvtok, Bcols], [Bvb])
        TT(P, "pool", kdec[0:64, :].rearrange("p (i c) -> p i c", c=128), k3,
           ex3[:, :, 1:2].to_broadcast([64, NCK, 128]), ALU.mult, [Bktok, Bex], [Bkdec])
        TT(P, "dve", attn[0:64, :].rearrange("p (i c) -> p i c", c=64),
           ps[7][0:64, :n].rearrange("p (i c) -> p i c", c=64),
           cv3[:, :, 0:1].to_broadcast([64, NCK, 64]), ALU.subtract, [Bps[7], Bcols], [Battn])
        TT(P, "dve", dm[0:64, :], attn[0:64, :], mU[:, :], ALU.add, [Battn, BmU], [Bdm])
        TT(P, "pool", dmT[0:64, :], attn[0:64, :], mL[:, :], ALU.add, [Battn, BmL], [BdmT])
        ACT(P, dm[0:64, :], dm[0:64, :], AF.Exp, [Bdm], [Bdm])
        ACT(P, dmT[0:64, :], dmT[0:64, :], AF.Exp, [BdmT], [BdmT], scale=-1.0)
        for i in range(NCK):
            sl = slice(i * 64, (i + 1) * 64)
            MM(P, ps[4][0:64, sl], cv[1][:, sl], cv[1][:, sl], True, True, [Bcv[1]], [Bps[4]])
        for i in range(NCK):
            sl = slice(i * 64, (i + 1) * 64)
            MM(P, ps[5][0:64, sl], cv[1][:, sl], cv[0][:, sl], True, True, [Bcv[1], Bcv[0]], [Bps[5]])
        MM(P, ps[6][0:64, :n], M.ones[0:1, 0:64], br, True, True, [M.Bones, Bab[s]], [Bps[6]])
        STT(P, X[0][0:64, :], ps[4][0:64, :n], -1.0, dm[0:64, :], ALU.mult, ALU.mult, [Bps[4], Bdm], [BX[0]])
        TT(P, "dve", X[0][0:64, :], X[0][0:64, :], ps[6][0:64, :n], ALU.mult, [BX[0], Bps[6]], [BX[0]])
        TT(P, "pool", X[0][0:64, :], X[0][0:64, :], mS[:, :], ALU.mult, [BX[0], BmS], [BX[0]])
        TT(P, "dve", Xt[0][0:64, :], ps[4][0:64, :n], dmT[0:64, :], ALU.mult, [Bps[4], BdmT], [BXt[0]])
        TT(P, "dve", Xt[0][0:64, :].rearrange("p (i c) -> p i c", c=64),
           Xt[0][0:64, :].rearrange("p (i c) -> p i c", c=64),
           ex3[:, :, 2:3].to_broadcast([64, NCK, 64]), ALU.mult, [BXt[0], Bex], [BXt[0]])
        TT(P, "dve", attn[0:64, :], ps[5][0:64, :n], dm[0:64, :], ALU.mult, [Bps[5], Bdm], [Battn])
        TT(P, "pool", Rm[0:64, :], X[0][0:64, :], id8[:, :], ALU.add, [BX[0], Bid8], [BR])
        cur = 0
        for j in range(5):
            nx = 1 - cur
            for i in range(NCK):
                sl = slice(i * 64, (i + 1) * 64)
                MM(P, ps[0][0:64, sl], Xt[cur][0:64, sl], X[cur][0:64, sl], True, True, [BXt[cur], BX[cur]], [Bps[0]])
            for i in range(NCK):
                sl = slice(i * 64, (i + 1) * 64)
                MM(P, ps[1][0:64, sl], X[cur][0:64, sl], Xt[cur][0:64, sl], True, True, [BXt[cur], BX[cur]], [Bps[1]])
            CP(P, "act", X[nx][0:64, :], ps[0][0:64, :n], [Bps[0]], [BX[nx]])
            CP(P, "dve", Xt[nx][0:64, :], ps[1][0:64, :n], [Bps[1]], [BXt[nx]])
            for i in range(NCK):
                sl = slice(i * 64, (i + 1) * 64)
                MM(P, ps[2][0:64, sl], Xt[nx][0:64, sl], Rm[0:64, sl], True, True, [BXt[nx], BR], [Bps[2]])
            TT(P, "dve", Rm[0:64, :], Rm[0:64, :], ps[2][0:64, :n], ALU.add, [BR, Bps[2]], [BR])
            cur = nx
        for i in range(NCK):
            sl = slice(i * 64, (i + 1) * 64)
            bk = i // 4
            MM(P, ps[bk][0:64, (i % 4) * 128:(i % 4 + 1) * 128], Rm[0:64, sl], vb[0:64, i * 128:(i + 1) * 128],
               True, True, [BR, Bvb], [Bps[bk]])
        for hh in range(2):
            CP(P, "act" if hh else "dve", u[0:64, hh * 512:(hh + 1) * 512], ps[hh][0:64, :], [Bps[hh]], [Bu])
        for i in range(NCK):
            sl = slice(i * 64, (i + 1) * 64)
            MM(P, ps[2][:, sl], kb[0:64, i * 128:(i + 1) * 128], Rm[0:64, sl], True, True, [Bkb, BR], [Bps[2]])
        CP(P, "act", wT, ps[2][:, :n], [Bps[2]], [BwT])
        TT(P, "pool", qd, cv[0], E1, ALU.mult, [Bcv[0], BE1], [Bqd])
        for i in range(NCK if DEBUG_CUT != 31 else 0):
            sl = slice(i * 64, (i + 1) * 64)
            s128 = slice(i * 128, (i + 1) * 128)
            vb_ = i % 2
            MM(P, ps[3][0:64, 0:128], wT[:, sl], S, True, True, [BwT, BS], [Bps[3]])
            TT(P, "dve", vnew[vb_][0:64, :], u[0:64, s128], ps[3][0:64, 0:128], ALU.subtract, [Bu, Bps[3]], [Bvn[vb_]])
            MM(P, ps[5][:, sl], S, qd[:, sl], True, False, [BS, Bqd], [Bps[5]])
            MM(P, ps[5][:, sl], vnew[vb_][0:64, :], attn[0:64, sl], False, True, [Bvn[vb_], Battn], [Bps[5]])
            MM(P, ps[4][:, 0:128], kdec[0:64, s128], vnew[vb_][0:64, :], True, True, [Bkdec, Bvn[vb_]], [Bps[4]])
            STT(P, S, S, E1[:, i * 64 + 63:i * 64 + 64], ps[4][:, 0:128], ALU.mult, ALU.add, [BS, BE1, Bps[4]], [BS])
        CP(P, "act", osb, ps[5][:, :n], [Bps[5]], [Bosb])
        ACT(P, sq, osb, AF.Square, [Bosb], [Bsq])
        MM(P, ps[7][:, :n], M.ones[:], sq, True, True, [M.Bones, Bsq], [Bps[7]])
        ACT(P, rs, ps[7][:, :n], AF.Sqrt, [Bps[7]], [Brs_], bias=EPS, scale=1.0 / 128)
        P.op("dve", lambda e: e.reciprocal(out=rs, in_=rs), [Brs_], [Brs_])
        STT(P, osb, osb, gnorm[:, 0:1], rs, ALU.mult, ALU.mult, [Bosb, Bgn, Brs_], [Bosb])
        TT(P, "dve", osb, osb, zr[s], ALU.mult, [Bosb, Bzr[s]], [Bosb])
        DMA(P, "sp", M.yo[0:128, t0:t0 + n], osb, [Bosb], [], "ya")
        if DEBUG_CUT == 11:
            P.barrier()
    P.barrier()


def MMA(P, out, lhsT, rhs, reads, writes):
    P.op("pe", lambda e: e.matmul(out, lhsT=lhsT, rhs=rhs, start=False, stop=False, skip_group_check=True),
         reads, writes, pe_acc=True)


PD_S = 2048
PD_C = 5120
W_S = 1792
W_W = 1408
W_C = 3072


def m_nsa(M):
    P, T, A, nc = M.P, M.T, M.A, M.nc
    n = 512
    NKT = T // 128
    NQ = T // n
    pj = M.pj
    ps, Bps = M.ps, M.Bps
    qn, Bqn = M.const("nqn", [128, 1])
    kn, Bkn = M.const("nkn", [128, 1])
    pekT, Bpek = M.const("pekT", [128, 32])
    pevT, Bpev = M.const("pevT", [128, 32])
    tabs, Btabs = M.const("ntab", [32, 2])
    t31, Bt31 = M.const("nt31", [128, 2])
    wck_d = M.din("wck", [128, 4096])
    wcv_d = M.din("wcv", [128, 4096])
    ohS = M.din("ohS", [32, PD_S]); ngS = M.din("ngS", [1, PD_S])
    ohW = M.din("ohW", [32, PD_S]); ngW = M.din("ngW", [1, PD_S])
    ohC = M.din("ohC", [32, PD_C]); ngC = M.din("ngC", [1, PD_C])
    E_d = M.din("Esel", [128, 8192])
    wimp_d = M.din("wimp", [128, 512])
    dS = nc.dram_tensor("dS", [129 * PD_S], F32)
    dW = nc.dram_tensor("dW", [129 * PD_S], F32)
    dCM = nc.dram_tensor("dCM", [129 * PD_C], F32)
    dCO = nc.dram_tensor("dCO", [129 * PD_C], F32)
    A.reset()
    BTs = A.f32(W_S); BTw = A.f32(W_W); BTcM = A.f32(W_C); BTcO = A.f32(W_C)
    BBTs, BBTw, BBTcM, BBTcO = P.buf("BTs"), P.buf("BTw"), P.buf("BTcM"), P.buf("BTcO")
    mark0 = A.off
    fb = A.f32(PD_C)
    Bfb = P.buf("fb")
    frow = A.f32(PD_C)
    Bfrow = P.buf("frow")
    oh = A.f32(PD_C)
    Boh = P.buf("oh")
    ngt = A.f32(PD_C)
    Bng = P.buf("ng")

    def strip(oh_d, ng_d, Pd, tcol, dram, dst, Bdst, W, step, base, key):
        DMA(P, "sp", oh[0:32, 0:Pd], oh_d, [], [Boh], "oh")
        DMA(P, "sp", ngt[0:1, 0:Pd], ng_d, [], [Bng], "ng")
        for b0 in range(0, Pd, 512):
            MM(P, ps[0][0:1, :], tabs[:, tcol:tcol + 1], oh[0:32, b0:b0 + 512], True, True, [Btabs, Boh], [Bps[0]])
            TT(P, "dve", frow[0:1, b0:b0 + 512], ps[0][0:1, :], ngt[0:1, b0:b0 + 512], ALU.add,
               [Bps[0], Bng], [Bfrow])
        for b0 in range(0, Pd, 512):
            pb = 1 + (b0 // 512) % 2
            MM(P, ps[pb][:, :], M.ones[0:1, 0:128], frow[0:1, b0:b0 + 512], True, True, [M.Bones, Bfrow], [Bps[pb]])
            CP(P, "act" if pb == 1 else "dve", fb[:, b0:b0 + 512], ps[pb][:, :], [Bps[pb]], [Bfb])
        Bd = P.buf("d" + key)
        dap = dram.ap()
        DMA(P, "sp", dap[0:128 * Pd].rearrange("(p c) -> p c", c=Pd), fb[:, 0:Pd], [Bfb], [Bd], "dw" + key)
        DMA(P, "sp", dap[128 * Pd:129 * Pd].rearrange("(p c) -> p c", c=Pd), fb[0:1, 0:Pd], [Bfb], [Bd], "dw" + key)
        src = bass.AP(tensor=dap.tensor, offset=base, ap=[[Pd - step, 128], [1, W]])
        DMA(P, "sp", dst, src, [Bd], [Bdst], "bt" + key)

    strip(ohS, ngS, PD_S, 0, dS, BTs, BBTs, W_S, 1, 127, "S")
    strip(ohW, ngW, PD_S, 0, dW, BTw, BBTw, W_W, 1, 127, "W")
    strip(ohC, ngC, PD_C, 0, dCM, BTcM, BBTcM, W_C, 16, 2032, "CM")
    strip(ohC, ngC, PD_C, 1, dCO, BTcO, BBTcO, W_C, 16, 2032, "CO")
    P.barrier()
    A.off = mark0
    ksT = A.bf16(T); kwT = A.bf16(T)
    Bks, Bkw = P.buf("ksT"), P.buf("kwT")
    Vs = A.bf16(NKT * 129); Vw = A.bf16(NKT * 129)
    BVs, BVw = P.buf("Vs"), P.buf("Vw")
    Eb = A.bf16(T)
    BE = P.buf("E")
    kcmpT = A.f32(512)
    Bkcmp = P.buf("kcmp")
    WV = A.f32(4 * 256)
    BWV = P.buf("WV")
    Wimp = A.f32(512)
    BWimp = P.buf("Wimp")
    mark = A.off
    DMA(P, "pool", Eb, E_d[:, 0:T], [], [BE], "E")
    DMA(P, "sp", Wimp, wimp_d, [], [BWimp], "wimp")
    DMA(P, "sp", WV.rearrange("p (c x) -> p c x", c=4)[:, :, 0:128], wimp_d.rearrange("p (c x) -> p c x", c=4),
        [], [BWV], "wv")
    MS(P, "pool", Vs.rearrange("p (k x) -> p k x", x=129)[:, :, 128:129], 1.0, [BVs])
    MS(P, "pool", Vw.rearrange("p (k x) -> p k x", x=129)[:, :, 128:129], 1.0, [BVw])
    kraw = [A.f32(n) for _ in range(2)]
    Bkraw = P.bufs(2, "kraw")
    sq = A.f32(n); Bsq = P.buf("nsq")
    rs = A.f32(n); Brs_ = P.buf("nrs")
    it = 0
    for (chunk, dstT, Bdst) in ((8, ksT, Bks), (10, kwT, Bkw)):
        for ti in range(NQ):
            s = it % 2
            it += 1
            t0 = ti * n
            DMA(P, "sp", kraw[s], pj[chunk * 128:(chunk + 1) * 128, t0:t0 + n], [M.Bpj], [Bkraw[s]], f"kraw{s}")
            ACT(P, sq, kraw[s], AF.Square, [Bkraw[s]], [Bsq])
            MM(P, ps[0][:, :n], M.ones[:], sq, True, True, [M.Bones, Bsq], [Bps[0]])
            ACT(P, rs, ps[0][:, :n], AF.Sqrt, [Bps[0]], [Brs_], bias=EPS, scale=1.0 / 128)
            P.op("dve", lambda e: e.reciprocal(out=rs, in_=rs), [Brs_], [Brs_])
            STT(P, dstT[:, t0:t0 + n], kraw[s], kn[:, 0:1], rs, ALU.mult, ALU.mult, [Bkraw[s], Bkn, Brs_], [Bdst])
    for (chunk, dstV, Bdst) in ((9, Vs, BVs), (11, Vw, BVw)):
        dv3 = dstV.rearrange("p (k x) -> p k x", x=129)
        for ti in range(NQ):
            s = it % 2
            it += 1
            t0 = ti * n
            DMA(P, "sp", kraw[s], pj[chunk * 128:(chunk + 1) * 128, t0:t0 + n], [M.Bpj], [Bkraw[s]], f"kraw{s}")
            pb = 1 + ti % 2
            for c in range(4):
                TR(P, ps[pb][:, c * 128:(c + 1) * 128], kraw[s][:, c * 128:(c + 1) * 128], M.ident[:],
                   [Bkraw[s], M.Bid], [Bps[pb]])
            CP(P, "act" if ti % 2 else "dve", dv3[:, ti * 4:ti * 4 + 4, 0:128],
               ps[pb][:, :].rearrange("p (c x) -> p c x", c=4), [Bps[pb]], [Bdst])
    wc = A.f32(4096)
    Bwc = P.buf("wc")
    kct = [A.f32(1040) for _ in range(2)]
    Bkct = P.bufs(2, "kct")
    cbias = A.f32(1)
    Bcb = P.buf("cbias")
    vcT = A.f32(512)
    BvcT = P.buf("vcT")
    MS(P, "dve", kcmpT, 0.0, [Bkcmp])
    MS(P, "dve", vcT, 0.0, [BvcT])
    for (chunk, w_d, peT, Bpe, dst, Bdst) in ((6, wck_d, pekT, Bpek, kcmpT, Bkcmp), (7, wcv_d, pevT, Bpev, vcT, BvcT)):
        DMA(P, "sp", wc, w_d, [], [Bwc], "wc")
        for l in range(32):
            MM(P, ps[0][:, 0:1], wc[:, l * 128:(l + 1) * 128], peT[:, l:l + 1], l == 0, l == 31, [Bwc, Bpe], [Bps[0]])
        CP(P, "dve", cbias, ps[0][:, 0:1], [Bps[0]], [Bcb])
        ntile = T // 1024
        for ti in range(ntile):
            s = it % 2
            it += 1
            t0 = ti * 1024
            ntok = min(1040, T - t0)
            nb = 64 if ntok == 1040 else 63
            DMA(P, "sp", kct[s][:, 0:ntok], pj[chunk * 128:(chunk + 1) * 128, t0:t0 + ntok], [M.Bpj], [Bkct[s]],
                f"kct{s}")
            pb = 1 + ti % 2
            for l in range(32):
                rhs = kct[s][:, l:l + 16 * (nb - 1) + 1:16]
                MM(P, ps[pb][:, 0:nb], wc[:, l * 128:(l + 1) * 128], rhs, l == 0, l == 31, [Bwc, Bkct[s]], [Bps[pb]])
            TS(P, "dve", dst[:, ti * 64:ti * 64 + nb], ps[pb][:, 0:nb], cbias[:, 0:1], None, ALU.add, None,
               [Bps[pb], Bcb], [Bdst])
    ACT(P, sq, kcmpT, AF.Square, [Bkcmp], [Bsq])
    MM(P, ps[0][:, :n], M.ones[:], sq, True, True, [M.Bones, Bsq], [Bps[0]])
    ACT(P, rs, ps[0][:, :n], AF.Sqrt, [Bps[0]], [Brs_], bias=EPS, scale=1.0 / 128)
    P.op("dve", lambda e: e.reciprocal(out=rs, in_=rs), [Brs_], [Brs_])
    STT(P, kcmpT, kcmpT, kn[:, 0:1], rs, ALU.mult, ALU.mult, [Bkcmp, Bkn, Brs_], [Bkcmp])
    for c in range(4):
        TR(P, ps[1][:, c * 128:(c + 1) * 128], vcT[:, c * 128:(c + 1) * 128], M.ident[:], [BvcT, M.Bid], [Bps[1]])
    CP(P, "dve", WV.rearrange("p (c x) -> p c x", c=4)[:, :, 128:256], ps[1][:, :].rearrange("p (c x) -> p c x", c=4),
       [Bps[1]], [BWV])
    P.barrier()
    A.off = mark
    qraw = [A.f32(n) for _ in range(2)]; Bqraw = P.bufs(2, "qraw")
    qM = A.f32(n); qO = A.f32(n); BqM, BqO = P.buf("qM"), P.buf("qO")
    qMb = A.bf16(n); BqMb = P.buf("qMb")
    glog = A.f32(n); Bglog = P.buf("glog")
    gq = A.f32(12); Bgq = P.buf("gq")
    tmpf = [A.f32(n) for _ in range(3)]; Btmpf = P.bufs(3, "ntmp")
    Pf = [A.f32(n) for _ in range(2)]; BPf = P.bufs(2, "Pf")
    Pb = [A.bf16(n) for _ in range(3)]; BPb = P.bufs(3, "Pb")
    negmT = A.bf16(n); BnegmT = P.buf("negmT")
    imp = A.f32(128); Bimp = P.buf("imp")
    imp2 = A.f32(128); Bimp2 = P.buf("imp2")
    scr = A.f32(128); Bscr = P.buf("scr")
    m8 = A.f32(16); Bm8 = P.buf("m8")
    sm = A.f32(16); Bsm = P.buf("sm")
    negm = A.f32(128); Bnegm = P.buf("negm")
    ocomb = [A.f32(128) for _ in range(4)]; Boc = P.bufs(4, "ocomb")
    ost = A.f32(n); Bost = P.buf("ost")
    ti_ = 0
    pi = 0
    for Q in range(NQ):
        if DEBUG_CUT == 21:
            break
        t0 = Q * n
        for hd, (chunk, dst, Bdst) in enumerate(((4, qM, BqM), (5, qO, BqO))):
            s = hd
            DMA(P, "sp", qraw[s], pj[chunk * 128:(chunk + 1) * 128, t0:t0 + n], [M.Bpj], [Bqraw[s]], f"qraw{s}")
            ACT(P, sq, qraw[s], AF.Square, [Bqraw[s]], [Bsq])
            MM(P, ps[7][:, :n], M.ones[:], sq, True, True, [M.Bones, Bsq], [Bps[7]])
            ACT(P, rs, ps[7][:, :n], AF.Sqrt, [Bps[7]], [Brs_], bias=128 * EPS, scale=1.0)
            P.op("dve", lambda e: e.reciprocal(out=rs, in_=rs), [Brs_], [Brs_])
            STT(P, dst, qraw[s], qn[:, 0:1], rs, ALU.mult, ALU.mult, [Bqraw[s], Bqn, Brs_], [Bdst])
        CP(P, "pool", qMb, qM, [BqM], [BqMb])
        DMA(P, "sp", glog[0:3, :], pj[PJ_SMALL + 2:PJ_SMALL + 5, t0:t0 + n], [M.Bpj], [Bglog], "glog")
        for sb in range(4):
            TR(P, ps[7][:, sb * 3:sb * 3 + 3], glog[0:3, sb * 128:(sb + 1) * 128], M.ident[0:3, 0:3],
               [Bglog, M.Bid], [Bps[7]])
        ACT(P, gq, ps[7][:, 0:12], AF.Sigmoid, [Bps[7]], [Bgq])
        for bk in (2, 3, 4):
            MS(P, "dve", ps[bk][:, :], 0.0, [Bps[bk]])
        def cmp_s1(hd, ct, qh, Bqh, BT, BBT):
            nonlocal pi, ti_
            Mq = Q - 4 * ct
            pb = pi % 2
            pi += 1
            MM(P, ps[pb][:, :n], kcmpT[:, ct * 128:(ct + 1) * 128], qh, True, True, [Bkcmp, Bqh], [Bps[pb]])
            f = ti_ % 2
            ti_ += 1
            if Mq <= 5:
                t = ti_ % 3
                TT(P, "dve", tmpf[t], ps[pb][:, :n], BT[:, 512 * Mq:512 * Mq + 512], ALU.add,
                   [Bps[pb], BBT], [Btmpf[t]])
                ACT(P, Pf[f], tmpf[t], AF.Exp, [Btmpf[t]], [BPf[f]])
            else:
                ACT(P, Pf[f], ps[pb][:, :n], AF.Exp, [Bps[pb], Bt31], [BPf[f]], bias=t31[:, hd:hd + 1])
            return f

        def cmp_s2(hd, ct, f):
            for sb in range(4):
                lhsT = Pf[f][:, sb * 128:(sb + 1) * 128]
                if hd == 0:
                    bk = 2 + sb // 2
                    MMA(P, ps[bk][:, (sb % 2) * 256:(sb % 2) * 256 + 256], lhsT, WV[:, ct * 256:(ct + 1) * 256],
                        [BPf[f], BWV], [Bps[bk]])
                else:
                    MMA(P, ps[4][:, sb * 128:(sb + 1) * 128], lhsT, Wimp[:, ct * 128:(ct + 1) * 128],
                        [BPf[f], BWimp], [Bps[4]])

        prev = None
        for hd, (qh, Bqh, BT, BBT) in enumerate(((qM, BqM, BTcM, BBTcM), (qO, BqO, BTcO, BBTcO))):
            for ct in range(Q // 4 + 1):
                f = cmp_s1(hd, ct, qh, Bqh, BT, BBT)
                if prev is not None:
                    cmp_s2(*prev)
                prev = (hd, ct, f)
        cmp_s2(*prev)
        for sb in range(4):
            qt = 4 * Q + sb
            bk = 2 + sb // 2
            aM = ps[bk][:, (sb % 2) * 256:(sb % 2) * 256 + 128]
            vM = ps[bk][:, (sb % 2) * 256 + 128:(sb % 2) * 256 + 256]
            aO = ps[4][:, sb * 128:(sb + 1) * 128]
            TS(P, "dve", scr, aM, 0.5, 0.0, ALU.mult, ALU.add, [Bps[bk]], [Bscr, Bsm], accum=sm[:, 0:1])
            TS(P, "dve", scr, aO, 0.5, 0.0, ALU.mult, ALU.add, [Bps[4]], [Bscr, Bsm], accum=sm[:, 1:2])
            TS(P, "dve", sm[:, 0:2], sm[:, 0:2], 1e-30, None, ALU.max, None, [Bsm], [Bsm])
            P.op("dve", lambda e: e.reciprocal(out=sm[:, 2:4], in_=sm[:, 0:2]), [Bsm], [Bsm])
            TS(P, "dve", imp, aM, sm[:, 2:3], None, ALU.mult, None, [Bps[bk], Bsm], [Bimp])
            STT(P, imp, aO, sm[:, 3:4], imp, ALU.mult, ALU.add, [Bps[4], Bsm, Bimp], [Bimp])
            TT(P, "dve", sm[:, 4:5], sm[:, 2:3], gq[:, sb * 3:sb * 3 + 1], ALU.mult, [Bsm, Bgq], [Bsm])
            TS(P, "dve", ocomb[sb], vM, sm[:, 4:5], None, ALU.mult, None, [Bps[bk], Bsm], [Boc[sb]])
            MS(P, "pool", imp[:, 0:1], 1e4, [Bimp])
            MS(P, "pool", imp[:, 2 * qt:2 * qt + 1], 1e4, [Bimp])
            if qt > 0:
                MS(P, "pool", imp[0:64, 2 * qt - 1:2 * qt], 1e4, [Bimp])
            MS(P, "pool", imp[64:128, 2 * qt + 1:2 * qt + 2], 1e4, [Bimp])
            P.op("dve", lambda e: e.max(out=m8[:, 0:8], in_=imp), [Bimp], [Bm8])
            P.op("dve", lambda e: e.match_replace(out=imp2, in_to_replace=m8[:, 0:8], in_values=imp, imm_value=-1e30),
                 [Bimp, Bm8], [Bimp2])
            P.op("dve", lambda e: e.max(out=m8[:, 8:16], in_=imp2), [Bimp2], [Bm8])
            TS(P, "dve", negm, imp, m8[:, 15:16], -32768.0, ALU.is_lt, ALU.mult, [Bimp, Bm8], [Bnegm])
            TR(P, ps[7][:, 128:256], negm, M.ident[:], [Bnegm, M.Bid], [Bps[7]])
            CP(P, "act", negmT[:, sb * 128:(sb + 1) * 128], ps[7][:, 128:256], [Bps[7]], [BnegmT])
        if DEBUG_CUT == 22:
            continue
        for br_, (kT, BkT, Vv, BVv, BT, BBT, lo, accb) in enumerate(
                ((ksT, Bks, Vs, BVs, BTs, BBTs, 0, (5, 6)), (kwT, Bkw, Vw, BVw, BTw, BBTw, max(0, 4 * Q - 4), (2, 3)))):
            for bk in accb:
                MS(P, "dve", ps[bk][:, :], 0.0, [Bps[bk]])
            def sw_s1(kt):
                nonlocal pi, ti_
                m = 4 * Q - kt
                pb = pi % 2
                pi += 1
                if br_ == 0:
                    MM(P, ps[pb][:, :n], kT[:, kt * 128:(kt + 1) * 128], qMb, True, False, [BkT, BqMb], [Bps[pb]])
                    MM(P, ps[pb][:, :n], Eb[:, kt * 128:(kt + 1) * 128], negmT, False, True, [BE, BnegmT], [Bps[pb]])
                else:
                    MM(P, ps[pb][:, :n], kT[:, kt * 128:(kt + 1) * 128], qMb, True, True, [BkT, BqMb], [Bps[pb]])
                f = ti_ % 3
                ti_ += 1
                if m <= 7:
                    t = ti_ % 3
                    TT(P, "dve", tmpf[t], ps[pb][:, :n], BT[:, 128 * (m + 3):128 * (m + 3) + 512], ALU.add,
                       [Bps[pb], BBT], [Btmpf[t]])
                    ACT(P, Pb[f], tmpf[t], AF.Exp, [Btmpf[t]], [BPb[f]])
                else:
                    ACT(P, Pb[f], ps[pb][:, :n], AF.Exp, [Bps[pb], Bt31], [BPb[f]], bias=t31[:, 0:1])
                return f

            def sw_s2(kt, f):
                for sb in range(4):
                    if 4 * Q + sb < kt:
                        continue
                    bk = accb[sb // 2]
                    MMA(P, ps[bk][:, (sb % 2) * 129:(sb % 2) * 129 + 129], Pb[f][:, sb * 128:(sb + 1) * 128],
                        Vv[:, kt * 129:(kt + 1) * 129], [BPb[f], BVv], [Bps[bk]])

            prev = None
            for kt in range(lo, 4 * Q + 4):
                f = sw_s1(kt)
                if prev is not None:
                    sw_s2(*prev)
                prev = (kt, f)
            sw_s2(*prev)
            for sb in range(4):
                bk = accb[sb // 2]
                acc = ps[bk][:, (sb % 2) * 129:(sb % 2) * 129 + 129]
                P.op("dve", lambda e, acc=acc: e.reciprocal(out=sm[:, 5:6], in_=acc[:, 128:129]), [Bps[bk]], [Bsm])
                TT(P, "dve", sm[:, 6:7], sm[:, 5:6], gq[:, sb * 3 + 1 + br_:sb * 3 + 2 + br_], ALU.mult, [Bsm, Bgq], [Bsm])
                STT(P, ocomb[sb], acc[:, 0:128], sm[:, 6:7], ocomb[sb], ALU.mult, ALU.add, [Bps[bk], Bsm, Boc[sb]], [Boc[sb]])
        for sb in range(4):
            TR(P, ps[7][:, 256:384], ocomb[sb], M.ident[:], [Boc[sb], M.Bid], [Bps[7]])
            CP(P, "act", ost[:, sb * 128:(sb + 1) * 128], ps[7][:, 256:384], [Bps[7]], [Bost])
        DMA(P, "sp", M.yo[128:256, t0:t0 + n], ost, [Bost], [], "yb")
    P.barrier()


def _bucket(dist):
    n = np.maximum(dist, 0)
    nf = np.maximum(n, 1).astype(np.float32)
    large = 16 + (np.log(nf / np.float32(16)) / np.float32(math.log(1024 / 16)) * np.float32(16)).astype(np.int32)
    large = np.minimum(large, 31)
    return np.where(n < 16, n, large)


_NSA_CONSTS = {}


def nsa_consts():
    if _NSA_CONSTS:
        return _NSA_CONSTS
    c = _NSA_CONSTS
    j = np.arange(PD_S)
    dist = j - 511
    bk = _bucket(dist)
    for nm, valid in (("S", dist >= 0), ("W", (dist >= 0) & (dist < 512))):
        oh = np.zeros((32, PD_S), np.float32)
        oh[bk[valid], j[valid]] = 1.0
        c["oh" + nm] = oh
        c["ng" + nm] = np.where(valid, 0.0, NEGBIG).astype(np.float32)[None, :]
    j = np.arange(PD_C)
    dist = j - 2063
    bk = _bucket(dist)
    valid = dist >= 0
    oh = np.zeros((32, PD_C), np.float32)
    oh[bk[valid], j[valid]] = 1.0
    c["ohC"] = oh
    c["ngC"] = np.where(valid, 0.0, NEGBIG).astype(np.float32)[None, :]
    E = np.zeros((128, 64, 128), np.float32)
    for kt in range(64):
        for k in range(128):
            E[2 * kt + k // 64, kt, k] = 1.0
    c["Esel"] = E.reshape(128, 8192)
    W = np.zeros((4, 128, 128), np.float32)
    wts = (1.0, 2.0, 2.0, 2.0, 1.0)
    for ct in range(4):
        for i in range(128):
            gi = ct * 128 + i
            for jj in range(128):
                w = gi - 4 * jj + 1
                if 0 <= w <= 4 and gi <= 510:
                    W[ct, i, jj] = wts[w]
    c["wimp"] = np.ascontiguousarray(W.transpose(1, 0, 2).reshape(128, 512))
    return c


_PROGS = {}


def _prog(kind):
    if kind not in _PROGS:
        _PROGS[kind] = build_M(8192) if kind == "M" else build_R(kind, 2048)
    return _PROGS[kind]


def _ffn_inputs(prm, l, which, suffix):
    return {"g" + suffix: gl(prm["g_ffn" + which][l]),
            "wu" + suffix: np.ascontiguousarray(prm["w_up" + which][l]),
            "wd" + suffix: np.ascontiguousarray(prm["w_down" + which][l])}


def kernel(**inputs):
    prm = {k: np.asarray(v, dtype=np.float32) for k, v in inputs.items()}
    x = prm.pop("x")
    B, T, D = x.shape
    NCORE = 8
    TC = T // 4
    cores = list(range(NCORE))

    def run(kind, maps):
        res = run_bass_kernel_spmd(_prog(kind), maps, core_ids=cores)
        return res.results

    xs = [np.ascontiguousarray(x[c // 4, (c % 4) * TC:(c % 4 + 1) * TC, :].T) for c in cores]
    shared = _ffn_inputs(prm, 0, "1", "1")
    outs = run("A", [dict(xin=xs[c], **shared) for c in cores])
    x1s = [o["xo"] for o in outs]
    consts = m_consts(T)
    selbs = [np.ascontiguousarray(np.tile(np.eye(2, dtype=np.float32)[c // 4][None, :], (128, 1))) for c in cores]
    n_layers = prm["g_mix"].shape[0]
    for l in range(n_layers):
        x1f = np.empty((2, D, T), np.float32)
        for c in cores:
            x1f[c // 4][:, (c % 4) * TC:(c % 4 + 1) * TC] = x1s[c]
        lay = [m_layer_inputs(prm, l, h, T) for h in range(4)]
        outs = run("M", [dict(x1f=x1f, selb=selbs[c], **consts, **lay[c % 4]) for c in cores])
        yfull = np.empty((2, D, T), np.float32)
        for c in cores:
            b, h = c // 4, c % 4
            yo = outs[c]["yo"]
            yfull[b][h * 128:(h + 1) * 128] = yo[0:128]
            yfull[b][512 + h * 128:512 + (h + 1) * 128] = yo[128:256]
            yfull[b][1024 + h * 256:1024 + (h + 1) * 256] = yo[256:512]
        shared = {"gm": gl(prm["g_mix"][l]), "sn": gl(prm["ssm_norm"][l], 8),
                  "wg": np.ascontiguousarray(prm["w_in"][l][:, 6692:]),
                  "pa": np.ascontiguousarray(prm["p_a"][l]), "pb": np.ascontiguousarray(prm["p_b"][l]),
                  "pc": np.ascontiguousarray(prm["p_c"][l]), "wo": np.ascontiguousarray(prm["w_o"][l])}
        shared.update(_ffn_inputs(prm, l, "2", "2"))
        last = l == n_layers - 1
        if not last:
            shared.update(_ffn_inputs(prm, l + 1, "1", "1"))
        maps = [dict(xin=x1s[c],
                     yin=np.ascontiguousarray(yfull[c // 4][:, (c % 4) * TC:(c % 4 + 1) * TC]), **shared)
                for c in cores]
        outs = run("R" if last else "RA", maps)
        x1s = [o["xo"] for o in outs]
    out = np.empty((B, T, D), np.float32)
    for c in cores:
        out[c // 4, (c % 4) * TC:(c % 4 + 1) * TC, :] = x1s[c].T
    return out
```

```python
import bisect
import contextlib
import math
import numpy as np
import concourse.bass as bass
import concourse.mybir as mybir
from concourse.bass_utils import run_bass_kernel_spmd

F32 = mybir.dt.float32
BF16 = mybir.dt.bfloat16
AF = mybir.ActivationFunctionType
ALU = mybir.AluOpType
AX = mybir.AxisListType


class Buf:
    __slots__ = ("name", "lw", "rd", "excl")

    def __init__(self, name):
        self.name = name
        self.excl = False
        self.lw = None
        self.rd = {}


class Prog:
    ENG = ("pe", "act", "dve", "pool", "sp")

    def __init__(self, nc):
        self.nc = nc
        self.stack = contextlib.ExitStack()
        self.q = {e: [] for e in self.ENG}
        self.cnt = {e: 0 for e in self.ENG}
        self.seen = {e: {} for e in self.ENG}
        self.dcount = {}
        self.waited = {e: set() for e in self.ENG}
        self.nbuf = 0

    def buf(self, name=None):
        self.nbuf += 1
        return Buf(name or f"b{self.nbuf}")

    def bufs(self, n, name=None):
        return [self.buf(f"{name}{i}") for i in range(n)]

    def sb(self, name, shape, dtype):
        return self.stack.enter_context(self.nc.sbuf_tensor(name, list(shape), dtype))

    def ps(self, name, shape, dtype=F32):
        return self.stack.enter_context(self.nc.psum_tensor(name, list(shape), dtype))

    def op(self, eng, fn, reads=(), writes=(), dma=None, pe_acc=False):
        deps = {}

        def add(ev):
            if ev is None:
                return
            k, v = ev
            if deps.get(k, 0) < v:
                deps[k] = v

        for b in reads:
            add(b.lw)
            if b.excl:
                for k, v in b.rd.items():
                    if k != eng:
                        add((k, v))
        for b in writes:
            if not (pe_acc and b.lw is not None and b.lw[0] == "pe"):
                add(b.lw)
            for k, v in b.rd.items():
                add((k, v))
        waits = []
        seen = self.seen[eng]
        for k, v in deps.items():
            if seen.get(k, 0) >= v:
                continue
            seen[k] = v
            waits.append((k, v))
            if k in self.waited:
                self.waited[k].add(v)
        if dma is None:
            self.cnt[eng] += 1
            ev = (eng, self.cnt[eng])
        else:
            key = dma if dma.startswith("c:") else "d:" + dma
            self.dcount[key] = self.dcount.get(key, 0) + 1
            ev = (key, self.dcount[key])
        for b in reads:
            if b.rd.get(ev[0], 0) < ev[1]:
                b.rd[ev[0]] = ev[1]
        for b in writes:
            b.lw = ev
            b.rd = {}
        self.q[eng].append((waits, fn, ev))
        return ev

    def barrier(self):
        evs = [(e, self.cnt[e]) for e in self.ENG if self.cnt[e] > 0]
        evs += [(k, c) for k, c in self.dcount.items()]
        for eng in self.ENG:
            waits = []
            seen = self.seen[eng]
            for k, v in evs:
                if seen.get(k, 0) >= v:
                    continue
                seen[k] = v
                waits.append((k, v))
                if k in self.waited:
                    self.waited[k].add(v)
            if waits:
                self.q[eng].append((waits, None, None))

    def emit(self):
        nc = self.nc
        self.barrier()
        sems = {e: self.stack.enter_context(nc.semaphore("s_" + e)) for e in self.ENG}
        for i, k in enumerate(sorted(self.dcount)):
            sems[k] = self.stack.enter_context(nc.semaphore(f"sd{i}"))
        miles = {e: sorted(self.waited[e]) for e in self.ENG}

        def val(k, v):
            if k in miles:
                return bisect.bisect_right(miles[k], v)
            return v if k.startswith("c:") else 16 * v

        wsets = {e: self.waited[e] for e in self.ENG}
        with nc.Block() as block:
            decs = {"pe": block.tensor, "act": block.scalar, "dve": block.vector,
                    "pool": block.gpsimd, "sp": block.sync}
            for e in self.ENG:
                items = self.q[e]

                def body(engobj, items=items, e=e):
                    for waits, fn, ev in items:
                        for k, v in waits:
                            engobj.wait_ge(sems[k], val(k, v))
                        if fn is None:
                            continue
                        ins = fn(engobj)
                        if ev[0] in miles:
                            if ev[1] in wsets[ev[0]]:
                                ins.then_inc(sems[ev[0]], 1)
                        else:
                            ins.then_inc(sems[ev[0]], 1 if ev[0].startswith("c:") else 16)
                decs[e](body)
        self.stack.close()


def MM(P, out, lhsT, rhs, start, stop, reads, writes):
    P.op("pe", lambda e: e.matmul(out, lhsT=lhsT, rhs=rhs, start=start, stop=stop),
         reads, writes, pe_acc=not start)


def TR(P, out, in_, ident, reads, writes):
    P.op("pe", lambda e: e.transpose(out, in_, ident), reads, writes)


def ACT(P, out, in_, func, reads, writes, bias=None, scale=None, accum=None):
    kw = {}
    if bias is not None:
        kw["bias"] = bias
    if scale is not None:
        kw["scale"] = scale
    if accum is not None:
        kw["accum_out"] = accum
    P.op("act", lambda e: e.activation(out=out, in_=in_, func=func, **kw), reads, writes)


def TT(P, eng, out, in0, in1, op, reads, writes):
    P.op(eng, lambda e: e.tensor_tensor(out=out, in0=in0, in1=in1, op=op), reads, writes)


def TS(P, eng, out, in0, s1, s2, op0, op1, reads, writes, accum=None):
    if op1 is None:
        P.op(eng, lambda e: e.tensor_scalar(out=out, in0=in0, scalar1=s1, scalar2=None, op0=op0), reads, writes)
    elif accum is None:
        P.op(eng, lambda e: e.tensor_scalar(out=out, in0=in0, scalar1=s1, scalar2=s2, op0=op0, op1=op1), reads, writes)
    else:
        P.op(eng, lambda e: e.tensor_scalar(out=out, in0=in0, scalar1=s1, scalar2=s2, op0=op0, op1=op1,
                                            accum_out=accum), reads, writes)


def STT(P, out, in0, scalar, in1, op0, op1, reads, writes):
    P.op("dve", lambda e: e.scalar_tensor_tensor(out=out, in0=in0, scalar=scalar, in1=in1, op0=op0, op1=op1),
         reads, writes)


def CP(P, eng, out, in_, reads, writes):
    if eng == "act":
        P.op("act", lambda e: e.copy(out=out, in_=in_), reads, writes)
    else:
        P.op(eng, lambda e: e.tensor_copy(out=out, in_=in_), reads, writes)


def MS(P, eng, ap, val, writes):
    P.op(eng, lambda e: e.memset(ap, val), (), writes)


def DMA(P, eng, out, in_, reads, writes, key):
    P.op(eng, lambda e: e.dma_start(out=out, in_=in_), reads, writes, dma=key)


D_MODEL = 2048
DC = 16
D_FF = 5504
FC = 43
EPS = 1e-6
WSLOT = 5504
NWS = 6


class RCtx:
    def __init__(self, P, TT_):
        self.P = P
        self.TT = TT_
        n = TT_
        self.xT = P.sb("xT", [128, DC * n], F32)
        self.Bx = P.bufs(DC, "x")
        self.hT = P.sb("hT", [128, DC * n], BF16)
        self.Bh = P.bufs(DC, "h")
        self.big = P.sb("big", [128, FC * n], BF16)
        self.Bbig = P.bufs(FC, "big")
        self.ws = [P.sb(f"ws{i}", [128, WSLOT], BF16) for i in range(NWS)]
        self.Bws = P.bufs(NWS, "ws")
        self.wsi = 0
        self.sq = [P.sb(f"sq{i}", [128, n], F32) for i in range(2)]
        self.Bsq = P.bufs(2, "sq")
        self.rstd = P.sb("rstd", [128, n], F32)
        self.Brstd = P.buf("rstd")
        self.tmp = [P.sb(f"tmp{i}", [128, n], F32) for i in range(6)]
        self.Btmp = P.bufs(6, "tmp")
        self.tmpi = 0
        self.yc32 = P.sb("yc32", [128, 4 * n], F32)
        self.Byc = P.bufs(4, "yc")
        self.ones = P.sb("ones", [128, 128], F32)
        self.Bones = P.buf("ones")
        self.ps = [P.ps(f"ps{i}", [128, 512]) for i in range(8)]
        self.Bps = P.bufs(8, "ps")
        for b_ in self.Bps:
            b_.excl = True
        MS(P, "dve", self.ones[:], 1.0, [self.Bones])
        self.sqi = 0

    def x(self, k):
        return self.xT[:, k * self.TT:(k + 1) * self.TT]

    def h(self, k):
        return self.hT[:, k * self.TT:(k + 1) * self.TT]

    def bg(self, k):
        return self.big[:, k * self.TT:(k + 1) * self.TT]

    def slot(self):
        s = self.wsi % NWS
        self.wsi += 1
        return s

    def tmpslot(self):
        s = self.tmpi % 6
        self.tmpi += 1
        return s


def r_rstd(C, srcs, Bsrcs, nelem, out_rstd, Bout, psb):
    P = C.P
    n = len(srcs)
    for k in range(n):
        s = C.sqi % 2
        C.sqi += 1
        ACT(P, C.sq[s][:], srcs[k], AF.Square, [Bsrcs[k]], [C.Bsq[s]])
        MM(P, C.ps[psb][:, :C.TT], C.ones[:], C.sq[s][:], k == 0, k == n - 1, [C.Bones, C.Bsq[s]], [C.Bps[psb]])
    ACT(P, out_rstd, C.ps[psb][:, :C.TT], AF.Sqrt, [C.Bps[psb]], [Bout], bias=EPS, scale=1.0 / nelem)
    P.op("dve", lambda e: e.reciprocal(out=out_rstd, in_=out_rstd), [Bout], [Bout])


def r_norm(C, g_sb, Bg):
    P = C.P
    r_rstd(C, [C.x(k) for k in range(DC)], C.Bx, D_MODEL, C.rstd[:], C.Brstd, 6)
    for k in range(DC):
        STT(P, C.h(k), C.x(k), g_sb[:, k:k + 1], C.rstd[:], ALU.mult, ALU.mult,
            [C.Bx[k], Bg, C.Brstd], [C.Bh[k]])


def wload(C, s, dram_ap, nk, ncols):
    P = C.P
    out = C.ws[s][:, 0:nk * ncols].rearrange("p (k c) -> p k c", k=nk)
    DMA(P, "pool", out, dram_ap.rearrange("(k p) c -> p k c", p=128), [], [C.Bws[s]], f"ws{s}")


def r_ffn(C, g_sb, Bg, wu, wd):
    P = C.P
    n = C.TT
    r_norm(C, g_sb, Bg)
    GW = 256
    groups = [(c0, min(GW, D_FF - c0)) for c0 in range(0, D_FF, GW)]

    def load_up(gi):
        c0, nc_ = groups[gi]
        sa, sb_ = C.slot(), C.slot()
        wload(C, sa, wu[:, c0:c0 + nc_], DC, nc_)
        wload(C, sb_, wu[:, D_FF + c0:D_FF + c0 + nc_], DC, nc_)
        return sa, sb_

    pend = [load_up(0)]
    cnt = 0
    for gi, (c0, nc_) in enumerate(groups):
        if gi + 1 < len(groups):
            pend.append(load_up(gi + 1))
        sa, sb_ = pend.pop(0)
        for sub in range(nc_ // 128):
            c = (c0 + sub * 128) // 128
            pa, pb = cnt % 2, 2 + cnt % 2
            cnt += 1
            for k in range(DC):
                MM(P, C.ps[pa][:, :n], C.ws[sa][:, k * nc_ + sub * 128:k * nc_ + sub * 128 + 128], C.h(k),
                   k == 0, k == DC - 1, [C.Bws[sa], C.Bh[k]], [C.Bps[pa]])
            for k in range(DC):
                MM(P, C.ps[pb][:, :n], C.ws[sb_][:, k * nc_ + sub * 128:k * nc_ + sub * 128 + 128], C.h(k),
                   k == 0, k == DC - 1, [C.Bws[sb_], C.Bh[k]], [C.Bps[pb]])
            t = C.tmpslot()
            ACT(P, C.tmp[t][:], C.ps[pa][:, :n], AF.Silu, [C.Bps[pa]], [C.Btmp[t]])
            TT(P, "dve", C.bg(c), C.tmp[t][:], C.ps[pb][:, :n], ALU.mult, [C.Btmp[t], C.Bps[pb]], [C.Bbig[c]])

    def load_dn(j):
        s = C.slot()
        wload(C, s, wd[:, j * 128:(j + 1) * 128], FC, 128)
        return s

    pend = [load_dn(0)]
    for j in range(DC):
        if j + 1 < DC:
            pend.append(load_dn(j + 1))
        s = pend.pop(0)
        pb = 4 + j % 2
        for c in range(FC):
            MM(P, C.ps[pb][:, :n], C.ws[s][:, c * 128:(c + 1) * 128], C.bg(c), c == 0, c == FC - 1,
               [C.Bws[s], C.Bbig[c]], [C.Bps[pb]])
        STT(P, C.x(j), C.ps[pb][:, :n], 0.5, C.x(j), ALU.mult, ALU.add, [C.Bps[pb], C.Bx[j]], [C.Bx[j]])


def r_merge(C, t0, yin, gm_sb, Bgm, sn_sb, Bsn, wg, pa_w, pb_w, pc_w, wo):
    P = C.P
    n = C.TT
    r_norm(C, gm_sb, Bgm)
    for k in range(8):
        DMA(P, "pool", C.bg(k), yin[k * 128:(k + 1) * 128, t0:t0 + n], [], [C.Bbig[k]], f"y{k}")
    for g in range(2):
        for q in range(4):
            r = 1024 + (g * 4 + q) * 128
            DMA(P, "sp", C.yc32[:, q * n:(q + 1) * n], yin[r:r + 128, t0:t0 + n], [], [C.Byc[q]], f"yc{q}")
        t = C.tmpslot()
        r_rstd(C, [C.yc32[:, q * n:(q + 1) * n] for q in range(4)], C.Byc, 512, C.tmp[t][:], C.Btmp[t], 7)
        for q in range(4):
            k = g * 4 + q
            STT(P, C.bg(8 + k), C.yc32[:, q * n:(q + 1) * n], sn_sb[:, k:k + 1], C.tmp[t][:], ALU.mult, ALU.mult,
                [C.Byc[q], Bsn, C.Btmp[t]], [C.Bbig[8 + k]])
    GW = 256

    def load_m(j2):
        s = [C.slot() for _ in range(4)]
        for i in range(3):
            wload(C, s[i], wg[:, i * D_MODEL + j2 * GW: i * D_MODEL + (j2 + 1) * GW], DC, GW)
        o = C.ws[s[3]]
        DMA(P, "pool", o[:, 0:4 * GW].rearrange("p (k c) -> p k c", k=4),
            pa_w[:, j2 * GW:(j2 + 1) * GW].rearrange("(k p) c -> p k c", p=128), [], [C.Bws[s[3]]], f"ws{s[3]}")
        DMA(P, "pool", o[:, 4 * GW:8 * GW].rearrange("p (k c) -> p k c", k=4),
            pb_w[:, j2 * GW:(j2 + 1) * GW].rearrange("(k p) c -> p k c", p=128), [], [C.Bws[s[3]]], f"ws{s[3]}")
        DMA(P, "pool", o[:, 8 * GW:16 * GW].rearrange("p (k c) -> p k c", k=8),
            pc_w[:, j2 * GW:(j2 + 1) * GW].rearrange("(k p) c -> p k c", p=128), [], [C.Bws[s[3]]], f"ws{s[3]}")
        return s

    for j2 in range(D_MODEL // GW):
        s = load_m(j2)
        for sub in range(GW // 128):
            j = j2 * 2 + sub
            tg = []
            for i in range(3):
                for k in range(DC):
                    MM(P, C.ps[i][:, :n], C.ws[s[i]][:, k * GW + sub * 128:k * GW + sub * 128 + 128], C.h(k),
                       k == 0, k == DC - 1, [C.Bws[s[i]], C.Bh[k]], [C.Bps[i]])
                t = C.tmpslot()
                tg.append(t)
                ACT(P, C.tmp[t][:], C.ps[i][:, :n], AF.Sigmoid, [C.Bps[i]], [C.Btmp[t]])
            o = C.ws[s[3]]
            for i, (base, nk, yoff) in enumerate(((0, 4, 0), (4 * GW, 4, 4), (8 * GW, 8, 8))):
                for k in range(nk):
                    MM(P, C.ps[3 + i][:, :n], o[:, base + k * GW + sub * 128: base + k * GW + sub * 128 + 128],
                       C.bg(yoff + k), k == 0, k == nk - 1, [C.Bws[s[3]], C.Bbig[yoff + k]], [C.Bps[3 + i]])
            for i in range(3):
                TT(P, "dve", C.tmp[tg[i]][:], C.tmp[tg[i]][:], C.ps[3 + i][:, :n], ALU.mult,
                   [C.Btmp[tg[i]], C.Bps[3 + i]], [C.Btmp[tg[i]]])
            TT(P, "pool", C.tmp[tg[0]][:], C.tmp[tg[0]][:], C.tmp[tg[1]][:], ALU.add,
               [C.Btmp[tg[0]], C.Btmp[tg[1]]], [C.Btmp[tg[0]]])
            TT(P, "dve", C.bg(16 + j), C.tmp[tg[0]][:], C.tmp[tg[2]][:], ALU.add,
               [C.Btmp[tg[0]], C.Btmp[tg[2]]], [C.Bbig[16 + j]])
    for j2 in range(D_MODEL // GW):
        s = C.slot()
        wload(C, s, wo[:, j2 * GW:(j2 + 1) * GW], DC, GW)
        for sub in range(2):
            j = j2 * 2 + sub
            pb = 6 + j % 2
            for k in range(DC):
                MM(P, C.ps[pb][:, :n], C.ws[s][:, k * GW + sub * 128:k * GW + sub * 128 + 128], C.bg(16 + k),
                   k == 0, k == DC - 1, [C.Bws[s], C.Bbig[16 + k]], [C.Bps[pb]])
            TT(P, "dve", C.x(j), C.x(j), C.ps[pb][:, :n], ALU.add, [C.Bx[j], C.Bps[pb]], [C.Bx[j]])


def build_R(mode, TC=2048, TT_=512):
    nc = bass.Bass("TRN2", target_bir_lowering=False)
    P = Prog(nc)

    def din(name, shape):
        return nc.dram_tensor(name, list(shape), F32, kind="ExternalInput").ap()

    xin = din("xin", [D_MODEL, TC])
    xo = nc.dram_tensor("xo", [D_MODEL, TC], F32, kind="ExternalOutput").ap()
    C = RCtx(P, TT_)
    gains = {}

    def gain(name, ncol=DC):
        a = din(name, [128, ncol])
        t = P.sb(name + "_sb", [128, ncol], F32)
        b = P.buf(name)
        DMA(P, "sp", t[:], a[:, :], [], [b], name)
        gains[name] = (t, b)

    if mode in ("RA", "R"):
        yin = din("yin", [D_MODEL, TC])
        gain("gm"); gain("sn", 8); gain("g2")
        wg = din("wg", [D_MODEL, 3 * D_MODEL])
        pa_w = din("pa", [512, D_MODEL]); pb_w = din("pb", [512, D_MODEL]); pc_w = din("pc", [1024, D_MODEL])
        wo = din("wo", [D_MODEL, D_MODEL])
        wu2 = din("wu2", [D_MODEL, 2 * D_FF]); wd2 = din("wd2", [D_FF, D_MODEL])
    if mode in ("A", "RA"):
        gain("g1")
        wu1 = din("wu1", [D_MODEL, 2 * D_FF]); wd1 = din("wd1", [D_FF, D_MODEL])
    n = TT_
    for ti in range(TC // n):
        t0 = ti * n
        DMA(P, "sp", C.xT[:, :].rearrange("p (k t) -> p k t", k=DC),
            xin[:, t0:t0 + n].rearrange("(k p) t -> p k t", p=128), [], C.Bx, "xin")
        if mode in ("RA", "R"):
            r_merge(C, t0, yin, gains["gm"][0], gains["gm"][1], gains["sn"][0], gains["sn"][1],
                    wg, pa_w, pb_w, pc_w, wo)
            r_ffn(C, gains["g2"][0], gains["g2"][1], wu2, wd2)
        if mode in ("A", "RA"):
            r_ffn(C, gains["g1"][0], gains["g1"][1], wu1, wd1)
        DMA(P, "sp", xo[:, t0:t0 + n].rearrange("(k p) t -> p k t", p=128),
            C.xT[:, :].rearrange("p (k t) -> p k t", k=DC), C.Bx, [], "xout")
    P.emit()
    return nc


NCH_M = 18 * 128 + 9
PJ_SMALL = 18 * 128
NEGBIG = -30000.0
DEBUG_CUT = 0


class Arena:
    def __init__(self, P, name, ncols):
        self.t = P.sb(name, [128, ncols], F32)
        self.n = ncols
        self.off = 0

    def reset(self):
        self.off = 0

    def f32(self, n):
        assert self.off + n <= self.n, (self.off, n, self.n)
        ap = self.t[:, self.off:self.off + n]
        self.off += n
        return ap

    def bf16(self, n):
        m = (n + 1) // 2
        assert self.off + m <= self.n, (self.off, m, self.n)
        ap = self.t[:, self.off:self.off + m].bitcast(BF16)[:, 0:n]
        self.off += m
        return ap


class MCtx:
    def __init__(self, P, nc, T):
        self.P = P
        self.nc = nc
        self.T = T
        self.A = Arena(P, "arena", 47000)
        self.ps = [P.ps(f"ps{i}", [128, 512]) for i in range(8)]
        self.Bps = P.bufs(8, "ps")
        for b_ in self.Bps:
            b_.excl = True
        self.ones = P.sb("ones", [128, 128], F32)
        self.Bones = P.buf("ones")
        self.ident = P.sb("ident_sb", [128, 128], F32)
        self.Bid = P.buf("ident")
        MS(P, "dve", self.ones[:], 1.0, [self.Bones])
        self.pj = nc.dram_tensor("pj", [19 * 128, T], F32).ap()
        self.Bpj = P.buf("pj")

    def din(self, name, shape):
        return self.nc.dram_tensor(name, list(shape), F32, kind="ExternalInput").ap()

    def const(self, name, shape, eng="sp"):
        a = self.din(name, shape)
        t = self.P.sb(name + "_sb", shape, F32)
        b = self.P.buf(name)
        DMA(self.P, eng, t[:], a, [], [b], name)
        return t, b


def m_inproj(M):
    P, T, A = M.P, M.T, M.A
    n = 512
    x1f = M.din("x1o", [D_MODEL, T])
    gm, Bgm = M.const("gmix", [128, DC])
    wm = M.din("wm", [D_MODEL, NCH_M])
    A.reset()
    wsb = A.bf16(DC * NCH_M)
    Bw = P.buf("wm")
    for k0 in range(0, DC, 4):
        DMA(P, "pool", wsb[:, k0 * NCH_M:(k0 + 4) * NCH_M].rearrange("p (k c) -> p k c", k=4),
            wm[k0 * 128:(k0 + 4) * 128, :].rearrange("(k p) c -> p k c", p=128), [], [Bw], "wm")
    xa = A.f32(DC * n)
    xb = A.f32(DC * n)
    Bxa4, Bxb4 = P.bufs(4, "xa"), P.bufs(4, "xb")
    hTs = [A.bf16(DC * n) for _ in range(2)]
    Bhs = [P.bufs(DC, f"h{i}") for i in range(2)]
    sq = [A.f32(n) for _ in range(2)]
    Bsq = P.bufs(2, "sq")
    rstd = A.f32(n)
    Brstd = P.buf("rstd")
    st = [A.f32(n) for _ in range(4)]
    Bst = P.bufs(4, "st")
    cnt = 0

    def load_x(ti):
        t0 = ti * n
        if DEBUG_CUT == 41 and ti > 0:
            return
        for q in range(4):
            DMA(P, "sp", xa[:, q * 4 * n:(q + 1) * 4 * n].rearrange("p (k t) -> p k t", k=4),
                x1f[q * 512:(q + 1) * 512, t0:t0 + n].rearrange("(k p) t -> p k t", p=128), [], [Bxa4[q]], f"xa{q}")

    NT_ = T // n

    def norm_x(ti):
        hT = hTs[ti % 2]
        Bh = Bhs[ti % 2]
        for k in range(DC):
            s = k % 2
            Bxa = Bxa4[k // 4]
            ACT(P, sq[s], xa[:, k * n:(k + 1) * n], AF.Square, [Bxa], [Bsq[s]])
            MM(P, M.ps[6][:, :n], M.ones[:], sq[s], k == 0, k == DC - 1, [M.Bones, Bsq[s]], [M.Bps[6]])
        ACT(P, rstd, M.ps[6][:, :n], AF.Sqrt, [M.Bps[6]], [Brstd], bias=EPS, scale=1.0 / D_MODEL)
        P.op("dve", lambda e: e.reciprocal(out=rstd, in_=rstd), [Brstd], [Brstd])
        for k in range(DC):
            STT(P, hT[:, k * n:(k + 1) * n], xa[:, k * n:(k + 1) * n], gm[:, k:k + 1], rstd, ALU.mult, ALU.mult,
                [Bxa4[k // 4], Bgm, Brstd], [Bh[k]])
        if ti + 1 < NT_:
            load_x(ti + 1)

    load_x(0)
    norm_x(0)
    for ti in range(NT_):
        t0 = ti * n
        hT = hTs[ti % 2]
        Bh = Bhs[ti % 2]
        for c in range(19):
            if c == 8 and ti + 1 < NT_:
                norm_x(ti + 1)
            cols = 128 if c < 18 else 9
            pb = cnt % 4
            s = cnt % 4
            cnt += 1
            for k in range(DC):
                MM(P, M.ps[pb][0:cols, :n], wsb[:, k * NCH_M + c * 128:k * NCH_M + c * 128 + cols],
                   hT[:, k * n:(k + 1) * n], k == 0, k == DC - 1, [Bw, Bh[k]], [M.Bps[pb]])
            CP(P, "act" if cnt % 2 else "dve", st[s][0:cols, :], M.ps[pb][0:cols, :n], [M.Bps[pb]], [Bst[s]])
            DMA(P, "sp", M.pj[c * 128:c * 128 + cols, t0:t0 + n], st[s][0:cols, :], [Bst[s]], [], f"pj{s}")
    P.barrier()


def conv_silu(P, out, raw, w4, bias, n, reads, writes, eng="dve"):
    if bias is None:
        TS(P, eng, out, raw[:, 3:3 + n], w4[:, 3:4], None, ALU.mult, None, reads, writes)
    else:
        TS(P, eng, out, raw[:, 3:3 + n], w4[:, 3:4], bias, ALU.mult, ALU.add, reads, writes)
    for k in range(3):
        STT(P, out, raw[:, k:k + n], w4[:, k:k + 1], out, ALU.mult, ALU.add, reads + writes, writes)
    ACT(P, out, out, AF.Silu, writes, writes)


def load_halo(P, dst, pj_rows, t0, n, reads, writes, key, eng="sp"):
    if t0 == 0:
        MS(P, "pool", dst[:, 0:3], 0.0, writes)
        DMA(P, eng, dst[:, 3:3 + n], pj_rows[:, 0:n], reads, writes, key)
    else:
        DMA(P, eng, dst[:, 0:3 + n], pj_rows[:, t0 - 3:t0 + n], reads, writes, key)


def m_ssd(M):
    P, T, A = M.P, M.T, M.A
    n = 512
    sconv, Bsc = M.const("sconv", [128, 16])
    sbias, Bsb = M.const("sbias", [128, 4])
    sdtb, Bdtb = M.const("sdtb", [1, 4])
    salog, Balog = M.const("salog", [1, 4])
    sD, BsD = M.const("sD", [128, 2])
    negtri, Bnt = M.const("negtriS", [128, 128])
    reset, Brs = M.const("reset128", [1, 2048])
    pj = M.pj
    A.reset()
    sT = A.f32(256)
    BsT = P.bufs(4, "sT")
    MS(P, "dve", sT, 0.0, BsT)
    negA = P.sb("negA", [1, 4], F32)
    BnA = P.buf("negA")
    ACT(P, negA[:], salog[:], AF.Exp, [Balog], [BnA])
    TS(P, "dve", negA[:], negA[:], -1.0, None, ALU.mult, None, [BnA], [BnA])
    raw = [[A.f32(n + 3) for _ in range(4)] for _ in range(2)]
    Braw = [P.bufs(4, f"raw{i}") for i in range(2)]
    zr = [[A.f32(n) for _ in range(2)] for _ in range(2)]
    Bzr = [P.bufs(2, f"zr{i}") for i in range(2)]
    dtr = [A.f32(4 * n) for _ in range(2)]
    Bdtr = P.bufs(2, "dtr")
    cv = [A.f32(n) for _ in range(4)]
    Bcv = P.bufs(4, "cv")
    dA = A.f32(4 * n)
    BdA = P.buf("dA")
    ac = A.f32(4 * n)
    Bac = P.buf("ac")
    yst = [A.f32(n) for _ in range(2)]
    Byst = P.bufs(2, "yst")
    tok = A.f32(384)
    Btok = P.buf("tok")
    NB = 4
    cl3 = [A.f32(3) for _ in range(NB)]
    Bcl = P.bufs(NB, "cl3")
    e1 = [A.f32(128) for _ in range(NB)]
    Be1 = P.bufs(NB, "e1")
    sg = [A.f32(128) for _ in range(NB)]
    Bsg = P.bufs(NB, "sg")
    sc = [A.f32(128) for _ in range(NB)]
    Bscb = P.bufs(NB, "sc")
    xdt = [A.f32(64) for _ in range(NB)]
    Bxdt = P.bufs(NB, "xdt")
    xdd = [A.f32(64) for _ in range(NB)]
    Bxdd = P.bufs(NB, "xdd")
    decl = [A.f32(1) for _ in range(NB)]
    Bdecl = P.bufs(NB, "decl")
    cdec = [A.f32(128) for _ in range(NB)]
    Bcdec = P.bufs(NB, "cdec")
    ps, Bps = M.ps, M.Bps
    chrow = (14, 15, 16, 17)
    nst = T // n

    def load(si):
        s = si % 2
        t0 = si * n
        for q in range(4):
            r = chrow[q] * 128
            load_halo(P, raw[s][q], pj[r:r + 128, :], t0, n, [M.Bpj], [Braw[s][q]], f"sraw{s}{q}")
        for q in range(2):
            r = (12 + q) * 128
            DMA(P, "sp", zr[s][q], pj[r:r + 128, t0:t0 + n], [M.Bpj], [Bzr[s][q]], f"sz{s}{q}")
        DMA(P, "sp", dtr[s][0:1, :].rearrange("o (r t) -> o r t", r=4),
            pj[PJ_SMALL + 5:PJ_SMALL + 9, t0:t0 + n].rearrange("(o r) t -> o r t", o=1), [M.Bpj], [Bdtr[s]], f"sdt{s}")

    load(0)
    it = 0
    for si in range(nst):
        s = si % 2
        t0 = si * n
        if si + 1 < nst:
            load(si + 1)
        for q in range(4):
            conv_silu(P, cv[q], raw[s][q], sconv[:, q * 4:q * 4 + 4], sbias[:, q:q + 1], n,
                      [Braw[s][q], Bsc, Bsb], [Bcv[q]])
        for q in range(2):
            ACT(P, zr[s][q], zr[s][q], AF.Silu, [Bzr[s][q]], [Bzr[s][q]])
        d = dtr[s]
        if DEBUG_CUT == 1:
            continue
        for p in range(4):
            ACT(P, d[0:1, p * n:(p + 1) * n], d[0:1, p * n:(p + 1) * n], AF.Exp, [Bdtr[s], Bdtb], [Bdtr[s]],
                bias=sdtb[0:1, p:p + 1])
        ACT(P, d[0:1, :], d[0:1, :], AF.Ln, [Bdtr[s]], [Bdtr[s]], bias=1.0)
        for p in range(4):
            TS(P, "dve", dA[0:1, p * n:(p + 1) * n], d[0:1, p * n:(p + 1) * n], negA[0:1, p:p + 1], None,
               ALU.mult, None, [Bdtr[s], BnA], [BdA])
        P.op("dve", lambda e: e.tensor_tensor_scan(out=ac[0:1, :], data0=reset[0:1, :], data1=dA[0:1, :],
                                                   initial=0.0, op0=ALU.mult, op1=ALU.add),
             [BdA, Brs], [Bac])
        if DEBUG_CUT == 2:
            continue
        for c in range(4):
            l0 = c * 128
            TR(P, ps[7][:, 0:128], cv[2][:, l0:l0 + 128], M.ident[:], [Bcv[2], M.Bid], [Bps[7]])
            TR(P, ps[7][:, 128:256], cv[0][:, l0:l0 + 128], M.ident[:], [Bcv[0], M.Bid], [Bps[7]])
            TR(P, ps[7][:, 256:384], cv[1][:, l0:l0 + 128], M.ident[:], [Bcv[1], M.Bid], [Bps[7]])
            CP(P, "act", tok, ps[7][:, 0:384], [Bps[7]], [Btok])
            MM(P, ps[6][:, 0:128], cv[2][:, l0:l0 + 128], cv[3][:, l0:l0 + 128], True, True,
               [Bcv[2], Bcv[3]], [Bps[6]])
            if DEBUG_CUT == 3:
                continue
            def head_ops(p):
                o = p * n + l0
                b = p
                pbk = 4 + p % 2
                c0 = (p // 2) * 256
                MM(P, ps[pbk][:, c0:c0 + 128], M.ones[0:1, 0:128], ac[0:1, o:o + 128], True, True,
                   [M.Bones, Bac], [Bps[pbk]])
                MM(P, ps[pbk][:, c0 + 128:c0 + 129], ac[0:1, o:o + 128], M.ones[0:1, 0:1], True, True,
                   [M.Bones, Bac], [Bps[pbk]])
                MM(P, ps[pbk][:, c0 + 129:c0 + 130], d[0:1, o:o + 128], M.ones[0:1, 0:1], True, True,
                   [M.Bones, Bdtr[s]], [Bps[pbk]])
                yield
                CP(P, "dve", cl3[b], ps[pbk][:, c0 + 127:c0 + 130], [Bps[pbk]], [Bcl[b]])
                ACT(P, e1[b], ps[pbk][:, c0:c0 + 128], AF.Exp, [Bps[pbk]], [Be1[b]])
                yield
                STT(P, sg[b], ps[pbk][:, c0:c0 + 128], cl3[b][:, 1:2], negtri[:], ALU.subtract, ALU.add,
                    [Bps[pbk], Bcl[b], Bnt], [Bsg[b]])
                TS(P, "pool", xdt[b], tok[:, 128 + p * 64:128 + (p + 1) * 64], cl3[b][:, 2:3], None, ALU.mult, None,
                   [Btok, Bcl[b]], [Bxdt[b]])
                ACT(P, decl[b], cl3[b][:, 1:2], AF.Exp, [Bcl[b]], [Bdecl[b]], bias=cl3[b][:, 0:1], scale=-1.0)
                yield
                ACT(P, sg[b], sg[b], AF.Exp, [Bsg[b]], [Bsg[b]])
                TS(P, "pool", xdd[b], xdt[b], decl[b][:, 0:1], None, ALU.mult, None, [Bxdt[b], Bdecl[b]], [Bxdd[b]])
                TT(P, "pool", cdec[b], cv[3][:, l0:l0 + 128], e1[b], ALU.mult, [Bcv[3], Be1[b]], [Bcdec[b]])
                yield
                TT(P, "dve", sc[b], ps[6][:, 0:128], sg[b], ALU.mult, [Bps[6], Bsg[b]], [Bscb[b]])
                yield
                pr = p // 2
                yo_ = ps[pr][(p % 2) * 64:(p % 2) * 64 + 64, 0:128]
                MM(P, yo_, xdt[b], sc[b], True, False, [Bxdt[b], Bscb[b]], [Bps[pr]])
                MM(P, yo_, sT[:, p * 64:(p + 1) * 64], cdec[b], False, True, [BsT[p], Bcdec[b]], [Bps[pr]])
                pd = 2 + p % 2
                MM(P, ps[pd][:, (p // 2) * 64:(p // 2) * 64 + 64], tok[:, 0:128], xdd[b], True, True, [Btok, Bxdd[b]], [Bps[pd]])
                yield
                STT(P, sT[:, p * 64:(p + 1) * 64], sT[:, p * 64:(p + 1) * 64], e1[b][:, 127:128], ps[pd][:, (p // 2) * 64:(p // 2) * 64 + 64],
                    ALU.mult, ALU.add, [BsT[p], Be1[b], Bps[pd]], [BsT[p]])

            gens = [head_ops(p) for p in range(4)]
            while gens:
                for g_ in list(gens):
                    try:
                        next(g_)
                    except StopIteration:
                        gens.remove(g_)
            for pr in range(2):
                STT(P, yst[pr][:, l0:l0 + 128], cv[pr][:, l0:l0 + 128], sD[:, pr:pr + 1], ps[pr][:, 0:128],
                    ALU.mult, ALU.add, [Bcv[pr], BsD, Bps[pr]], [Byst[pr]])
        for pr in range(2):
            TT(P, "pool", yst[pr], yst[pr], zr[s][pr], ALU.mult, [Byst[pr], Bzr[s][pr]], [Byst[pr]])
            DMA(P, "sp", M.yo[256 + pr * 128:256 + (pr + 1) * 128, t0:t0 + n], yst[pr], [Byst[pr]], [], f"yc{pr}")
    P.barrier()


def build_M(T=8192, stages=("ssd", "gdn", "nsa")):
    nc = bass.Bass("TRN2", target_bir_lowering=False)
    P = Prog(nc)
    M = MCtx(P, nc, T)
    idd = M.din("ident", [128, 128])
    DMA(P, "sp", M.ident[:], idd, [], [M.Bid], "ident")
    M.yo = nc.dram_tensor("yo", [512, T], F32, kind="ExternalOutput").ap()
    m_inproj(M)
    if "ssd" in stages:
        m_ssd(M)
    if "gdn" in stages:
        m_gdn(M)
    if "nsa" in stages:
        m_nsa(M)
    P.emit()
    return nc


def gl(g, ncol=DC):
    return np.ascontiguousarray(np.asarray(g, np.float32).reshape(ncol, 128).T)


def m_cols(h):
    g = h // 2
    cols = []
    for base in (0 + 128 * h, 512 + 128 * h, 1024 + 128 * h, 1536 + 128 * h,
                 2056 + 128 * h, 2056 + 128 * (h ^ 1), 2568 + 128 * g, 2824 + 128 * g,
                 3080 + 128 * g, 3336 + 128 * g, 3592 + 128 * g, 3848 + 128 * g,
                 4116 + 256 * h, 4116 + 256 * h + 128, 5140 + 256 * h, 5140 + 256 * h + 128,
                 5140 + 1024 + 128 * g, 5140 + 1280 + 128 * g):
        cols += list(range(base, base + 128))
    cols += [2048 + h, 2052 + h, 4104 + 3 * h, 4104 + 3 * h + 1, 4104 + 3 * h + 2]
    cols += [6676 + 4 * h + i for i in range(4)]
    return np.array(cols)


def m_consts(T):
    c = {}
    c["ident"] = np.eye(128, dtype=np.float32)
    p = np.arange(128)[:, None]
    f = np.arange(128)[None, :]
    c["negtriS"] = np.where(f < p, NEGBIG, 0.0).astype(np.float32)
    r = np.ones((1, 2048), np.float32)
    r[0, ::128] = 0.0
    c["reset128"] = r
    r = np.ones((1, 512), np.float32)
    r[0, ::64] = 0.0
    c["reset64"] = r
    p = np.arange(64)[:, None]
    f = np.arange(64)[None, :]
    c["gmaskU"] = np.tile(np.where(f < p, NEGBIG, 0.0).astype(np.float32), (1, 8))
    c["gmaskL"] = np.tile(np.where(f >= p, -NEGBIG, 0.0).astype(np.float32), (1, 8))
    c["gstrict"] = np.tile((f > p).astype(np.float32), (1, 8))
    c["gident8"] = np.tile(np.eye(64, dtype=np.float32), (1, 8))
    c.update(nsa_consts())
    return c


def m_layer_inputs(prm, l, h, T):
    g = h // 2
    d = {}
    d["gmix"] = gl(prm["g_mix"][l])
    d["wm"] = np.ascontiguousarray(prm["w_in"][l][:, m_cols(h)])
    cw = prm["ssm_conv_w"][l]
    cb = prm["ssm_conv_b"][l]
    chans = [256 * h + np.arange(128), 256 * h + 128 + np.arange(128), 1024 + 128 * g + np.arange(128),
             1280 + 128 * g + np.arange(128)]
    d["sconv"] = np.ascontiguousarray(np.concatenate([cw[:, ch].T for ch in chans], axis=1))
    d["sbias"] = np.ascontiguousarray(np.stack([cb[ch] for ch in chans], axis=1))
    d["sdtb"] = np.ascontiguousarray(prm["ssm_dt_bias"][l][4 * h:4 * h + 4][None, :])
    d["salog"] = np.ascontiguousarray(prm["ssm_a_log"][l][4 * h:4 * h + 4][None, :])
    dd = prm["ssm_d"][l][4 * h:4 * h + 4]
    d["sD"] = np.ascontiguousarray(np.stack([np.repeat(dd[0:2], 64), np.repeat(dd[2:4], 64)], axis=1))
    gcw = prm["gdn_conv"][l]
    d["gconv"] = np.ascontiguousarray(np.concatenate([gcw[:, q * 512 + 128 * h + np.arange(128)].T for q in range(3)], axis=1))
    d["gsc"] = np.array([[prm["gdn_a_log"][l][h], prm["gdn_dt_bias"][l][h]]], np.float32)
    d["gnorm"] = np.ascontiguousarray(prm["gdn_norm"][l][:, None])
    d["nqn"] = np.ascontiguousarray(prm["nsa_q_norm"][l][:, None])
    d["nkn"] = np.ascontiguousarray(prm["nsa_k_norm"][l][:, None])
    d["pekT"] = np.ascontiguousarray(prm["nsa_pe_k"][l].T)
    d["pevT"] = np.ascontiguousarray(prm["nsa_pe_v"][l].T)
    d["wck"] = np.ascontiguousarray(prm["nsa_w_ck"][l].transpose(1, 0, 2).reshape(128, 4096))
    d["wcv"] = np.ascontiguousarray(prm["nsa_w_cv"][l].transpose(1, 0, 2).reshape(128, 4096))
    rt = prm["rel_table"]
    d["ntab"] = np.ascontiguousarray(rt[:, [h, h ^ 1]])
    d["nt31"] = np.ascontiguousarray(np.tile(rt[31, [h, h ^ 1]][None, :], (128, 1)))
    return d


def m_gdn(M):
    P, T, A = M.P, M.T, M.A
    n = 512
    NCK = 8
    gconv, Bgc = M.const("gconv", [128, 12])
    gsc, Bgs = M.const("gsc", [1, 2])
    gnorm, Bgn = M.const("gnorm", [128, 1])
    mU, BmU = M.const("gmaskU", [64, 512])
    mL, BmL = M.const("gmaskL", [64, 512])
    mS, BmS = M.const("gstrict", [64, 512])
    reset, Brs = M.const("reset64", [1, 512])
    id8, Bid8 = M.const("gident8", [64, 512])
    pj = M.pj
    ps, Bps = M.ps, M.Bps
    A.reset()
    S = A.f32(128)
    BS = P.buf("S")
    MS(P, "dve", S, 0.0, [BS])
    negA = P.sb("gnegA", [1, 1], F32)
    BnA = P.buf("gnegA")
    ACT(P, negA[:], gsc[0:1, 0:1], AF.Exp, [Bgs], [BnA])
    TS(P, "dve", negA[:], negA[:], -1.0, None, ALU.mult, None, [BnA], [BnA])
    raw = [[A.f32(n + 3) for _ in range(3)] for _ in range(2)]
    Braw = [P.bufs(3, f"graw{i}") for i in range(2)]
    zr = [A.f32(n) for _ in range(2)]
    Bzr = P.bufs(2, "gz")
    abr = [A.f32(2 * n) for _ in range(2)]
    Bab = P.bufs(2, "gab")
    cv = [A.f32(n) for _ in range(3)]
    Bcv = P.bufs(3, "gcv")
    sq = A.f32(n)
    Bsq = P.buf("gsq")
    rs = A.f32(n)
    Brs_ = P.buf("grs")
    gcr = A.f32(n)
    Bgcr = P.buf("gcr")
    ktok = A.f32(NCK * 128)
    vtok = A.f32(NCK * 128)
    Bktok, Bvtok = P.buf("ktok"), P.buf("vtok")
    cols = A.f32(NCK * 4)
    Bcols = P.buf("gcols")
    ex = A.f32(NCK * 4)
    Bex = P.buf("gex")
    E1 = A.f32(n)
    BE1 = P.buf("gE1")
    dm = A.f32(n)
    Bdm = P.buf("gdm")
    dmT = A.f32(n)
    BdmT = P.buf("gdmT")
    X = [A.f32(n) for _ in range(2)]
    Xt = [A.f32(n) for _ in range(2)]
    BX = P.bufs(2, "gX")
    BXt = P.bufs(2, "gXt")
    Rm = A.f32(n)
    BR = P.buf("gR")
    attn = A.f32(n)
    Battn = P.buf("gattn")
    kb = A.f32(NCK * 128)
    vb = A.f32(NCK * 128)
    kdec = A.f32(NCK * 128)
    Bkb, Bvb, Bkdec = P.buf("kb"), P.buf("vb"), P.buf("kdec")
    qd = A.f32(n)
    Bqd = P.buf("qd")
    u = A.f32(NCK * 128)
    Bu = P.buf("gu")
    wT = A.f32(n)
    BwT = P.buf("gwT")
    vnew = [A.f32(128) for _ in range(2)]
    Bvn = P.bufs(2, "gvn")
    osb = A.f32(n)
    Bosb = P.buf("gosb")
    nst = T // n

    def load(si):
        s = si % 2
        t0 = si * n
        for q in range(3):
            load_halo(P, raw[s][q], pj[q * 128:(q + 1) * 128, :], t0, n, [M.Bpj], [Braw[s][q]], f"graw{s}{q}")
        DMA(P, "sp", zr[s], pj[3 * 128:4 * 128, t0:t0 + n], [M.Bpj], [Bzr[s]], f"gz{s}")
        DMA(P, "sp", abr[s][0:1, :].rearrange("o (r t) -> o r t", r=2),
            pj[PJ_SMALL:PJ_SMALL + 2, t0:t0 + n].rearrange("(o r) t -> o r t", o=1), [M.Bpj], [Bab[s]], f"gab{s}")

    load(0)
    for si in range(nst):
        s = si % 2
        t0 = si * n
        if si + 1 < nst:
            load(si + 1)
        for q in range(3):
            conv_silu(P, cv[q], raw[s][q], gconv[:, q * 4:q * 4 + 4], None, n, [Braw[s][q], Bgc], [Bcv[q]])
        ACT(P, zr[s], zr[s], AF.Silu, [Bzr[s]], [Bzr[s]])
        for q in range(2):
            ACT(P, sq, cv[q], AF.Square, [Bcv[q]], [Bsq])
            MM(P, ps[7][:, :n], M.ones[:], sq, True, True, [M.Bones, Bsq], [Bps[7]])
            ACT(P, rs, ps[7][:, :n], AF.Sqrt, [Bps[7]], [Brs_], bias=EPS)
            P.op("dve", lambda e: e.reciprocal(out=rs, in_=rs), [Brs_], [Brs_])
            if q == 0:
                STT(P, cv[q], cv[q], 128.0 ** -0.5, rs, ALU.mult, ALU.mult, [Bcv[q], Brs_], [Bcv[q]])
            else:
                TT(P, "dve", cv[q], cv[q], rs, ALU.mult, [Bcv[q], Brs_], [Bcv[q]])
        if DEBUG_CUT == 12:
            for q in range(3):
                DMA(P, "sp", M.yo[128 + q * 128:256 + q * 128, t0:t0 + n], cv[q], [Bcv[q]], [], f"dbg{q}")
        ar = abr[s][0:1, 0:n]
        br = abr[s][0:1, n:2 * n]
        ACT(P, ar, ar, AF.Exp, [Bab[s], Bgs], [Bab[s]], bias=gsc[0:1, 1:2])
        ACT(P, ar, ar, AF.Ln, [Bab[s]], [Bab[s]], bias=1.0)
        TS(P, "dve", ar, ar, negA[0:1, 0:1], None, ALU.mult, None, [Bab[s], BnA], [Bab[s]])
        ACT(P, br, br, AF.Sigmoid, [Bab[s]], [Bab[s]])
        P.op("dve", lambda e, ar=ar: e.tensor_tensor_scan(out=gcr[0:1, :], data0=reset[0:1, :], data1=ar,
                                                          initial=0.0, op0=ALU.mult, op1=ALU.add),
             [Bab[s], Brs], [Bgcr])
        MM(P, ps[7][:, :n], M.ones[0:1, 0:128], gcr[0:1, :], True, True, [M.Bones, Bgcr], [Bps[7]])
        ACT(P, E1, ps[7][:, :n], AF.Exp, [Bps[7]], [BE1])
        for i in range(NCK):
            c0 = i * 64
            MM(P, ps[6][0:64, 2 * i:2 * i + 1], gcr[0:1, c0:c0 + 64], M.ones[0:1, 0:1], True, True,
               [M.Bones, Bgcr], [Bps[6]])
            MM(P, ps[6][0:64, 2 * i + 1:2 * i + 2], br[:, c0:c0 + 64], M.ones[0:1, 0:1], True, True,
               [M.Bones, Bab[s]], [Bps[6]])
        cv3 = cols[0:64, :].rearrange("p (i c) -> p i c", c=4)
        CP(P, "dve", cv3[:, :, 0:2], ps[6][0:64, 0:2 * NCK].rearrange("p (i c) -> p i c", c=2), [Bps[6]], [Bcols])
        CP(P, "dve", cv3[:, :, 2:3], ps[7][0:64, :n].rearrange("p (i c) -> p i c", c=64)[:, :, 63:64],
           [Bps[7]], [Bcols])
        ex3 = ex[0:64, :].rearrange("p (i c) -> p i c", c=4)
        ACT(P, ex3[:, :, 0:1], cv3[:, :, 0:1], AF.Exp, [Bcols], [Bex])
        TT(P, "dve", ex3[:, :, 0:1], ex3[:, :, 0:1], cv3[:, :, 1:2], ALU.mult, [Bex, Bcols], [Bex])
        TT(P, "dve", ex3[:, :, 1:2], cv3[:, :, 2:3], cv3[:, :, 0:1], ALU.subtract, [Bcols], [Bex])
        ACT(P, ex3[:, :, 1:2], ex3[:, :, 1:2], AF.Exp, [Bex], [Bex])
        TS(P, "dve", ex3[:, :, 2:3], cv3[:, :, 1:2], -1.0, None, ALU.mult, None, [Bcols], [Bex])
        for i in range(NCK):
            c0 = i * 64
            bk = 0 + i // 4
            TR(P, ps[bk][0:64, (i % 4) * 128:(i % 4 + 1) * 128], cv[1][:, c0:c0 + 64], M.ident[:],
               [Bcv[1], M.Bid], [Bps[bk]])
        for i in range(NCK):
            c0 = i * 64
            bk = 2 + i // 4
            TR(P, ps[bk][0:64, (i % 4) * 128:(i % 4 + 1) * 128], cv[2][:, c0:c0 + 64], M.ident[:],
               [Bcv[2], M.Bid], [Bps[bk]])
        for hh in range(2):
            CP(P, "act", ktok[0:64, hh * 512:(hh + 1) * 512], ps[hh][0:64, :], [Bps[hh]], [Bktok])
            CP(P, "dve", vtok[0:64, hh * 512:(hh + 1) * 512], ps[2 + hh][0:64, :], [Bps[2 + hh]], [Bvtok])
        k3 = ktok[0:64, :].rearrange("p (i c) -> p i c", c=128)
        v3 = vtok[0:64, :].rearrange("p (i c) -> p i c", c=128)
        TT(P, "dve", kb[0:64, :].rearrange("p (i c) -> p i c", c=128), k3,
           ex3[:, :, 0:1].to_broadcast([64, NCK, 128]), ALU.mult, [Bktok, Bex], [Bkb])
        TT(P, "pool", vb[0:64, :].rearrange("p (i c) -> p i c", c=128), v3,
           cv3[:, :, 1:2].to_broadcast([64, NCK, 128]), ALU.mult, [Bvtok, Bcols], [Bvb])
        TT(P, "pool", kdec[0:64, :].rearrange("p (i c) -> p i c", c=128), k3,
           ex3[:, :, 1:2].to_broadcast([64, NCK, 128]), ALU.mult, [Bktok, Bex], [Bkdec])
        TT(P, "dve", attn[0:64, :].rearrange("p (i c) -> p i c", c=64),
           ps[7][0:64, :n].rearrange("p (i c) -> p i c", c=64),
           cv3[:, :, 0:1].to_broadcast([64, NCK, 64]), ALU.subtract, [Bps[7], Bcols], [Battn])
        TT(P, "dve", dm[0:64, :], attn[0:64, :], mU[:, :], ALU.add, [Battn, BmU], [Bdm])
        TT(P, "pool", dmT[0:64, :], attn[0:64, :], mL[:, :], ALU.add, [Battn, BmL], [BdmT])
        ACT(P, dm[0:64, :], dm[0:64, :], AF.Exp, [Bdm], [Bdm])
        ACT(P, dmT[0:64, :], dmT[0:64, :], AF.Exp, [BdmT], [BdmT], scale=-1.0)
        for i in range(NCK):
            sl = slice(i * 64, (i + 1) * 64)
            MM(P, ps[4][0:64, sl], cv[1][:, sl], cv[1][:, sl], True, True, [Bcv[1]], [Bps[4]])
        for i in range(NCK):
            sl = slice(i * 64, (i + 1) * 64)
            MM(P, ps[5][0:64, sl], cv[1][:, sl], cv[0][:, sl], True, True, [Bcv[1], Bcv[0]], [Bps[5]])
        MM(P, ps[6][0:64, :n], M.ones[0:1, 0:64], br, True, True, [M.Bones, Bab[s]], [Bps[6]])
        STT(P, X[0][0:64, :], ps[4][0:64, :n], -1.0, dm[0:64, :], ALU.mult, ALU.mult, [Bps[4], Bdm], [BX[0]])
        TT(P, "dve", X[0][0:64, :], X[0][0:64, :], ps[6][0:64, :n], ALU.mult, [BX[0], Bps[6]], [BX[0]])
        TT(P, "pool", X[0][0:64, :], X[0][0:64, :], mS[:, :], ALU.mult, [BX[0], BmS], [BX[0]])
        TT(P, "dve", Xt[0][0:64, :], ps[4][0:64, :n], dmT[0:64, :], ALU.mult, [Bps[4], BdmT], [BXt[0]])
        TT(P, "dve", Xt[0][0:64, :].rearrange("p (i c) -> p i c", c=64),
           Xt[0][0:64, :].rearrange("p (i c) -> p i c", c=64),
           ex3[:, :, 2:3].to_broadcast([64, NCK, 64]), ALU.mult, [BXt[0], Bex], [BXt[0]])
        TT(P, "dve", attn[0:64, :], ps[5][0:64, :n], dm[0:64, :], ALU.mult, [Bps[5], Bdm], [Battn])
        TT(P, "pool", Rm[0:64, :], X[0][0:64, :], id8[:, :], ALU.add, [BX[0], Bid8], [BR])
        cur = 0
        for j in range(5):
            nx = 1 - cur
            for i in range(NCK):
                sl = slice(i * 64, (i + 1) * 64)
                MM(P, ps[0][0:64, sl], Xt[cur][0:64, sl], X[cur][0:64, sl], True, True, [BXt[cur], BX[cur]], [Bps[0]])
            for i in range(NCK):
                sl = slice(i * 64, (i + 1) * 64)
                MM(P, ps[1][0:64, sl], X[cur][0:64, sl], Xt[cur][0:64, sl], True, True, [BXt[cur], BX[cur]], [Bps[1]])
            CP(P, "act", X[nx][0:64, :], ps[0][0:64, :n], [Bps[0]], [BX[nx]])
            CP(P, "dve", Xt[nx][0:64, :], ps[1][0:64, :n], [Bps[1]], [BXt[nx]])
            for i in range(NCK):
                sl = slice(i * 64, (i + 1) * 64)
                MM(P, ps[2][0:64, sl], Xt[nx][0:64, sl], Rm[0:64, sl], True, True, [BXt[nx], BR], [Bps[2]])
            TT(P, "dve", Rm[0:64, :], Rm[0:64, :], ps[2][0:64, :n], ALU.add, [BR, Bps[2]], [BR])
            cur = nx
        for i in range(NCK):
            sl = slice(i * 64, (i + 1) * 64)
            bk = i // 4
            MM(P, ps[bk][0:64, (i % 4) * 128:(i % 4 + 1) * 128], Rm[0:64, sl], vb[0:64, i * 128:(i + 1) * 128],
               True, True, [BR, Bvb], [Bps[bk]])
        for hh in range(2):
            CP(P, "act" if hh else "dve", u[0:64, hh * 512:(hh + 1) * 512], ps[hh][0:64, :], [Bps[hh]], [Bu])
        for i in range(NCK):
            sl = slice(i * 64, (i + 1) * 64)
            MM(P, ps[2][:, sl], kb[0:64, i * 128:(i + 1) * 128], Rm[0:64, sl], True, True, [Bkb, BR], [Bps[2]])
        CP(P, "act", wT, ps[2][:, :n], [Bps[2]], [BwT])
        TT(P, "pool", qd, cv[0], E1, ALU.mult, [Bcv[0], BE1], [Bqd])
        for i in range(NCK if DEBUG_CUT != 31 else 0):
            sl = slice(i * 64, (i + 1) * 64)
            s128 = slice(i * 128, (i + 1) * 128)
            vb_ = i % 2
            MM(P, ps[3][0:64, 0:128], wT[:, sl], S, True, True, [BwT, BS], [Bps[3]])
            TT(P, "dve", vnew[vb_][0:64, :], u[0:64, s128], ps[3][0:64, 0:128], ALU.subtract, [Bu, Bps[3]], [Bvn[vb_]])
            MM(P, ps[5][:, sl], S, qd[:, sl], True, False, [BS, Bqd], [Bps[5]])
            MM(P, ps[5][:, sl], vnew[vb_][0:64, :], attn[0:64, sl], False, True, [Bvn[vb_], Battn], [Bps[5]])
            MM(P, ps[4][:, 0:128], kdec[0:64, s128], vnew[vb_][0:64, :], True, True, [Bkdec, Bvn[vb_]], [Bps[4]])
            STT(P, S, S, E1[:, i * 64 + 63:i * 64 + 64], ps[4][:, 0:128], ALU.mult, ALU.add, [BS, BE1, Bps[4]], [BS])
        CP(P, "act", osb, ps[5][:, :n], [Bps[5]], [Bosb])
        ACT(P, sq, osb, AF.Square, [Bosb], [Bsq])
        MM(P, ps[7][:, :n], M.ones[:], sq, True, True, [M.Bones, Bsq], [Bps[7]])
        ACT(P, rs, ps[7][:, :n], AF.Sqrt, [Bps[7]], [Brs_], bias=EPS, scale=1.0 / 128)
        P.op("dve", lambda e: e.reciprocal(out=rs, in_=rs), [Brs_], [Brs_])
        STT(P, osb, osb, gnorm[:, 0:1], rs, ALU.mult, ALU.mult, [Bosb, Bgn, Brs_], [Bosb])
        TT(P, "dve", osb, osb, zr[s], ALU.mult, [Bosb, Bzr[s]], [Bosb])
        DMA(P, "sp", M.yo[0:128, t0:t0 + n], osb, [Bosb], [], "ya")
        if DEBUG_CUT == 11:
            P.barrier()
    P.barrier()


def MMA(P, out, lhsT, rhs, reads, writes):
    P.op("pe", lambda e: e.matmul(out, lhsT=lhsT, rhs=rhs, start=False, stop=False, skip_group_check=True),
         reads, writes, pe_acc=True)


PD_S = 2048
PD_C = 5120
W_S = 1792
W_W = 1408
W_C = 3072


def m_nsa(M):
    P, T, A, nc = M.P, M.T, M.A, M.nc
    n = 512
    NKT = T // 128
    NQ = T // n
    pj = M.pj
    ps, Bps = M.ps, M.Bps
    qn, Bqn = M.const("nqn", [128, 1])
    kn, Bkn = M.const("nkn", [128, 1])
    pekT, Bpek = M.const("pekT", [128, 32])
    pevT, Bpev = M.const("pevT", [128, 32])
    tabs, Btabs = M.const("ntab", [32, 2])
    t31, Bt31 = M.const("nt31", [128, 2])
    wck_d = M.din("wck", [128, 4096])
    wcv_d = M.din("wcv", [128, 4096])
    ohS = M.din("ohS", [32, PD_S]); ngS = M.din("ngS", [1, PD_S])
    ohW = M.din("ohW", [32, PD_S]); ngW = M.din("ngW", [1, PD_S])
    ohC = M.din("ohC", [32, PD_C]); ngC = M.din("ngC", [1, PD_C])
    E_d = M.din("Esel", [128, 8192])
    wimp_d = M.din("wimp", [128, 512])
    dS = nc.dram_tensor("dS", [129 * PD_S], F32)
    dW = nc.dram_tensor("dW", [129 * PD_S], F32)
    dCM = nc.dram_tensor("dCM", [129 * PD_C], F32)
    dCO = nc.dram_tensor("dCO", [129 * PD_C], F32)
    A.reset()
    BTs = A.f32(W_S); BTw = A.f32(W_W); BTcM = A.f32(W_C); BTcO = A.f32(W_C)
    BBTs, BBTw, BBTcM, BBTcO = P.buf("BTs"), P.buf("BTw"), P.buf("BTcM"), P.buf("BTcO")
    mark0 = A.off
    fb = A.f32(PD_C)
    Bfb = P.buf("fb")
    frow = A.f32(PD_C)
    Bfrow = P.buf("frow")
    oh = A.f32(PD_C)
    Boh = P.buf("oh")
    ngt = A.f32(PD_C)
    Bng = P.buf("ng")

    def strip(oh_d, ng_d, Pd, tcol, dram, dst, Bdst, W, step, base, key):
        DMA(P, "sp", oh[0:32, 0:Pd], oh_d, [], [Boh], "oh")
        DMA(P, "sp", ngt[0:1, 0:Pd], ng_d, [], [Bng], "ng")
        for b0 in range(0, Pd, 512):
            MM(P, ps[0][0:1, :], tabs[:, tcol:tcol + 1], oh[0:32, b0:b0 + 512], True, True, [Btabs, Boh], [Bps[0]])
            TT(P, "dve", frow[0:1, b0:b0 + 512], ps[0][0:1, :], ngt[0:1, b0:b0 + 512], ALU.add,
               [Bps[0], Bng], [Bfrow])
        for b0 in range(0, Pd, 512):
            pb = 1 + (b0 // 512) % 2
            MM(P, ps[pb][:, :], M.ones[0:1, 0:128], frow[0:1, b0:b0 + 512], True, True, [M.Bones, Bfrow], [Bps[pb]])
            CP(P, "act" if pb == 1 else "dve", fb[:, b0:b0 + 512], ps[pb][:, :], [Bps[pb]], [Bfb])
        Bd = P.buf("d" + key)
        dap = dram.ap()
        DMA(P, "sp", dap[0:128 * Pd].rearrange("(p c) -> p c", c=Pd), fb[:, 0:Pd], [Bfb], [Bd], "dw" + key)
        DMA(P, "sp", dap[128 * Pd:129 * Pd].rearrange("(p c) -> p c", c=Pd), fb[0:1, 0:Pd], [Bfb], [Bd], "dw" + key)
        src = bass.AP(tensor=dap.tensor, offset=base, ap=[[Pd - step, 128], [1, W]])
        DMA(P, "sp", dst, src, [Bd], [Bdst], "bt" + key)

    strip(ohS, ngS, PD_S, 0, dS, BTs, BBTs, W_S, 1, 127, "S")
    strip(ohW, ngW, PD_S, 0, dW, BTw, BBTw, W_W, 1, 127, "W")
    strip(ohC, ngC, PD_C, 0, dCM, BTcM, BBTcM, W_C, 16, 2032, "CM")
    strip(ohC, ngC, PD_C, 1, dCO, BTcO, BBTcO, W_C, 16, 2032, "CO")
    P.barrier()
    A.off = mark0
    ksT = A.bf16(T); kwT = A.bf16(T)
    Bks, Bkw = P.buf("ksT"), P.buf("kwT")
    Vs = A.bf16(NKT * 129); Vw = A.bf16(NKT * 129)
    BVs, BVw = P.buf("Vs"), P.buf("Vw")
    Eb = A.bf16(T)
    BE = P.buf("E")
    kcmpT = A.f32(512)
    Bkcmp = P.buf("kcmp")
    WV = A.f32(4 * 256)
    BWV = P.buf("WV")
    Wimp = A.f32(512)
    BWimp = P.buf("Wimp")
    mark = A.off
    DMA(P, "pool", Eb, E_d[:, 0:T], [], [BE], "E")
    DMA(P, "sp", Wimp, wimp_d, [], [BWimp], "wimp")
    DMA(P, "sp", WV.rearrange("p (c x) -> p c x", c=4)[:, :, 0:128], wimp_d.rearrange("p (c x) -> p c x", c=4),
        [], [BWV], "wv")
    MS(P, "pool", Vs.rearrange("p (k x) -> p k x", x=129)[:, :, 128:129], 1.0, [BVs])
    MS(P, "pool", Vw.rearrange("p (k x) -> p k x", x=129)[:, :, 128:129], 1.0, [BVw])
    kraw = [A.f32(n) for _ in range(2)]
    Bkraw = P.bufs(2, "kraw")
    sq = A.f32(n); Bsq = P.buf("nsq")
    rs = A.f32(n); Brs_ = P.buf("nrs")
    it = 0
    for (chunk, dstT, Bdst) in ((8, ksT, Bks), (10, kwT, Bkw)):
        for ti in range(NQ):
            s = it % 2
            it += 1
            t0 = ti * n
            DMA(P, "sp", kraw[s], pj[chunk * 128:(chunk + 1) * 128, t0:t0 + n], [M.Bpj], [Bkraw[s]], f"kraw{s}")
            ACT(P, sq, kraw[s], AF.Square, [Bkraw[s]], [Bsq])
            MM(P, ps[0][:, :n], M.ones[:], sq, True, True, [M.Bones, Bsq], [Bps[0]])
            ACT(P, rs, ps[0][:, :n], AF.Sqrt, [Bps[0]], [Brs_], bias=EPS, scale=1.0 / 128)
            P.op("dve", lambda e: e.reciprocal(out=rs, in_=rs), [Brs_], [Brs_])
            STT(P, dstT[:, t0:t0 + n], kraw[s], kn[:, 0:1], rs, ALU.mult, ALU.mult, [Bkraw[s], Bkn, Brs_], [Bdst])
    for (chunk, dstV, Bdst) in ((9, Vs, BVs), (11, Vw, BVw)):
        dv3 = dstV.rearrange("p (k x) -> p k x", x=129)
        for ti in range(NQ):
            s = it % 2
            it += 1
            t0 = ti * n
            DMA(P, "sp", kraw[s], pj[chunk * 128:(chunk + 1) * 128, t0:t0 + n], [M.Bpj], [Bkraw[s]], f"kraw{s}")
            pb = 1 + ti % 2
            for c in range(4):
                TR(P, ps[pb][:, c * 128:(c + 1) * 128], kraw[s][:, c * 128:(c + 1) * 128], M.ident[:],
                   [Bkraw[s], M.Bid], [Bps[pb]])
            CP(P, "act" if ti % 2 else "dve", dv3[:, ti * 4:ti * 4 + 4, 0:128],
               ps[pb][:, :].rearrange("p (c x) -> p c x", c=4), [Bps[pb]], [Bdst])
    wc = A.f32(4096)
    Bwc = P.buf("wc")
    kct = [A.f32(1040) for _ in range(2)]
    Bkct = P.bufs(2, "kct")
    cbias = A.f32(1)
    Bcb = P.buf("cbias")
    vcT = A.f32(512)
    BvcT = P.buf("vcT")
    MS(P, "dve", kcmpT, 0.0, [Bkcmp])
    MS(P, "dve", vcT, 0.0, [BvcT])
    for (chunk, w_d, peT, Bpe, dst, Bdst) in ((6, wck_d, pekT, Bpek, kcmpT, Bkcmp), (7, wcv_d, pevT, Bpev, vcT, BvcT)):
        DMA(P, "sp", wc, w_d, [], [Bwc], "wc")
        for l in range(32):
            MM(P, ps[0][:, 0:1], wc[:, l * 128:(l + 1) * 128], peT[:, l:l + 1], l == 0, l == 31, [Bwc, Bpe], [Bps[0]])
        CP(P, "dve", cbias, ps[0][:, 0:1], [Bps[0]], [Bcb])
        ntile = T // 1024
        for ti in range(ntile):
            s = it % 2
            it += 1
            t0 = ti * 1024
            ntok = min(1040, T - t0)
            nb = 64 if ntok == 1040 else 63
            DMA(P, "sp", kct[s][:, 0:ntok], pj[chunk * 128:(chunk + 1) * 128, t0:t0 + ntok], [M.Bpj], [Bkct[s]],
                f"kct{s}")
            pb = 1 + ti % 2
            for l in range(32):
                rhs = kct[s][:, l:l + 16 * (nb - 1) + 1:16]
                MM(P, ps[pb][:, 0:nb], wc[:, l * 128:(l + 1) * 128], rhs, l == 0, l == 31, [Bwc, Bkct[s]], [Bps[pb]])
            TS(P, "dve", dst[:, ti * 64:ti * 64 + nb], ps[pb][:, 0:nb], cbias[:, 0:1], None, ALU.add, None,
               [Bps[pb], Bcb], [Bdst])
    ACT(P, sq, kcmpT, AF.Square, [Bkcmp], [Bsq])
    MM(P, ps[0][:, :n], M.ones[:], sq, True, True, [M.Bones, Bsq], [Bps[0]])
    ACT(P, rs, ps[0][:, :n], AF.Sqrt, [Bps[0]], [Brs_], bias=EPS, scale=1.0 / 128)
    P.op("dve", lambda e: e.reciprocal(out=rs, in_=rs), [Brs_], [Brs_])
    STT(P, kcmpT, kcmpT, kn[:, 0:1], rs, ALU.mult, ALU.mult, [Bkcmp, Bkn, Brs_], [Bkcmp])
    for c in range(4):
        TR(P, ps[1][:, c * 128:(c + 1) * 128], vcT[:, c * 128:(c + 1) * 128], M.ident[:], [BvcT, M.Bid], [Bps[1]])
    CP(P, "dve", WV.rearrange("p (c x) -> p c x", c=4)[:, :, 128:256], ps[1][:, :].rearrange("p (c x) -> p c x", c=4),
       [Bps[1]], [BWV])
    P.barrier()
    A.off = mark
    qraw = [A.f32(n) for _ in range(2)]; Bqraw = P.bufs(2, "qraw")
    qM = A.f32(n); qO = A.f32(n); BqM, BqO = P.buf("qM"), P.buf("qO")
    qMb = A.bf16(n); BqMb = P.buf("qMb")
    glog = A.f32(n); Bglog = P.buf("glog")
    gq = A.f32(12); Bgq = P.buf("gq")
    tmpf = [A.f32(n) for _ in range(3)]; Btmpf = P.bufs(3, "ntmp")
    Pf = [A.f32(n) for _ in range(2)]; BPf = P.bufs(2, "Pf")
    Pb = [A.bf16(n) for _ in range(3)]; BPb = P.bufs(3, "Pb")
    negmT = A.bf16(n); BnegmT = P.buf("negmT")
    imp = A.f32(128); Bimp = P.buf("imp")
    imp2 = A.f32(128); Bimp2 = P.buf("imp2")
    scr = A.f32(128); Bscr = P.buf("scr")
    m8 = A.f32(16); Bm8 = P.buf("m8")
    sm = A.f32(16); Bsm = P.buf("sm")
    negm = A.f32(128); Bnegm = P.buf("negm")
    ocomb = [A.f32(128) for _ in range(4)]; Boc = P.bufs(4, "ocomb")
    ost = A.f32(n); Bost = P.buf("ost")
    ti_ = 0
    pi = 0
    for Q in range(NQ):
        if DEBUG_CUT == 21:
            break
        t0 = Q * n
        for hd, (chunk, dst, Bdst) in enumerate(((4, qM, BqM), (5, qO, BqO))):
            s = hd
            DMA(P, "sp", qraw[s], pj[chunk * 128:(chunk + 1) * 128, t0:t0 + n], [M.Bpj], [Bqraw[s]], f"qraw{s}")
            ACT(P, sq, qraw[s], AF.Square, [Bqraw[s]], [Bsq])
            MM(P, ps[7][:, :n], M.ones[:], sq, True, True, [M.Bones, Bsq], [Bps[7]])
            ACT(P, rs, ps[7][:, :n], AF.Sqrt, [Bps[7]], [Brs_], bias=128 * EPS, scale=1.0)
            P.op("dve", lambda e: e.reciprocal(out=rs, in_=rs), [Brs_], [Brs_])
            STT(P, dst, qraw[s], qn[:, 0:1], rs, ALU.mult, ALU.mult, [Bqraw[s], Bqn, Brs_], [Bdst])
        CP(P, "pool", qMb, qM, [BqM], [BqMb])
        DMA(P, "sp", glog[0:3, :], pj[PJ_SMALL + 2:PJ_SMALL + 5, t0:t0 + n], [M.Bpj], [Bglog], "glog")
        for sb in range(4):
            TR(P, ps[7][:, sb * 3:sb * 3 + 3], glog[0:3, sb * 128:(sb + 1) * 128], M.ident[0:3, 0:3],
               [Bglog, M.Bid], [Bps[7]])
        ACT(P, gq, ps[7][:, 0:12], AF.Sigmoid, [Bps[7]], [Bgq])
        for bk in (2, 3, 4):
            MS(P, "dve", ps[bk][:, :], 0.0, [Bps[bk]])
        def cmp_s1(hd, ct, qh, Bqh, BT, BBT):
            nonlocal pi, ti_
            Mq = Q - 4 * ct
            pb = pi % 2
            pi += 1
            MM(P, ps[pb][:, :n], kcmpT[:, ct * 128:(ct + 1) * 128], qh, True, True, [Bkcmp, Bqh], [Bps[pb]])
            f = ti_ % 2
            ti_ += 1
            if Mq <= 5:
                t = ti_ % 3
                TT(P, "dve", tmpf[t], ps[pb][:, :n], BT[:, 512 * Mq:512 * Mq + 512], ALU.add,
                   [Bps[pb], BBT], [Btmpf[t]])
                ACT(P, Pf[f], tmpf[t], AF.Exp, [Btmpf[t]], [BPf[f]])
            else:
                ACT(P, Pf[f], ps[pb][:, :n], AF.Exp, [Bps[pb], Bt31], [BPf[f]], bias=t31[:, hd:hd + 1])
            return f

        def cmp_s2(hd, ct, f):
            for sb in range(4):
                lhsT = Pf[f][:, sb * 128:(sb + 1) * 128]
                if hd == 0:
                    bk = 2 + sb // 2
                    MMA(P, ps[bk][:, (sb % 2) * 256:(sb % 2) * 256 + 256], lhsT, WV[:, ct * 256:(ct + 1) * 256],
                        [BPf[f], BWV], [Bps[bk]])
                else:
                    MMA(P, ps[4][:, sb * 128:(sb + 1) * 128], lhsT, Wimp[:, ct * 128:(ct + 1) * 128],
                        [BPf[f], BWimp], [Bps[4]])

        prev = None
        for hd, (qh, Bqh, BT, BBT) in enumerate(((qM, BqM, BTcM, BBTcM), (qO, BqO, BTcO, BBTcO))):
            for ct in range(Q // 4 + 1):
                f = cmp_s1(hd, ct, qh, Bqh, BT, BBT)
                if prev is not None:
                    cmp_s2(*prev)
                prev = (hd, ct, f)
        cmp_s2(*prev)
        for sb in range(4):
            qt = 4 * Q + sb
            bk = 2 + sb // 2
            aM = ps[bk][:, (sb % 2) * 256:(sb % 2) * 256 + 128]
            vM = ps[bk][:, (sb % 2) * 256 + 128:(sb % 2) * 256 + 256]
            aO = ps[4][:, sb * 128:(sb + 1) * 128]
            TS(P, "dve", scr, aM, 0.5, 0.0, ALU.mult, ALU.add, [Bps[bk]], [Bscr, Bsm], accum=sm[:, 0:1])
            TS(P, "dve", scr, aO, 0.5, 0.0, ALU.mult, ALU.add, [Bps[4]], [Bscr, Bsm], accum=sm[:, 1:2])
            TS(P, "dve", sm[:, 0:2], sm[:, 0:2], 1e-30, None, ALU.max, None, [Bsm], [Bsm])
            P.op("dve", lambda e: e.reciprocal(out=sm[:, 2:4], in_=sm[:, 0:2]), [Bsm], [Bsm])
            TS(P, "dve", imp, aM, sm[:, 2:3], None, ALU.mult, None, [Bps[bk], Bsm], [Bimp])
            STT(P, imp, aO, sm[:, 3:4], imp, ALU.mult, ALU.add, [Bps[4], Bsm, Bimp], [Bimp])
            TT(P, "dve", sm[:, 4:5], sm[:, 2:3], gq[:, sb * 3:sb * 3 + 1], ALU.mult, [Bsm, Bgq], [Bsm])
            TS(P, "dve", ocomb[sb], vM, sm[:, 4:5], None, ALU.mult, None, [Bps[bk], Bsm], [Boc[sb]])
            MS(P, "pool", imp[:, 0:1], 1e4, [Bimp])
            MS(P, "pool", imp[:, 2 * qt:2 * qt + 1], 1e4, [Bimp])
            if qt > 0:
                MS(P, "pool", imp[0:64, 2 * qt - 1:2 * qt], 1e4, [Bimp])
            MS(P, "pool", imp[64:128, 2 * qt + 1:2 * qt + 2], 1e4, [Bimp])
            P.op("dve", lambda e: e.max(out=m8[:, 0:8], in_=imp), [Bimp], [Bm8])
            P.op("dve", lambda e: e.match_replace(out=imp2, in_to_replace=m8[:, 0:8], in_values=imp, imm_value=-1e30),
                 [Bimp, Bm8], [Bimp2])
            P.op("dve", lambda e: e.max(out=m8[:, 8:16], in_=imp2), [Bimp2], [Bm8])
            TS(P, "dve", negm, imp, m8[:, 15:16], -32768.0, ALU.is_lt, ALU.mult, [Bimp, Bm8], [Bnegm])
            TR(P, ps[7][:, 128:256], negm, M.ident[:], [Bnegm, M.Bid], [Bps[7]])
            CP(P, "act", negmT[:, sb * 128:(sb + 1) * 128], ps[7][:, 128:256], [Bps[7]], [BnegmT])
        if DEBUG_CUT == 22:
            continue
        for br_, (kT, BkT, Vv, BVv, BT, BBT, lo, accb) in enumerate(
                ((ksT, Bks, Vs, BVs, BTs, BBTs, 0, (5, 6)), (kwT, Bkw, Vw, BVw, BTw, BBTw, max(0, 4 * Q - 4), (2, 3)))):
            for bk in accb:
                MS(P, "dve", ps[bk][:, :], 0.0, [Bps[bk]])
            def sw_s1(kt):
                nonlocal pi, ti_
                m = 4 * Q - kt
                pb = pi % 2
                pi += 1
                if br_ == 0:
                    MM(P, ps[pb][:, :n], kT[:, kt * 128:(kt + 1) * 128], qMb, True, False, [BkT, BqMb], [Bps[pb]])
                    MM(P, ps[pb][:, :n], Eb[:, kt * 128:(kt + 1) * 128], negmT, False, True, [BE, BnegmT], [Bps[pb]])
                else:
                    MM(P, ps[pb][:, :n], kT[:, kt * 128:(kt + 1) * 128], qMb, True, True, [BkT, BqMb], [Bps[pb]])
                f = ti_ % 3
                ti_ += 1
                if m <= 7:
                    t = ti_ % 3
                    TT(P, "dve", tmpf[t], ps[pb][:, :n], BT[:, 128 * (m + 3):128 * (m + 3) + 512], ALU.add,
                       [Bps[pb], BBT], [Btmpf[t]])
                    ACT(P, Pb[f], tmpf[t], AF.Exp, [Btmpf[t]], [BPb[f]])
                else:
                    ACT(P, Pb[f], ps[pb][:, :n], AF.Exp, [Bps[pb], Bt31], [BPb[f]], bias=t31[:, 0:1])
                return f

            def sw_s2(kt, f):
                for sb in range(4):
                    if 4 * Q + sb < kt:
                        continue
                    bk = accb[sb // 2]
                    MMA(P, ps[bk][:, (sb % 2) * 129:(sb % 2) * 129 + 129], Pb[f][:, sb * 128:(sb + 1) * 128],
                        Vv[:, kt * 129:(kt + 1) * 129], [BPb[f], BVv], [Bps[bk]])

            prev = None
            for kt in range(lo, 4 * Q + 4):
                f = sw_s1(kt)
                if prev is not None:
                    sw_s2(*prev)
                prev = (kt, f)
            sw_s2(*prev)
            for sb in range(4):
                bk = accb[sb // 2]
                acc = ps[bk][:, (sb % 2) * 129:(sb % 2) * 129 + 129]
                P.op("dve", lambda e, acc=acc: e.reciprocal(out=sm[:, 5:6], in_=acc[:, 128:129]), [Bps[bk]], [Bsm])
                TT(P, "dve", sm[:, 6:7], sm[:, 5:6], gq[:, sb * 3 + 1 + br_:sb * 3 + 2 + br_], ALU.mult, [Bsm, Bgq], [Bsm])
                STT(P, ocomb[sb], acc[:, 0:128], sm[:, 6:7], ocomb[sb], ALU.mult, ALU.add, [Bps[bk], Bsm, Boc[sb]], [Boc[sb]])
        for sb in range(4):
            TR(P, ps[7][:, 256:384], ocomb[sb], M.ident[:], [Boc[sb], M.Bid], [Bps[7]])
            CP(P, "act", ost[:, sb * 128:(sb + 1) * 128], ps[7][:, 256:384], [Bps[7]], [Bost])
        DMA(P, "sp", M.yo[128:256, t0:t0 + n], ost, [Bost], [], "yb")
    P.barrier()


def _bucket(dist):
    n = np.maximum(dist, 0)
    nf = np.maximum(n, 1).astype(np.float32)
    large = 16 + (np.log(nf / np.float32(16)) / np.float32(math.log(1024 / 16)) * np.float32(16)).astype(np.int32)
    large = np.minimum(large, 31)
    return np.where(n < 16, n, large)


_NSA_CONSTS = {}


def nsa_consts():
    if _NSA_CONSTS:
        return _NSA_CONSTS
    c = _NSA_CONSTS
    j = np.arange(PD_S)
    dist = j - 511
    bk = _bucket(dist)
    for nm, valid in (("S", dist >= 0), ("W", (dist >= 0) & (dist < 512))):
        oh = np.zeros((32, PD_S), np.float32)
        oh[bk[valid], j[valid]] = 1.0
        c["oh" + nm] = oh
        c["ng" + nm] = np.where(valid, 0.0, NEGBIG).astype(np.float32)[None, :]
    j = np.arange(PD_C)
    dist = j - 2063
    bk = _bucket(dist)
    valid = dist >= 0
    oh = np.zeros((32, PD_C), np.float32)
    oh[bk[valid], j[valid]] = 1.0
    c["ohC"] = oh
    c["ngC"] = np.where(valid, 0.0, NEGBIG).astype(np.float32)[None, :]
    E = np.zeros((128, 64, 128), np.float32)
    for kt in range(64):
        for k in range(128):
            E[2 * kt + k // 64, kt, k] = 1.0
    c["Esel"] = E.reshape(128, 8192)
    W = np.zeros((4, 128, 128), np.float32)
    wts = (1.0, 2.0, 2.0, 2.0, 1.0)
    for ct in range(4):
        for i in range(128):
            gi = ct * 128 + i
            for jj in range(128):
                w = gi - 4 * jj + 1
                if 0 <= w <= 4 and gi <= 510:
                    W[ct, i, jj] = wts[w]
    c["wimp"] = np.ascontiguousarray(W.transpose(1, 0, 2).reshape(128, 512))
    return c


_PROGS = {}


def _prog(kind):
    if kind not in _PROGS:
        _PROGS[kind] = build_M(8192) if kind == "M" else build_R(kind, 2048)
    return _PROGS[kind]


def _ffn_inputs(prm, l, which, suffix):
    return {"g" + suffix: gl(prm["g_ffn" + which][l]),
            "wu" + suffix: np.ascontiguousarray(prm["w_up" + which][l]),
            "wd" + suffix: np.ascontiguousarray(prm["w_down" + which][l])}


def kernel(**inputs):
    prm = {k: np.asarray(v, dtype=np.float32) for k, v in inputs.items()}
    x = prm.pop("x")
    B, T, D = x.shape
    NCORE = 8
    TC = T // 4
    cores = list(range(NCORE))

    def run(kind, maps):
        res = run_bass_kernel_spmd(_prog(kind), maps, core_ids=cores)
        return res.results

    xs = [np.ascontiguousarray(x[c // 4, (c % 4) * TC:(c % 4 + 1) * TC, :].T) for c in cores]
    shared = _ffn_inputs(prm, 0, "1", "1")
    outs = run("A", [dict(xin=xs[c], **shared) for c in cores])
    x1s = [o["xo"] for o in outs]
    consts = m_consts(T)
    selbs = [np.ascontiguousarray(np.tile(np.eye(2, dtype=np.float32)[c // 4][None, :], (128, 1))) for c in cores]
    n_layers = prm["g_mix"].shape[0]
    for l in range(n_layers):
        x1f = np.empty((2, D, T), np.float32)
        for c in cores:
            x1f[c // 4][:, (c % 4) * TC:(c % 4 + 1) * TC] = x1s[c]
        lay = [m_layer_inputs(prm, l, h, T) for h in range(4)]
        outs = run("M", [dict(x1o=x1f[c // 4], **consts, **lay[c % 4]) for c in cores])
        yfull = np.empty((2, D, T), np.float32)
        for c in cores:
            b, h = c // 4, c % 4
            yo = outs[c]["yo"]
            yfull[b][h * 128:(h + 1) * 128] = yo[0:128]
            yfull[b][512 + h * 128:512 + (h + 1) * 128] = yo[128:256]
            yfull[b][1024 + h * 256:1024 + (h + 1) * 256] = yo[256:512]
        shared = {"gm": gl(prm["g_mix"][l]), "sn": gl(prm["ssm_norm"][l], 8),
                  "wg": np.ascontiguousarray(prm["w_in"][l][:, 6692:]),
                  "pa": np.ascontiguousarray(prm["p_a"][l]), "pb": np.ascontiguousarray(prm["p_b"][l]),
                  "pc": np.ascontiguousarray(prm["p_c"][l]), "wo": np.ascontiguousarray(prm["w_o"][l])}
        shared.update(_ffn_inputs(prm, l, "2", "2"))
        last = l == n_layers - 1
        if not last:
            shared.update(_ffn_inputs(prm, l + 1, "1", "1"))
        maps = [dict(xin=x1s[c],
                     yin=np.ascontiguousarray(yfull[c // 4][:, (c % 4) * TC:(c % 4 + 1) * TC]), **shared)
                for c in cores]
        outs = run("R" if last else "RA", maps)
        x1s = [o["xo"] for o in outs]
    out = np.empty((B, T, D), np.float32)
    for c in cores:
        out[c // 4, (c % 4) * TC:(c % 4 + 1) * TC, :] = x1s[c].T
    return out
```

```python
import bisect
import contextlib
import math
import numpy as np
import concourse.bass as bass
import concourse.mybir as mybir
from concourse.bass_utils import run_bass_kernel_spmd

F32 = mybir.dt.float32
BF16 = mybir.dt.bfloat16
AF = mybir.ActivationFunctionType
ALU = mybir.AluOpType
AX = mybir.AxisListType


class Buf:
    __slots__ = ("name", "lw", "rd", "excl")

    def __init__(self, name):
        self.name = name
        self.excl = False
        self.lw = None
        self.rd = {}


class Prog:
    ENG = ("pe", "act", "dve", "pool", "sp")

    def __init__(self, nc):
        self.nc = nc
        self.stack = contextlib.ExitStack()
        self.q = {e: [] for e in self.ENG}
        self.cnt = {e: 0 for e in self.ENG}
        self.seen = {e: {} for e in self.ENG}
        self.dcount = {}
        self.waited = {e: set() for e in self.ENG}
        self.nbuf = 0

    def buf(self, name=None):
        self.nbuf += 1
        return Buf(name or f"b{self.nbuf}")

    def bufs(self, n, name=None):
        return [self.buf(f"{name}{i}") for i in range(n)]

    def sb(self, name, shape, dtype):
        return self.stack.enter_context(self.nc.sbuf_tensor(name, list(shape), dtype))

    def ps(self, name, shape, dtype=F32):
        return self.stack.enter_context(self.nc.psum_tensor(name, list(shape), dtype))

    def op(self, eng, fn, reads=(), writes=(), dma=None, pe_acc=False):
        deps = {}

        def add(ev):
            if ev is None:
                return
            k, v = ev
            if deps.get(k, 0) < v:
                deps[k] = v

        for b in reads:
            add(b.lw)
            if b.excl:
                for k, v in b.rd.items():
                    if k != eng:
                        add((k, v))
        for b in writes:
            if not (pe_acc and b.lw is not None and b.lw[0] == "pe"):
                add(b.lw)
            for k, v in b.rd.items():
                add((k, v))
        waits = []
        seen = self.seen[eng]
        for k, v in deps.items():
            if seen.get(k, 0) >= v:
                continue
            seen[k] = v
            waits.append((k, v))
            if k in self.waited:
                self.waited[k].add(v)
        if dma is None:
            self.cnt[eng] += 1
            ev = (eng, self.cnt[eng])
        else:
            key = dma if dma.startswith("c:") else "d:" + dma
            self.dcount[key] = self.dcount.get(key, 0) + 1
            ev = (key, self.dcount[key])
        for b in reads:
            if b.rd.get(ev[0], 0) < ev[1]:
                b.rd[ev[0]] = ev[1]
        for b in writes:
            b.lw = ev
            b.rd = {}
        self.q[eng].append((waits, fn, ev))
        return ev

    def barrier(self):
        evs = [(e, self.cnt[e]) for e in self.ENG if self.cnt[e] > 0]
        evs += [(k, c) for k, c in self.dcount.items()]
        for eng in self.ENG:
            waits = []
            seen = self.seen[eng]
            for k, v in evs:
                if seen.get(k, 0) >= v:
                    continue
                seen[k] = v
                waits.append((k, v))
                if k in self.waited:
                    self.waited[k].add(v)
            if waits:
                self.q[eng].append((waits, None, None))

    def emit(self):
        nc = self.nc
        self.barrier()
        sems = {e: self.stack.enter_context(nc.semaphore("s_" + e)) for e in self.ENG}
        for i, k in enumerate(sorted(self.dcount)):
            sems[k] = self.stack.enter_context(nc.semaphore(f"sd{i}"))
        miles = {e: sorted(self.waited[e]) for e in self.ENG}

        def val(k, v):
            if k in miles:
                return bisect.bisect_right(miles[k], v)
            return v if k.startswith("c:") else 16 * v

        wsets = {e: self.waited[e] for e in self.ENG}
        with nc.Block() as block:
            decs = {"pe": block.tensor, "act": block.scalar, "dve": block.vector,
                    "pool": block.gpsimd, "sp": block.sync}
            for e in self.ENG:
                items = self.q[e]

                def body(engobj, items=items, e=e):
                    for waits, fn, ev in items:
                        for k, v in waits:
                            engobj.wait_ge(sems[k], val(k, v))
                        if fn is None:
                            continue
                        ins = fn(engobj)
                        if ev[0] in miles:
                            if ev[1] in wsets[ev[0]]:
                                ins.then_inc(sems[ev[0]], 1)
                        else:
                            ins.then_inc(sems[ev[0]], 1 if ev[0].startswith("c:") else 16)
                decs[e](body)
        self.stack.close()


def MM(P, out, lhsT, rhs, start, stop, reads, writes):
    P.op("pe", lambda e: e.matmul(out, lhsT=lhsT, rhs=rhs, start=start, stop=stop),
         reads, writes, pe_acc=not start)


def TR(P, out, in_, ident, reads, writes):
    P.op("pe", lambda e: e.transpose(out, in_, ident), reads, writes)


def ACT(P, out, in_, func, reads, writes, bias=None, scale=None, accum=None):
    kw = {}
    if bias is not None:
        kw["bias"] = bias
    if scale is not None:
        kw["scale"] = scale
    if accum is not None:
        kw["accum_out"] = accum
    P.op("act", lambda e: e.activation(out=out, in_=in_, func=func, **kw), reads, writes)


def TT(P, eng, out, in0, in1, op, reads, writes):
    P.op(eng, lambda e: e.tensor_tensor(out=out, in0=in0, in1=in1, op=op), reads, writes)


def TS(P, eng, out, in0, s1, s2, op0, op1, reads, writes, accum=None):
    if op1 is None:
        P.op(eng, lambda e: e.tensor_scalar(out=out, in0=in0, scalar1=s1, scalar2=None, op0=op0), reads, writes)
    elif accum is None:
        P.op(eng, lambda e: e.tensor_scalar(out=out, in0=in0, scalar1=s1, scalar2=s2, op0=op0, op1=op1), reads, writes)
    else:
        P.op(eng, lambda e: e.tensor_scalar(out=out, in0=in0, scalar1=s1, scalar2=s2, op0=op0, op1=op1,
                                            accum_out=accum), reads, writes)


def STT(P, out, in0, scalar, in1, op0, op1, reads, writes):
    P.op("dve", lambda e: e.scalar_tensor_tensor(out=out, in0=in0, scalar=scalar, in1=in1, op0=op0, op1=op1),
         reads, writes)


def CP(P, eng, out, in_, reads, writes):
    if eng == "act":
        P.op("act", lambda e: e.copy(out=out, in_=in_), reads, writes)
    else:
        P.op(eng, lambda e: e.tensor_copy(out=out, in_=in_), reads, writes)


def MS(P, eng, ap, val, writes):
    P.op(eng, lambda e: e.memset(ap, val), (), writes)


def DMA(P, eng, out, in_, reads, writes, key):
    P.op(eng, lambda e: e.dma_start(out=out, in_=in_), reads, writes, dma=key)


D_MODEL = 2048
DC = 16
D_FF = 5504
FC = 43
EPS = 1e-6
WSLOT = 5504
NWS = 6


class RCtx:
    def __init__(self, P, TT_):
        self.P = P
        self.TT = TT_
        n = TT_
        self.xT = P.sb("xT", [128, DC * n], F32)
        self.Bx = P.bufs(DC, "x")
        self.hT = P.sb("hT", [128, DC * n], BF16)
        self.Bh = P.bufs(DC, "h")
        self.big = P.sb("big", [128, FC * n], BF16)
        self.Bbig = P.bufs(FC, "big")
        self.ws = [P.sb(f"ws{i}", [128, WSLOT], BF16) for i in range(NWS)]
        self.Bws = P.bufs(NWS, "ws")
        self.wsi = 0
        self.sq = [P.sb(f"sq{i}", [128, n], F32) for i in range(2)]
        self.Bsq = P.bufs(2, "sq")
        self.rstd = P.sb("rstd", [128, n], F32)
        self.Brstd = P.buf("rstd")
        self.tmp = [P.sb(f"tmp{i}", [128, n], F32) for i in range(6)]
        self.Btmp = P.bufs(6, "tmp")
        self.tmpi = 0
        self.yc32 = P.sb("yc32", [128, 4 * n], F32)
        self.Byc = P.bufs(4, "yc")
        self.ones = P.sb("ones", [128, 128], F32)
        self.Bones = P.buf("ones")
        self.ps = [P.ps(f"ps{i}", [128, 512]) for i in range(8)]
        self.Bps = P.bufs(8, "ps")
        for b_ in self.Bps:
            b_.excl = True
        MS(P, "dve", self.ones[:], 1.0, [self.Bones])
        self.sqi = 0

    def x(self, k):
        return self.xT[:, k * self.TT:(k + 1) * self.TT]

    def h(self, k):
        return self.hT[:, k * self.TT:(k + 1) * self.TT]

    def bg(self, k):
        return self.big[:, k * self.TT:(k + 1) * self.TT]

    def slot(self):
        s = self.wsi % NWS
        self.wsi += 1
        return s

    def tmpslot(self):
        s = self.tmpi % 6
        self.tmpi += 1
        return s


def r_rstd(C, srcs, Bsrcs, nelem, out_rstd, Bout, psb):
    P = C.P
    n = len(srcs)
    for k in range(n):
        s = C.sqi % 2
        C.sqi += 1
        ACT(P, C.sq[s][:], srcs[k], AF.Square, [Bsrcs[k]], [C.Bsq[s]])
        MM(P, C.ps[psb][:, :C.TT], C.ones[:], C.sq[s][:], k == 0, k == n - 1, [C.Bones, C.Bsq[s]], [C.Bps[psb]])
    ACT(P, out_rstd, C.ps[psb][:, :C.TT], AF.Sqrt, [C.Bps[psb]], [Bout], bias=EPS, scale=1.0 / nelem)
    P.op("dve", lambda e: e.reciprocal(out=out_rstd, in_=out_rstd), [Bout], [Bout])


def r_norm(C, g_sb, Bg):
    P = C.P
    r_rstd(C, [C.x(k) for k in range(DC)], C.Bx, D_MODEL, C.rstd[:], C.Brstd, 6)
    for k in range(DC):
        STT(P, C.h(k), C.x(k), g_sb[:, k:k + 1], C.rstd[:], ALU.mult, ALU.mult,
            [C.Bx[k], Bg, C.Brstd], [C.Bh[k]])


def wload(C, s, dram_ap, nk, ncols):
    P = C.P
    out = C.ws[s][:, 0:nk * ncols].rearrange("p (k c) -> p k c", k=nk)
    DMA(P, "pool", out, dram_ap.rearrange("(k p) c -> p k c", p=128), [], [C.Bws[s]], f"ws{s}")


def r_ffn(C, g_sb, Bg, wu, wd, store=None):
    P = C.P
    n = C.TT
    r_norm(C, g_sb, Bg)
    GW = 256
    groups = [(c0, min(GW, D_FF - c0)) for c0 in range(0, D_FF, GW)]

    def load_up(gi):
        c0, nc_ = groups[gi]
        sa, sb_ = C.slot(), C.slot()
        wload(C, sa, wu[:, c0:c0 + nc_], DC, nc_)
        wload(C, sb_, wu[:, D_FF + c0:D_FF + c0 + nc_], DC, nc_)
        return sa, sb_

    pend = [load_up(0)]
    cnt = 0
    for gi, (c0, nc_) in enumerate(groups):
        if gi + 1 < len(groups):
            pend.append(load_up(gi + 1))
        sa, sb_ = pend.pop(0)
        for sub in range(nc_ // 128):
            c = (c0 + sub * 128) // 128
            pa, pb = cnt % 2, 2 + cnt % 2
            cnt += 1
            for k in range(DC):
                MM(P, C.ps[pa][:, :n], C.ws[sa][:, k * nc_ + sub * 128:k * nc_ + sub * 128 + 128], C.h(k),
                   k == 0, k == DC - 1, [C.Bws[sa], C.Bh[k]], [C.Bps[pa]])
            for k in range(DC):
                MM(P, C.ps[pb][:, :n], C.ws[sb_][:, k * nc_ + sub * 128:k * nc_ + sub * 128 + 128], C.h(k),
                   k == 0, k == DC - 1, [C.Bws[sb_], C.Bh[k]], [C.Bps[pb]])
            t = C.tmpslot()
            ACT(P, C.tmp[t][:], C.ps[pa][:, :n], AF.Silu, [C.Bps[pa]], [C.Btmp[t]])
            TT(P, "dve", C.bg(c), C.tmp[t][:], C.ps[pb][:, :n], ALU.mult, [C.Btmp[t], C.Bps[pb]], [C.Bbig[c]])

    def load_dn(j):
        s = C.slot()
        wload(C, s, wd[:, j * 128:(j + 1) * 128], FC, 128)
        return s

    pend = [load_dn(0)]
    for j in range(DC):
        if j + 1 < DC:
            pend.append(load_dn(j + 1))
        s = pend.pop(0)
        pb = 4 + j % 2
        for c in range(FC):
            MM(P, C.ps[pb][:, :n], C.ws[s][:, c * 128:(c + 1) * 128], C.bg(c), c == 0, c == FC - 1,
               [C.Bws[s], C.Bbig[c]], [C.Bps[pb]])
        STT(P, C.x(j), C.ps[pb][:, :n], 0.5, C.x(j), ALU.mult, ALU.add, [C.Bps[pb], C.Bx[j]], [C.Bx[j]])
        if store is not None:
            DMA(P, "sp", store[0][j * 128:(j + 1) * 128, store[1]:store[1] + n], C.x(j), [C.Bx[j]], [], f"xout{j % 4}")


def r_merge(C, t0, yin, gm_sb, Bgm, sn_sb, Bsn, wg, pa_w, pb_w, pc_w, wo):
    P = C.P
    n = C.TT
    r_norm(C, gm_sb, Bgm)
    for k in range(8):
        DMA(P, "pool", C.bg(k), yin[k * 128:(k + 1) * 128, t0:t0 + n], [], [C.Bbig[k]], f"y{k}")
    for g in range(2):
        for q in range(4):
            r = 1024 + (g * 4 + q) * 128
            DMA(P, "sp", C.yc32[:, q * n:(q + 1) * n], yin[r:r + 128, t0:t0 + n], [], [C.Byc[q]], f"yc{q}")
        t = C.tmpslot()
        r_rstd(C, [C.yc32[:, q * n:(q + 1) * n] for q in range(4)], C.Byc, 512, C.tmp[t][:], C.Btmp[t], 7)
        for q in range(4):
            k = g * 4 + q
            STT(P, C.bg(8 + k), C.yc32[:, q * n:(q + 1) * n], sn_sb[:, k:k + 1], C.tmp[t][:], ALU.mult, ALU.mult,
                [C.Byc[q], Bsn, C.Btmp[t]], [C.Bbig[8 + k]])
    GW = 256

    def load_m(j2):
        s = [C.slot() for _ in range(4)]
        for i in range(3):
            wload(C, s[i], wg[:, i * D_MODEL + j2 * GW: i * D_MODEL + (j2 + 1) * GW], DC, GW)
        o = C.ws[s[3]]
        DMA(P, "pool", o[:, 0:4 * GW].rearrange("p (k c) -> p k c", k=4),
            pa_w[:, j2 * GW:(j2 + 1) * GW].rearrange("(k p) c -> p k c", p=128), [], [C.Bws[s[3]]], f"ws{s[3]}")
        DMA(P, "pool", o[:, 4 * GW:8 * GW].rearrange("p (k c) -> p k c", k=4),
            pb_w[:, j2 * GW:(j2 + 1) * GW].rearrange("(k p) c -> p k c", p=128), [], [C.Bws[s[3]]], f"ws{s[3]}")
        DMA(P, "pool", o[:, 8 * GW:16 * GW].rearrange("p (k c) -> p k c", k=8),
            pc_w[:, j2 * GW:(j2 + 1) * GW].rearrange("(k p) c -> p k c", p=128), [], [C.Bws[s[3]]], f"ws{s[3]}")
        return s

    for j2 in range(D_MODEL // GW):
        s = load_m(j2)
        for sub in range(GW // 128):
            j = j2 * 2 + sub
            tg = []
            for i in range(3):
                for k in range(DC):
                    MM(P, C.ps[i][:, :n], C.ws[s[i]][:, k * GW + sub * 128:k * GW + sub * 128 + 128], C.h(k),
                       k == 0, k == DC - 1, [C.Bws[s[i]], C.Bh[k]], [C.Bps[i]])
                t = C.tmpslot()
                tg.append(t)
                ACT(P, C.tmp[t][:], C.ps[i][:, :n], AF.Sigmoid, [C.Bps[i]], [C.Btmp[t]])
            o = C.ws[s[3]]
            for i, (base, nk, yoff) in enumerate(((0, 4, 0), (4 * GW, 4, 4), (8 * GW, 8, 8))):
                for k in range(nk):
                    MM(P, C.ps[3 + i][:, :n], o[:, base + k * GW + sub * 128: base + k * GW + sub * 128 + 128],
                       C.bg(yoff + k), k == 0, k == nk - 1, [C.Bws[s[3]], C.Bbig[yoff + k]], [C.Bps[3 + i]])
            for i in range(3):
                TT(P, "dve", C.tmp[tg[i]][:], C.tmp[tg[i]][:], C.ps[3 + i][:, :n], ALU.mult,
                   [C.Btmp[tg[i]], C.Bps[3 + i]], [C.Btmp[tg[i]]])
            TT(P, "pool", C.tmp[tg[0]][:], C.tmp[tg[0]][:], C.tmp[tg[1]][:], ALU.add,
               [C.Btmp[tg[0]], C.Btmp[tg[1]]], [C.Btmp[tg[0]]])
            TT(P, "dve", C.bg(16 + j), C.tmp[tg[0]][:], C.tmp[tg[2]][:], ALU.add,
               [C.Btmp[tg[0]], C.Btmp[tg[2]]], [C.Bbig[16 + j]])
    for j2 in range(D_MODEL // GW):
        s = C.slot()
        wload(C, s, wo[:, j2 * GW:(j2 + 1) * GW], DC, GW)
        for sub in range(2):
            j = j2 * 2 + sub
            pb = 6 + j % 2
            for k in range(DC):
                MM(P, C.ps[pb][:, :n], C.ws[s][:, k * GW + sub * 128:k * GW + sub * 128 + 128], C.bg(16 + k),
                   k == 0, k == DC - 1, [C.Bws[s], C.Bbig[16 + k]], [C.Bps[pb]])
            TT(P, "dve", C.x(j), C.x(j), C.ps[pb][:, :n], ALU.add, [C.Bx[j], C.Bps[pb]], [C.Bx[j]])


def build_R(mode, TC=2048, TT_=512):
    nc = bass.Bass("TRN2", target_bir_lowering=False)
    P = Prog(nc)

    def din(name, shape):
        return nc.dram_tensor(name, list(shape), F32, kind="ExternalInput").ap()

    xin = din("xin", [D_MODEL, TC])
    xo = nc.dram_tensor("xo", [D_MODEL, TC], F32, kind="ExternalOutput").ap()
    C = RCtx(P, TT_)
    gains = {}

    def gain(name, ncol=DC):
        a = din(name, [128, ncol])
        t = P.sb(name + "_sb", [128, ncol], F32)
        b = P.buf(name)
        DMA(P, "sp", t[:], a[:, :], [], [b], name)
        gains[name] = (t, b)

    if mode in ("RA", "R"):
        yin = din("yin", [D_MODEL, TC])
        gain("gm"); gain("sn", 8); gain("g2")
        wg = din("wg", [D_MODEL, 3 * D_MODEL])
        pa_w = din("pa", [512, D_MODEL]); pb_w = din("pb", [512, D_MODEL]); pc_w = din("pc", [1024, D_MODEL])
        wo = din("wo", [D_MODEL, D_MODEL])
        wu2 = din("wu2", [D_MODEL, 2 * D_FF]); wd2 = din("wd2", [D_FF, D_MODEL])
    if mode in ("A", "RA"):
        gain("g1")
        wu1 = din("wu1", [D_MODEL, 2 * D_FF]); wd1 = din("wd1", [D_FF, D_MODEL])
    n = TT_
    for ti in range(TC // n):
        t0 = ti * n
        for q in range(4):
            DMA(P, "sp", C.xT[:, q * 4 * n:(q + 1) * 4 * n].rearrange("p (k t) -> p k t", k=4),
                xin[q * 512:(q + 1) * 512, t0:t0 + n].rearrange("(k p) t -> p k t", p=128), [],
                C.Bx[q * 4:(q + 1) * 4], f"xin{q}")
        if mode in ("RA", "R"):
            r_merge(C, t0, yin, gains["gm"][0], gains["gm"][1], gains["sn"][0], gains["sn"][1],
                    wg, pa_w, pb_w, pc_w, wo)
            r_ffn(C, gains["g2"][0], gains["g2"][1], wu2, wd2, store=(xo, t0) if mode == "R" else None)
        if mode in ("A", "RA"):
            r_ffn(C, gains["g1"][0], gains["g1"][1], wu1, wd1, store=(xo, t0))
    P.emit()
    return nc


NCH_M = 18 * 128 + 9
PJ_SMALL = 18 * 128
NEGBIG = -30000.0
DEBUG_CUT = 0


class Arena:
    def __init__(self, P, name, ncols):
        self.t = P.sb(name, [128, ncols], F32)
        self.n = ncols
        self.off = 0

    def reset(self):
        self.off = 0

    def f32(self, n):
        assert self.off + n <= self.n, (self.off, n, self.n)
        ap = self.t[:, self.off:self.off + n]
        self.off += n
        return ap

    def bf16(self, n):
        m = (n + 1) // 2
        assert self.off + m <= self.n, (self.off, m, self.n)
        ap = self.t[:, self.off:self.off + m].bitcast(BF16)[:, 0:n]
        self.off += m
        return ap


class MCtx:
    def __init__(self, P, nc, T):
        self.P = P
        self.nc = nc
        self.T = T
        self.A = Arena(P, "arena", 47000)
        self.ps = [P.ps(f"ps{i}", [128, 512]) for i in range(8)]
        self.Bps = P.bufs(8, "ps")
        for b_ in self.Bps:
            b_.excl = True
        self.ones = P.sb("ones", [128, 128], F32)
        self.Bones = P.buf("ones")
        self.ident = P.sb("ident_sb", [128, 128], F32)
        self.Bid = P.buf("ident")
        MS(P, "dve", self.ones[:], 1.0, [self.Bones])
        self.pj = nc.dram_tensor("pj", [19 * 128, T], F32).ap()
        self.Bpj = P.buf("pj")

    def din(self, name, shape):
        return self.nc.dram_tensor(name, list(shape), F32, kind="ExternalInput").ap()

    def const(self, name, shape, eng="sp"):
        a = self.din(name, shape)
        t = self.P.sb(name + "_sb", shape, F32)
        b = self.P.buf(name)
        DMA(self.P, eng, t[:], a, [], [b], name)
        return t, b


def m_inproj(M):
    P, T, A = M.P, M.T, M.A
    n = 512
    x1f = M.din("x1o", [D_MODEL, T])
    gm, Bgm = M.const("gmix", [128, DC])
    wm = M.din("wm", [D_MODEL, NCH_M])
    A.reset()
    wsb = A.bf16(DC * NCH_M)
    Bw = P.buf("wm")
    for k0 in range(0, DC, 4):
        DMA(P, "pool", wsb[:, k0 * NCH_M:(k0 + 4) * NCH_M].rearrange("p (k c) -> p k c", k=4),
            wm[k0 * 128:(k0 + 4) * 128, :].rearrange("(k p) c -> p k c", p=128), [], [Bw], "wm")
    xa = A.f32(DC * n)
    xb = A.f32(DC * n)
    Bxa4, Bxb4 = P.bufs(4, "xa"), P.bufs(4, "xb")
    hTs = [A.bf16(DC * n) for _ in range(2)]
    Bhs = [P.bufs(DC, f"h{i}") for i in range(2)]
    sq = [A.f32(n) for _ in range(2)]
    Bsq = P.bufs(2, "sq")
    rstd = A.f32(n)
    Brstd = P.buf("rstd")
    st = [A.f32(n) for _ in range(4)]
    Bst = P.bufs(4, "st")
    cnt = 0

    def load_x(ti):
        t0 = ti * n
        if DEBUG_CUT == 41 and ti > 0:
            return
        for q in range(4):
            DMA(P, "sp", xa[:, q * 4 * n:(q + 1) * 4 * n].rearrange("p (k t) -> p k t", k=4),
                x1f[q * 512:(q + 1) * 512, t0:t0 + n].rearrange("(k p) t -> p k t", p=128), [], [Bxa4[q]], f"xa{q}")

    NT_ = T // n

    def norm_x(ti):
        hT = hTs[ti % 2]
        Bh = Bhs[ti % 2]
        for k in range(DC):
            s = k % 2
            Bxa = Bxa4[k // 4]
            ACT(P, sq[s], xa[:, k * n:(k + 1) * n], AF.Square, [Bxa], [Bsq[s]])
            MM(P, M.ps[6][:, :n], M.ones[:], sq[s], k == 0, k == DC - 1, [M.Bones, Bsq[s]], [M.Bps[6]])
        ACT(P, rstd, M.ps[6][:, :n], AF.Sqrt, [M.Bps[6]], [Brstd], bias=EPS, scale=1.0 / D_MODEL)
        P.op("dve", lambda e: e.reciprocal(out=rstd, in_=rstd), [Brstd], [Brstd])
        for k in range(DC):
            STT(P, hT[:, k * n:(k + 1) * n], xa[:, k * n:(k + 1) * n], gm[:, k:k + 1], rstd, ALU.mult, ALU.mult,
                [Bxa4[k // 4], Bgm, Brstd], [Bh[k]])
        if ti + 1 < NT_:
            load_x(ti + 1)

    load_x(0)
    norm_x(0)
    for ti in range(NT_):
        t0 = ti * n
        hT = hTs[ti % 2]
        Bh = Bhs[ti % 2]
        for c in range(19):
            if c == 8 and ti + 1 < NT_:
                norm_x(ti + 1)
            cols = 128 if c < 18 else 9
            pb = cnt % 4
            s = cnt % 4
            cnt += 1
            for k in range(DC):
                MM(P, M.ps[pb][0:cols, :n], wsb[:, k * NCH_M + c * 128:k * NCH_M + c * 128 + cols],
                   hT[:, k * n:(k + 1) * n], k == 0, k == DC - 1, [Bw, Bh[k]], [M.Bps[pb]])
            CP(P, "act" if cnt % 2 else "dve", st[s][0:cols, :], M.ps[pb][0:cols, :n], [M.Bps[pb]], [Bst[s]])
            DMA(P, "sp", M.pj[c * 128:c * 128 + cols, t0:t0 + n], st[s][0:cols, :], [Bst[s]], [], f"pj{s}")
    P.barrier()


def conv_silu(P, out, raw, w4, bias, n, reads, writes, eng="dve"):
    if bias is None:
        TS(P, eng, out, raw[:, 3:3 + n], w4[:, 3:4], None, ALU.mult, None, reads, writes)
    else:
        TS(P, eng, out, raw[:, 3:3 + n], w4[:, 3:4], bias, ALU.mult, ALU.add, reads, writes)
    for k in range(3):
        STT(P, out, raw[:, k:k + n], w4[:, k:k + 1], out, ALU.mult, ALU.add, reads + writes, writes)
    ACT(P, out, out, AF.Silu, writes, writes)


def load_halo(P, dst, pj_rows, t0, n, reads, writes, key, eng="sp"):
    if t0 == 0:
        MS(P, "pool", dst[:, 0:3], 0.0, writes)
        DMA(P, eng, dst[:, 3:3 + n], pj_rows[:, 0:n], reads, writes, key)
    else:
        DMA(P, eng, dst[:, 0:3 + n], pj_rows[:, t0 - 3:t0 + n], reads, writes, key)


def m_ssd(M):
    P, T, A = M.P, M.T, M.A
    n = 512
    sconv, Bsc = M.const("sconv", [128, 16])
    sbias, Bsb = M.const("sbias", [128, 4])
    sdtb, Bdtb = M.const("sdtb", [1, 4])
    salog, Balog = M.const("salog", [1, 4])
    sD, BsD = M.const("sD", [128, 2])
    negtri, Bnt = M.const("negtriS", [128, 128])
    reset, Brs = M.const("reset128", [1, 2048])
    pj = M.pj
    A.reset()
    sT = A.f32(256)
    BsT = P.bufs(4, "sT")
    MS(P, "dve", sT, 0.0, BsT)
    negA = P.sb("negA", [1, 4], F32)
    BnA = P.buf("negA")
    ACT(P, negA[:], salog[:], AF.Exp, [Balog], [BnA])
    TS(P, "dve", negA[:], negA[:], -1.0, None, ALU.mult, None, [BnA], [BnA])
    raw = [[A.f32(n + 3) for _ in range(4)] for _ in range(2)]
    Braw = [P.bufs(4, f"raw{i}") for i in range(2)]
    zr = [[A.f32(n) for _ in range(2)] for _ in range(2)]
    Bzr = [P.bufs(2, f"zr{i}") for i in range(2)]
    dtr = [A.f32(4 * n) for _ in range(2)]
    Bdtr = P.bufs(2, "dtr")
    cv = [A.f32(n) for _ in range(4)]
    Bcv = P.bufs(4, "cv")
    dA = A.f32(4 * n)
    BdA = P.buf("dA")
    ac = A.f32(4 * n)
    Bac = P.buf("ac")
    yst = [A.f32(n) for _ in range(2)]
    Byst = P.bufs(2, "yst")
    tok = A.f32(384)
    Btok = P.buf("tok")
    NB = 4
    cl3 = [A.f32(3) for _ in range(NB)]
    Bcl = P.bufs(NB, "cl3")
    e1 = [A.f32(128) for _ in range(NB)]
    Be1 = P.bufs(NB, "e1")
    sg = [A.f32(128) for _ in range(NB)]
    Bsg = P.bufs(NB, "sg")
    sc = [A.f32(128) for _ in range(NB)]
    Bscb = P.bufs(NB, "sc")
    xdt = [A.f32(64) for _ in range(NB)]
    Bxdt = P.bufs(NB, "xdt")
    xdd = [A.f32(64) for _ in range(NB)]
    Bxdd = P.bufs(NB, "xdd")
    decl = [A.f32(1) for _ in range(NB)]
    Bdecl = P.bufs(NB, "decl")
    cdec = [A.f32(128) for _ in range(NB)]
    Bcdec = P.bufs(NB, "cdec")
    ps, Bps = M.ps, M.Bps
    chrow = (14, 15, 16, 17)
    nst = T // n

    def load(si):
        s = si % 2
        t0 = si * n
        for q in range(4):
            r = chrow[q] * 128
            load_halo(P, raw[s][q], pj[r:r + 128, :], t0, n, [M.Bpj], [Braw[s][q]], f"sraw{s}{q}")
        for q in range(2):
            r = (12 + q) * 128
            DMA(P, "sp", zr[s][q], pj[r:r + 128, t0:t0 + n], [M.Bpj], [Bzr[s][q]], f"sz{s}{q}")
        DMA(P, "sp", dtr[s][0:1, :].rearrange("o (r t) -> o r t", r=4),
            pj[PJ_SMALL + 5:PJ_SMALL + 9, t0:t0 + n].rearrange("(o r) t -> o r t", o=1), [M.Bpj], [Bdtr[s]], f"sdt{s}")

    load(0)
    it = 0
    for si in range(nst):
        s = si % 2
        t0 = si * n
        if si + 1 < nst:
            load(si + 1)
        for q in range(4):
            conv_silu(P, cv[q], raw[s][q], sconv[:, q * 4:q * 4 + 4], sbias[:, q:q + 1], n,
                      [Braw[s][q], Bsc, Bsb], [Bcv[q]])
        for q in range(2):
            ACT(P, zr[s][q], zr[s][q], AF.Silu, [Bzr[s][q]], [Bzr[s][q]])
        d = dtr[s]
        if DEBUG_CUT == 1:
            continue
        for p in range(4):
            ACT(P, d[0:1, p * n:(p + 1) * n], d[0:1, p * n:(p + 1) * n], AF.Exp, [Bdtr[s], Bdtb], [Bdtr[s]],
                bias=sdtb[0:1, p:p + 1])
        ACT(P, d[0:1, :], d[0:1, :], AF.Ln, [Bdtr[s]], [Bdtr[s]], bias=1.0)
        for p in range(4):
            TS(P, "dve", dA[0:1, p * n:(p + 1) * n], d[0:1, p * n:(p + 1) * n], negA[0:1, p:p + 1], None,
               ALU.mult, None, [Bdtr[s], BnA], [BdA])
        P.op("dve", lambda e: e.tensor_tensor_scan(out=ac[0:1, :], data0=reset[0:1, :], data1=dA[0:1, :],
                                                   initial=0.0, op0=ALU.mult, op1=ALU.add),
             [BdA, Brs], [Bac])
        if DEBUG_CUT == 2:
            continue
        for c in range(4):
            l0 = c * 128
            TR(P, ps[7][:, 0:128], cv[2][:, l0:l0 + 128], M.ident[:], [Bcv[2], M.Bid], [Bps[7]])
            TR(P, ps[7][:, 128:256], cv[0][:, l0:l0 + 128], M.ident[:], [Bcv[0], M.Bid], [Bps[7]])
            TR(P, ps[7][:, 256:384], cv[1][:, l0:l0 + 128], M.ident[:], [Bcv[1], M.Bid], [Bps[7]])
            CP(P, "act", tok, ps[7][:, 0:384], [Bps[7]], [Btok])
            MM(P, ps[6][:, 0:128], cv[2][:, l0:l0 + 128], cv[3][:, l0:l0 + 128], True, True,
               [Bcv[2], Bcv[3]], [Bps[6]])
            if DEBUG_CUT == 3:
                continue
            def head_ops(p):
                o = p * n + l0
                b = p
                pbk = 4 + p % 2
                c0 = (p // 2) * 256
                MM(P, ps[pbk][:, c0:c0 + 128], M.ones[0:1, 0:128], ac[0:1, o:o + 128], True, True,
                   [M.Bones, Bac], [Bps[pbk]])
                MM(P, ps[pbk][:, c0 + 128:c0 + 129], ac[0:1, o:o + 128], M.ones[0:1, 0:1], True, True,
                   [M.Bones, Bac], [Bps[pbk]])
                MM(P, ps[pbk][:, c0 + 129:c0 + 130], d[0:1, o:o + 128], M.ones[0:1, 0:1], True, True,
                   [M.Bones, Bdtr[s]], [Bps[pbk]])
                yield
                CP(P, "dve", cl3[b], ps[pbk][:, c0 + 127:c0 + 130], [Bps[pbk]], [Bcl[b]])
                ACT(P, e1[b], ps[pbk][:, c0:c0 + 128], AF.Exp, [Bps[pbk]], [Be1[b]])
                yield
                STT(P, sg[b], ps[pbk][:, c0:c0 + 128], cl3[b][:, 1:2], negtri[:], ALU.subtract, ALU.add,
                    [Bps[pbk], Bcl[b], Bnt], [Bsg[b]])
                TS(P, "pool", xdt[b], tok[:, 128 + p * 64:128 + (p + 1) * 64], cl3[b][:, 2:3], None, ALU.mult, None,
                   [Btok, Bcl[b]], [Bxdt[b]])
                ACT(P, decl[b], cl3[b][:, 1:2], AF.Exp, [Bcl[b]], [Bdecl[b]], bias=cl3[b][:, 0:1], scale=-1.0)
                yield
                ACT(P, sg[b], sg[b], AF.Exp, [Bsg[b]], [Bsg[b]])
                TS(P, "pool", xdd[b], xdt[b], decl[b][:, 0:1], None, ALU.mult, None, [Bxdt[b], Bdecl[b]], [Bxdd[b]])
                TT(P, "pool", cdec[b], cv[3][:, l0:l0 + 128], e1[b], ALU.mult, [Bcv[3], Be1[b]], [Bcdec[b]])
                yield
                TT(P, "dve", sc[b], ps[6][:, 0:128], sg[b], ALU.mult, [Bps[6], Bsg[b]], [Bscb[b]])
                yield
                pr = p // 2
                yo_ = ps[pr][(p % 2) * 64:(p % 2) * 64 + 64, 0:128]
                MM(P, yo_, xdt[b], sc[b], True, False, [Bxdt[b], Bscb[b]], [Bps[pr]])
                MM(P, yo_, sT[:, p * 64:(p + 1) * 64], cdec[b], False, True, [BsT[p], Bcdec[b]], [Bps[pr]])
                pd = 2 + p % 2
                MM(P, ps[pd][:, (p // 2) * 64:(p // 2) * 64 + 64], tok[:, 0:128], xdd[b], True, True, [Btok, Bxdd[b]], [Bps[pd]])
                yield
                STT(P, sT[:, p * 64:(p + 1) * 64], sT[:, p * 64:(p + 1) * 64], e1[b][:, 127:128], ps[pd][:, (p // 2) * 64:(p // 2) * 64 + 64],
                    ALU.mult, ALU.add, [BsT[p], Be1[b], Bps[pd]], [BsT[p]])

            gens = [head_ops(p) for p in range(4)]
            while gens:
                for g_ in list(gens):
                    try:
                        next(g_)
                    except StopIteration:
                        gens.remove(g_)
            for pr in range(2):
                STT(P, yst[pr][:, l0:l0 + 128], cv[pr][:, l0:l0 + 128], sD[:, pr:pr + 1], ps[pr][:, 0:128],
                    ALU.mult, ALU.add, [Bcv[pr], BsD, Bps[pr]], [Byst[pr]])
        for pr in range(2):
            TT(P, "pool", yst[pr], yst[pr], zr[s][pr], ALU.mult, [Byst[pr], Bzr[s][pr]], [Byst[pr]])
            DMA(P, "sp", M.yo[256 + pr * 128:256 + (pr + 1) * 128, t0:t0 + n], yst[pr], [Byst[pr]], [], f"yc{pr}")
    P.barrier()


def build_M(T=8192, stages=("ssd", "gdn", "nsa")):
    nc = bass.Bass("TRN2", target_bir_lowering=False)
    P = Prog(nc)
    M = MCtx(P, nc, T)
    idd = M.din("ident", [128, 128])
    DMA(P, "sp", M.ident[:], idd, [], [M.Bid], "ident")
    M.yo = nc.dram_tensor("yo", [512, T], F32, kind="ExternalOutput").ap()
    m_inproj(M)
    if "ssd" in stages:
        m_ssd(M)
    if "gdn" in stages:
        m_gdn(M)
    if "nsa" in stages:
        m_nsa(M)
    P.emit()
    return nc


def gl(g, ncol=DC):
    return np.ascontiguousarray(np.asarray(g, np.float32).reshape(ncol, 128).T)


def m_cols(h):
    g = h // 2
    cols = []
    for base in (0 + 128 * h, 512 + 128 * h, 1024 + 128 * h, 1536 + 128 * h,
                 2056 + 128 * h, 2056 + 128 * (h ^ 1), 2568 + 128 * g, 2824 + 128 * g,
                 3080 + 128 * g, 3336 + 128 * g, 3592 + 128 * g, 3848 + 128 * g,
                 4116 + 256 * h, 4116 + 256 * h + 128, 5140 + 256 * h, 5140 + 256 * h + 128,
                 5140 + 1024 + 128 * g, 5140 + 1280 + 128 * g):
        cols += list(range(base, base + 128))
    cols += [2048 + h, 2052 + h, 4104 + 3 * h, 4104 + 3 * h + 1, 4104 + 3 * h + 2]
    cols += [6676 + 4 * h + i for i in range(4)]
    return np.array(cols)


def m_consts(T):
    c = {}
    c["ident"] = np.eye(128, dtype=np.float32)
    p = np.arange(128)[:, None]
    f = np.arange(128)[None, :]
    c["negtriS"] = np.where(f < p, NEGBIG, 0.0).astype(np.float32)
    r = np.ones((1, 2048), np.float32)
    r[0, ::128] = 0.0
    c["reset128"] = r
    r = np.ones((1, 512), np.float32)
    r[0, ::64] = 0.0
    c["reset64"] = r
    p = np.arange(64)[:, None]
    f = np.arange(64)[None, :]
    c["gmaskU"] = np.tile(np.where(f < p, NEGBIG, 0.0).astype(np.float32), (1, 8))
    c["gmaskL"] = np.tile(np.where(f >= p, -NEGBIG, 0.0).astype(np.float32), (1, 8))
    c["gstrict"] = np.tile((f > p).astype(np.float32), (1, 8))
    c["gident8"] = np.tile(np.eye(64, dtype=np.float32), (1, 8))
    c.update(nsa_consts())
    return c


def m_layer_inputs(prm, l, h, T):
    g = h // 2
    d = {}
    d["gmix"] = gl(prm["g_mix"][l])
    d["wm"] = np.ascontiguousarray(prm["w_in"][l][:, m_cols(h)])
    cw = prm["ssm_conv_w"][l]
    cb = prm["ssm_conv_b"][l]
    chans = [256 * h + np.arange(128), 256 * h + 128 + np.arange(128), 1024 + 128 * g + np.arange(128),
             1280 + 128 * g + np.arange(128)]
    d["sconv"] = np.ascontiguousarray(np.concatenate([cw[:, ch].T for ch in chans], axis=1))
    d["sbias"] = np.ascontiguousarray(np.stack([cb[ch] for ch in chans], axis=1))
    d["sdtb"] = np.ascontiguousarray(prm["ssm_dt_bias"][l][4 * h:4 * h + 4][None, :])
    d["salog"] = np.ascontiguousarray(prm["ssm_a_log"][l][4 * h:4 * h + 4][None, :])
    dd = prm["ssm_d"][l][4 * h:4 * h + 4]
    d["sD"] = np.ascontiguousarray(np.stack([np.repeat(dd[0:2], 64), np.repeat(dd[2:4], 64)], axis=1))
    gcw = prm["gdn_conv"][l]
    d["gconv"] = np.ascontiguousarray(np.concatenate([gcw[:, q * 512 + 128 * h + np.arange(128)].T for q in range(3)], axis=1))
    d["gsc"] = np.array([[prm["gdn_a_log"][l][h], prm["gdn_dt_bias"][l][h]]], np.float32)
    d["gnorm"] = np.ascontiguousarray(prm["gdn_norm"][l][:, None])
    d["nqn"] = np.ascontiguousarray(prm["nsa_q_norm"][l][:, None])
    d["nkn"] = np.ascontiguousarray(prm["nsa_k_norm"][l][:, None])
    d["pekT"] = np.ascontiguousarray(prm["nsa_pe_k"][l].T)
    d["pevT"] = np.ascontiguousarray(prm["nsa_pe_v"][l].T)
    d["wck"] = np.ascontiguousarray(prm["nsa_w_ck"][l].transpose(1, 0, 2).reshape(128, 4096))
    d["wcv"] = np.ascontiguousarray(prm["nsa_w_cv"][l].transpose(1, 0, 2).reshape(128, 4096))
    rt = prm["rel_table"]
    d["ntab"] = np.ascontiguousarray(rt[:, [h, h ^ 1]])
    d["nt31"] = np.ascontiguousarray(np.tile(rt[31, [h, h ^ 1]][None, :], (128, 1)))
    return d


def m_gdn(M):
    P, T, A = M.P, M.T, M.A
    n = 512
    NCK = 8
    gconv, Bgc = M.const("gconv", [128, 12])
    gsc, Bgs = M.const("gsc", [1, 2])
    gnorm, Bgn = M.const("gnorm", [128, 1])
    mU, BmU = M.const("gmaskU", [64, 512])
    mL, BmL = M.const("gmaskL", [64, 512])
    mS, BmS = M.const("gstrict", [64, 512])
    reset, Brs = M.const("reset64", [1, 512])
    id8, Bid8 = M.const("gident8", [64, 512])
    pj = M.pj
    ps, Bps = M.ps, M.Bps
    A.reset()
    S = A.f32(128)
    BS = P.buf("S")
    MS(P, "dve", S, 0.0, [BS])
    negA = P.sb("gnegA", [1, 1], F32)
    BnA = P.buf("gnegA")
    ACT(P, negA[:], gsc[0:1, 0:1], AF.Exp, [Bgs], [BnA])
    TS(P, "dve", negA[:], negA[:], -1.0, None, ALU.mult, None, [BnA], [BnA])
    raw = [[A.f32(n + 3) for _ in range(3)] for _ in range(2)]
    Braw = [P.bufs(3, f"graw{i}") for i in range(2)]
    zr = [A.f32(n) for _ in range(2)]
    Bzr = P.bufs(2, "gz")
    abr = [A.f32(2 * n) for _ in range(2)]
    Bab = P.bufs(2, "gab")
    cv = [A.f32(n) for _ in range(3)]
    Bcv = P.bufs(3, "gcv")
    sq = A.f32(n)
    Bsq = P.buf("gsq")
    rs = A.f32(n)
    Brs_ = P.buf("grs")
    gcr = A.f32(n)
    Bgcr = P.buf("gcr")
    ktok = A.f32(NCK * 128)
    vtok = A.f32(NCK * 128)
    Bktok, Bvtok = P.buf("ktok"), P.buf("vtok")
    cols = A.f32(NCK * 4)
    Bcols = P.buf("gcols")
    ex = A.f32(NCK * 4)
    Bex = P.buf("gex")
    E1 = A.f32(n)
    BE1 = P.buf("gE1")
    dm = A.f32(n)
    Bdm = P.buf("gdm")
    dmT = A.f32(n)
    BdmT = P.buf("gdmT")
    X = [A.f32(n) for _ in range(2)]
    Xt = [A.f32(n) for _ in range(2)]
    BX = P.bufs(2, "gX")
    BXt = P.bufs(2, "gXt")
    Rm = A.f32(n)
    BR = P.buf("gR")
    attn = A.f32(n)
    Battn = P.buf("gattn")
    kb = A.f32(NCK * 128)
    vb = A.f32(NCK * 128)
    kdec = A.f32(NCK * 128)
    Bkb, Bvb, Bkdec = P.buf("kb"), P.buf("vb"), P.buf("kdec")
    qd = A.f32(n)
    Bqd = P.buf("qd")
    u = A.f32(NCK * 128)
    Bu = P.buf("gu")
    wT = A.f32(n)
    BwT = P.buf("gwT")
    vnew = [A.f32(128) for _ in range(2)]
    Bvn = P.bufs(2, "gvn")
    osb = A.f32(n)
    Bosb = P.buf("gosb")
    nst = T // n

    def load(si):
        s = si % 2
        t0 = si * n
        for q in range(3):
            load_halo(P, raw[s][q], pj[q * 128:(q + 1) * 128, :], t0, n, [M.Bpj], [Braw[s][q]], f"graw{s}{q}")
        DMA(P, "sp", zr[s], pj[3 * 128:4 * 128, t0:t0 + n], [M.Bpj], [Bzr[s]], f"gz{s}")
        DMA(P, "sp", abr[s][0:1, :].rearrange("o (r t) -> o r t", r=2),
            pj[PJ_SMALL:PJ_SMALL + 2, t0:t0 + n].rearrange("(o r) t -> o r t", o=1), [M.Bpj], [Bab[s]], f"gab{s}")

    load(0)
    for si in range(nst):
        s = si % 2
        t0 = si * n
        if si + 1 < nst:
            load(si + 1)
        for q in range(3):
            conv_silu(P, cv[q], raw[s][q], gconv[:, q * 4:q * 4 + 4], None, n, [Braw[s][q], Bgc], [Bcv[q]])
        ACT(P, zr[s], zr[s], AF.Silu, [Bzr[s]], [Bzr[s]])
        for q in range(2):
            ACT(P, sq, cv[q], AF.Square, [Bcv[q]], [Bsq])
            MM(P, ps[7][:, :n], M.ones[:], sq, True, True, [M.Bones, Bsq], [Bps[7]])
            ACT(P, rs, ps[7][:, :n], AF.Sqrt, [Bps[7]], [Brs_], bias=EPS)
            P.op("dve", lambda e: e.reciprocal(out=rs, in_=rs), [Brs_], [Brs_])
            if q == 0:
                STT(P, cv[q], cv[q], 128.0 ** -0.5, rs, ALU.mult, ALU.mult, [Bcv[q], Brs_], [Bcv[q]])
            else:
                TT(P, "dve", cv[q], cv[q], rs, ALU.mult, [Bcv[q], Brs_], [Bcv[q]])
        if DEBUG_CUT == 12:
            for q in range(3):
                DMA(P, "sp", M.yo[128 + q * 128:256 + q * 128, t0:t0 + n], cv[q], [Bcv[q]], [], f"dbg{q}")
        ar = abr[s][0:1, 0:n]
        br = abr[s][0:1, n:2 * n]
        ACT(P, ar, ar, AF.Exp, [Bab[s], Bgs], [Bab[s]], bias=gsc[0:1, 1:2])
        ACT(P, ar, ar, AF.Ln, [Bab[s]], [Bab[s]], bias=1.0)
        TS(P, "dve", ar, ar, negA[0:1, 0:1], None, ALU.mult, None, [Bab[s], BnA], [Bab[s]])
        ACT(P, br, br, AF.Sigmoid, [Bab[s]], [Bab[s]])
        P.op("dve", lambda e, ar=ar: e.tensor_tensor_scan(out=gcr[0:1, :], data0=reset[0:1, :], data1=ar,
                                                          initial=0.0, op0=ALU.mult, op1=ALU.add),
             [Bab[s], Brs], [Bgcr])
        MM(P, ps[7][:, :n], M.ones[0:1, 0:128], gcr[0:1, :], True, True, [M.Bones, Bgcr], [Bps[7]])
        ACT(P, E1, ps[7][:, :n], AF.Exp, [Bps[7]], [BE1])
        for i in range(NCK):
            c0 = i * 64
            MM(P, ps[6][0:64, 2 * i:2 * i + 1], gcr[0:1, c0:c0 + 64], M.ones[0:1, 0:1], True, True,
               [M.Bones, Bgcr], [Bps[6]])
            MM(P, ps[6][0:64, 2 * i + 1:2 * i + 2], br[:, c0:c0 + 64], M.ones[0:1, 0:1], True, True,
               [M.Bones, Bab[s]], [Bps[6]])
        cv3 = cols[0:64, :].rearrange("p (i c) -> p i c", c=4)
        CP(P, "dve", cv3[:, :, 0:2], ps[6][0:64, 0:2 * NCK].rearrange("p (i c) -> p i c", c=2), [Bps[6]], [Bcols])
        CP(P, "dve", cv3[:, :, 2:3], ps[7][0:64, :n].rearrange("p (i c) -> p i c", c=64)[:, :, 63:64],
           [Bps[7]], [Bcols])
        ex3 = ex[0:64, :].rearrange("p (i c) -> p i c", c=4)
        ACT(P, ex3[:, :, 0:1], cv3[:, :, 0:1], AF.Exp, [Bcols], [Bex])
        TT(P, "dve", ex3[:, :, 0:1], ex3[:, :, 0:1], cv3[:, :, 1:2], ALU.mult, [Bex, Bcols], [Bex])
        TT(P, "dve", ex3[:, :, 1:2], cv3[:, :, 2:3], cv3[:, :, 0:1], ALU.subtract, [Bcols], [Bex])
        ACT(P, ex3[:, :, 1:2], ex3[:, :, 1:2], AF.Exp, [Bex], [Bex])
        TS(P, "dve", ex3[:, :, 2:3], cv3[:, :, 1:2], -1.0, None, ALU.mult, None, [Bcols], [Bex])
        for i in range(NCK):
            c0 = i * 64
            bk = 0 + i // 4
            TR(P, ps[bk][0:64, (i % 4) * 128:(i % 4 + 1) * 128], cv[1][:, c0:c0 + 64], M.ident[:],
               [Bcv[1], M.Bid], [Bps[bk]])
        for i in range(NCK):
            c0 = i * 64
            bk = 2 + i // 4
            TR(P, ps[bk][0:64, (i % 4) * 128:(i % 4 + 1) * 128], cv[2][:, c0:c0 + 64], M.ident[:],
               [Bcv[2], M.Bid], [Bps[bk]])
        for hh in range(2):
            CP(P, "act", ktok[0:64, hh * 512:(hh + 1) * 512], ps[hh][0:64, :], [Bps[hh]], [Bktok])
            CP(P, "dve", vtok[0:64, hh * 512:(hh + 1) * 512], ps[2 + hh][0:64, :], [Bps[2 + hh]], [Bvtok])
        k3 = ktok[0:64, :].rearrange("p (i c) -> p i c", c=128)
        v3 = vtok[0:64, :].rearrange("p (i c) -> p i c", c=128)
        TT(P, "dve", kb[0:64, :].rearrange("p (i c) -> p i c", c=128), k3,
           ex3[:, :, 0:1].to_broadcast([64, NCK, 128]), ALU.mult, [Bktok, Bex], [Bkb])
        TT(P, "pool", vb[0:64, :].rearrange("p (i c) -> p i c", c=128), v3,
           cv3[:, :, 1:2].to_broadcast([64, NCK, 128]), ALU.mult, [Bvtok, Bcols], [Bvb])
        TT(P, "pool", kdec[0:64, :].rearrange("p (i c) -> p i c", c=128), k3,
           ex3[:, :, 1:2].to_broadcast([64, NCK, 128]), ALU.mult, [Bktok, Bex], [Bkdec])
        TT(P, "dve", attn[0:64, :].rearrange("p (i c) -> p i c", c=64),
           ps[7][0:64, :n].rearrange("p (i c) -> p i c", c=64),
           cv3[:, :, 0:1].to_broadcast([64, NCK, 64]), ALU.subtract, [Bps[7], Bcols], [Battn])
        TT(P, "dve", dm[0:64, :], attn[0:64, :], mU[:, :], ALU.add, [Battn, BmU], [Bdm])
        TT(P, "pool", dmT[0:64, :], attn[0:64, :], mL[:, :], ALU.add, [Battn, BmL], [BdmT])
        ACT(P, dm[0:64, :], dm[0:64, :], AF.Exp, [Bdm], [Bdm])
        ACT(P, dmT[0:64, :], dmT[0:64, :], AF.Exp, [BdmT], [BdmT], scale=-1.0)
        for i in range(NCK):
            sl = slice(i * 64, (i + 1) * 64)
            MM(P, ps[4][0:64, sl], cv[1][:, sl], cv[1][:, sl], True, True, [Bcv[1]], [Bps[4]])
        for i in range(NCK):
            sl = slice(i * 64, (i + 1) * 64)
            MM(P, ps[5][0:64, sl], cv[1][:, sl], cv[0][:, sl], True, True, [Bcv[1], Bcv[0]], [Bps[5]])
        MM(P, ps[6][0:64, :n], M.ones[0:1, 0:64], br, True, True, [M.Bones, Bab[s]], [Bps[6]])
        STT(P, X[0][0:64, :], ps[4][0:64, :n], -1.0, dm[0:64, :], ALU.mult, ALU.mult, [Bps[4], Bdm], [BX[0]])
        TT(P, "dve", X[0][0:64, :], X[0][0:64, :], ps[6][0:64, :n], ALU.mult, [BX[0], Bps[6]], [BX[0]])
        TT(P, "pool", X[0][0:64, :], X[0][0:64, :], mS[:, :], ALU.mult, [BX[0], BmS], [BX[0]])
        TT(P, "dve", Xt[0][0:64, :], ps[4][0:64, :n], dmT[0:64, :], ALU.mult, [Bps[4], BdmT], [BXt[0]])
        TT(P, "dve", Xt[0][0:64, :].rearrange("p (i c) -> p i c", c=64),
           Xt[0][0:64, :].rearrange("p (i c) -> p i c", c=64),
           ex3[:, :, 2:3].to_broadcast([64, NCK, 64]), ALU.mult, [BXt[0], Bex], [BXt[0]])
        TT(P, "dve", attn[0:64, :], ps[5][0:64, :n], dm[0:64, :], ALU.mult, [Bps[5], Bdm], [Battn])
        TT(P, "pool", Rm[0:64, :], X[0][0:64, :], id8[:, :], ALU.add, [BX[0], Bid8], [BR])
        cur = 0
        for j in range(5):
            nx = 1 - cur
            for i in range(NCK):
                sl = slice(i * 64, (i + 1) * 64)
                MM(P, ps[0][0:64, sl], Xt[cur][0:64, sl], X[cur][0:64, sl], True, True, [BXt[cur], BX[cur]], [Bps[0]])
            for i in range(NCK):
                sl = slice(i * 64, (i + 1) * 64)
                MM(P, ps[1][0:64, sl], X[cur][0:64, sl], Xt[cur][0:64, sl], True, True, [BXt[cur], BX[cur]], [Bps[1]])
            CP(P, "act", X[nx][0:64, :], ps[0][0:64, :n], [Bps[0]], [BX[nx]])
            CP(P, "dve", Xt[nx][0:64, :], ps[1][0:64, :n], [Bps[1]], [BXt[nx]])
            for i in range(NCK):
                sl = slice(i * 64, (i + 1) * 64)
                MM(P, ps[2][0:64, sl], Xt[nx][0:64, sl], Rm[0:64, sl], True, True, [BXt[nx], BR], [Bps[2]])
            TT(P, "dve", Rm[0:64, :], Rm[0:64, :], ps[2][0:64, :n], ALU.add, [BR, Bps[2]], [BR])
            cur = nx
        for i in range(NCK):
            sl = slice(i * 64, (i + 1) * 64)
            bk = i // 4
            MM(P, ps[bk][0:64, (i % 4) * 128:(i % 4 + 1) * 128], Rm[0:64, sl], vb[0:64, i * 128:(i + 1) * 128],
               True, True, [BR, Bvb], [Bps[bk]])
        for hh in range(2):
            CP(P, "act" if hh else "dve", u[0:64, hh * 512:(hh + 1) * 512], ps[hh][0:64, :], [Bps[hh]], [Bu])
        for i in range(NCK):
            sl = slice(i * 64, (i + 1) * 64)
            MM(P, ps[2][:, sl], kb[0:64, i * 128:(i + 1) * 128], Rm[0:64, sl], True, True, [Bkb, BR], [Bps[2]])
        CP(P, "act", wT, ps[2][:, :n], [Bps[2]], [BwT])
        TT(P, "pool", qd, cv[0], E1, ALU.mult, [Bcv[0], BE1], [Bqd])
        for i in range(NCK if DEBUG_CUT != 31 else 0):
            sl = slice(i * 64, (i + 1) * 64)
            s128 = slice(i * 128, (i + 1) * 128)
            vb_ = i % 2
            MM(P, ps[3][0:64, 0:128], wT[:, sl], S, True, True, [BwT, BS], [Bps[3]])
            TT(P, "dve", vnew[vb_][0:64, :], u[0:64, s128], ps[3][0:64, 0:128], ALU.subtract, [Bu, Bps[3]], [Bvn[vb_]])
            MM(P, ps[5][:, sl], S, qd[:, sl], True, False, [BS, Bqd], [Bps[5]])
            MM(P, ps[5][:, sl], vnew[vb_][0:64, :], attn[0:64, sl], False, True, [Bvn[vb_], Battn], [Bps[5]])
            MM(P, ps[4][:, 0:128], kdec[0:64, s128], vnew[vb_][0:64, :], True, True, [Bkdec, Bvn[vb_]], [Bps[4]])
            STT(P, S, S, E1[:, i * 64 + 63:i * 64 + 64], ps[4][:, 0:128], ALU.mult, ALU.add, [BS, BE1, Bps[4]], [BS])
        CP(P, "act", osb, ps[5][:, :n], [Bps[5]], [Bosb])
        ACT(P, sq, osb, AF.Square, [Bosb], [Bsq])
        MM(P, ps[7][:, :n], M.ones[:], sq, True, True, [M.Bones, Bsq], [Bps[7]])
        ACT(P, rs, ps[7][:, :n], AF.Sqrt, [Bps[7]], [Brs_], bias=EPS, scale=1.0 / 128)
        P.op("dve", lambda e: e.reciprocal(out=rs, in_=rs), [Brs_], [Brs_])
        STT(P, osb, osb, gnorm[:, 0:1], rs, ALU.mult, ALU.mult, [Bosb, Bgn, Brs_], [Bosb])
        TT(P, "dve", osb, osb, zr[s], ALU.mult, [Bosb, Bzr[s]], [Bosb])
        DMA(P, "sp", M.yo[0:128, t0:t0 + n], osb, [Bosb], [], "ya")
        if DEBUG_CUT == 11:
            P.barrier()
    P.barrier()


def MMA(P, out, lhsT, rhs, reads, writes):
    P.op("pe", lambda e: e.matmul(out, lhsT=lhsT, rhs=rhs, start=False, stop=False, skip_group_check=True),
         reads, writes, pe_acc=True)


PD_S = 2048
PD_C = 5120
W_S = 1792
W_W = 1408
W_C = 3072


def m_nsa(M):
    P, T, A, nc = M.P, M.T, M.A, M.nc
    n = 512
    NKT = T // 128
    NQ = T // n
    pj = M.pj
    ps, Bps = M.ps, M.Bps
    qn, Bqn = M.const("nqn", [128, 1])
    kn, Bkn = M.const("nkn", [128, 1])
    pekT, Bpek = M.const("pekT", [128, 32])
    pevT, Bpev = M.const("pevT", [128, 32])
    tabs, Btabs = M.const("ntab", [32, 2])
    t31, Bt31 = M.const("nt31", [128, 2])
    wck_d = M.din("wck", [128, 4096])
    wcv_d = M.din("wcv", [128, 4096])
    ohS = M.din("ohS", [32, PD_S]); ngS = M.din("ngS", [1, PD_S])
    ohW = M.din("ohW", [32, PD_S]); ngW = M.din("ngW", [1, PD_S])
    ohC = M.din("ohC", [32, PD_C]); ngC = M.din("ngC", [1, PD_C])
    E_d = M.din("Esel", [128, 8192])
    wimp_d = M.din("wimp", [128, 512])
    dS = nc.dram_tensor("dS", [129 * PD_S], F32)
    dW = nc.dram_tensor("dW", [129 * PD_S], F32)
    dCM = nc.dram_tensor("dCM", [129 * PD_C], F32)
    dCO = nc.dram_tensor("dCO", [129 * PD_C], F32)
    A.reset()
    BTs = A.f32(W_S); BTw = A.f32(W_W); BTcM = A.f32(W_C); BTcO = A.f32(W_C)
    BBTs, BBTw, BBTcM, BBTcO = P.buf("BTs"), P.buf("BTw"), P.buf("BTcM"), P.buf("BTcO")
    mark0 = A.off
    fb = A.f32(PD_C)
    Bfb = P.buf("fb")
    frow = A.f32(PD_C)
    Bfrow = P.buf("frow")
    oh = A.f32(PD_C)
    Boh = P.buf("oh")
    ngt = A.f32(PD_C)
    Bng = P.buf("ng")

    def strip(oh_d, ng_d, Pd, tcol, dram, dst, Bdst, W, step, base, key):
        DMA(P, "sp", oh[0:32, 0:Pd], oh_d, [], [Boh], "oh")
        DMA(P, "sp", ngt[0:1, 0:Pd], ng_d, [], [Bng], "ng")
        for b0 in range(0, Pd, 512):
            MM(P, ps[0][0:1, :], tabs[:, tcol:tcol + 1], oh[0:32, b0:b0 + 512], True, True, [Btabs, Boh], [Bps[0]])
            TT(P, "dve", frow[0:1, b0:b0 + 512], ps[0][0:1, :], ngt[0:1, b0:b0 + 512], ALU.add,
               [Bps[0], Bng], [Bfrow])
        for b0 in range(0, Pd, 512):
            pb = 1 + (b0 // 512) % 2
            MM(P, ps[pb][:, :], M.ones[0:1, 0:128], frow[0:1, b0:b0 + 512], True, True, [M.Bones, Bfrow], [Bps[pb]])
            CP(P, "act" if pb == 1 else "dve", fb[:, b0:b0 + 512], ps[pb][:, :], [Bps[pb]], [Bfb])
        Bd = P.buf("d" + key)
        dap = dram.ap()
        DMA(P, "sp", dap[0:128 * Pd].rearrange("(p c) -> p c", c=Pd), fb[:, 0:Pd], [Bfb], [Bd], "dw" + key)
        DMA(P, "sp", dap[128 * Pd:129 * Pd].rearrange("(p c) -> p c", c=Pd), fb[0:1, 0:Pd], [Bfb], [Bd], "dw" + key)
        src = bass.AP(tensor=dap.tensor, offset=base, ap=[[Pd - step, 128], [1, W]])
        DMA(P, "sp", dst, src, [Bd], [Bdst], "bt" + key)

    strip(ohS, ngS, PD_S, 0, dS, BTs, BBTs, W_S, 1, 127, "S")
    strip(ohW, ngW, PD_S, 0, dW, BTw, BBTw, W_W, 1, 127, "W")
    strip(ohC, ngC, PD_C, 0, dCM, BTcM, BBTcM, W_C, 16, 2032, "CM")
    strip(ohC, ngC, PD_C, 1, dCO, BTcO, BBTcO, W_C, 16, 2032, "CO")
    P.barrier()
    A.off = mark0
    ksT = A.bf16(T); kwT = A.bf16(T)
    Bks, Bkw = P.buf("ksT"), P.buf("kwT")
    Vs = A.bf16(NKT * 129); Vw = A.bf16(NKT * 129)
    BVs, BVw = P.buf("Vs"), P.buf("Vw")
    Eb = A.bf16(T)
    BE = P.buf("E")
    kcmpT = A.f32(512)
    Bkcmp = P.buf("kcmp")
    WV = A.f32(4 * 256)
    BWV = P.buf("WV")
    Wimp = A.f32(512)
    BWimp = P.buf("Wimp")
    mark = A.off
    DMA(P, "pool", Eb, E_d[:, 0:T], [], [BE], "E")
    DMA(P, "sp", Wimp, wimp_d, [], [BWimp], "wimp")
    DMA(P, "sp", WV.rearrange("p (c x) -> p c x", c=4)[:, :, 0:128], wimp_d.rearrange("p (c x) -> p c x", c=4),
        [], [BWV], "wv")
    MS(P, "pool", Vs.rearrange("p (k x) -> p k x", x=129)[:, :, 128:129], 1.0, [BVs])
    MS(P, "pool", Vw.rearrange("p (k x) -> p k x", x=129)[:, :, 128:129], 1.0, [BVw])
    kraw = [A.f32(n) for _ in range(2)]
    Bkraw = P.bufs(2, "kraw")
    sq = A.f32(n); Bsq = P.buf("nsq")
    rs = A.f32(n); Brs_ = P.buf("nrs")
    it = 0
    for (chunk, dstT, Bdst) in ((8, ksT, Bks), (10, kwT, Bkw)):
        for ti in range(NQ):
            s = it % 2
            it += 1
            t0 = ti * n
            DMA(P, "sp", kraw[s], pj[chunk * 128:(chunk + 1) * 128, t0:t0 + n], [M.Bpj], [Bkraw[s]], f"kraw{s}")
            ACT(P, sq, kraw[s], AF.Square, [Bkraw[s]], [Bsq])
            MM(P, ps[0][:, :n], M.ones[:], sq, True, True, [M.Bones, Bsq], [Bps[0]])
            ACT(P, rs, ps[0][:, :n], AF.Sqrt, [Bps[0]], [Brs_], bias=EPS, scale=1.0 / 128)
            P.op("dve", lambda e: e.reciprocal(out=rs, in_=rs), [Brs_], [Brs_])
            STT(P, dstT[:, t0:t0 + n], kraw[s], kn[:, 0:1], rs, ALU.mult, ALU.mult, [Bkraw[s], Bkn, Brs_], [Bdst])
    for (chunk, dstV, Bdst) in ((9, Vs, BVs), (11, Vw, BVw)):
        dv3 = dstV.rearrange("p (k x) -> p k x", x=129)
        for ti in range(NQ):
            s = it % 2
            it += 1
            t0 = ti * n
            DMA(P, "sp", kraw[s], pj[chunk * 128:(chunk + 1) * 128, t0:t0 + n], [M.Bpj], [Bkraw[s]], f"kraw{s}")
            pb = 1 + ti % 2
            for c in range(4):
                TR(P, ps[pb][:, c * 128:(c + 1) * 128], kraw[s][:, c * 128:(c + 1) * 128], M.ident[:],
                   [Bkraw[s], M.Bid], [Bps[pb]])
            CP(P, "act" if ti % 2 else "dve", dv3[:, ti * 4:ti * 4 + 4, 0:128],
               ps[pb][:, :].rearrange("p (c x) -> p c x", c=4), [Bps[pb]], [Bdst])
    wc = A.f32(4096)
    Bwc = P.buf("wc")
    kct = [A.f32(1040) for _ in range(2)]
    Bkct = P.bufs(2, "kct")
    cbias = A.f32(1)
    Bcb = P.buf("cbias")
    vcT = A.f32(512)
    BvcT = P.buf("vcT")
    MS(P, "dve", kcmpT, 0.0, [Bkcmp])
    MS(P, "dve", vcT, 0.0, [BvcT])
    for (chunk, w_d, peT, Bpe, dst, Bdst) in ((6, wck_d, pekT, Bpek, kcmpT, Bkcmp), (7, wcv_d, pevT, Bpev, vcT, BvcT)):
        DMA(P, "sp", wc, w_d, [], [Bwc], "wc")
        for l in range(32):
            MM(P, ps[0][:, 0:1], wc[:, l * 128:(l + 1) * 128], peT[:, l:l + 1], l == 0, l == 31, [Bwc, Bpe], [Bps[0]])
        CP(P, "dve", cbias, ps[0][:, 0:1], [Bps[0]], [Bcb])
        ntile = T // 1024
        for ti in range(ntile):
            s = it % 2
            it += 1
            t0 = ti * 1024
            ntok = min(1040, T - t0)
            nb = 64 if ntok == 1040 else 63
            DMA(P, "sp", kct[s][:, 0:ntok], pj[chunk * 128:(chunk + 1) * 128, t0:t0 + ntok], [M.Bpj], [Bkct[s]],
                f"kct{s}")
            pb = 1 + ti % 2
            for l in range(32):
                rhs = kct[s][:, l:l + 16 * (nb - 1) + 1:16]
                MM(P, ps[pb][:, 0:nb], wc[:, l * 128:(l + 1) * 128], rhs, l == 0, l == 31, [Bwc, Bkct[s]], [Bps[pb]])
            TS(P, "dve", dst[:, ti * 64:ti * 64 + nb], ps[pb][:, 0:nb], cbias[:, 0:1], None, ALU.add, None,
               [Bps[pb], Bcb], [Bdst])
    ACT(P, sq, kcmpT, AF.Square, [Bkcmp], [Bsq])
    MM(P, ps[0][:, :n], M.ones[:], sq, True, True, [M.Bones, Bsq], [Bps[0]])
    ACT(P, rs, ps[0][:, :n], AF.Sqrt, [Bps[0]], [Brs_], bias=EPS, scale=1.0 / 128)
    P.op("dve", lambda e: e.reciprocal(out=rs, in_=rs), [Brs_], [Brs_])
    STT(P, kcmpT, kcmpT, kn[:, 0:1], rs, ALU.mult, ALU.mult, [Bkcmp, Bkn, Brs_], [Bkcmp])
    for c in range(4):
        TR(P, ps[1][:, c * 128:(c + 1) * 128], vcT[:, c * 128:(c + 1) * 128], M.ident[:], [BvcT, M.Bid], [Bps[1]])
    CP(P, "dve", WV.rearrange("p (c x) -> p c x", c=4)[:, :, 128:256], ps[1][:, :].rearrange("p (c x) -> p c x", c=4),
       [Bps[1]], [BWV])
    P.barrier()
    A.off = mark
    qraw = [A.f32(n) for _ in range(2)]; Bqraw = P.bufs(2, "qraw")
    qM = A.f32(n); qO = A.f32(n); BqM, BqO = P.buf("qM"), P.buf("qO")
    qMb = A.bf16(n); BqMb = P.buf("qMb")
    glog = A.f32(n); Bglog = P.buf("glog")
    gq = A.f32(12); Bgq = P.buf("gq")
    tmpf = [A.f32(n) for _ in range(3)]; Btmpf = P.bufs(3, "ntmp")
    Pf = [A.f32(n) for _ in range(2)]; BPf = P.bufs(2, "Pf")
    Pb = [A.bf16(n) for _ in range(3)]; BPb = P.bufs(3, "Pb")
    negmT = A.bf16(n); BnegmT = P.buf("negmT")
    imp = A.f32(128); Bimp = P.buf("imp")
    imp2 = A.f32(128); Bimp2 = P.buf("imp2")
    scr = A.f32(128); Bscr = P.buf("scr")
    m8 = A.f32(16); Bm8 = P.buf("m8")
    sm = A.f32(16); Bsm = P.buf("sm")
    negm = A.f32(128); Bnegm = P.buf("negm")
    ocomb = [A.f32(128) for _ in range(4)]; Boc = P.bufs(4, "ocomb")
    ost = A.f32(n); Bost = P.buf("ost")
    ti_ = 0
    pi = 0
    for Q in range(NQ):
        if DEBUG_CUT == 21:
            break
        t0 = Q * n
        for hd, (chunk, dst, Bdst) in enumerate(((4, qM, BqM), (5, qO, BqO))):
            s = hd
            DMA(P, "sp", qraw[s], pj[chunk * 128:(chunk + 1) * 128, t0:t0 + n], [M.Bpj], [Bqraw[s]], f"qraw{s}")
            ACT(P, sq, qraw[s], AF.Square, [Bqraw[s]], [Bsq])
            MM(P, ps[7][:, :n], M.ones[:], sq, True, True, [M.Bones, Bsq], [Bps[7]])
            ACT(P, rs, ps[7][:, :n], AF.Sqrt, [Bps[7]], [Brs_], bias=128 * EPS, scale=1.0)
            P.op("dve", lambda e: e.reciprocal(out=rs, in_=rs), [Brs_], [Brs_])
            STT(P, dst, qraw[s], qn[:, 0:1], rs, ALU.mult, ALU.mult, [Bqraw[s], Bqn, Brs_], [Bdst])
        CP(P, "pool", qMb, qM, [BqM], [BqMb])
        DMA(P, "sp", glog[0:3, :], pj[PJ_SMALL + 2:PJ_SMALL + 5, t0:t0 + n], [M.Bpj], [Bglog], "glog")
        for sb in range(4):
            TR(P, ps[7][:, sb * 3:sb * 3 + 3], glog[0:3, sb * 128:(sb + 1) * 128], M.ident[0:3, 0:3],
               [Bglog, M.Bid], [Bps[7]])
        ACT(P, gq, ps[7][:, 0:12], AF.Sigmoid, [Bps[7]], [Bgq])
        for bk in (2, 3, 4):
            MS(P, "dve", ps[bk][:, :], 0.0, [Bps[bk]])
        def cmp_s1(hd, ct, qh, Bqh, BT, BBT):
            nonlocal pi, ti_
            Mq = Q - 4 * ct
            pb = pi % 2
            pi += 1
            MM(P, ps[pb][:, :n], kcmpT[:, ct * 128:(ct + 1) * 128], qh, True, True, [Bkcmp, Bqh], [Bps[pb]])
            f = ti_ % 2
            ti_ += 1
            if Mq <= 5:
                t = ti_ % 3
                TT(P, "dve", tmpf[t], ps[pb][:, :n], BT[:, 512 * Mq:512 * Mq + 512], ALU.add,
                   [Bps[pb], BBT], [Btmpf[t]])
                ACT(P, Pf[f], tmpf[t], AF.Exp, [Btmpf[t]], [BPf[f]])
            else:
                ACT(P, Pf[f], ps[pb][:, :n], AF.Exp, [Bps[pb], Bt31], [BPf[f]], bias=t31[:, hd:hd + 1])
            return f

        def cmp_s2(hd, ct, f):
            for sb in range(4):
                lhsT = Pf[f][:, sb * 128:(sb + 1) * 128]
                if hd == 0:
                    bk = 2 + sb // 2
                    MMA(P, ps[bk][:, (sb % 2) * 256:(sb % 2) * 256 + 256], lhsT, WV[:, ct * 256:(ct + 1) * 256],
                        [BPf[f], BWV], [Bps[bk]])
                else:
                    MMA(P, ps[4][:, sb * 128:(sb + 1) * 128], lhsT, Wimp[:, ct * 128:(ct + 1) * 128],
                        [BPf[f], BWimp], [Bps[4]])

        prev = None
        for hd, (qh, Bqh, BT, BBT) in enumerate(((qM, BqM, BTcM, BBTcM), (qO, BqO, BTcO, BBTcO))):
            for ct in range(Q // 4 + 1):
                f = cmp_s1(hd, ct, qh, Bqh, BT, BBT)
                if prev is not None:
                    cmp_s2(*prev)
                prev = (hd, ct, f)
        cmp_s2(*prev)
        for sb in range(4):
            qt = 4 * Q + sb
            bk = 2 + sb // 2
            aM = ps[bk][:, (sb % 2) * 256:(sb % 2) * 256 + 128]
            vM = ps[bk][:, (sb % 2) * 256 + 128:(sb % 2) * 256 + 256]
            aO = ps[4][:, sb * 128:(sb + 1) * 128]
            TS(P, "dve", scr, aM, 0.5, 0.0, ALU.mult, ALU.add, [Bps[bk]], [Bscr, Bsm], accum=sm[:, 0:1])
            TS(P, "dve", scr, aO, 0.5, 0.0, ALU.mult, ALU.add, [Bps[4]], [Bscr, Bsm], accum=sm[:, 1:2])
            TS(P, "dve", sm[:, 0:2], sm[:, 0:2], 1e-30, None, ALU.max, None, [Bsm], [Bsm])
            P.op("dve", lambda e: e.reciprocal(out=sm[:, 2:4], in_=sm[:, 0:2]), [Bsm], [Bsm])
            TS(P, "dve", imp, aM, sm[:, 2:3], None, ALU.mult, None, [Bps[bk], Bsm], [Bimp])
            STT(P, imp, aO, sm[:, 3:4], imp, ALU.mult, ALU.add, [Bps[4], Bsm, Bimp], [Bimp])
            TT(P, "dve", sm[:, 4:5], sm[:, 2:3], gq[:, sb * 3:sb * 3 + 1], ALU.mult, [Bsm, Bgq], [Bsm])
            TS(P, "dve", ocomb[sb], vM, sm[:, 4:5], None, ALU.mult, None, [Bps[bk], Bsm], [Boc[sb]])
            MS(P, "pool", imp[:, 0:1], 1e4, [Bimp])
            MS(P, "pool", imp[:, 2 * qt:2 * qt + 1], 1e4, [Bimp])
            if qt > 0:
                MS(P, "pool", imp[0:64, 2 * qt - 1:2 * qt], 1e4, [Bimp])
            MS(P, "pool", imp[64:128, 2 * qt + 1:2 * qt + 2], 1e4, [Bimp])
            P.op("dve", lambda e: e.max(out=m8[:, 0:8], in_=imp), [Bimp], [Bm8])
            P.op("dve", lambda e: e.match_replace(out=imp2, in_to_replace=m8[:, 0:8], in_values=imp, imm_value=-1e30),
                 [Bimp, Bm8], [Bimp2])
            P.op("dve", lambda e: e.max(out=m8[:, 8:16], in_=imp2), [Bimp2], [Bm8])
            TS(P, "dve", negm, imp, m8[:, 15:16], -32768.0, ALU.is_lt, ALU.mult, [Bimp, Bm8], [Bnegm])
            TR(P, ps[7][:, 128:256], negm, M.ident[:], [Bnegm, M.Bid], [Bps[7]])
            CP(P, "act", negmT[:, sb * 128:(sb + 1) * 128], ps[7][:, 128:256], [Bps[7]], [BnegmT])
        if DEBUG_CUT == 22:
            continue
        for br_, (kT, BkT, Vv, BVv, BT, BBT, lo, accb) in enumerate(
                ((ksT, Bks, Vs, BVs, BTs, BBTs, 0, (5, 6)), (kwT, Bkw, Vw, BVw, BTw, BBTw, max(0, 4 * Q - 4), (2, 3)))):
            for bk in accb:
                MS(P, "dve", ps[bk][:, :], 0.0, [Bps[bk]])
            def sw_s1(kt):
                nonlocal pi, ti_
                m = 4 * Q - kt
                pb = pi % 2
                pi += 1
                if br_ == 0:
                    MM(P, ps[pb][:, :n], kT[:, kt * 128:(kt + 1) * 128], qMb, True, False, [BkT, BqMb], [Bps[pb]])
                    MM(P, ps[pb][:, :n], Eb[:, kt * 128:(kt + 1) * 128], negmT, False, True, [BE, BnegmT], [Bps[pb]])
                else:
                    MM(P, ps[pb][:, :n], kT[:, kt * 128:(kt + 1) * 128], qMb, True, True, [BkT, BqMb], [Bps[pb]])
                f = ti_ % 3
                ti_ += 1
                if m <= 7:
                    t = ti_ % 3
                    TT(P, "dve", tmpf[t], ps[pb][:, :n], BT[:, 128 * (m + 3):128 * (m + 3) + 512], ALU.add,
                       [Bps[pb], BBT], [Btmpf[t]])
                    ACT(P, Pb[f], tmpf[t], AF.Exp, [Btmpf[t]], [BPb[f]])
                else:
                    ACT(P, Pb[f], ps[pb][:, :n], AF.Exp, [Bps[pb], Bt31], [BPb[f]], bias=t31[:, 0:1])
                return f

            def sw_s2(kt, f):
                for sb in range(4):
                    if 4 * Q + sb < kt:
                        continue
                    bk = accb[sb // 2]
                    MMA(P, ps[bk][:, (sb % 2) * 129:(sb % 2) * 129 + 129], Pb[f][:, sb * 128:(sb + 1) * 128],
                        Vv[:, kt * 129:(kt + 1) * 129], [BPb[f], BVv], [Bps[bk]])

            prev = None
            for kt in range(lo, 4 * Q + 4):
                f = sw_s1(kt)
                if prev is not None:
                    sw_s2(*prev)
                prev = (kt, f)
            sw_s2(*prev)
            for sb in range(4):
                bk = accb[sb // 2]
                acc = ps[bk][:, (sb % 2) * 129:(sb % 2) * 129 + 129]
                P.op("dve", lambda e, acc=acc: e.reciprocal(out=sm[:, 5:6], in_=acc[:, 128:129]), [Bps[bk]], [Bsm])
                TT(P, "dve", sm[:, 6:7], sm[:, 5:6], gq[:, sb * 3 + 1 + br_:sb * 3 + 2 + br_], ALU.mult, [Bsm, Bgq], [Bsm])
                STT(P, ocomb[sb], acc[:, 0:128], sm[:, 6:7], ocomb[sb], ALU.mult, ALU.add, [Bps[bk], Bsm, Boc[sb]], [Boc[sb]])
        for sb in range(4):
            TR(P, ps[7][:, 256:384], ocomb[sb], M.ident[:], [Boc[sb], M.Bid], [Bps[7]])
            CP(P, "act", ost[:, sb * 128:(sb + 1) * 128], ps[7][:, 256:384], [Bps[7]], [Bost])
        DMA(P, "sp", M.yo[128:256, t0:t0 + n], ost, [Bost], [], "yb")
    P.barrier()


def _bucket(dist):
    n = np.maximum(dist, 0)
    nf = np.maximum(n, 1).astype(np.float32)
    large = 16 + (np.log(nf / np.float32(16)) / np.float32(math.log(1024 / 16)) * np.float32(16)).astype(np.int32)
    large = np.minimum(large, 31)
    return np.where(n < 16, n, large)


_NSA_CONSTS = {}


def nsa_consts():
    if _NSA_CONSTS:
        return _NSA_CONSTS
    c = _NSA_CONSTS
    j = np.arange(PD_S)
    dist = j - 511
    bk = _bucket(dist)
    for nm, valid in (("S", dist >= 0), ("W", (dist >= 0) & (dist < 512))):
        oh = np.zeros((32, PD_S), np.float32)
        oh[bk[valid], j[valid]] = 1.0
        c["oh" + nm] = oh
        c["ng" + nm] = np.where(valid, 0.0, NEGBIG).astype(np.float32)[None, :]
    j = np.arange(PD_C)
    dist = j - 2063
    bk = _bucket(dist)
    valid = dist >= 0
    oh = np.zeros((32, PD_C), np.float32)
    oh[bk[valid], j[valid]] = 1.0
    c["ohC"] = oh
    c["ngC"] = np.where(valid, 0.0, NEGBIG).astype(np.float32)[None, :]
    E = np.zeros((128, 64, 128), np.float32)
    for kt in range(64):
        for k in range(128):
            E[2 * kt + k // 64, kt, k] = 1.0
    c["Esel"] = E.reshape(128, 8192)
    W = np.zeros((4, 128, 128), np.float32)
    wts = (1.0, 2.0, 2.0, 2.0, 1.0)
    for ct in range(4):
        for i in range(128):
            gi = ct * 128 + i
            for jj in range(128):
                w = gi - 4 * jj + 1
                if 0 <= w <= 4 and gi <= 510:
                    W[ct, i, jj] = wts[w]
    c["wimp"] = np.ascontiguousarray(W.transpose(1, 0, 2).reshape(128, 512))
    return c


_PROGS = {}


def _prog(kind):
    if kind not in _PROGS:
        _PROGS[kind] = build_M(8192) if kind == "M" else build_R(kind, 2048)
    return _PROGS[kind]


def _ffn_inputs(prm, l, which, suffix):
    return {"g" + suffix: gl(prm["g_ffn" + which][l]),
            "wu" + suffix: np.ascontiguousarray(prm["w_up" + which][l]),
            "wd" + suffix: np.ascontiguousarray(prm["w_down" + which][l])}


def kernel(**inputs):
    prm = {k: np.asarray(v, dtype=np.float32) for k, v in inputs.items()}
    x = prm.pop("x")
    B, T, D = x.shape
    NCORE = 8
    TC = T // 4
    cores = list(range(NCORE))

    def run(kind, maps):
        res = run_bass_kernel_spmd(_prog(kind), maps, core_ids=cores)
        return res.results

    xs = [np.ascontiguousarray(x[c // 4, (c % 4) * TC:(c % 4 + 1) * TC, :].T) for c in cores]
    shared = _ffn_inputs(prm, 0, "1", "1")
    outs = run("A", [dict(xin=xs[c], **shared) for c in cores])
    x1s = [o["xo"] for o in outs]
    consts = m_consts(T)
    selbs = [np.ascontiguousarray(np.tile(np.eye(2, dtype=np.float32)[c // 4][None, :], (128, 1))) for c in cores]
    n_layers = prm["g_mix"].shape[0]
    for l in range(n_layers):
        x1f = np.empty((2, D, T), np.float32)
        for c in cores:
            x1f[c // 4][:, (c % 4) * TC:(c % 4 + 1) * TC] = x1s[c]
        lay = [m_layer_inputs(prm, l, h, T) for h in range(4)]
        outs = run("M", [dict(x1o=x1f[c // 4], **consts, **lay[c % 4]) for c in cores])
        yfull = np.empty((2, D, T), np.float32)
        for c in cores:
            b, h = c // 4, c % 4
            yo = outs[c]["yo"]
            yfull[b][h * 128:(h + 1) * 128] = yo[0:128]
            yfull[b][512 + h * 128:512 + (h + 1) * 128] = yo[128:256]
            yfull[b][1024 + h * 256:1024 + (h + 1) * 256] = yo[256:512]
        shared = {"gm": gl(prm["g_mix"][l]), "sn": gl(prm["ssm_norm"][l], 8),
                  "wg": np.ascontiguousarray(prm["w_in"][l][:, 6692:]),
                  "pa": np.ascontiguousarray(prm["p_a"][l]), "pb": np.ascontiguousarray(prm["p_b"][l]),
                  "pc": np.ascontiguousarray(prm["p_c"][l]), "wo": np.ascontiguousarray(prm["w_o"][l])}
        shared.update(_ffn_inputs(prm, l, "2", "2"))
        last = l == n_layers - 1
        if not last:
            shared.update(_ffn_inputs(prm, l + 1, "1", "1"))
        maps = [dict(xin=x1s[c],
                     yin=np.ascontiguousarray(yfull[c // 4][:, (c % 4) * TC:(c % 4 + 1) * TC]), **shared)
                for c in cores]
        outs = run("R" if last else "RA", maps)
        x1s = [o["xo"] for o in outs]
    out = np.empty((B, T, D), np.float32)
    for c in cores:
        out[c // 4, (c % 4) * TC:(c % 4 + 1) * TC, :] = x1s[c].T
    return out
```
